# Optimizing a Trainium2 kernel written in Bass

```python
import jax
import jax.numpy as jnp
from jax import lax
import numpy as np

D_MODEL = 2048
BATCH = 4
SEQ = 2048
DEPTH = 2
DEC_BATCH = 8
DEC_SEQ = 1
PAST_LEN = 16384
PAGE_SIZE = 128

NSA_HEADS = 16
NSA_KV_HEADS = 4
NSA_GROUP = NSA_HEADS // NSA_KV_HEADS
NSA_HD = 64
NSA_BLOCK = 64
NSA_TOPK = 16
NSA_WINDOW = 512
NSA_CMP_HID = 128
NSA_Q_BLOCK = 32
GLA_HEADS = 4
GLA_DK = 64
GLA_DV = 128
GLA_GATE_RANK = 16
GLA_TAU = 16.0
GLA_CHUNK = 64
RW_HEADS = 8
RW_HD = 64
RW_DECAY_RANK = 32
RW_A_RANK = 32
RW_G_RANK = 96
RW_GN_EPS = 64e-5
D_FF = 4 * D_MODEL
NORM_EPS = 1e-6

NSA_Q_DIM = NSA_HEADS * NSA_HD
NSA_KV_DIM = NSA_KV_HEADS * NSA_HD
GLA_QK_DIM = GLA_HEADS * GLA_DK
GLA_V_DIM = GLA_HEADS * GLA_DV
RW_DIM = RW_HEADS * RW_HD
RW_SIZES = (RW_DIM, RW_DIM, RW_DIM, RW_DECAY_RANK, RW_A_RANK, RW_G_RANK)
RW_PROJ = 3 * RW_DIM + RW_DECAY_RANK + RW_A_RANK + RW_G_RANK
IN_SIZES = (NSA_Q_DIM, 2 * NSA_KV_DIM, 2 * NSA_KV_DIM, 2 * NSA_KV_DIM, 3 * NSA_HEADS,
            GLA_QK_DIM, GLA_QK_DIM, GLA_V_DIM, GLA_GATE_RANK, GLA_V_DIM, RW_PROJ, 3 * D_MODEL)
N_IN = NSA_Q_DIM + 6 * NSA_KV_DIM + 3 * NSA_HEADS + 2 * GLA_QK_DIM + 2 * GLA_V_DIM + GLA_GATE_RANK + RW_PROJ + 3 * D_MODEL

F32 = jnp.float32

kernel_name = "hybrid_nsa_gla_rwkv7_step"


def split_cols(z, sizes):
    cuts = np.cumsum(np.array(sizes))[:-1]
    return jnp.split(z, [int(c) for c in cuts], axis=-1)


def rms_norm(x, g, eps=NORM_EPS):
    xf = x.astype(F32)
    y = xf * lax.rsqrt(jnp.mean(xf * xf, axis=-1, keepdims=True) + eps)
    return (y * g.astype(F32)).astype(x.dtype)


def masked_softmax(s, mask):
    s = jnp.where(mask, s, -jnp.inf)
    m = jnp.max(s, axis=-1, keepdims=True)
    m = jnp.where(jnp.isfinite(m), m, 0.0)
    e = jnp.where(mask, jnp.exp(s - m), 0.0)
    return e / jnp.maximum(jnp.sum(e, axis=-1, keepdims=True), 1e-30)


def alibi_slopes():
    h = jnp.arange(1, NSA_HEADS + 1, dtype=F32)
    return jnp.exp2(-8.0 * h / NSA_HEADS).reshape(NSA_KV_HEADS, NSA_GROUP)


def pad_rows(rows, blk):
    pad = (-rows.shape[1]) % blk
    return jnp.pad(rows, [(0, 0), (0, pad)] + [(0, 0)] * (rows.ndim - 2))


def compress(rows, pos_emb, w1, w2):
    rows = pad_rows(rows, NSA_BLOCK)
    b, l, g, d = rows.shape
    nb = l // NSA_BLOCK
    z = rows.reshape(b, nb, NSA_BLOCK, g, d) + pos_emb[None, None, :, None, :]
    z = z.transpose(0, 1, 3, 2, 4).reshape(b, nb, g, NSA_BLOCK * d)
    return jax.nn.gelu(z @ w1) @ w2


def to_blocks(rows):
    rows = pad_rows(rows, NSA_BLOCK)
    b, l, g, d = rows.shape
    return rows.reshape(b, l // NSA_BLOCK, NSA_BLOCK, g, d).transpose(0, 3, 1, 2, 4).astype(F32)


def nsa_attend(q, q_pos, gates, k_cmp, v_cmp, k_blk, v_blk, k_win, v_win, win_pos, slopes):
    b = q.shape[0]
    scale = NSA_HD ** -0.5
    qf = q.astype(F32)
    sl = slopes[None, :, :, None, None]
    nb = k_cmp.shape[1]
    blk = jnp.arange(nb, dtype=jnp.int32)
    d_cmp = q_pos[:, None] - ((blk[None, :] + 1) * NSA_BLOCK - 1)
    s = jnp.einsum("bqgrd,bngd->bgrqn", qf, k_cmp.astype(F32)) * scale - sl * d_cmp.astype(F32)
    p_cmp = masked_softmax(s, d_cmp >= 0)
    o_cmp = jnp.einsum("bgrqn,bngd->bqgrd", p_cmp, v_cmp.astype(F32))
    cur = q_pos // NSA_BLOCK
    forced = (blk[None, :] == 0) | (blk[None, :] == cur[:, None]) | (blk[None, :] == cur[:, None] - 1)
    score = jnp.where(forced, jnp.inf, jnp.sum(p_cmp, axis=2))
    score = jnp.where(blk[None, :] <= cur[:, None], score, -jnp.inf)
    _, idx = lax.top_k(score, min(NSA_TOPK, nb))
    ok = idx <= cur[None, None, :, None]
    bi = jnp.arange(b)[:, None, None, None]
    gi = jnp.arange(NSA_KV_HEADS)[None, :, None, None]
    k_sel = k_blk[bi, gi, idx]
    v_sel = v_blk[bi, gi, idx]
    d_sel = q_pos[None, None, :, None, None] - (idx[..., None] * NSA_BLOCK + jnp.arange(NSA_BLOCK, dtype=jnp.int32))
    m_sel = ((d_sel >= 0) & ok[..., None])[:, :, None]
    s = (jnp.einsum("bqgrd,bgqkld->bgrqkl", qf, k_sel) * scale
         - slopes[None, :, :, None, None, None] * d_sel[:, :, None].astype(F32))
    shp = s.shape
    p = masked_softmax(s.reshape(shp[:4] + (-1,)), m_sel.reshape(m_sel.shape[:4] + (-1,))).reshape(shp)
    o_sel = jnp.einsum("bgrqkl,bgqkld->bqgrd", p, v_sel)
    d_win = q_pos[:, None] - win_pos[None, :]
    m_win = (d_win >= 0) & (d_win < NSA_WINDOW) & (win_pos >= 0)[None, :]
    s = jnp.einsum("bqgrd,bsgd->bgrqs", qf, k_win.astype(F32)) * scale - sl * d_win.astype(F32)
    p = masked_softmax(s, m_win)
    o_win = jnp.einsum("bgrqs,bsgd->bqgrd", p, v_win.astype(F32))
    g = gates.astype(F32)[..., None]
    o = g[:, :, 0] * o_cmp + g[:, :, 1] * o_sel + g[:, :, 2] * o_win
    return o.astype(q.dtype)


def nsa_prompt(q, gates, cmp_rows, sel_rows, win_rows, cmp_pos, cmp_w1, cmp_w2, kcmp_g, slopes):
    b, t = q.shape[:2]
    k_cmp = rms_norm(compress(cmp_rows[:, :, 0], cmp_pos[0], cmp_w1[0], cmp_w2[0]), kcmp_g)
    v_cmp = compress(cmp_rows[:, :, 1], cmp_pos[1], cmp_w1[1], cmp_w2[1])
    k_blk = to_blocks(sel_rows[:, :, 0])
    v_blk = to_blocks(sel_rows[:, :, 1])
    win_pad = jnp.pad(win_rows, ((0, 0), (NSA_WINDOW, 0), (0, 0), (0, 0), (0, 0)))
    nq = t // NSA_Q_BLOCK
    band = NSA_WINDOW + NSA_Q_BLOCK
    q_b = jnp.swapaxes(q.reshape((b, nq, NSA_Q_BLOCK) + q.shape[2:]), 0, 1)
    g_b = jnp.swapaxes(gates.reshape((b, nq, NSA_Q_BLOCK) + gates.shape[2:]), 0, 1)
    starts = jnp.arange(nq, dtype=jnp.int32) * NSA_Q_BLOCK

    def one_block(args):
        q_, g_, s0 = args
        kv = lax.dynamic_slice_in_dim(win_pad, s0, band, axis=1)
        win_pos = s0 - NSA_WINDOW + jnp.arange(band, dtype=jnp.int32)
        q_pos = s0 + jnp.arange(NSA_Q_BLOCK, dtype=jnp.int32)
        return nsa_attend(q_, q_pos, g_, k_cmp, v_cmp, k_blk, v_blk, kv[:, :, 0], kv[:, :, 1], win_pos, slopes)

    o = lax.map(one_block, (q_b, g_b, starts))
    return jnp.swapaxes(o, 0, 1).reshape(q.shape)


def nsa_sample(q, gates, cmp_rows, sel_rows, win_rows, cache_cmp, cache_sel, cache_win, page_table,
               cmp_pos, cmp_w1, cmp_w2, kcmp_g, slopes):
    nbatch, t = q.shape[:2]
    past = page_table.shape[1] * cache_cmp.shape[1]

    def with_past(cache, rows):
        old = cache[page_table].reshape((nbatch, past) + cache.shape[2:])
        return jnp.concatenate([old.astype(rows.dtype), rows], axis=1)

    cmp_all = with_past(cache_cmp, cmp_rows)
    sel_all = with_past(cache_sel, sel_rows)
    k_cmp = rms_norm(compress(cmp_all[:, :, 0], cmp_pos[0], cmp_w1[0], cmp_w2[0]), kcmp_g)
    v_cmp = compress(cmp_all[:, :, 1], cmp_pos[1], cmp_w1[1], cmp_w2[1])
    w_buf = cache_win.shape[1]
    win_all = jnp.concatenate([cache_win.astype(win_rows.dtype), win_rows], axis=1)
    win_pos = past - w_buf + jnp.arange(w_buf + t, dtype=jnp.int32)
    q_pos = past + jnp.arange(t, dtype=jnp.int32)
    return nsa_attend(q, q_pos, gates, k_cmp, v_cmp, to_blocks(sel_all[:, :, 0]), to_blocks(sel_all[:, :, 1]),
                      win_all[:, :, 0], win_all[:, :, 1], win_pos, slopes)


def gla_chunked(q, k, v, log_a, s0):
    b, t, h, dk = q.shape
    dv = v.shape[-1]
    c = min(GLA_CHUNK, t)
    q, k, v, log_a = (pad_rows(z, c) for z in (q, k, v, log_a))
    n = q.shape[1] // c
    chunks = lambda z: z.reshape(b, n, c, h, z.shape[-1]).transpose(1, 0, 3, 2, 4)
    causal = jnp.tril(jnp.ones((c, c), dtype=bool))

    def step(state, inp):
        q_c, k_c, v_c, a_c = inp
        cum = jnp.cumsum(a_c, axis=2)
        last = cum[:, :, -1:, :]
        q_dec = q_c * jnp.exp(cum)
        att = jnp.where(causal, jnp.einsum("bhtd,bhsd->bhts", q_dec, k_c * jnp.exp(-cum)), 0.0)
        o = jnp.einsum("bhts,bhsv->bhtv", att, v_c) + jnp.einsum("bhtd,bhdv->bhtv", q_dec, state)
        state = (state * jnp.exp(last)[:, :, 0, :, None]
                 + jnp.einsum("bhsd,bhsv->bhdv", k_c * jnp.exp(last - cum), v_c))
        return state, o

    s_t, o = lax.scan(step, s0, (chunks(q), chunks(k), chunks(v), chunks(log_a)))
    o = o.transpose(1, 0, 3, 2, 4).reshape(b, n * c, h, dv)[:, :t]
    return o, s_t


def gla_branch(q, k, v, a_lr, g_out, s0, a2, a_b, norm_g):
    b, t, _ = q.shape
    log_a = jax.nn.log_sigmoid(a_lr.astype(F32) @ a2 + a_b) / GLA_TAU
    hk = lambda z: z.astype(F32).reshape(b, t, GLA_HEADS, GLA_DK)
    o, s_t = gla_chunked(hk(q) * GLA_DK ** -0.5, hk(k), v.astype(F32).reshape(b, t, GLA_HEADS, GLA_DV),
                         hk(log_a), s0.astype(F32))
    o = rms_norm(o, norm_g).reshape(b, t, GLA_V_DIM)
    return o * jax.nn.silu(g_out.astype(F32)), s_t


def rwkv7_branch(rw, prev_row, s0, mu, w0, w2, a0, a2, g2, k_k, k_a, r_k, ln_g, ln_b):
    b, t, _ = rw.shape
    shifted = jnp.concatenate([prev_row[:, None, :].astype(rw.dtype), rw[:, :-1]], axis=1)
    xm = (rw + (shifted - rw) * mu).astype(F32)
    r, k, v, wl, al, gl = split_cols(xm, RW_SIZES)
    w = -jax.nn.softplus(-(w0 + jnp.tanh(wl) @ w2)) - 0.5
    decay = jnp.exp(-jnp.exp(w))
    a = jax.nn.sigmoid(a0 + al @ a2)
    g = jax.nn.sigmoid(gl) @ g2
    heads = lambda z: z.reshape(b, t, RW_HEADS, RW_HD)
    kk = heads(k * k_k)
    kk = kk / jnp.maximum(jnp.sqrt(jnp.sum(kk * kk, axis=-1, keepdims=True)), 1e-12)
    k = k * (1.0 + (a - 1.0) * k_a)
    rh, kh, vh = heads(r), heads(k), heads(v)
    tm = lambda z: jnp.swapaxes(z, 0, 1)

    def step(state, inp):
        r_t, dec_t, k_t, v_t, kk_t, kka_t = inp
        sa = jnp.einsum("bhvk,bhk->bhv", state, -kk_t)
        state = (state * dec_t[:, :, None, :] + sa[..., None] * kka_t[:, :, None, :]
                 + v_t[..., None] * k_t[:, :, None, :])
        return state, jnp.einsum("bhvk,bhk->bhv", state, r_t)

    s_t, o = lax.scan(step, s0.astype(F32),
                      (tm(rh), tm(heads(decay)), tm(kh), tm(vh), tm(kk), tm(kk * heads(a))))
    o = tm(o)
    mean = jnp.mean(o, axis=-1, keepdims=True)
    var = jnp.mean(jnp.square(o - mean), axis=-1, keepdims=True)
    o = ((o - mean) * lax.rsqrt(var + RW_GN_EPS)).reshape(b, t, RW_DIM) * ln_g + ln_b
    bonus = jnp.sum(rh * kh * r_k, axis=-1, keepdims=True) * vh
    return (o + bonus.reshape(b, t, RW_DIM)) * g, s_t


def trunk_layer(x, lp, nsa_fn, shift_prev, gla_s0, rw_s0):
    b, t, _ = x.shape
    hn = rms_norm(x, lp["norm1_g"])
    (nsa_q, nsa_cmp, nsa_sel, nsa_win, nsa_gate, gla_q, gla_k, gla_v, gla_a, gla_g, rw, merge) = split_cols(
        hn @ lp["w_in"], IN_SIZES)
    qk_g = lp["nsa_qk_g"]
    kv_shape = (b, t, 2, NSA_KV_HEADS, NSA_HD)
    q = rms_norm(nsa_q.reshape(b, t, NSA_KV_HEADS, NSA_GROUP, NSA_HD), qk_g[0])
    cmp_rows = nsa_cmp.reshape(kv_shape)
    sel = nsa_sel.reshape(kv_shape)
    sel_rows = jnp.stack([rms_norm(sel[:, :, 0], qk_g[2]), sel[:, :, 1]], axis=2)
    win = nsa_win.reshape(kv_shape)
    win_rows = jnp.stack([rms_norm(win[:, :, 0], qk_g[3]), win[:, :, 1]], axis=2)
    gates = jax.nn.sigmoid(nsa_gate.reshape(b, t, 3, NSA_KV_HEADS, NSA_GROUP))
    o_nsa = nsa_fn(q, gates, cmp_rows, sel_rows, win_rows).reshape(b, t, NSA_Q_DIM)
    o_gla, gla_st = gla_branch(gla_q, gla_k, gla_v, gla_a, gla_g, gla_s0, lp["gla_a2"], lp["gla_a_b"], lp["gla_norm_g"])
    o_rw, rw_st = rwkv7_branch(rw, shift_prev, rw_s0, lp["rw_mu"], lp["rw_w0"], lp["rw_w2"], lp["rw_a0"], lp["rw_a2"],
                               lp["rw_g2"], lp["rw_kk"], lp["rw_ka"], lp["rw_rk"], lp["rw_ln_g"], lp["rw_ln_b"])
    mg = jax.nn.sigmoid(merge.reshape(b, t, 3, D_MODEL).astype(F32))
    merged = (mg[:, :, 0] * (o_nsa @ lp["nsa_up"]) + mg[:, :, 1] * (o_gla @ lp["gla_up"])
              + mg[:, :, 2] * (o_rw @ lp["rw_up"]))
    x = x + (merged @ lp["w_out"]).astype(x.dtype)
    h2 = rms_norm(x, lp["norm2_g"])
    x = x + (jnp.square(jax.nn.relu(h2 @ lp["mlp_w1"])) @ lp["mlp_w2"]).astype(x.dtype)
    return x, cmp_rows, sel_rows, win_rows, gla_st, rw_st, rw[:, -1]


def setup_inputs(seed: int = 0) -> dict:
    key = jax.random.key(seed)
    keys = iter(jax.random.split(key, 64))

    def nrm(shape, scale=1.0):
        return scale * jax.random.normal(next(keys), shape, F32)

    def gain(shape):
        return 1.0 + nrm(shape, 0.05)

    n_pages = PAST_LEN // PAGE_SIZE
    n_pool = (5 * DEC_BATCH * n_pages) // 4
    w_buf = min(NSA_WINDOW, PAST_LEN)
    kv_row = (2, NSA_KV_HEADS, NSA_HD)
    page_table = jax.random.permutation(next(keys), n_pool)[: DEC_BATCH * n_pages]
    page_table = page_table.reshape(DEC_BATCH, n_pages).astype(jnp.int32)
    return {
        "x_prompt": nrm((BATCH, SEQ, D_MODEL)),
        "x_sample": nrm((DEC_BATCH, DEC_SEQ, D_MODEL)),
        "cache_cmp_kv": nrm((DEPTH, n_pool, PAGE_SIZE) + kv_row),
        "cache_sel_kv": nrm((DEPTH, n_pool, PAGE_SIZE) + kv_row),
        "cache_win_kv": nrm((DEPTH, DEC_BATCH, w_buf) + kv_row),
        "state_gla": nrm((DEPTH, DEC_BATCH, GLA_HEADS, GLA_DK, GLA_DV)),
        "state_rwkv": nrm((DEPTH, DEC_BATCH, RW_HEADS, RW_HD, RW_HD), 0.5),
        "state_rwkv_shift": nrm((DEPTH, DEC_BATCH, RW_PROJ)),
        "page_table": page_table,
        "norm1_g": gain((DEPTH, D_MODEL)),
        "w_in": nrm((DEPTH, D_MODEL, N_IN), D_MODEL ** -0.5),
        "nsa_qk_g": gain((DEPTH, 4, NSA_HD)),
        "cmp_pos": nrm((DEPTH, 2, NSA_BLOCK, NSA_HD), 0.2),
        "cmp_w1": nrm((DEPTH, 2, NSA_BLOCK * NSA_HD, NSA_CMP_HID), (NSA_BLOCK * NSA_HD) ** -0.5),
        "cmp_w2": nrm((DEPTH, 2, NSA_CMP_HID, NSA_HD), NSA_CMP_HID ** -0.5),
        "nsa_up": nrm((DEPTH, NSA_Q_DIM, D_MODEL), NSA_Q_DIM ** -0.5),
        "gla_a2": nrm((DEPTH, GLA_GATE_RANK, GLA_QK_DIM), GLA_GATE_RANK ** -0.5),
        "gla_a_b": nrm((DEPTH, GLA_QK_DIM), 0.1),
        "gla_norm_g": gain((DEPTH, GLA_DV)),
        "gla_up": nrm((DEPTH, GLA_V_DIM, D_MODEL), GLA_V_DIM ** -0.5),
        "rw_mu": jax.random.uniform(next(keys), (DEPTH, RW_PROJ), F32),
        "rw_w0": nrm((DEPTH, RW_DIM), 0.5),
        "rw_w2": nrm((DEPTH, RW_DECAY_RANK, RW_DIM), 0.5 * RW_DECAY_RANK ** -0.5),
        "rw_a0": nrm((DEPTH, RW_DIM), 0.1),
        "rw_a2": nrm((DEPTH, RW_A_RANK, RW_DIM), RW_A_RANK ** -0.5),
        "rw_g2": nrm((DEPTH, RW_G_RANK, RW_DIM), RW_G_RANK ** -0.5),
        "rw_kk": 0.85 + nrm((DEPTH, RW_DIM), 0.05),
        "rw_ka": gain((DEPTH, RW_DIM)),
        "rw_rk": nrm((DEPTH, RW_HEADS, RW_HD), 0.1),
        "rw_ln_g": gain((DEPTH, RW_DIM)),
        "rw_ln_b": nrm((DEPTH, RW_DIM), 0.01),
        "rw_up": nrm((DEPTH, RW_DIM, D_MODEL), RW_DIM ** -0.5),
        "w_out": nrm((DEPTH, D_MODEL, D_MODEL), D_MODEL ** -0.5),
        "norm2_g": gain((DEPTH, D_MODEL)),
        "mlp_w1": nrm((DEPTH, D_MODEL, D_FF), D_MODEL ** -0.5),
        "mlp_w2": nrm((DEPTH, D_FF, D_MODEL), D_FF ** -0.5),
    }


def reference(x_prompt, x_sample, cache_cmp_kv, cache_sel_kv, cache_win_kv, state_gla, state_rwkv, state_rwkv_shift,
              page_table, norm1_g, w_in, nsa_qk_g, cmp_pos, cmp_w1, cmp_w2, nsa_up, gla_a2, gla_a_b, gla_norm_g, gla_up,
              rw_mu, rw_w0, rw_w2, rw_a0, rw_a2, rw_g2, rw_kk, rw_ka, rw_rk, rw_ln_g, rw_ln_b, rw_up, w_out, norm2_g,
              mlp_w1, mlp_w2):
    slopes = alibi_slopes()
    b, t, _ = x_prompt.shape
    w_keep = min(NSA_WINDOW, t)
    gla0 = jnp.zeros((b, GLA_HEADS, GLA_DK, GLA_DV), F32)
    rw0 = jnp.zeros((b, RW_HEADS, RW_HD, RW_HD), F32)
    shift0 = jnp.zeros((b, RW_PROJ), x_prompt.dtype)
    xp, xs = x_prompt, x_sample
    p_cmp, p_sel, p_win, p_gla, p_rw, p_sh = [], [], [], [], [], []
    s_cmp, s_sel, s_win, s_gla, s_rw, s_sh = [], [], [], [], [], []
    for l in range(DEPTH):
        lp = {"norm1_g": norm1_g[l], "w_in": w_in[l], "nsa_qk_g": nsa_qk_g[l], "nsa_up": nsa_up[l],
              "gla_a2": gla_a2[l], "gla_a_b": gla_a_b[l], "gla_norm_g": gla_norm_g[l], "gla_up": gla_up[l],
              "rw_mu": rw_mu[l], "rw_w0": rw_w0[l], "rw_w2": rw_w2[l], "rw_a0": rw_a0[l], "rw_a2": rw_a2[l],
              "rw_g2": rw_g2[l], "rw_kk": rw_kk[l], "rw_ka": rw_ka[l], "rw_rk": rw_rk[l], "rw_ln_g": rw_ln_g[l],
              "rw_ln_b": rw_ln_b[l], "rw_up": rw_up[l], "w_out": w_out[l], "norm2_g": norm2_g[l],
              "mlp_w1": mlp_w1[l], "mlp_w2": mlp_w2[l]}

        def nsa_p(q, g, c, s, w, l=l):
            return nsa_prompt(q, g, c, s, w, cmp_pos[l], cmp_w1[l], cmp_w2[l], nsa_qk_g[l, 1], slopes)

        def nsa_s(q, g, c, s, w, l=l):
            return nsa_sample(q, g, c, s, w, cache_cmp_kv[l], cache_sel_kv[l], cache_win_kv[l], page_table,
                              cmp_pos[l], cmp_w1[l], cmp_w2[l], nsa_qk_g[l, 1], slopes)

        xp, c, s, w, g_st, r_st, sh = trunk_layer(xp, lp, nsa_p, shift0, gla0, rw0)
        p_cmp.append(c)
        p_sel.append(s)
        p_win.append(w[:, t - w_keep:])
        p_gla.append(g_st.astype(state_gla.dtype))
        p_rw.append(r_st.astype(state_rwkv.dtype))
        p_sh.append(sh.astype(state_rwkv_shift.dtype))
        xs, c, s, w, g_st, r_st, sh = trunk_layer(xs, lp, nsa_s, state_rwkv_shift[l], state_gla[l], state_rwkv[l])
        s_cmp.append(c)
        s_sel.append(s)
        s_win.append(w)
        s_gla.append(g_st.astype(state_gla.dtype))
        s_rw.append(r_st.astype(state_rwkv.dtype))
        s_sh.append(sh.astype(state_rwkv_shift.dtype))
    return (xp, xs,
            jnp.stack(p_cmp), jnp.stack(p_sel), jnp.stack(p_win), jnp.stack(p_gla), jnp.stack(p_rw), jnp.stack(p_sh),
            jnp.stack(s_cmp), jnp.stack(s_sel), jnp.stack(s_win), jnp.stack(s_gla), jnp.stack(s_rw), jnp.stack(s_sh))
```

```python
import os
import numpy as np
import concourse.bass as bass
import concourse.mybir as mybir
from concourse.bass_utils import run_bass_kernel_spmd

F32 = mybir.dt.float32
BF16 = mybir.dt.bfloat16
I32 = mybir.dt.int32
AF = mybir.ActivationFunctionType
ALU = mybir.AluOpType
AX = mybir.AxisListType

NQ = 1024
C_Q, C_CMP, C_SEL, C_WIN, C_GATE = 0, 1024, 1536, 2048, 2560
C_GQ, C_GK, C_GV, C_GA, C_GG, C_RW, C_MG = 2608, 2864, 3120, 3632, 3648, 4160, 5856
RWP = 1696
EPS = 1e-6


import types


def _freeze(fn):
    if fn.__closure__ is None:
        return fn
    cells = []
    for c in fn.__closure__:
        try:
            cells.append(types.CellType(c.cell_contents))
        except ValueError:
            cells.append(c)
    return types.FunctionType(fn.__code__, fn.__globals__, fn.__name__, fn.__defaults__, tuple(cells))


class Sched:
    def __init__(self, nc, n_dma_sems=8):
        self.nc = nc
        self.eng = {"pe": nc.tensor, "act": nc.scalar, "dve": nc.vector, "pool": nc.gpsimd, "sp": nc.sync}
        self.prog = {e: [] for e in self.eng}
        self.sem = {e: nc.alloc_semaphore("c_" + e) for e in self.eng}
        self.cnt = {e: 0 for e in self.eng}
        self.dsem = {e: [nc.alloc_semaphore(f"d_{e}{i}") for i in range(n_dma_sems)] for e in ("sp", "pool", "act")}
        self.dcnt = {}
        self.semobj = {}
        for e in self.dsem:
            for s in self.dsem[e]:
                self.dcnt[id(s)] = 0
                self.semobj[id(s)] = s
        for e in self.sem:
            self.semobj[id(self.sem[e])] = self.sem[e]
        self.drr = {e: 0 for e in self.dsem}
        self.seen = {e: {} for e in self.eng}
        self.bufs = {}
        self.n_wait = 0
        self.pending = []
        self.max_pending = 2

    def _deps(self, e, reads, writes):
        waits = {}

        def need(tok, pe_ok=False):
            if tok is None:
                return
            sem, val, src = tok
            if e == "pe" and src == "pe" and pe_ok:
                return
            k = id(sem)
            if self.seen[e].get(k, 0) >= val:
                return
            if waits.get(k, 0) < val:
                waits[k] = val

        for b in reads:
            st = self.bufs.get(b)
            if st:
                need(st["w"])
        for b in writes:
            st = self.bufs.get(b)
            if st:
                need(st["w"], True)
                for r in st["r"]:
                    need(r, True)
        for k, v in waits.items():
            self.seen[e][k] = v
        self.n_wait += len(waits)
        return [(self.semobj[k], v) for k, v in waits.items()]

    def _commit(self, tok, reads, writes):
        for b in reads:
            st = self.bufs.setdefault(b, {"w": None, "r": []})
            st["r"].append(tok)
            if len(st["r"]) > 40:
                best = {}
                for t in st["r"]:
                    k = id(t[0])
                    if k not in best or best[k][1] < t[1]:
                        best[k] = t
                st["r"] = list(best.values())
        for b in writes:
            self.bufs[b] = {"w": tok, "r": []}

    def _flush_one(self):
        q, out, in_, reads, writes, indirect, kw = self.pending.pop(0)
        self.dma(q, out, in_, reads=reads, writes=writes, indirect=indirect, _nocheck=True, **kw)

    def _flush_conflicts(self, reads, writes):
        if not self.pending:
            return
        ws, rs = set(writes), set(reads)
        idx = -1
        for i, p in enumerate(self.pending):
            pr, pw = set(p[3]), set(p[4])
            if (ws & (pr | pw)) or (rs & pw):
                idx = i
        for _ in range(idx + 1):
            self._flush_one()

    def flush(self):
        while self.pending:
            self._flush_one()

    def store(self, q, out, in_, **kw):
        self.dma(q, out, in_, defer=True, **kw)

    def op(self, e, fn, reads=(), writes=()):
        fn = _freeze(fn)
        self._flush_conflicts(reads, writes)
        waits = self._deps(e, reads, writes)
        self.cnt[e] += 1
        sem = self.sem[e]
        tok = (sem, self.cnt[e], e)

        def emit(eng):
            for s, v in waits:
                eng.wait_ge(s, v)
            fn(eng).then_inc(sem, 1)

        self.prog[e].append(emit)
        self._commit(tok, reads, writes)

    def dma(self, q, out, in_, reads=(), writes=(), indirect=None, defer=False, _nocheck=False, **kw):
        if defer:
            self.pending.append((q, out, in_, tuple(reads), tuple(writes), indirect, kw))
            while len(self.pending) > self.max_pending:
                self._flush_one()
            return
        if not _nocheck:
            self._flush_conflicts(reads, writes)
        waits = self._deps(q, reads, writes)
        i = self.drr[q]
        self.drr[q] = (i + 1) % len(self.dsem[q])
        s = self.dsem[q][i]
        prev = self.dcnt[id(s)]
        self.dcnt[id(s)] = prev + 16
        tok = (s, prev + 16, "dma")
        if prev > 0 and self.seen[q].get(id(s), 0) < prev:
            waits.append((s, prev))
            self.seen[q][id(s)] = prev

        def emit(eng):
            for s_, v in waits:
                eng.wait_ge(s_, v)
            if indirect is not None:
                eng.indirect_dma_start(out=out, out_offset=None, in_=in_, in_offset=indirect, **kw).then_inc(s, 16)
            else:
                eng.dma_start(out=out, in_=in_, **kw).then_inc(s, 16)

        self.prog[q].append(emit)
        self._commit(tok, reads, writes)

    def barrier(self):
        self.flush()
        allw = []
        for e in self.dsem:
            for s in self.dsem[e]:
                if self.dcnt[id(s)] > 0:
                    allw.append((s, self.dcnt[id(s)]))
        for e in self.eng:
            if self.cnt[e] > 0:
                allw.append((self.sem[e], self.cnt[e]))
        for e in self.eng:
            waits = [(s, v) for (s, v) in allw if s is not self.sem[e] and self.seen[e].get(id(s), 0) < v]
            for s, v in waits:
                self.seen[e][id(s)] = v

            def emit(eng, waits=waits):
                for s_, v in waits:
                    eng.wait_ge(s_, v)

            self.prog[e].append(emit)
        self.bufs = {}

    def emit_all(self):
        self.flush()
        with self.nc.Block() as block:
            for e, deco in (("sp", block.sync), ("act", block.scalar), ("dve", block.vector),
                            ("pool", block.gpsimd), ("pe", block.tensor)):
                prog = self.prog[e]

                def body(eng, prog=prog):
                    for f in prog:
                        f(eng)

                deco(body)


class Pool:
    def __init__(self, nc):
        self.nc = nc
        self.stack = []

    uid = [0]

    def t(self, name, shape, dt):
        Pool.uid[0] += 1
        g = self.nc.sbuf_tensor(f"{name}_u{Pool.uid[0]}", list(shape), dt)
        h = g.__enter__()
        self.stack.append(g)
        assert self.nc.sbuf_bytes_remaining >= 0, f"SBUF overflow allocating {name}: {self.nc.sbuf_bytes_remaining}"
        return h

    def release(self):
        while self.stack:
            self.stack.pop().__exit__(None, None, None)


def build(cfg, dbg=False):
    D, T, PAST, NPOOL, L = cfg["D"], cfg["T"], cfg["PAST"], cfg["NPOOL"], cfg["L"]
    DFF = 4 * D
    NIN = C_MG + 3 * D
    NT = T // 128
    NTA = NT + 1
    TA = NTA * 128
    KD = D // 128
    WKEEP = min(512, T)
    NPG = PAST // 128

    nc = bass.Bass("TRN2", target_bir_lowering=False)
    S = Sched(nc)

    declared = set()

    def din(name, shape, dt=F32):
        declared.add(name)
        return nc.dram_tensor(name, list(shape), dt, kind="ExternalInput").ap()

    def dout(name, shape):
        return nc.dram_tensor(name, list(shape), F32, kind="ExternalOutput").ap()

    def dscr(name, shape, dt=F32):
        return nc.dram_tensor(name, list(shape), dt, kind=("ExternalOutput" if dbg else "Internal")).ap()

    x_prompt = din("x_prompt", [T, D])
    x_sample = din("x_sample", [1, D])
    norm1_g = din("norm1_g", [L, D])
    norm2_g = din("norm2_g", [L, D])
    w_in = din("w_in", [L, D, NIN])
    nsa_qk_g = din("nsa_qk_g", [L, 4, 64])
    nsa_up = din("nsa_up", [L, 1024, D])
    gla_up = din("gla_up", [L, 512, D])
    rw_up = din("rw_up", [L, 512, D])
    w_out = din("w_out", [L, D, D])
    mlp_w1 = din("mlp_w1", [L, D, DFF])
    mlp_w2 = din("mlp_w2", [L, DFF, D])
    cache_cmp = din("cache_cmp", [L * NPOOL * 128, 512])
    cache_sel = din("cache_sel", [L * NPOOL * 128, 512])
    cache_win = din("cache_win", [L, 512, 512])
    page_table = din("page_table", [NPG], I32)
    cmp_pos = din("cmp_pos", [L, 2, 64, 64])
    cmp_w1 = din("cmp_w1", [L, 2, 4096, 128])
    cmp_w2 = din("cmp_w2", [L, 2, 128, 64])
    gla_a2 = din("gla_a2", [L, 16, 256])
    gla_a_b = din("gla_a_b", [L, 256])
    gla_norm_g = din("gla_norm_g", [L, 128])
    state_gla = din("state_gla", [L, 4, 64, 128])
    rw_mu = din("rw_mu", [L, RWP])
    rw_w0 = din("rw_w0", [L, 512])
    rw_w2 = din("rw_w2", [L, 32, 512])
    rw_a0 = din("rw_a0", [L, 512])
    rw_a2 = din("rw_a2", [L, 32, 512])
    rw_g2 = din("rw_g2", [L, 96, 512])
    rw_kk = din("rw_kk", [L, 512])
    rw_ka = din("rw_ka", [L, 512])
    rw_rk = din("rw_rk", [L, 8, 64])
    rw_ln_g = din("rw_ln_g", [L, 512])
    rw_ln_b = din("rw_ln_b", [L, 512])
    state_rwkv = din("state_rwkv", [L, 8, 64, 64])
    state_shift = din("state_shift", [L, RWP])
    y_prompt = dout("y_prompt", [T, D])
    y_sample = dout("y_sample", [1, D])
    p_cmp_o = dout("p_cmp", [L, T, 512])
    p_sel_o = dout("p_sel", [L, T, 512])
    p_win_o = dout("p_win", [L, WKEEP, 512])
    p_sh_o = dout("p_sh", [L, RWP])
    s_cmp_o = dout("s_cmp", [L, 1, 512])
    s_sel_o = dout("s_sel", [L, 1, 512])
    s_win_o = dout("s_win", [L, 1, 512])
    s_sh_o = dout("s_sh", [L, RWP])
    p_gla_o = dout("p_gla", [L, 4, 64, 128])
    p_rw_o = dout("p_rw", [L, 8, 64, 64])
    s_rw_o = dout("s_rw", [L, 8, 64, 64])
    s_gla_o = dout("s_gla", [L, 4, 64, 128])
    omix_l0 = dout("omix_l0", [TA, 2048]) if dbg else None
    dbg_sel = dout("dbg_sel", [1, 4 * (PAST // 64)]) if dbg else None
    dbg_srow = dout("dbg_srow", [1, 4 * (PAST // 64)]) if dbg else None
    dbg_o = dout("dbg_o", [4, 3 * 256]) if dbg else None
    dbg_gs = dout("dbg_gs", [4, 12]) if dbg else None
    xa = dscr("xa", [TA, D])
    xm = dscr("xm", [TA, D])
    proj = dscr("proj", [TA, NIN])
    omix = dscr("omix", [TA, 2048])
    hidS = dscr("hidS", [NTA, 128, DFF // 128, 128], BF16)
    mrg = dscr("mrg", [TA, D])

    ident_b = nc.alloc_sbuf_tensor("ident_b", [128, 128], BF16)
    ident_f = nc.alloc_sbuf_tensor("ident_f", [128, 128], F32)
    zeros_f = nc.alloc_sbuf_tensor("zeros_f", [128, 2048], F32)
    PS = [nc.alloc_psum_tensor(f"ps{i}", [128, 512], F32) for i in range(6)]
    PB = [nc.alloc_psum_tensor(f"pb{i}", [128, 1024], BF16) for i in range(2)]

    S.op("pool", lambda e: e.memset(ident_f[:], 1.0), writes=["ident_f"])
    S.op("pool", lambda e: e.affine_select(out=ident_f[:], in_=ident_f[:], pattern=[[-1, 128]], compare_op=ALU.is_equal,
                                           fill=0.0, base=0, channel_multiplier=1), reads=["ident_f"], writes=["ident_f"])
    S.op("dve", lambda e: e.tensor_copy(out=ident_b[:], in_=ident_f[:]), reads=["ident_f"], writes=["ident_b"])
    S.op("dve", lambda e: e.memset(zeros_f[:], 0.0), writes=["zeros_f"])
    tri_f = nc.alloc_sbuf_tensor("tri_f", [64, 64], F32)
    ones_f = nc.alloc_sbuf_tensor("ones_f", [128, 64], F32)
    S.op("pool", lambda e: e.memset(ones_f[:], 1.0), writes=["ones_f"])
    trs_f = nc.alloc_sbuf_tensor("trs_f", [64, 64], F32)
    trl_f = nc.alloc_sbuf_tensor("trl_f", [64, 64], F32)
    S.op("pool", lambda e: e.memset(trs_f[:], 1.0), writes=["trs_f"])
    S.op("pool", lambda e: e.affine_select(out=trs_f[:], in_=trs_f[:], pattern=[[1, 64]], compare_op=ALU.is_gt,
                                           fill=0.0, base=0, channel_multiplier=-1), reads=["trs_f"], writes=["trs_f"])
    S.op("pool", lambda e: e.memset(trl_f[:], 1.0), writes=["trl_f"])
    S.op("pool", lambda e: e.affine_select(out=trl_f[:], in_=trl_f[:], pattern=[[-1, 64]], compare_op=ALU.is_gt,
                                           fill=0.0, base=0, channel_multiplier=1), reads=["trl_f"], writes=["trl_f"])
    S.op("pool", lambda e: e.memset(tri_f[:], 1.0), writes=["tri_f"])
    S.op("pool", lambda e: e.affine_select(out=tri_f[:], in_=tri_f[:], pattern=[[1, 64]], compare_op=ALU.is_ge,
                                           fill=0.0, base=0, channel_multiplier=-1), reads=["tri_f"], writes=["tri_f"])
    NBP = T // 64
    SLOPES = [2.0 ** (-8.0 * (h + 1) / 16) for h in range(16)]
    BIGN = -30000.0
    ci = [0]

    def iota_f(shape, pattern, base, cm):
        ci[0] += 1
        ti = nc.alloc_sbuf_tensor(f"iota_i{ci[0]}", list(shape), I32)
        tf = nc.alloc_sbuf_tensor(f"iota_f{ci[0]}", list(shape), F32)
        S.op("pool", lambda e: e.iota(ti[:], pattern=pattern, base=base, channel_multiplier=cm), writes=[("iota", ci[0])])
        S.op("dve", lambda e: e.tensor_copy(out=tf[:], in_=ti[:]), reads=[("iota", ci[0])], writes=[("iotaf", ci[0])])
        return tf

    Dc = iota_f([128, NBP], [[-64, NBP]], -63, 1)
    Dblk_i = iota_f([128, NBP], [[-64, NBP]], 0, 1)
    KDt = iota_f([128, NT + 1], [[-128, NT + 1]], -64, 1)
    QK_ = iota_f([128, 128], [[-1, 128]], 0, 1)
    S.barrier()
    negsl = nc.alloc_sbuf_tensor("negsl", [128, 16, NBP], F32)
    for h in range(16):
        S.op("pool", lambda e, h=h: e.memset(negsl[:, h, :], -SLOPES[h]), writes=[("negsl", h)])
    bcol = nc.alloc_sbuf_tensor("bcol", [128, 16, NT + 1], F32)
    for h in range(16):
        S.op("dve", lambda e, h=h: e.tensor_scalar(out=bcol[:, h, :], in0=KDt[:], scalar1=SLOPES[h], scalar2=None, op0=ALU.mult),
             writes=[("bcol", h)])
    Mc_b = nc.alloc_sbuf_tensor("Mc_b", [128, 128], BF16)
    Mw_b = nc.alloc_sbuf_tensor("Mw_b", [128, 128], BF16)
    S.op("dve", lambda e: e.tensor_scalar(out=Mc_b[:], in0=QK_[:], scalar1=0.0, scalar2=BIGN, op0=ALU.is_gt, op1=ALU.mult), writes=["Mc_b"])
    S.op("dve", lambda e: e.tensor_scalar(out=Mw_b[:], in0=QK_[:], scalar1=0.0, scalar2=BIGN, op0=ALU.is_le, op1=ALU.mult), writes=["Mw_b"])
    E_all = nc.alloc_sbuf_tensor("E_all", [max(NBP, 2), T], BF16)
    S.op("pool", lambda e: e.memset(E_all[:], 1.0), writes=["E_all"])
    S.op("pool", lambda e: e.affine_select(out=E_all[:], in_=E_all[:], pattern=[[1, T]], compare_op=ALU.is_ge, fill=0.0, base=0,
                                           channel_multiplier=-64), reads=["E_all"], writes=["E_all"])
    S.op("pool", lambda e: e.affine_select(out=E_all[:], in_=E_all[:], pattern=[[-1, T]], compare_op=ALU.is_ge, fill=0.0, base=63,
                                           channel_multiplier=64), reads=["E_all"], writes=["E_all"])
    NBS = PAST // 64
    SEGT = min(NT, NPG)
    SEGB = 2 * SEGT
    NSEG = NPG // SEGT
    pt_i = nc.alloc_sbuf_tensor("pt_i", [128, NPG], I32)
    pt_f = nc.alloc_sbuf_tensor("pt_f", [128, NPG], F32)
    idx_f = nc.alloc_sbuf_tensor("idx_f", [128, L, NPG], F32)
    idx_i = nc.alloc_sbuf_tensor("idx_i", [128, L, NPG], I32)
    pcol = iota_f([128, 1], [[0, 1]], 0, 1)
    slg_raw = iota_f([4, 4], [[4, 4]], 1, 1)
    dcs = iota_f([4, NBS], [[-64, NBS]], PAST - 63, 0)
    tb = iota_f([128, NPG], [[-128, NPG]], PAST, -1)
    tbw = iota_f([128, 4], [[-128, 4]], 512, -1)
    S.barrier()
    S.dma("sp", pt_i[:], page_table.partition_broadcast(128), writes=["pt_i"])
    S.op("dve", lambda e: e.tensor_copy(out=pt_f[:], in_=pt_i[:]), reads=["pt_i"], writes=["pt_f"])
    for l_ in range(L):
        S.op("dve", lambda e, l_=l_: e.tensor_scalar(out=idx_f[:, l_, :], in0=pt_f[:], scalar1=float(l_ * NPOOL), scalar2=128.0,
                                                    op0=ALU.add, op1=ALU.mult), reads=["pt_f"], writes=[("idx_f", l_)])
        S.op("dve", lambda e, l_=l_: e.tensor_scalar(out=idx_f[:, l_, :], in0=idx_f[:, l_, :], scalar1=pcol[:, 0:1], scalar2=None, op0=ALU.add),
             reads=[("idx_f", l_)], writes=[("idx_f", l_)])
        S.op("dve", lambda e, l_=l_: e.tensor_copy(out=idx_i[:, l_, :], in_=idx_f[:, l_, :]), reads=[("idx_f", l_)], writes=[("idx_i", l_)])
    slg = nc.alloc_sbuf_tensor("slg", [4, 4], F32)
    S.op("act", lambda e: e.activation(out=slg[:], in_=slg_raw[:], func=AF.Exp, scale=-0.34657359027997264), writes=["slg"])
    S.op("dve", lambda e: e.tensor_scalar(out=slg[:], in0=slg[:], scalar1=-1.0, scalar2=None, op0=ALU.mult), reads=["slg"], writes=["slg"])
    negsl16 = nc.alloc_sbuf_tensor("negsl16", [128, 16], F32)
    for h in range(16):
        S.op("pool", lambda e, h=h: e.memset(negsl16[:, h:h + 1], -SLOPES[h]), writes=[("negsl16", h)])
    half_sel = nc.alloc_sbuf_tensor("half_sel", [1, 2, 128], BF16)
    S.op("pool", lambda e: e.memset(half_sel[:], 0.0), writes=["half_sel"])
    S.op("pool", lambda e: e.memset(half_sel[0:1, 0, 0:64], 1.0), reads=["half_sel"], writes=["half_sel"])
    S.op("pool", lambda e: e.memset(half_sel[0:1, 1, 64:128], 1.0), reads=["half_sel"], writes=["half_sel"])
    mw0 = nc.alloc_sbuf_tensor("mw0", [128, 1], F32)
    S.op("dve", lambda e: e.tensor_scalar(out=mw0[:], in0=ident_f[:, 0:1], scalar1=BIGN, scalar2=None, op0=ALU.mult), writes=["mw0"])
    S.barrier()

    S.store("sp", xa[0:T, :], x_prompt)
    S.store("sp", xa[T:T + 1, :], x_sample)
    S.store("sp", xa[T + 1:TA, :], zeros_f[0:127, 0:D])
    S.barrier()

    evac_rr = [0]

    def evac(out, in_, reads, writes):
        evac_rr[0] ^= 1
        if evac_rr[0]:
            S.op("act", lambda e: e.copy(out=out, in_=in_), reads=reads, writes=writes)
        else:
            S.op("dve", lambda e: e.tensor_copy(out=out, in_=in_), reads=reads, writes=writes)

    def rmsnorm_T(P, src, g_row, hT, tag, NC=None):
        NC = NC or D
        KD = NC // 128
        D_ = NC
        xt = [P.t(f"{tag}_x{i}", [128, NC], F32) for i in range(2)]
        xb = [P.t(f"{tag}_xb{i}", [128, NC], BF16) for i in range(2)]
        if g_row is not None:
            gb = P.t(tag + "_g", [128, NC], F32)
            S.dma("sp", gb[:], g_row, writes=[tag + "g"])
            sq = P.t(tag + "_sq", [128, NC], F32)
            ss = P.t(tag + "_ss", [128, 2 * NTA], F32)
        for tt in range(NTA):
            b = tt % 2
            S.dma("sp", xt[b][:], src[tt * 128:(tt + 1) * 128, 0:NC], writes=[(tag, "x", b)])
            if g_row is None:
                S.op("act" if tt % 2 else "dve", (lambda e, b=b: e.copy(out=xb[b][:], in_=xt[b][:])) if tt % 2 else
                     (lambda e, b=b: e.tensor_copy(out=xb[b][:], in_=xt[b][:])), reads=[(tag, "x", b)], writes=[(tag, "xb", b)])
            if g_row is not None:
              S.op("act", lambda e, b=b, tt=tt: e.activation(out=sq[:], in_=xt[b][:], func=AF.Square,
                                                         accum_out=ss[:, 2 * tt:2 * tt + 1]),
                 reads=[(tag, "x", b)], writes=[(tag, "sq"), (tag, "ss", tt)])
            if g_row is not None:
              S.op("dve", lambda e, tt=tt: e.tensor_scalar(out=ss[:, 2 * tt + 1:2 * tt + 2], in0=ss[:, 2 * tt:2 * tt + 1],
                                                          scalar1=1.0 / D_, scalar2=EPS, op0=ALU.mult, op1=ALU.add),
                   reads=[(tag, "ss", tt)], writes=[(tag, "r0", tt)])
              S.op("act", lambda e, tt=tt: e.activation(out=ss[:, 2 * tt + 1:2 * tt + 2], in_=ss[:, 2 * tt + 1:2 * tt + 2], func=AF.Ln),
                   reads=[(tag, "r0", tt)], writes=[(tag, "r0b", tt)])
              S.op("act", lambda e, tt=tt: e.activation(out=ss[:, 2 * tt + 1:2 * tt + 2], in_=ss[:, 2 * tt + 1:2 * tt + 2], func=AF.Exp,
                                                       scale=-0.5),
                   reads=[(tag, "r0b", tt)], writes=[(tag, "r1", tt)])
              S.op("dve", lambda e, b=b, tt=tt: e.scalar_tensor_tensor(out=xb[b][:], in0=xt[b][:], scalar=ss[:, 2 * tt + 1:2 * tt + 2],
                                                                      in1=gb[:], op0=ALU.mult, op1=ALU.mult),
                   reads=[(tag, "x", b), (tag, "r1", tt), tag + "g"], writes=[(tag, "xb", b)])
            for k4 in range(KD // 4 if KD >= 4 else 1):
                nk = min(4, KD)
                pb = PB[k4 % 2]
                for kk in range(nk):
                    k = k4 * 4 + kk
                    S.op("pe", lambda e, b=b, k=k, kk=kk, pb=pb: e.transpose(out=pb[:, kk * 128:(kk + 1) * 128],
                                                                         in_=xb[b][:, k * 128:(k + 1) * 128], identity=ident_b[:]),
                         reads=[(tag, "xb", b)], writes=[("pb", k4 % 2)])
                evac(hT[:, k4 * 4:k4 * 4 + nk, tt * 128:(tt + 1) * 128],
                     pb[:, 0:nk * 128].rearrange("p (k t) -> p k t", k=nk),
                     reads=[("pb", k4 % 2)], writes=[(tag, "hT", tt, k4)])

    def bcast_row(ap_row, n):
        return ap_row.partition_broadcast(128)

    NCH = T // 64 + 1

    def gla_mixer(l):
        P = Pool(nc)
        NCG = 1552
        pg = [P.t(f"pgG{i}", [64, NCG], F32) for i in range(2)]
        a2 = P.t("a2G", [16, 256], F32)
        ab = P.t("abG", [64, 256], F32)
        gng = P.t("gngG", [64, 128], F32)
        S.dma("sp", a2[:], gla_a2[l], writes=["a2G"])
        S.dma("sp", ab[:], gla_a_b[l].partition_broadcast(64), writes=["abG"])
        S.dma("sp", gng[:], gla_norm_g[l].partition_broadcast(64), writes=["gngG"])
        aT = P.t("aTG", [16, 64], F32)
        la = P.t("laG", [64, 256], F32)
        cum = P.t("cumG", [64, 256], F32)
        e_q = P.t("eqG", [64, 256], F32)
        e_k = P.t("ekG", [64, 256], F32)
        e_l = P.t("elG", [64, 256], F32)
        qd = P.t("qdG", [64, 256], BF16)
        kd = P.t("kdG", [64, 256], BF16)
        kl = P.t("klG", [64, 256], BF16)
        vb = P.t("vbG", [64, 512], BF16)
        qkT = P.t("qkTG", [64, 8, 64], BF16)
        att = P.t("attG", [64, 4, 64], BF16)
        St = P.t("StG", [64, 4, 128], F32)
        Sb = P.t("SbG", [64, 4, 128], BF16)
        ecol = P.t("ecolG", [64, 4], F32)
        og = P.t("ogG", [64, 4, 128], F32)
        o2 = P.t("o2G", [64, 4, 128], F32)
        sg = P.t("sgG", [64, 512], F32)
        ssg = P.t("ssgG", [64, 16], F32)
        S.op("dve", lambda e: e.memset(St[:], 0.0), writes=["StG"])
        S.op("dve", lambda e: e.memset(Sb[:], 0.0), writes=["SbG"])
        QO, KO, VO, AO, GO = 0, 256, 512, 1024, 1040
        for c in range(NCH):
            is_s = (c == NCH - 1)
            r0 = c * 64
            p = pg[c % 2]
            pk = ("pgG", c % 2)
            if is_s:
                S.store("sp", p_gla_o[l].rearrange("h d v -> d h v"), St[:], reads=["StG"])
                S.dma("sp", St[:], state_gla[l].rearrange("h d v -> d h v"), writes=["StG"])
                S.op("dve", lambda e: e.tensor_copy(out=Sb[:], in_=St[:]), reads=["StG"], writes=["SbG"])
            S.dma("sp", p[:], proj[r0:r0 + 64, C_GQ:C_GQ + NCG], writes=[pk])
            S.op("pe", lambda e, p=p: e.transpose(out=PS[0][0:16, 0:64], in_=p[:, AO:AO + 16], identity=ident_f[0:64, 0:64]),
                 reads=[pk], writes=[("ps", 0)])
            S.op("act", lambda e: e.copy(out=aT[:], in_=PS[0][0:16, 0:64]), reads=[("ps", 0)], writes=["aTG"])
            S.op("pe", lambda e: e.matmul(PS[1][0:64, 0:256], lhsT=aT[:], rhs=a2[:], start=True, stop=True),
                 reads=["aTG", "a2G"], writes=[("ps", 1)])
            S.op("dve", lambda e: e.tensor_tensor(out=la[:], in0=PS[1][0:64, 0:256], in1=ab[:], op=ALU.add),
                 reads=[("ps", 1), "abG"], writes=["laG"])
            S.op("act", lambda e: e.activation(out=la[:], in_=la[:], func=AF.Exp, scale=-1.0), reads=["laG"], writes=["laG"])
            S.op("act", lambda e: e.activation(out=la[:], in_=la[:], func=AF.Ln, bias=1.0), reads=["laG"], writes=["laG"])
            mcol = ident_f[0:64, 0:1] if is_s else ones_f[0:64, 0:1]
            S.op("dve", lambda e, mcol=mcol: e.tensor_scalar(out=la[:], in0=la[:], scalar1=-1.0 / 16, scalar2=mcol,
                                                            op0=ALU.mult, op1=ALU.mult), reads=["laG"], writes=["laG"])
            S.op("pe", lambda e: e.matmul(PS[2][0:64, 0:256], lhsT=tri_f[:], rhs=la[:], start=True, stop=True),
                 reads=["laG"], writes=[("ps", 2)])
            S.op("pe", lambda e: e.matmul(PS[3][0:64, 0:256], lhsT=ones_f[0:64, 0:64], rhs=la[:], start=True, stop=True),
                 reads=["laG"], writes=[("ps", 3)])
            for h in range(4):
                S.op("pe", lambda e, h=h: e.matmul(PS[4][0:64, h:h + 1], lhsT=la[:, h * 64:(h + 1) * 64], rhs=ones_f[0:64, 0:1],
                                                   start=True, stop=True), reads=["laG"], writes=[("ps", 4)])
            S.op("act", lambda e: e.activation(out=ecol[:], in_=PS[4][0:64, 0:4], func=AF.Exp), reads=[("ps", 4)], writes=["ecolG"])
            S.op("dve", lambda e: e.tensor_copy(out=cum[:], in_=PS[2][0:64, 0:256]), reads=[("ps", 2)], writes=["cumG"])
            S.op("act", lambda e: e.activation(out=e_q[:], in_=cum[:], func=AF.Exp), reads=["cumG"], writes=["eqG"])
            S.op("act", lambda e: e.activation(out=e_k[:], in_=cum[:], func=AF.Exp, scale=-1.0), reads=["cumG"], writes=["ekG"])
            S.op("dve", lambda e: e.tensor_tensor(out=e_l[:], in0=PS[3][0:64, 0:256], in1=cum[:], op=ALU.subtract),
                 reads=[("ps", 3), "cumG"], writes=["elG"])
            S.op("act", lambda e: e.activation(out=e_l[:], in_=e_l[:], func=AF.Exp), reads=["elG"], writes=["elG"])
            S.op("dve", lambda e, p=p: e.scalar_tensor_tensor(out=qd[:], in0=p[:, QO:QO + 256], scalar=0.125, in1=e_q[:],
                                                              op0=ALU.mult, op1=ALU.mult), reads=[pk, "eqG"], writes=["qdG"])
            S.op("dve", lambda e, p=p: e.tensor_tensor(out=kd[:], in0=p[:, KO:KO + 256], in1=e_k[:], op=ALU.mult),
                 reads=[pk, "ekG"], writes=["kdG"])
            S.op("dve", lambda e, p=p: e.tensor_tensor(out=kl[:], in0=p[:, KO:KO + 256], in1=e_l[:], op=ALU.mult),
                 reads=[pk, "elG"], writes=["klG"])
            S.op("act", lambda e, p=p: e.copy(out=vb[:], in_=p[:, VO:VO + 512]), reads=[pk], writes=["vbG"])
            for h in range(4):
                S.op("pe", lambda e, h=h: e.transpose(out=PB[0][0:64, h * 64:(h + 1) * 64], in_=qd[:, h * 64:(h + 1) * 64],
                                                      identity=ident_b[0:64, 0:64]), reads=["qdG"], writes=[("pb", 0)])
                S.op("pe", lambda e, h=h: e.transpose(out=PB[0][0:64, (4 + h) * 64:(5 + h) * 64], in_=kd[:, h * 64:(h + 1) * 64],
                                                      identity=ident_b[0:64, 0:64]), reads=["kdG"], writes=[("pb", 0)])
            S.op("dve", lambda e: e.tensor_copy(out=qkT[:].rearrange("p a t -> p (a t)"), in_=PB[0][0:64, 0:512]),
                 reads=[("pb", 0)], writes=["qkTG"])
            for h in range(4):
                S.op("pe", lambda e, h=h: e.matmul(PS[5][0:64, h * 64:(h + 1) * 64], lhsT=qkT[:, 4 + h, :], rhs=qkT[:, h, :],
                                                   start=True, stop=True), reads=["qkTG"], writes=[("ps", 5)])
            S.op("dve", lambda e: e.tensor_tensor(out=att[:], in0=PS[5][0:64, 0:256].rearrange("p (h t) -> p h t", h=4),
                                                  in1=tri_f[:].unsqueeze(1).to_broadcast([64, 4, 64]), op=ALU.mult),
                 reads=[("ps", 5)], writes=["attG"])
            for h in range(4):
                S.op("pe", lambda e, h=h: e.matmul(PS[0][0:64, h * 128:(h + 1) * 128], lhsT=att[:, h, :], rhs=vb[:, h * 128:(h + 1) * 128],
                                                   start=True, stop=False), reads=["attG", "vbG"], writes=[("ps", 0)])
                S.op("pe", lambda e, h=h: e.matmul(PS[0][0:64, h * 128:(h + 1) * 128], lhsT=qkT[:, h, :], rhs=Sb[:, h, :],
                                                   start=False, stop=True), reads=["qkTG", "SbG"], writes=[("ps", 0)])
            S.op("act", lambda e: e.copy(out=og[:].rearrange("p h v -> p (h v)"), in_=PS[0][0:64, 0:512]), reads=[("ps", 0)], writes=["ogG"])
            for h in range(4):
                S.op("pe", lambda e, h=h: e.matmul(PS[1][0:64, h * 128:(h + 1) * 128], lhsT=kl[:, h * 64:(h + 1) * 64],
                                                   rhs=vb[:, h * 128:(h + 1) * 128], start=True, stop=True),
                     reads=["klG", "vbG"], writes=[("ps", 1)])
            for h in range(4):
                S.op("dve", lambda e, h=h: e.scalar_tensor_tensor(out=St[:, h, :], in0=St[:, h, :], scalar=ecol[:, h:h + 1],
                                                                  in1=PS[1][0:64, h * 128:(h + 1) * 128], op0=ALU.mult, op1=ALU.add),
                     reads=["StG", "ecolG", ("ps", 1)], writes=["StG"])
            S.op("dve", lambda e: e.tensor_copy(out=Sb[:], in_=St[:]), reads=["StG"], writes=["SbG"])
            S.op("dve", lambda e: e.tensor_tensor(out=o2[:], in0=og[:], in1=og[:], op=ALU.mult), reads=["ogG"], writes=["o2G"])
            S.op("dve", lambda e: e.tensor_reduce(out=ssg[:, 0:4], in_=o2[:], axis=AX.X, op=ALU.add), reads=["o2G"], writes=["ssg0"])
            S.op("dve", lambda e: e.tensor_scalar(out=ssg[:, 4:8], in0=ssg[:, 0:4], scalar1=1.0 / 128, scalar2=EPS,
                                                  op0=ALU.mult, op1=ALU.add), reads=["ssg0"], writes=["ssg1"])
            S.op("act", lambda e: e.activation(out=ssg[:, 8:12], in_=ssg[:, 4:8], func=AF.Ln), reads=["ssg1"], writes=["ssg2"])
            S.op("act", lambda e: e.activation(out=ssg[:, 12:16], in_=ssg[:, 8:12], func=AF.Exp, scale=-0.5), reads=["ssg2"], writes=["ssg3"])
            S.op("dve", lambda e: e.tensor_tensor(out=o2[:], in0=og[:], in1=ssg[:, 12:16].unsqueeze(2).to_broadcast([64, 4, 128]),
                                                  op=ALU.mult), reads=["ogG", "ssg3"], writes=["o2G"])
            S.op("dve", lambda e: e.tensor_tensor(out=o2[:], in0=o2[:], in1=gng[:].unsqueeze(1).to_broadcast([64, 4, 128]),
                                                  op=ALU.mult), reads=["o2G", "gngG"], writes=["o2G"])
            S.op("act", lambda e, p=p: e.activation(out=sg[:], in_=p[:, GO:GO + 512], func=AF.Sigmoid), reads=[pk], writes=["sgG"])
            S.op("dve", lambda e, p=p: e.tensor_tensor(out=sg[:], in0=sg[:], in1=p[:, GO:GO + 512], op=ALU.mult),
                 reads=["sgG", pk], writes=["sgG"])
            S.op("dve", lambda e: e.tensor_tensor(out=o2[:].rearrange("p h v -> p (h v)"), in0=o2[:].rearrange("p h v -> p (h v)"),
                                                  in1=sg[:], op=ALU.mult), reads=["o2G", "sgG"], writes=["o2G"])
            S.store("sp", omix[r0:r0 + 64, 1024:1536], o2[:].rearrange("p h v -> p (h v)"), reads=["o2G"])
        S.store("sp", s_gla_o[l].rearrange("h d v -> d h v"), St[:], reads=["StG"])
        S.barrier()
        P.release()


    def rw_mixer(l):
        P = Pool(nc)
        H8 = lambda ap: ap.rearrange("p (h d) -> p h d", h=8)
        cnt = [0]

        def T_(name, shape=(64, 512), dt=F32):
            return P.t(name + "R", list(shape), dt)

        rwt = [T_(f"rwt{i}", (64, RWP)) for i in range(2)]
        sht = [T_(f"sht{i}", (64, RWP)) for i in range(2)]
        mu = T_("mu", (64, RWP)); S.dma("sp", mu[:], rw_mu[l].partition_broadcast(64), writes=["muR"])
        cb = {}
        for nm, src in (("w0", rw_w0), ("a0", rw_a0), ("kkw", rw_kk), ("ka", rw_ka), ("lng", rw_ln_g), ("lnb", rw_ln_b)):
            cb[nm] = T_(nm)
            S.dma("sp", cb[nm][:], src[l].partition_broadcast(64), writes=[nm + "R"])
        rk = T_("rk"); S.dma("sp", rk[:], rw_rk[l].rearrange("h d -> (h d)").partition_broadcast(64), writes=["rkR"])
        w2 = T_("w2", (32, 512)); S.dma("sp", w2[:], rw_w2[l], writes=["w2R"])
        a2 = T_("a2", (32, 512)); S.dma("sp", a2[:], rw_a2[l], writes=["a2R"])
        g2 = T_("g2", (96, 512)); S.dma("sp", g2[:], rw_g2[l], writes=["g2R"])
        xm = T_("xm", (64, RWP))
        sm = T_("sm", (64, 160))
        smT = T_("smT", (96, 3, 64))
        names = ["lw", "av", "gv", "kk", "kp", "tmp", "cum", "E1", "E2", "E3", "E4", "Rt", "At", "Bt", "Kt", "Bh", "Kh", "bv",
                 "vv", "bon", "oo", "o2"]
        BFN = ("Rt", "At", "Bt", "Kt", "Bh", "Kh")
        t = {n: (T_(n, (64, 512), BF16) if n in BFN else T_(n)) for n in names}
        vvb = T_("vvb", (64, 512), BF16)
        ssr = T_("ssr", (64, 64))
        gcol = T_("gcol", (64, 8))
        TT = {n: T_(n + "T", (64, 8, 64), BF16) for n in ("Rt", "At", "Bt", "Kt")}
        mats = {n: T_(n, (64, 8, 64), BF16) for n in ("N", "N2", "AK", "KR", "BR", "Q", "MAT", "LV", "U")}
        H = T_("H", (64, 8, 64))
        Hb = T_("Hb", (64, 8, 64), BF16)
        Hv = T_("Hv", (64, 8, 64))
        lvl = P.t("lvlR", [64, 5, 8, 64], BF16)
        S.op("dve", lambda e: e.memset(H[:], 0.0), writes=["HR"])
        S.op("dve", lambda e: e.memset(Hb[:], 0.0), writes=["HbR"])

        def op3(eng, out, in0, in1, op, rk_, wk_):
            S.op(eng, lambda e: e.tensor_tensor(out=out, in0=in0, in1=in1, op=op), reads=rk_, writes=wk_)

        def store_state(dst):
            for h in range(8):
                S.op("pe", lambda e, h=h: e.transpose(out=PS[0][0:64, h * 64:(h + 1) * 64], in_=H[:, h, :], identity=ident_f[0:64, 0:64]),
                     reads=["HR"], writes=[("ps", 0)])
            S.op("act", lambda e: e.copy(out=Hv[:].rearrange("p h k -> p (h k)"), in_=PS[0][0:64, 0:512]), reads=[("ps", 0)], writes=["HvR"])
            S.store("sp", dst.rearrange("h v k -> v h k"), Hv[:], reads=["HvR"])

        def load_state(src):
            S.dma("sp", Hv[:], src.rearrange("h v k -> v h k"), writes=["HvR"])
            for h in range(8):
                S.op("pe", lambda e, h=h: e.transpose(out=PS[0][0:64, h * 64:(h + 1) * 64], in_=Hv[:, h, :], identity=ident_f[0:64, 0:64]),
                     reads=["HvR"], writes=[("ps", 0)])
            S.op("act", lambda e: e.copy(out=H[:].rearrange("p h k -> p (h k)"), in_=PS[0][0:64, 0:512]), reads=[("ps", 0)], writes=["HR"])
            S.op("dve", lambda e: e.tensor_copy(out=Hb[:], in_=H[:]), reads=["HR"], writes=["HbR"])

        def mm8(ps_i, lhs_fn, rhs_fn, rk_, start=True, stop=True):
            for h in range(8):
                S.op("pe", lambda e, h=h: e.matmul(PS[ps_i][0:64, h * 64:(h + 1) * 64], lhsT=lhs_fn(h), rhs=rhs_fn(h), start=start, stop=stop),
                     reads=rk_, writes=[("ps", ps_i)])

        def ps8(i):
            return PS[i][0:64, 0:512].rearrange("p (h d) -> p h d", h=8)

        for c in range(NCH):
            is_s = (c == NCH - 1)
            r0 = c * 64
            b = c % 2
            rwb, shb = rwt[b], sht[b]
            kr, ks = ("rwtR", b), ("shtR", b)
            if is_s:
                store_state(p_rw_o[l])
                load_state(state_rwkv[l])
            S.dma("sp", rwb[:], proj[r0:r0 + 64, C_RW:C_RW + RWP], writes=[kr])
            if c == 0:
                S.dma("sp", shb[0:1, :], zeros_f[0:1, 0:RWP], writes=[(ks, 0)])
            elif is_s:
                S.dma("sp", shb[0:1, :], state_shift[l:l + 1, :], writes=[(ks, 0)])
            else:
                S.dma("sp", shb[0:1, :], proj[r0 - 1:r0, C_RW:C_RW + RWP], writes=[(ks, 0)])
            S.dma("sp", shb[1:64, :], proj[r0:r0 + 63, C_RW:C_RW + RWP], writes=[(ks, 1)])
            rsh = [kr, (ks, 0), (ks, 1)]
            op3("pool", xm[:], shb[:], rwb[:], ALU.subtract, rsh, ["xmR"])
            op3("pool", xm[:], xm[:], mu[:], ALU.mult, ["xmR", "muR"], ["xmR"])
            op3("pool", xm[:], xm[:], rwb[:], ALU.add, ["xmR", kr], ["xmR"])
            r_, k_, v_ = xm[:, 0:512], xm[:, 512:1024], xm[:, 1024:1536]
            S.op("act", lambda e: e.activation(out=sm[:, 0:32], in_=xm[:, 1536:1568], func=AF.Tanh), reads=["xmR"], writes=[("smR", 0)])
            S.op("act", lambda e: e.copy(out=sm[:, 32:64], in_=xm[:, 1568:1600]), reads=["xmR"], writes=[("smR", 1)])
            S.op("act", lambda e: e.activation(out=sm[:, 64:160], in_=xm[:, 1600:1696], func=AF.Sigmoid), reads=["xmR"], writes=[("smR", 2)])
            for i, (o0, n) in enumerate(((0, 32), (32, 32), (64, 96))):
                S.op("pe", lambda e, i=i, o0=o0, n=n: e.transpose(out=PS[0][0:n, i * 64:(i + 1) * 64], in_=sm[:, o0:o0 + n],
                                                                  identity=ident_f[0:64, 0:64]), reads=[("smR", i)], writes=[("ps", 0)])
                S.op("act", lambda e, i=i, n=n: e.copy(out=smT[0:n, i, :], in_=PS[0][0:n, i * 64:(i + 1) * 64]),
                     reads=[("ps", 0)], writes=[("smTR", i)])
            for i, (wsb, n, wk) in enumerate(((w2, 32, "w2R"), (a2, 32, "a2R"), (g2, 96, "g2R"))):
                S.op("pe", lambda e, i=i, wsb=wsb, n=n: e.matmul(PS[1 + i][0:64, 0:512], lhsT=smT[0:n, i, :], rhs=wsb[:], start=True, stop=True),
                     reads=[("smTR", i), wk], writes=[("ps", 1 + i)])
            mcol = ident_f[0:64, 0:1] if is_s else ones_f[0:64, 0:1]
            op3("dve", t["lw"][:], PS[1][0:64, 0:512], cb["w0"][:], ALU.add, [("ps", 1), "w0R"], ["lwR"])
            S.op("act", lambda e: e.activation(out=t["lw"][:], in_=t["lw"][:], func=AF.Sigmoid), reads=["lwR"], writes=["lwR"])
            S.op("dve", lambda e, mcol=mcol: e.tensor_scalar(out=t["lw"][:], in0=t["lw"][:], scalar1=-0.6065306597, scalar2=mcol,
                                                            op0=ALU.mult, op1=ALU.mult), reads=["lwR"], writes=["lwR"])
            op3("dve", t["av"][:], PS[2][0:64, 0:512], cb["a0"][:], ALU.add, [("ps", 2), "a0R"], ["avR"])
            S.op("act", lambda e: e.activation(out=t["av"][:], in_=t["av"][:], func=AF.Sigmoid), reads=["avR"], writes=["avR"])
            S.op("act", lambda e: e.copy(out=t["gv"][:], in_=PS[3][0:64, 0:512]), reads=[("ps", 3)], writes=["gvR"])
            op3("dve", t["kk"][:], k_, cb["kkw"][:], ALU.mult, ["xmR", "kkwR"], ["kkR"])
            op3("dve", t["tmp"][:], t["kk"][:], t["kk"][:], ALU.mult, ["kkR"], ["tmpR"])
            S.op("dve", lambda e: e.tensor_reduce(out=ssr[:, 0:8], in_=H8(t["tmp"][:]), axis=AX.X, op=ALU.add), reads=["tmpR"], writes=["ss0R"])
            S.op("dve", lambda e: e.tensor_scalar(out=ssr[:, 8:16], in0=ssr[:, 0:8], scalar1=1e-24, scalar2=None, op0=ALU.max),
                 reads=["ss0R"], writes=["ss1R"])
            S.op("act", lambda e: e.activation(out=ssr[:, 16:24], in_=ssr[:, 8:16], func=AF.Ln), reads=["ss1R"], writes=["ss2R"])
            S.op("act", lambda e: e.activation(out=ssr[:, 24:32], in_=ssr[:, 16:24], func=AF.Exp, scale=-0.5), reads=["ss2R"], writes=["ss3R"])
            S.op("dve", lambda e, mcol=mcol: e.tensor_scalar(out=ssr[:, 24:32], in0=ssr[:, 24:32], scalar1=mcol, scalar2=None, op0=ALU.mult),
                 reads=["ss3R"], writes=["ss3R"])
            op3("dve", H8(t["kk"][:]), H8(t["kk"][:]), ssr[:, 24:32].unsqueeze(2).to_broadcast([64, 8, 64]), ALU.mult, ["kkR", "ss3R"], ["kkR"])
            S.op("dve", lambda e: e.scalar_tensor_tensor(out=t["kp"][:], in0=t["av"][:], scalar=-1.0, in1=cb["ka"][:], op0=ALU.add, op1=ALU.mult),
                 reads=["avR", "kaR"], writes=["kpR"])
            S.op("dve", lambda e: e.scalar_tensor_tensor(out=t["kp"][:], in0=t["kp"][:], scalar=1.0, in1=k_, op0=ALU.add, op1=ALU.mult),
                 reads=["kpR", "xmR"], writes=["kpR"])
            S.op("pool", lambda e, mcol=mcol: e.tensor_scalar(out=t["kp"][:], in0=t["kp"][:], scalar1=mcol, scalar2=None, op0=ALU.mult),
                 reads=["kpR"], writes=["kpR"])
            S.op("pool", lambda e, mcol=mcol: e.tensor_scalar(out=t["vv"][:], in0=v_, scalar1=mcol, scalar2=None, op0=ALU.mult),
                 reads=["xmR"], writes=["vvR"])
            S.op("act", lambda e: e.copy(out=vvb[:], in_=t["vv"][:]), reads=["vvR"], writes=["vvbR"])
            op3("dve", t["bv"][:], t["kk"][:], t["av"][:], ALU.mult, ["kkR", "avR"], ["bvR"])
            op3("dve", t["tmp"][:], r_, t["kp"][:], ALU.mult, ["xmR", "kpR"], ["tmpR"])
            op3("dve", t["tmp"][:], t["tmp"][:], rk[:], ALU.mult, ["tmpR", "rkR"], ["tmpR"])
            S.op("dve", lambda e: e.tensor_reduce(out=ssr[:, 32:40], in_=H8(t["tmp"][:]), axis=AX.X, op=ALU.add), reads=["tmpR"], writes=["ss4R"])
            op3("dve", H8(t["bon"][:]), H8(t["vv"][:]), ssr[:, 32:40].unsqueeze(2).to_broadcast([64, 8, 64]), ALU.mult, ["vvR", "ss4R"], ["bonR"])
            S.op("pe", lambda e: e.matmul(PS[4][0:64, 0:512], lhsT=tri_f[:], rhs=t["lw"][:], start=True, stop=True), reads=["lwR"], writes=[("ps", 4)])
            S.op("pe", lambda e: e.matmul(PS[5][0:64, 0:512], lhsT=ones_f[0:64, 0:64], rhs=t["lw"][:], start=True, stop=True),
                 reads=["lwR"], writes=[("ps", 5)])
            for h in range(8):
                S.op("pe", lambda e, h=h: e.matmul(PS[0][0:64, h:h + 1], lhsT=t["lw"][:, h * 64:(h + 1) * 64], rhs=ones_f[0:64, 0:1],
                                                   start=True, stop=True), reads=["lwR"], writes=[("ps", 0)])
            S.op("act", lambda e: e.activation(out=gcol[:], in_=PS[0][0:64, 0:8], func=AF.Exp), reads=[("ps", 0)], writes=["gcolR"])
            S.op("dve", lambda e: e.tensor_copy(out=t["cum"][:], in_=PS[4][0:64, 0:512]), reads=[("ps", 4)], writes=["cumR"])
            S.op("act", lambda e: e.activation(out=t["E1"][:], in_=t["cum"][:], func=AF.Exp), reads=["cumR"], writes=["E1R"])
            S.op("act", lambda e: e.activation(out=t["E2"][:], in_=t["cum"][:], func=AF.Exp, scale=-1.0), reads=["cumR"], writes=["E2R"])
            op3("dve", t["E3"][:], t["cum"][:], t["lw"][:], ALU.subtract, ["cumR", "lwR"], ["E3R"])
            S.op("act", lambda e: e.activation(out=t["E3"][:], in_=t["E3"][:], func=AF.Exp), reads=["E3R"], writes=["E3R"])
            op3("dve", t["E4"][:], PS[5][0:64, 0:512], t["cum"][:], ALU.subtract, [("ps", 5), "cumR"], ["E4R"])
            S.op("act", lambda e: e.activation(out=t["E4"][:], in_=t["E4"][:], func=AF.Exp), reads=["E4R"], writes=["E4R"])
            op3("dve", t["Rt"][:], r_, t["E1"][:], ALU.mult, ["xmR", "E1R"], ["RtR"])
            S.op("dve", lambda e: e.scalar_tensor_tensor(out=t["At"][:], in0=t["kk"][:], scalar=-1.0, in1=t["E3"][:], op0=ALU.mult, op1=ALU.mult),
                 reads=["kkR", "E3R"], writes=["AtR"])
            op3("pool", t["Bt"][:], t["bv"][:], t["E2"][:], ALU.mult, ["bvR", "E2R"], ["BtR"])
            op3("pool", t["Kt"][:], t["kp"][:], t["E2"][:], ALU.mult, ["kpR", "E2R"], ["KtR"])
            op3("pool", t["Bh"][:], t["bv"][:], t["E4"][:], ALU.mult, ["bvR", "E4R"], ["BhR"])
            op3("pool", t["Kh"][:], t["kp"][:], t["E4"][:], ALU.mult, ["kpR", "E4R"], ["KhR"])
            for i, n in enumerate(("Rt", "At", "Bt", "Kt")):
                pbt, po_ = PB[i // 2], (i % 2) * 512
                for h in range(8):
                    S.op("pe", lambda e, h=h, n=n, pbt=pbt, po_=po_: e.transpose(out=pbt[0:64, po_ + h * 64:po_ + (h + 1) * 64],
                                                                               in_=t[n][:, h * 64:(h + 1) * 64], identity=ident_b[0:64, 0:64]),
                         reads=[n + "R"], writes=[("pb", i // 2)])
                evac(TT[n][:].rearrange("p h d -> p (h d)"), pbt[0:64, po_:po_ + 512], reads=[("pb", i // 2)], writes=[n + "TR"])
            hh = lambda m: (lambda h: m[:, h, :])
            for (nm, lh, rh, msk, psi) in (("N", "Bt", "At", trs_f, 5), ("Lm", "At", "Bt", trl_f, 0), ("AK", "Kt", "At", trs_f, 1),
                                           ("KR", "Kt", "Rt", tri_f, 2), ("BR", "Bt", "Rt", tri_f, 3)):
                mm8(psi, hh(TT[lh]), hh(TT[rh]), [lh + "TR", rh + "TR"])
                dst_, dk_ = (lvl[:, 0], ("lvlR", 0)) if nm == "Lm" else (mats[nm][:], nm + "R")
                op3("dve", dst_, ps8(psi), msk[:].unsqueeze(1).to_broadcast([64, 8, 64]), ALU.mult, [("ps", psi)], [dk_])
            Nb = [mats["N"], mats["N2"]]
            Nk = ["NR", "N2R"]
            for i in range(5):
                cur, nxt = i % 2, (i + 1) % 2
                mm8(4, (lambda h, i=i: lvl[:, i, h, :]), hh(Nb[cur]), [Nk[cur], ("lvlR", i)])
                if i < 4:
                    mm8(5, hh(Nb[cur]), (lambda h, i=i: lvl[:, i, h, :]), [Nk[cur], ("lvlR", i)])
                S.op("act", lambda e, nxt=nxt: e.copy(out=Nb[nxt][:], in_=ps8(4)), reads=[("ps", 4)], writes=[Nk[nxt]])
                if i < 4:
                    S.op("dve", lambda e, i=i: e.tensor_copy(out=lvl[:, i + 1], in_=ps8(5)), reads=[("ps", 5)], writes=[("lvlR", i + 1)])
            op3("dve", mats["Q"][:], Nb[1][:], ident_f[0:64, 0:64].unsqueeze(1).to_broadcast([64, 8, 64]), ALU.add, [Nk[1]], ["QR"])
            for i in (4, 3, 2, 1, 0):
                mm8(4, (lambda h, i=i: lvl[:, i, h, :]), hh(mats["Q"]), [("lvlR", i), "QR"])
                op3("dve", mats["Q"][:], mats["Q"][:], ps8(4), ALU.add, ["QR", ("ps", 4)], ["QR"])
            mm8(5, (lambda h: t["At"][:, h * 64:(h + 1) * 64]), hh(mats["Q"]), ["AtR", "QR"])
            evac(mats["MAT"][:], ps8(5), reads=[("ps", 5)], writes=["MATR"])
            mm8(0, hh(mats["AK"]), (lambda h: vvb[:, h * 64:(h + 1) * 64]), ["AKR", "vvbR"])
            evac(mats["LV"][:], ps8(0), reads=[("ps", 0)], writes=["LVR"])
            for h in range(8):
                S.op("pe", lambda e, h=h: e.matmul(PS[1][0:64, h * 64:(h + 1) * 64], lhsT=mats["Q"][:, h, :], rhs=mats["LV"][:, h, :],
                                                   start=True, stop=False), reads=["QR", "LVR"], writes=[("ps", 1)])
                S.op("pe", lambda e, h=h: e.matmul(PS[1][0:64, h * 64:(h + 1) * 64], lhsT=mats["MAT"][:, h, :], rhs=Hb[:, h, :],
                                                   start=False, stop=True), reads=["MATR", "HbR"], writes=[("ps", 1)])
            evac(mats["U"][:], ps8(1), reads=[("ps", 1)], writes=["UR"])
            for h in range(8):
                sl = slice(h * 64, (h + 1) * 64)
                S.op("pe", lambda e, h=h, sl=sl: e.matmul(PS[2][0:64, sl], lhsT=TT["Rt"][:, h, :], rhs=Hb[:, h, :], start=True, stop=False),
                     reads=["RtTR", "HbR"], writes=[("ps", 2)])
                S.op("pe", lambda e, h=h, sl=sl: e.matmul(PS[2][0:64, sl], lhsT=mats["BR"][:, h, :], rhs=mats["U"][:, h, :], start=False, stop=False),
                     reads=["BRR", "UR"], writes=[("ps", 2)])
                S.op("pe", lambda e, h=h, sl=sl: e.matmul(PS[2][0:64, sl], lhsT=mats["KR"][:, h, :], rhs=vvb[:, sl], start=False, stop=True),
                     reads=["KRR", "vvbR"], writes=[("ps", 2)])
            S.op("act", lambda e: e.copy(out=t["oo"][:], in_=PS[2][0:64, 0:512]), reads=[("ps", 2)], writes=["ooR"])
            for h in range(8):
                sl = slice(h * 64, (h + 1) * 64)
                S.op("pe", lambda e, h=h, sl=sl: e.matmul(PS[3][0:64, sl], lhsT=t["Bh"][:, sl], rhs=mats["U"][:, h, :], start=True, stop=False),
                     reads=["BhR", "UR"], writes=[("ps", 3)])
                S.op("pe", lambda e, h=h, sl=sl: e.matmul(PS[3][0:64, sl], lhsT=t["Kh"][:, sl], rhs=vvb[:, sl], start=False, stop=True),
                     reads=["KhR", "vvbR"], writes=[("ps", 3)])
            op3("dve", H[:], H[:], gcol[:].unsqueeze(2).to_broadcast([64, 8, 64]), ALU.mult, ["HR", "gcolR"], ["HR"])
            op3("dve", H[:], H[:], ps8(3), ALU.add, ["HR", ("ps", 3)], ["HR"])
            S.op("act", lambda e: e.copy(out=Hb[:], in_=H[:]), reads=["HR"], writes=["HbR"])
            S.op("dve", lambda e: e.tensor_reduce(out=ssr[:, 40:48], in_=H8(t["oo"][:]), axis=AX.X, op=ALU.add), reads=["ooR"], writes=["ss5R"])
            S.op("dve", lambda e: e.tensor_scalar(out=ssr[:, 40:48], in0=ssr[:, 40:48], scalar1=-1.0 / 64, scalar2=None, op0=ALU.mult),
                 reads=["ss5R"], writes=["ss5R"])
            op3("dve", H8(t["oo"][:]), H8(t["oo"][:]), ssr[:, 40:48].unsqueeze(2).to_broadcast([64, 8, 64]), ALU.add, ["ooR", "ss5R"], ["ooR"])
            op3("dve", t["o2"][:], t["oo"][:], t["oo"][:], ALU.mult, ["ooR"], ["o2R"])
            S.op("dve", lambda e: e.tensor_reduce(out=ssr[:, 48:56], in_=H8(t["o2"][:]), axis=AX.X, op=ALU.add), reads=["o2R"], writes=["ss6R"])
            S.op("dve", lambda e: e.tensor_scalar(out=ssr[:, 48:56], in0=ssr[:, 48:56], scalar1=1.0 / 64, scalar2=64e-5, op0=ALU.mult, op1=ALU.add),
                 reads=["ss6R"], writes=["ss6R"])
            S.op("act", lambda e: e.activation(out=ssr[:, 48:56], in_=ssr[:, 48:56], func=AF.Ln), reads=["ss6R"], writes=["ss6R"])
            S.op("act", lambda e: e.activation(out=ssr[:, 56:64], in_=ssr[:, 48:56], func=AF.Exp, scale=-0.5), reads=["ss6R"], writes=["ss7R"])
            op3("dve", H8(t["oo"][:]), H8(t["oo"][:]), ssr[:, 56:64].unsqueeze(2).to_broadcast([64, 8, 64]), ALU.mult, ["ooR", "ss7R"], ["ooR"])
            op3("pool", t["oo"][:], t["oo"][:], cb["lng"][:], ALU.mult, ["ooR", "lngR"], ["ooR"])
            op3("pool", t["oo"][:], t["oo"][:], cb["lnb"][:], ALU.add, ["ooR", "lnbR"], ["ooR"])
            op3("pool", t["oo"][:], t["oo"][:], t["bon"][:], ALU.add, ["ooR", "bonR"], ["ooR"])
            op3("pool", t["oo"][:], t["oo"][:], t["gv"][:], ALU.mult, ["ooR", "gvR"], ["ooR"])
            S.store("sp", omix[r0:r0 + 64, 1536:2048], t["oo"][:], reads=["ooR"])
        store_state(s_rw_o[l])
        S.barrier()
        P.release()


    def nsa_mixer(l):
        LP = Pool(nc)
        kcT_p = LP.t("kcT_p", [64, 4, NBP], BF16)
        vc_p = LP.t("vc_p", [max(NBP, 2), 4, 64], BF16)
        qkg = LP.t("qkgN", [128, 4, 64], F32)
        S.dma("sp", qkg[:].rearrange("p a d -> p (a d)"), nsa_qk_g[l].rearrange("a d -> (a d)").partition_broadcast(128), writes=["qkgN"])
        tmpN = LP.t("tmpN", [128, 16, 64], F32)
        ssN = LP.t("ssN", [128, 64], F32)
        kcT_s = LP.t("kcT_s", [64, 4, NBS], BF16)
        vc_s = LP.t("vc_s", [SEGB, NSEG, 4, 64], BF16)
        q_s = LP.t("q_s", [64, 16], BF16)
        k_s = LP.t("k_s", [64, 8], BF16)
        v_s = LP.t("v_s", [1, 8, 65], BF16)
        g_s = LP.t("g_s", [1, 48], F32)

        def headnormN(out3, in3, g_ap, H, rk, wk, sc=1.0, np_=128):
            S.op("dve", lambda e: e.tensor_tensor(out=tmpN[0:np_, 0:H, :], in0=in3, in1=in3, op=ALU.mult), reads=rk, writes=["tmpN"])
            S.op("dve", lambda e: e.tensor_reduce(out=ssN[0:np_, 0:H], in_=tmpN[0:np_, 0:H, :], axis=AX.X, op=ALU.add), reads=["tmpN"], writes=["ssN0"])
            S.op("dve", lambda e: e.tensor_scalar(out=ssN[0:np_, 16:16 + H], in0=ssN[0:np_, 0:H], scalar1=1.0 / 64, scalar2=EPS,
                                                  op0=ALU.mult, op1=ALU.add), reads=["ssN0"], writes=["ssN1"])
            S.op("act", lambda e: e.activation(out=ssN[0:np_, 48:48 + H], in_=ssN[0:np_, 16:16 + H], func=AF.Ln), reads=["ssN1"], writes=["ssN1b"])
            S.op("act", lambda e: e.activation(out=ssN[0:np_, 32:32 + H], in_=ssN[0:np_, 48:48 + H], func=AF.Exp, scale=-0.5,
                                               bias=float(np.log(sc))), reads=["ssN1b"], writes=["ssN2"])
            S.op("dve", lambda e: e.tensor_tensor(out=tmpN[0:np_, 0:H, :], in0=in3,
                                                  in1=ssN[0:np_, 32:32 + H].unsqueeze(2).to_broadcast([np_, H, 64]), op=ALU.mult),
                 reads=rk + ["ssN2"], writes=["tmpN"])
            S.op("dve", lambda e: e.tensor_tensor(out=out3, in0=tmpN[0:np_, 0:H, :],
                                                  in1=g_ap.unsqueeze(1).to_broadcast([np_, H, 64]), op=ALU.mult),
                 reads=["tmpN", "qkgN"], writes=wk)

        P = Pool(nc)
        w1 = P.t("w1N", [64, 2, 64, 128], BF16)
        w2 = P.t("w2N", [128, 2, 64], BF16)
        for kv in range(2):
            S.dma("pool", w1[:, kv], cmp_w1[l, kv].rearrange("(pos d) h -> d pos h", d=64), writes=[("w1N", kv)])
            S.dma("pool", w2[:, kv, :], cmp_w2[l, kv], writes=[("w2N", kv)])
        posb = P.t("posbN", [128, 2, 64], F32)
        for hf in range(2):
            S.dma("sp", posb[hf * 64:(hf + 1) * 64], cmp_pos[l].rearrange("kv pos d -> pos kv d"), writes=[("posbN", hf)])
        zT = P.t("zTN", [64, 8, T], BF16)
        ct = [P.t(f"ctN{i}", [128, 512], F32) for i in range(2)]
        zb = [P.t(f"zbN{i}", [128, 512], BF16) for i in range(2)]
        hx = [P.t(f"hxN{i}", [128, 4 * NBP], F32) for i in range(4)]
        hb = P.t("hbN", [128, 4 * NBP], BF16)
        kc = P.t("kcN", [max(NBP, 2), 4, 64], F32)
        kcb = P.t("kcbN", [max(NBP, 2), 4, 64], BF16)

        def compress_segment(load_fn, ntiles, kT_out, v_out):
            nb = ntiles * 2
            for tt in range(ntiles):
                b = tt % 2
                load_fn(tt, ct[b], ("ctN", b))
                S.op("dve", lambda e, b=b: e.tensor_tensor(out=zb[b][:].rearrange("p (kv g d) -> p kv g d", kv=2, g=4),
                                                           in0=ct[b][:].rearrange("p (kv g d) -> p kv g d", kv=2, g=4),
                                                           in1=posb[:].unsqueeze(2).to_broadcast([128, 2, 4, 64]), op=ALU.add),
                     reads=[("ctN", b), ("posbN", 0), ("posbN", 1)], writes=[("zbN", b)])
                for a in range(8):
                    S.op("pe", lambda e, a=a, b=b: e.transpose(out=PB[b][0:64, a * 128:(a + 1) * 128], in_=zb[b][:, a * 64:(a + 1) * 64],
                                                             identity=ident_b[:]), reads=[("zbN", b)], writes=[("pb", b)])
                evac(zT[:, :, tt * 128:(tt + 1) * 128], PB[b][0:64, 0:1024].rearrange("p (a t) -> p a t", a=8),
                     reads=[("pb", b)], writes=[("zTN", tt)])
            zr = [("zTN", tt) for tt in range(ntiles)]
            for kv in range(2):
                ps = PS[kv]
                for pos in range(64):
                    S.op("pe", lambda e, kv=kv, pos=pos, ps=ps: e.matmul(
                        ps[:, 0:4 * nb].rearrange("p (g n) -> p g n", g=4), lhsT=w1[:, kv, pos, :],
                        rhs=zT[:, kv * 4:(kv + 1) * 4, bass.DynSlice(pos, nb, step=64)] if False else
                        zT[:, kv * 4:(kv + 1) * 4, 0:nb * 64].rearrange("p g (n s) -> p g n s", s=64)[:, :, :, pos],
                        start=(pos == 0), stop=(pos == 63)), reads=zr + [("w1N", kv)], writes=[("ps", kv)])
                n4 = 4 * nb
                x_, x2, u_, th = hx[0], hx[1], hx[2], hx[3]
                S.op("act", lambda e, ps=ps: e.copy(out=x_[:, 0:n4], in_=ps[:, 0:n4]), reads=[("ps", kv)], writes=["hx0"])
                S.op("dve", lambda e: e.tensor_tensor(out=x2[:, 0:n4], in0=x_[:, 0:n4], in1=x_[:, 0:n4], op=ALU.mult), reads=["hx0"], writes=["hx1"])
                S.op("dve", lambda e: e.tensor_scalar(out=x2[:, 0:n4], in0=x2[:, 0:n4], scalar1=0.044715, scalar2=1.0, op0=ALU.mult, op1=ALU.add),
                     reads=["hx1"], writes=["hx1"])
                S.op("dve", lambda e: e.tensor_tensor(out=u_[:, 0:n4], in0=x2[:, 0:n4], in1=x_[:, 0:n4], op=ALU.mult), reads=["hx1", "hx0"], writes=["hx2"])
                S.op("act", lambda e: e.activation(out=th[:, 0:n4], in_=u_[:, 0:n4], func=AF.Tanh, scale=0.7978845608), reads=["hx2"], writes=["hx3"])
                S.op("dve", lambda e: e.scalar_tensor_tensor(out=th[:, 0:n4], in0=th[:, 0:n4], scalar=1.0, in1=x_[:, 0:n4], op0=ALU.add, op1=ALU.mult),
                     reads=["hx3", "hx0"], writes=["hx3"])
                S.op("dve", lambda e: e.tensor_scalar(out=hb[:, 0:n4], in0=th[:, 0:n4], scalar1=0.5, scalar2=None, op0=ALU.mult),
                     reads=["hx3"], writes=["hbN"])
                for g in range(4):
                    S.op("pe", lambda e, g=g, kv=kv: e.matmul(PS[2][0:nb, g * 64:(g + 1) * 64], lhsT=hb[:, g * nb:(g + 1) * nb], rhs=w2[:, kv, :],
                                                             start=True, stop=True), reads=["hbN", ("w2N", kv)], writes=[("ps", 2)])
                if kv == 0:
                    S.op("act", lambda e: e.copy(out=kc[0:nb].rearrange("p g d -> p (g d)"), in_=PS[2][0:nb, 0:256]), reads=[("ps", 2)], writes=["kcN"])
                    headnormN(kcb[0:nb], kc[0:nb], qkg[0:nb, 1, :], 4, ["kcN"], ["kcbN"], np_=nb)
                    for g in range(4):
                        S.op("pe", lambda e, g=g: e.transpose(out=PB[0][0:64, g * nb:(g + 1) * nb], in_=kcb[0:nb, g, :], identity=ident_b[0:nb, 0:nb]),
                             reads=["kcbN"], writes=[("pb", 0)])
                    S.op("dve", lambda e: e.tensor_copy(out=kT_out, in_=PB[0][0:64, 0:4 * nb].rearrange("p (g n) -> p g n", g=4)),
                         reads=[("pb", 0)], writes=["kcT"])
                else:
                    S.op("act", lambda e: e.copy(out=v_out, in_=PS[2][0:nb, 0:256].rearrange("p (g d) -> p g d", g=4)), reads=[("ps", 2)], writes=["vc"])

        def load_prompt_cmp(tt, dst, key):
            S.store("sp", dst[:], proj[tt * 128:(tt + 1) * 128, C_CMP:C_CMP + 512], writes=[key])

        compress_segment(load_prompt_cmp, NT, kcT_p[:], vc_p[0:NBP])
        for sgi in range(NSEG):
            def load_cache_cmp(tt, dst, key, sgi=sgi):
                j = sgi * SEGT + tt
                S.dma("pool", dst[:], cache_cmp, reads=[("idx_i", l)], writes=[key],
                      indirect=bass.IndirectOffsetOnAxis(ap=idx_i[:, l, j:j + 1], axis=0))
            compress_segment(load_cache_cmp, SEGT, kcT_s[:, :, sgi * SEGB:(sgi + 1) * SEGB], vc_s[0:SEGB, sgi])
        S.barrier()
        P.release()

        P = Pool(nc)
        qT = P.t("qTN", [64, 16, TA], BF16)
        kT = P.t("kTN", [64, 8, TA], BF16)
        Va = P.t("VaN", [128, NTA, 8, 65], BF16)
        gts = P.t("gtsN", [128, NTA, 48], F32)
        S.op("pool", lambda e: e.memset(Va[:], 1.0), writes=["VaN"])
        ptN = [P.t(f"ptN{i}", [128, C_GQ], F32) for i in range(1)]
        qn = [P.t(f"qnN{i}", [128, 16, 64], BF16) for i in range(2)]
        kn = [P.t(f"knN{i}", [128, 8, 64], BF16) for i in range(2)]
        for tt in range(NTA):
            b = tt % 2
            p_ = ptN[0]
            pk = ("ptN", 0)
            S.dma("sp", p_[:], proj[tt * 128:(tt + 1) * 128, 0:C_GQ], writes=[pk])
            headnormN(qn[b][:], p_[:, 0:1024].rearrange("p (h d) -> p h d", h=16), qkg[:, 0, :], 16, [pk], [("qnN", b)], sc=0.125)
            headnormN(kn[b][:, 0:4], p_[:, C_SEL:C_SEL + 256].rearrange("p (h d) -> p h d", h=4), qkg[:, 2, :], 4, [pk], [("knN", b, 0)])
            headnormN(kn[b][:, 4:8], p_[:, C_WIN:C_WIN + 256].rearrange("p (h d) -> p h d", h=4), qkg[:, 3, :], 4, [pk], [("knN", b, 1)])
            S.op("act", lambda e, p_=p_, tt=tt: e.copy(out=Va[:, tt, 0:4, 0:64], in_=p_[:, C_SEL + 256:C_SEL + 512].rearrange("p (g d) -> p g d", g=4)),
                 reads=[pk, "VaN"], writes=[("VaN", tt, 0)])
            S.op("act", lambda e, p_=p_, tt=tt: e.copy(out=Va[:, tt, 4:8, 0:64], in_=p_[:, C_WIN + 256:C_WIN + 512].rearrange("p (g d) -> p g d", g=4)),
                 reads=[pk, "VaN"], writes=[("VaN", tt, 1)])
            S.op("act", lambda e, p_=p_, tt=tt: e.activation(out=gts[:, tt, :], in_=p_[:, C_GATE:C_GATE + 48], func=AF.Sigmoid),
                 reads=[pk], writes=[("gtsN", tt)])
            for half in range(2):
                for a in range(8):
                    S.op("pe", lambda e, a=a, b=b, half=half: e.transpose(out=PB[half][0:64, a * 128:(a + 1) * 128], in_=qn[b][:, half * 8 + a, :],
                                                                        identity=ident_b[:]), reads=[("qnN", b)], writes=[("pb", half)])
                evac(qT[:, half * 8:(half + 1) * 8, tt * 128:(tt + 1) * 128], PB[half][0:64, 0:1024].rearrange("p (a t) -> p a t", a=8),
                     reads=[("pb", half)], writes=[("qTN", tt, half)])
            for a in range(8):
                S.op("pe", lambda e, a=a, b=b: e.transpose(out=PB[0][0:64, a * 128:(a + 1) * 128], in_=kn[b][:, a, :], identity=ident_b[:]),
                     reads=[("knN", b, 0), ("knN", b, 1)], writes=[("pb", 0)])
            evac(kT[:, :, tt * 128:(tt + 1) * 128], PB[0][0:64, 0:1024].rearrange("p (a t) -> p a t", a=8), reads=[("pb", 0)], writes=[("kTN", tt)])

        S.op("dve", lambda e: e.tensor_copy(out=q_s[:], in_=qT[:, :, T]), reads=[("qTN", NT, 0), ("qTN", NT, 1)], writes=["q_s"])
        S.op("dve", lambda e: e.tensor_copy(out=k_s[:], in_=kT[:, :, T]), reads=[("kTN", NT)], writes=["k_s"])
        S.op("dve", lambda e: e.tensor_copy(out=v_s[:], in_=Va[0:1, NT, :, :]), reads=[("VaN", NT, 0), ("VaN", NT, 1), "VaN"], writes=["v_s"])
        S.op("dve", lambda e: e.tensor_copy(out=g_s[:], in_=gts[0:1, NT, :]), reads=[("gtsN", NT)], writes=["g_s"])
        bc = P.t("bcN", [128, 16, NBP], F32)
        dI = P.t("dIN", [128, NBP], F32)
        pen = P.t("penN", [128, NBP], F32)
        ec = P.t("ecN", [128, 16, NBP], F32)
        pb16 = P.t("pb16N", [128, 16, NBP], BF16)
        sm = P.t("smN", [128, 64], F32)
        scg = P.t("scgN", [128, 4, NBP], F32)
        adj = P.t("adjN", [128, NBP], F32)
        adj2 = P.t("adj2N", [128, NBP], F32)
        vld = P.t("vldN", [128, NBP], F32)
        m8 = P.t("m8N", [128, 4, 16], F32)
        scw = P.t("scwN", [128, 4, NBP], F32)
        selp = P.t("selpN", [128, 4, NBP], BF16)
        penT = P.t("penTN", [max(NBP, 2), 4, 128], BF16)
        pT = P.t("pTN", [max(NBP, 2), 16, 128], BF16)
        oacc = P.t("oaccN", [128, 16, 64], F32)
        otmp = P.t("otmpN", [128, 4, 64], F32)
        wv = P.t("wvN", [128, 8], F32)
        PT = [P.t(f"PTN{i}", [128, 4, 128], BF16) for i in range(2)]
        sti = [0]

        def attend(i, g, kbase, vbase, tiles, use_pen, gate_idx):
            acc = PS[4]
            nj = len(tiles)
            S.op("dve", lambda e: e.memset(acc[:, 0:260], 0.0), writes=[("ps", 4)])
            base = sti[0]
            sti[0] += nj

            def qk(jn):
                j = tiles[jn]
                sb = (base + jn) % 2
                st = PS[2 + sb]
                stk = ("ps", 2 + sb)
                dl = i - j
                extra = []
                if use_pen:
                    extra.append((E_all[0:NBP, j * 128:(j + 1) * 128], penT[0:NBP, g, :].unsqueeze(1).to_broadcast([NBP, 4, 128]), ["penTN"]))
                if dl == 0:
                    extra.append((ident_b[:], Mc_b[:].unsqueeze(1).to_broadcast([128, 4, 128]), []))
                if (not use_pen) and dl == 4:
                    extra.append((ident_b[:], Mw_b[:].unsqueeze(1).to_broadcast([128, 4, 128]), []))
                ne = len(extra)
                for r in range(4):
                    h = 4 * g + r
                    sl = slice(r * 128, (r + 1) * 128)
                    S.op("pe", lambda e, st=st, sl=sl, j=j, h=h, r=r, ne=ne: e.matmul(
                        st[:, sl], lhsT=kT[:, kbase + g, j * 128:(j + 1) * 128], rhs=qT[:, h, i * 128:(i + 1) * 128],
                        start=(r == 0), stop=(ne == 0 and r == 3), skip_group_check=True),
                        reads=[("kTN", j), ("qTN", i, h // 8)], writes=[stk])
                for xi, (lh, rh, rk_) in enumerate(extra):
                    S.op("pe", lambda e, st=st, lh=lh, rh=rh, last=(xi == ne - 1): e.matmul(
                        st[:, 0:512].rearrange("p (r q) -> p r q", r=4), lhsT=lh, rhs=rh, start=False, stop=last, skip_group_check=True),
                        reads=rk_, writes=[stk])

            def ex_pv(jn):
                j = tiles[jn]
                sb = (base + jn) % 2
                st = PS[2 + sb]
                stk = ("ps", 2 + sb)
                ptile = PT[sb]
                dl = i - j
                for r in range(4):
                    h = 4 * g + r
                    S.op("act", lambda e, st=st, ptile=ptile, r=r, h=h, dl=dl: e.activation(
                        out=ptile[:, r, :], in_=st[:, r * 128:(r + 1) * 128], func=AF.Exp, bias=bcol[:, h, dl:dl + 1], scale=1.0),
                        reads=[stk], writes=[("PTN", sb, r)])
                for r in range(4):
                    S.op("pe", lambda e, ptile=ptile, r=r, j=j, jn=jn: e.matmul(
                        acc[:, r * 65:(r + 1) * 65], lhsT=ptile[:, r, :], rhs=Va[:, j, vbase + g, :], start=False, stop=(jn == nj - 1),
                        skip_group_check=True),
                        reads=[("PTN", sb, r), ("VaN", j, vbase // 4), "VaN"], writes=[("ps", 4)])

            qk(0)
            for jn in range(nj):
                if jn + 1 < nj:
                    qk(jn + 1)
                ex_pv(jn)
            a3 = acc[:, 0:260].rearrange("p (r c) -> p r c", r=4)
            S.op("dve", lambda e: e.reciprocal(out=wv[:, 0:4], in_=a3[:, :, 64]), reads=[("ps", 4)], writes=["wv0"])
            S.op("dve", lambda e: e.tensor_tensor(out=wv[:, 4:8], in0=wv[:, 0:4], in1=gts[:, i, gate_idx * 16 + 4 * g:gate_idx * 16 + 4 * g + 4], op=ALU.mult),
                 reads=["wv0", ("gtsN", i)], writes=["wv1"])
            S.op("dve", lambda e: e.tensor_tensor(out=otmp[:], in0=a3[:, :, 0:64], in1=wv[:, 4:8].unsqueeze(2).to_broadcast([128, 4, 64]), op=ALU.mult),
                 reads=[("ps", 4), "wv1"], writes=["otmpN"])
            S.op("dve", lambda e: e.tensor_tensor(out=oacc[:, 4 * g:4 * g + 4, :], in0=oacc[:, 4 * g:4 * g + 4, :], in1=otmp[:], op=ALU.add),
                 reads=["otmpN", "oaccN"], writes=["oaccN"])

        for i in range(NT):
            qr = [("qTN", i, 0), ("qTN", i, 1)]
            for h in range(16):
                S.op("pe", lambda e, h=h: e.matmul(PS[0][:, h * NBP:(h + 1) * NBP], lhsT=qT[:, h, i * 128:(i + 1) * 128], rhs=kcT_p[:, h // 4, :],
                                                   start=True, stop=True), reads=qr + ["kcT"], writes=[("ps", 0)])
            S.op("dve", lambda e: e.tensor_scalar(out=dI[:], in0=Dc[:], scalar1=float(128 * i), scalar2=None, op0=ALU.add), writes=["dIN"])
            S.op("dve", lambda e: e.tensor_scalar(out=pen[:], in0=dI[:], scalar1=0.0, scalar2=BIGN, op0=ALU.is_lt, op1=ALU.mult), reads=["dIN"], writes=["penN"])
            S.op("dve", lambda e: e.tensor_tensor(out=bc[:], in0=negsl[:], in1=dI[:].unsqueeze(1).to_broadcast([128, 16, NBP]), op=ALU.mult),
                 reads=["dIN"], writes=["bcN"])
            S.op("dve", lambda e: e.tensor_tensor(out=bc[:], in0=bc[:], in1=pen[:].unsqueeze(1).to_broadcast([128, 16, NBP]), op=ALU.add),
                 reads=["bcN", "penN"], writes=["bcN"])
            S.op("dve", lambda e: e.tensor_tensor(out=ec[:], in0=PS[0][:, 0:16 * NBP].rearrange("p (h n) -> p h n", h=16), in1=bc[:], op=ALU.add),
                 reads=[("ps", 0), "bcN"], writes=["ecN"])
            S.op("act", lambda e: e.activation(out=ec[:], in_=ec[:], func=AF.Exp), reads=["ecN"], writes=["ecN"])
            S.op("dve", lambda e: e.tensor_reduce(out=sm[:, 0:16], in_=ec[:], axis=AX.X, op=ALU.add), reads=["ecN"], writes=["sm0"])
            S.op("dve", lambda e: e.tensor_scalar(out=sm[:, 16:32], in0=sm[:, 0:16], scalar1=1e-30, scalar2=None, op0=ALU.max), reads=["sm0"], writes=["sm1"])
            S.op("dve", lambda e: e.reciprocal(out=sm[:, 32:48], in_=sm[:, 16:32]), reads=["sm1"], writes=["sm2"])
            S.op("dve", lambda e: e.tensor_tensor(out=ec[:], in0=ec[:], in1=sm[:, 32:48].unsqueeze(2).to_broadcast([128, 16, NBP]), op=ALU.mult),
                 reads=["ecN", "sm2"], writes=["ecN"])
            S.op("act", lambda e: e.copy(out=pb16[:], in_=ec[:]), reads=["ecN"], writes=["pb16N"])
            S.op("dve", lambda e: e.tensor_reduce(out=scg[:], in_=ec[:].rearrange("p (g r) n -> p g n r", g=4), axis=AX.X, op=ALU.add),
                 reads=["ecN"], writes=["scgN"])
            S.op("dve", lambda e: e.tensor_scalar(out=adj[:], in0=Dblk_i[:], scalar1=float(128 * i), scalar2=None, op0=ALU.add), writes=["adjN"])
            S.op("dve", lambda e: e.tensor_scalar(out=vld[:], in0=adj[:], scalar1=0.0, scalar2=None, op0=ALU.is_ge), reads=["adjN"], writes=["vldN"])
            S.op("dve", lambda e: e.tensor_scalar(out=adj2[:], in0=adj[:], scalar1=128.0, scalar2=None, op0=ALU.is_lt), reads=["adjN"], writes=["adj2N"])
            S.op("dve", lambda e: e.tensor_tensor(out=adj2[:], in0=adj2[:], in1=vld[:], op=ALU.mult), reads=["adj2N", "vldN"], writes=["adj2N"])
            S.op("dve", lambda e: e.tensor_scalar(out=adj2[:, 0:1], in0=adj2[:, 0:1], scalar1=1.0, scalar2=None, op0=ALU.max), reads=["adj2N"], writes=["adj2N"])
            S.op("dve", lambda e: e.tensor_scalar(out=adj[:], in0=vld[:], scalar1=-1.0, scalar2=2.0e4, op0=ALU.add, op1=ALU.mult), reads=["vldN"], writes=["adjN"])
            S.op("dve", lambda e: e.scalar_tensor_tensor(out=adj[:], in0=adj2[:], scalar=1.0e4, in1=adj[:], op0=ALU.mult, op1=ALU.add),
                 reads=["adj2N", "adjN"], writes=["adjN"])
            S.op("dve", lambda e: e.tensor_tensor(out=scg[:], in0=scg[:], in1=adj[:].unsqueeze(1).to_broadcast([128, 4, NBP]), op=ALU.add),
                 reads=["scgN", "adjN"], writes=["scgN"])
            if NBP > 16:
                for g in range(4):
                    S.op("dve", lambda e, g=g: e.max(out=m8[:, g, 0:8], in_=scg[:, g, :]), reads=["scgN"], writes=[("m8N", g)])
                    S.op("dve", lambda e, g=g: e.match_replace(out=scw[:, g, :], in_to_replace=m8[:, g, 0:8], in_values=scg[:, g, :], imm_value=-1.0e9),
                         reads=["scgN", ("m8N", g)], writes=[("scwN", g)])
                    S.op("dve", lambda e, g=g: e.max(out=m8[:, g, 8:16], in_=scw[:, g, :]), reads=[("scwN", g)], writes=[("m8bN", g)])
                    S.op("dve", lambda e, g=g: e.tensor_scalar(out=scw[:, g, :], in0=scg[:, g, :], scalar1=m8[:, g, 15:16], scalar2=None, op0=ALU.is_ge),
                         reads=["scgN", ("m8bN", g), ("scwN", g)], writes=[("scwN", g)])
                S.op("dve", lambda e: e.tensor_tensor(out=scw[:], in0=scw[:], in1=vld[:].unsqueeze(1).to_broadcast([128, 4, NBP]), op=ALU.mult),
                     reads=[("scwN", g) for g in range(4)] + ["vldN"], writes=["scwA"])
            else:
                S.op("dve", lambda e: e.tensor_copy(out=scw[:], in_=vld[:].unsqueeze(1).to_broadcast([128, 4, NBP])), reads=["vldN"], writes=["scwA"])
            S.op("dve", lambda e: e.tensor_scalar(out=selp[:], in0=scw[:], scalar1=-1.0, scalar2=-BIGN, op0=ALU.add, op1=ALU.mult),
                 reads=["scwA"], writes=["selpN"])
            for g in range(4):
                S.op("pe", lambda e, g=g: e.transpose(out=PB[0][0:NBP, g * 128:(g + 1) * 128], in_=selp[:, g, :], identity=ident_b[:]),
                     reads=["selpN"], writes=[("pb", 0)])
            S.op("act", lambda e: e.copy(out=penT[0:NBP], in_=PB[0][0:NBP, 0:512].rearrange("p (g t) -> p g t", g=4)), reads=[("pb", 0)], writes=["penTN"])
            for half in range(2):
                for a in range(8):
                    S.op("pe", lambda e, a=a, half=half: e.transpose(out=PB[1][0:NBP, a * 128:(a + 1) * 128], in_=pb16[:, half * 8 + a, :], identity=ident_b[:]),
                         reads=["pb16N"], writes=[("pb", 1)])
                S.op("dve", lambda e, half=half: e.tensor_copy(out=pT[0:NBP, half * 8:(half + 1) * 8, :],
                                                              in_=PB[1][0:NBP, 0:1024].rearrange("p (a t) -> p a t", a=8)),
                     reads=[("pb", 1)], writes=[("pTN", half)])
            for half in range(2):
                for a in range(8):
                    h = half * 8 + a
                    S.op("pe", lambda e, a=a, h=h, half=half: e.matmul(PS[half][:, a * 64:(a + 1) * 64], lhsT=pT[0:NBP, h, :], rhs=vc_p[0:NBP, h // 4, :],
                                                                       start=True, stop=True), reads=[("pTN", half), "vc"], writes=[("ps", half)])
                S.op("dve", lambda e, half=half: e.tensor_tensor(out=oacc[:, half * 8:(half + 1) * 8, :],
                                                                in0=PS[half][:, 0:512].rearrange("p (a d) -> p a d", a=8),
                                                                in1=gts[:, i, half * 8:(half + 1) * 8].unsqueeze(2).to_broadcast([128, 8, 64]), op=ALU.mult),
                     reads=[("ps", half), ("gtsN", i), "oaccN"], writes=["oaccN"])
            NSA_BR = int(os.environ.get("NSA_BR", "7"))
            if not (NSA_BR & 1):
                S.op("dve", lambda e: e.memset(oacc[:], 0.0), reads=["oaccN"], writes=["oaccN"])
            for g in range(4):
                if NSA_BR & 2:
                    attend(i, g, 0, 0, list(range(0, i + 1)), True, 1)
                if NSA_BR & 4:
                    attend(i, g, 4, 4, list(range(max(0, i - 4), i + 1)), False, 2)
            S.store("sp", omix[i * 128:(i + 1) * 128, 0:1024], oacc[:].rearrange("p h d -> p (h d)"), reads=["oaccN"])
        S.barrier()
        P.release()

        P = Pool(nc)
        bS = P.t("bS3", [4, NBS], F32)
        eS = P.t("eS3", [4, NBS], F32)
        sm3 = P.t("sm3", [4, 8], F32)
        srow = P.t("srow3", [1, 4, NBS], F32)
        srw = P.t("srw3", [1, 4, NBS], F32)
        m83 = P.t("m83", [1, 4, 16], F32)
        penr = P.t("penr3", [1, 4, NBS], BF16)
        penE = P.t("penE3", [128, NPG, 4], F32)
        pTs = P.t("pTs3", [SEGB, NSEG, 4], BF16)
        ocmp = P.t("ocmp3", [4, 4, 64], F32)
        osw = P.t("osw3", [4, 2, 4, 64], F32)
        gS = P.t("gS3", [4, 12], F32)
        wr = P.t("wr3", [4, 16], F32)
        pg = [P.t(f"pg3{i}", [128, 512], F32) for i in range(3)]
        kTs = [P.t(f"kTs3{i}", [64, 4, 128], BF16) for i in range(2)]
        Vs = [P.t(f"Vs3{i}", [128, 4, 65], BF16) for i in range(2)]
        b16 = [P.t(f"b163{i}", [128, 16], F32) for i in range(2)]
        sc16 = [P.t(f"sc163{i}", [128, 16], F32) for i in range(2)]
        PTs = [P.t(f"PTs3{i}", [128, 16], BF16) for i in range(2)]
        for i in range(2):
            S.op("pool", lambda e, i=i: e.memset(Vs[i][:], 1.0), writes=[("Vs3", i)])
        for bg in range(12):
            S.op("pe", lambda e, bg=bg: e.matmul(PS[5][0:4, bg:bg + 1], lhsT=g_s[0:1, (bg // 4) * 16 + (bg % 4) * 4:(bg // 4) * 16 + (bg % 4) * 4 + 4],
                                                 rhs=ones_f[0:1, 0:1], start=True, stop=True), reads=["g_s"], writes=[("ps", 5)])
        S.op("act", lambda e: e.copy(out=gS[:], in_=PS[5][0:4, 0:12]), reads=[("ps", 5)], writes=["gS3"])
        for g in range(4):
            S.op("pe", lambda e, g=g: e.matmul(PS[0][0:4, 0:NBS], lhsT=q_s[:, 4 * g:4 * g + 4], rhs=kcT_s[:, g, :], start=True, stop=True),
                 reads=["q_s", "kcT"], writes=[("ps", 0)])
            S.op("dve", lambda e, g=g: e.tensor_scalar(out=bS[:], in0=dcs[:], scalar1=slg[:, g:g + 1], scalar2=None, op0=ALU.mult), writes=["bS3"])
            S.op("dve", lambda e: e.tensor_tensor(out=eS[:], in0=PS[0][0:4, 0:NBS], in1=bS[:], op=ALU.add), reads=[("ps", 0), "bS3"], writes=["eS3"])
            S.op("act", lambda e: e.activation(out=eS[:], in_=eS[:], func=AF.Exp), reads=["eS3"], writes=["eS3"])
            S.op("dve", lambda e: e.tensor_reduce(out=sm3[:, 0:1], in_=eS[:], axis=AX.X, op=ALU.add), reads=["eS3"], writes=["sm30"])
            S.op("dve", lambda e: e.tensor_scalar(out=sm3[:, 1:2], in0=sm3[:, 0:1], scalar1=1e-30, scalar2=None, op0=ALU.max), reads=["sm30"], writes=["sm31"])
            S.op("dve", lambda e: e.reciprocal(out=sm3[:, 2:3], in_=sm3[:, 1:2]), reads=["sm31"], writes=["sm32"])
            S.op("dve", lambda e: e.tensor_scalar(out=eS[:], in0=eS[:], scalar1=sm3[:, 2:3], scalar2=None, op0=ALU.mult), reads=["eS3", "sm32"], writes=["eS3"])
            S.op("pe", lambda e: e.matmul(PS[1][0:1, 0:NBS], lhsT=ones_f[0:4, 0:1], rhs=eS[:], start=True, stop=True), reads=["eS3"], writes=[("ps", 1)])
            S.op("act", lambda e, g=g: e.copy(out=srow[0:1, g, :], in_=PS[1][0:1, 0:NBS]), reads=[("ps", 1)], writes=[("srow3", g)])
            for sg in range(NSEG):
                S.op("pe", lambda e, sg=sg: e.transpose(out=PS[2][0:SEGB, sg * 4:(sg + 1) * 4], in_=eS[0:4, sg * SEGB:(sg + 1) * SEGB],
                                                        identity=ident_f[0:4, 0:4]), reads=["eS3"], writes=[("ps", 2)])
            S.op("dve", lambda e: e.tensor_copy(out=pTs[:].rearrange("p s r -> p (s r)"), in_=PS[2][0:SEGB, 0:NSEG * 4]), reads=[("ps", 2)], writes=["pTs3"])
            for sg in range(NSEG):
                S.op("pe", lambda e, sg=sg, g=g: e.matmul(PS[3][0:4, g * 64:(g + 1) * 64], lhsT=pTs[:, sg, :], rhs=vc_s[0:SEGB, sg, g, :],
                                                         start=(sg == 0), stop=(sg == NSEG - 1)), reads=["pTs3", "vc"], writes=[("ps", 3)])
        S.op("act", lambda e: e.copy(out=ocmp[:].rearrange("p g d -> p (g d)"), in_=PS[3][0:4, 0:256]), reads=[("ps", 3)], writes=["ocmp3"])
        sr = [("srow3", g) for g in range(4)]
        S.op("dve", lambda e: e.memset(srow[0:1, :, 0:1], 1.0e4), reads=sr, writes=["srowA"])
        S.op("dve", lambda e: e.memset(srow[0:1, :, NBS - 1:NBS], 1.0e4), reads=sr, writes=["srowB"])
        srk = sr + ["srowA", "srowB"]
        for g in range(4):
            S.op("dve", lambda e, g=g: e.max(out=m83[0:1, g, 0:8], in_=srow[0:1, g, :]), reads=srk, writes=[("m83", g)])
            S.op("dve", lambda e, g=g: e.match_replace(out=srw[0:1, g, :], in_to_replace=m83[0:1, g, 0:8], in_values=srow[0:1, g, :], imm_value=-1.0e9),
                 reads=srk + [("m83", g)], writes=[("srw3", g)])
            S.op("dve", lambda e, g=g: e.max(out=m83[0:1, g, 8:16], in_=srw[0:1, g, :]), reads=[("srw3", g)], writes=[("m83b", g)])
            S.op("dve", lambda e, g=g: e.tensor_scalar(out=srw[0:1, g, :], in0=srow[0:1, g, :], scalar1=m83[0:1, g, 14:15], scalar2=None, op0=ALU.is_ge),
                 reads=srk + [("m83b", g), ("srw3", g)], writes=[("srw3", g)])
        S.op("dve", lambda e: e.tensor_scalar(out=penr[:], in0=srw[:], scalar1=-1.0, scalar2=-BIGN, op0=ALU.add, op1=ALU.mult),
             reads=[("srw3", g) for g in range(4)], writes=["penr3"])
        for g in range(4):
            for k in range(2):
                S.op("pe", lambda e, g=g, k=k: e.matmul(PS[4][:, g * NPG:(g + 1) * NPG], lhsT=half_sel[0:1, k, :],
                                                       rhs=penr[0:1, g, :].rearrange("p (j k) -> p j k", k=2)[:, :, k],
                                                       start=(k == 0), stop=(k == 1)), reads=["penr3"], writes=[("ps", 4)])
        S.op("dve", lambda e: e.tensor_copy(out=penE[:], in_=PS[4][:, 0:4 * NPG].rearrange("p (g j) -> p j g", g=4)), reads=[("ps", 4)], writes=["penE3"])

        def key_pass(br, ntile, load_fn, bias_tab, use_pen, knew, vnew):
            acc = PS[5]
            S.op("dve", lambda e: e.memset(acc[0:4, 0:260], 0.0), writes=[("ps", 5)])
            for j in range(ntile):
                b2, b3 = j % 2, j % 3
                load_fn(j, pg[b3], ("pg3", b3))
                for g in range(4):
                    S.op("pe", lambda e, g=g, b3=b3: e.transpose(out=PS[b2][0:64, g * 128:(g + 1) * 128], in_=pg[b3][:, g * 64:(g + 1) * 64], identity=ident_f[:]),
                         reads=[("pg3", b3)], writes=[("ps", b2)])
                evac(kTs[b2][:].rearrange("p g t -> p (g t)"), PS[b2][0:64, 0:512], reads=[("ps", b2)], writes=[("kTs3", b2)])
                S.op("act", lambda e, b2=b2, b3=b3: e.copy(out=Vs[b2][:, :, 0:64], in_=pg[b3][:, 256:512].rearrange("p (g d) -> p g d", g=4)),
                     reads=[("pg3", b3), ("Vs3", b2)], writes=[("Vs3v", b2)])
                for g in range(4):
                    S.op("pe", lambda e, g=g, b2=b2: e.matmul(PS[2 + b2][:, g * 4:(g + 1) * 4], lhsT=kTs[b2][:, g, :], rhs=q_s[:, 4 * g:4 * g + 4],
                                                             start=True, stop=True), reads=[("kTs3", b2), "q_s"], writes=[("ps", 2 + b2)])
                S.op("dve", lambda e, b2=b2, j=j: e.tensor_scalar(out=b16[b2][:], in0=negsl16[:], scalar1=bias_tab[:, j:j + 1], scalar2=None, op0=ALU.mult),
                     writes=[("b163", b2)])
                if (not use_pen) and j == 0:
                    S.op("dve", lambda e, b2=b2: e.tensor_scalar(out=b16[b2][:], in0=b16[b2][:], scalar1=mw0[:, 0:1], scalar2=None, op0=ALU.add),
                         reads=[("b163", b2)], writes=[("b163", b2)])
                S.op("dve", lambda e, b2=b2: e.tensor_tensor(out=sc16[b2][:], in0=PS[2 + b2][:, 0:16], in1=b16[b2][:], op=ALU.add),
                     reads=[("ps", 2 + b2), ("b163", b2)], writes=[("sc163", b2)])
                if use_pen:
                    S.op("dve", lambda e, b2=b2, j=j: e.tensor_tensor(out=sc16[b2][:].rearrange("p (g r) -> p g r", g=4),
                                                                       in0=sc16[b2][:].rearrange("p (g r) -> p g r", g=4),
                                                                       in1=penE[:, j, :].unsqueeze(2).to_broadcast([128, 4, 4]), op=ALU.add),
                         reads=[("sc163", b2), "penE3"], writes=[("sc163", b2)])
                S.op("act", lambda e, b2=b2: e.activation(out=PTs[b2][:], in_=sc16[b2][:], func=AF.Exp), reads=[("sc163", b2)], writes=[("PTs3", b2)])
                for g in range(4):
                    S.op("pe", lambda e, g=g, b2=b2: e.matmul(acc[0:4, g * 65:(g + 1) * 65], lhsT=PTs[b2][:, 4 * g:4 * g + 4], rhs=Vs[b2][:, g, :],
                                                             start=False, stop=False, skip_group_check=True),
                         reads=[("PTs3", b2), ("Vs3v", b2), ("Vs3", b2)], writes=[("ps", 5)])
            for g in range(4):
                S.op("pe", lambda e, g=g: e.matmul(PS[2][0:1, g * 4:(g + 1) * 4], lhsT=k_s[:, knew + g:knew + g + 1], rhs=q_s[:, 4 * g:4 * g + 4],
                                                   start=True, stop=True), reads=["k_s", "q_s"], writes=[("ps", 2)])
            S.op("act", lambda e: e.activation(out=PTs[0][0:1, :], in_=PS[2][0:1, 0:16], func=AF.Exp), reads=[("ps", 2)], writes=[("PTs3", 0)])
            for g in range(4):
                S.op("pe", lambda e, g=g: e.matmul(acc[0:4, g * 65:(g + 1) * 65], lhsT=PTs[0][0:1, 4 * g:4 * g + 4], rhs=v_s[0:1, vnew + g, :],
                                                   start=False, stop=True, skip_group_check=True), reads=[("PTs3", 0), "v_s"], writes=[("ps", 5)])
            a3 = acc[0:4, 0:260].rearrange("p (g c) -> p g c", g=4)
            S.op("dve", lambda e: e.reciprocal(out=wr[:, 0:4], in_=a3[:, :, 64]), reads=[("ps", 5)], writes=["wr0"])
            S.op("dve", lambda e: e.tensor_tensor(out=wr[:, 4:8], in0=wr[:, 0:4], in1=gS[:, (1 + br) * 4:(2 + br) * 4], op=ALU.mult),
                 reads=["wr0", "gS3"], writes=["wr1"])
            S.op("dve", lambda e: e.tensor_tensor(out=osw[:, br], in0=a3[:, :, 0:64], in1=wr[:, 4:8].unsqueeze(2).to_broadcast([4, 4, 64]), op=ALU.mult),
                 reads=[("ps", 5), "wr1"], writes=[("osw3", br)])

        def load_sel(j, dst, key):
            S.dma("pool", dst[:], cache_sel, reads=[("idx_i", l)], writes=[key],
                  indirect=bass.IndirectOffsetOnAxis(ap=idx_i[:, l, j:j + 1], axis=0))

        def load_win(j, dst, key):
            S.store("sp", dst[:], cache_win[l, j * 128:(j + 1) * 128, :], writes=[key])

        key_pass(0, NPG, load_sel, tb, True, 0, 0)
        key_pass(1, 4, load_win, tbw, False, 4, 4)
        if dbg and l == 0:
            S.dma("sp", dbg_sel, srw[:].rearrange("p g n -> p (g n)"), reads=[("srw3", g) for g in range(4)])
            S.dma("sp", dbg_srow, srow[:].rearrange("p g n -> p (g n)"), reads=srk)
            S.dma("sp", dbg_o[:, 0:256], ocmp[:].rearrange("p g d -> p (g d)"), reads=["ocmp3"])
            S.dma("sp", dbg_o[:, 256:768], osw[:].rearrange("p b g d -> p (b g d)"), reads=[("osw3", 0), ("osw3", 1)])
            S.dma("sp", dbg_gs, gS[:], reads=["gS3"])
        S.op("dve", lambda e: e.tensor_tensor(out=ocmp[:], in0=ocmp[:], in1=gS[:, 0:4].unsqueeze(2).to_broadcast([4, 4, 64]), op=ALU.mult),
             reads=["ocmp3", "gS3"], writes=["ocmp3"])
        NSA_BR3 = int(os.environ.get("NSA_BR", "7"))
        if not (NSA_BR3 & 1):
            S.op("dve", lambda e: e.memset(ocmp[:], 0.0), reads=["ocmp3"], writes=["ocmp3"])
        if NSA_BR3 & 2:
            S.op("dve", lambda e: e.tensor_tensor(out=ocmp[:], in0=ocmp[:], in1=osw[:, 0], op=ALU.add), reads=["ocmp3", ("osw3", 0)], writes=["ocmp3"])
        if NSA_BR3 & 4:
            S.op("dve", lambda e: e.tensor_tensor(out=ocmp[:], in0=ocmp[:], in1=osw[:, 1], op=ALU.add), reads=["ocmp3", ("osw3", 1)], writes=["ocmp3"])
        S.store("sp", omix[T:T + 1, 0:1024].rearrange("o (g r d) -> (o r) g d", g=4, r=4), ocmp[:], reads=["ocmp3"])
        S.barrier()
        P.release()
        LP.release()

    def mixers(l):
        for tt in range(NTA):
            S.store("sp", omix[tt * 128:(tt + 1) * 128, :], zeros_f[:, 0:2048])
        S.barrier()
        gla_mixer(l)
        rw_mixer(l)
        nsa_mixer(l)
        if dbg and l == 0:
            S.dma("sp", omix_l0, omix)
            S.barrier()

    for l in range(L):
        P = Pool(nc)
        hT = P.t("hT", [128, KD, TA], BF16)
        rmsnorm_T(P, xa, bcast_row(norm1_g[l], D), hT, f"n1")
        wt = [P.t(f"wA{i}", [128, KD, 512], BF16) for i in range(2)]
        stg = [P.t(f"stgA{i}", [128, 512], F32) for i in range(3)]
        w_in_v = w_in[l].rearrange("(ko ki) n -> ki ko n", ki=128)
        ncb = (NIN + 511) // 512
        it = 0
        for cb in range(ncb):
            c0 = cb * 512
            cw = min(512, NIN - c0)
            wb = wt[cb % 2]
            S.dma("pool", wb[:, :, 0:cw], w_in_v[:, :, c0:c0 + cw], writes=[("wA", cb % 2)])
            for tt in range(NTA):
                ps = PS[it % 4]
                for k in range(KD):
                    S.op("pe", lambda e, ps=ps, k=k, tt=tt, wb=wb, cw=cw: e.matmul(
                        ps[:, 0:cw], lhsT=hT[:, k, tt * 128:(tt + 1) * 128], rhs=wb[:, k, 0:cw],
                        start=(k == 0), stop=(k == KD - 1)),
                        reads=[("n1", "hT", tt, k // 4), ("wA", cb % 2)], writes=[("ps", it % 4)])
                sg = stg[it % 3]
                evac(sg[:, 0:cw], ps[:, 0:cw], reads=[("ps", it % 4)], writes=[("stgA", it % 3)])
                S.store("sp", proj[tt * 128:(tt + 1) * 128, c0:c0 + cw], sg[:, 0:cw], reads=[("stgA", it % 3)])
                it += 1
        S.barrier()
        P.release()

        P = Pool(nc)
        NB = C_MG
        pt = [P.t(f"ptB{i}", [128, NB], F32) for i in range(2)]
        qkg = P.t("qkg", [128, 4, 64], F32)
        S.dma("sp", qkg[:].rearrange("p a d -> p (a d)"), nsa_qk_g[l].rearrange("a d -> (a d)").partition_broadcast(128),
              writes=["qkg"])
        tmpB = P.t("tmpB", [128, 16, 64], F32)
        ssB = P.t("ssB", [128, 64], F32)
        rows = [P.t(f"rowsB{i}", [128, 2, 512], F32) for i in range(2)]

        def headnorm(out3, in3, g_ap, H, rk, wk, sc=1.0):
            S.op("dve", lambda e: e.tensor_tensor(out=tmpB[:, 0:H, :], in0=in3, in1=in3, op=ALU.mult),
                 reads=rk, writes=["tmpB"])
            S.op("dve", lambda e: e.tensor_reduce(out=ssB[:, 0:H], in_=tmpB[:, 0:H, :], axis=AX.X, op=ALU.add),
                 reads=["tmpB"], writes=["ssB0"])
            S.op("dve", lambda e: e.tensor_scalar(out=ssB[:, 16:16 + H], in0=ssB[:, 0:H], scalar1=1.0 / 64, scalar2=EPS,
                                                  op0=ALU.mult, op1=ALU.add), reads=["ssB0"], writes=["ssB1"])
            S.op("act", lambda e: e.activation(out=ssB[:, 48:48 + H], in_=ssB[:, 16:16 + H], func=AF.Ln),
                 reads=["ssB1"], writes=["ssB1b"])
            S.op("act", lambda e: e.activation(out=ssB[:, 32:32 + H], in_=ssB[:, 48:48 + H], func=AF.Exp, scale=-0.5,
                                               bias=float(np.log(sc))), reads=["ssB1b"], writes=["ssB2"])
            S.op("dve", lambda e: e.tensor_tensor(out=tmpB[:, 0:H, :], in0=in3,
                                                  in1=ssB[:, 32:32 + H].unsqueeze(2).to_broadcast([128, H, 64]), op=ALU.mult),
                 reads=rk + ["ssB2"], writes=["tmpB"])
            S.op("dve", lambda e: e.tensor_tensor(out=out3, in0=tmpB[:, 0:H, :],
                                                  in1=g_ap.unsqueeze(1).to_broadcast([128, H, 64]), op=ALU.mult),
                 reads=["tmpB", "qkg"], writes=wk)

        for tt in range(NTA):
            b = tt % 2
            ptb = pt[b]
            rw_ = rows[b]
            S.dma("sp", ptb[:], proj[tt * 128:(tt + 1) * 128, 0:NB], writes=[("ptB", b)])
            is_s = (tt == NT)
            if not is_s:
                S.store("sp", p_cmp_o[l, tt * 128:(tt + 1) * 128, :], ptb[:, C_CMP:C_CMP + 512], reads=[("ptB", b)])
            else:
                S.store("sp", s_cmp_o[l], ptb[0:1, C_CMP:C_CMP + 512], reads=[("ptB", b)])
            for wi, (c0, gi, po, so) in enumerate(((C_SEL, 2, p_sel_o, s_sel_o), (C_WIN, 3, p_win_o, s_win_o))):
                headnorm(rw_[:, wi, 0:256].rearrange("p (h d) -> p h d", h=4),
                         ptb[:, c0:c0 + 256].rearrange("p (h d) -> p h d", h=4), qkg[:, gi, :], 4,
                         [("ptB", b)], [("rowsB", b, wi, "k")])
                S.op("act", lambda e, rw_=rw_, ptb=ptb, wi=wi, c0=c0: e.copy(out=rw_[:, wi, 256:512], in_=ptb[:, c0 + 256:c0 + 512]),
                     reads=[("ptB", b)], writes=[("rowsB", b, wi, "v")])
                rk = [("rowsB", b, wi, "k"), ("rowsB", b, wi, "v")]
                if is_s:
                    S.store("sp", so[l], rw_[0:1, wi, :], reads=rk)
                elif wi == 0:
                    S.store("sp", po[l, tt * 128:(tt + 1) * 128, :], rw_[:, wi, :], reads=rk)
                else:
                    t0w = tt * 128 - (T - WKEEP)
                    if t0w >= 0:
                        S.store("sp", po[l, t0w:t0w + 128, :], rw_[:, wi, :], reads=rk)
            if tt == NT - 1:
                S.store("sp", p_sh_o[l:l + 1, :], ptb[127:128, C_RW:C_RW + RWP], reads=[("ptB", b)])
            if is_s:
                S.store("sp", s_sh_o[l:l + 1, :], ptb[0:1, C_RW:C_RW + RWP], reads=[("ptB", b)])
        S.barrier()
        P.release()

        mixers(l)

        P = Pool(nc)
        oT = P.t("oT", [128, 16, TA], BF16)
        rmsnorm_T(P, omix, None, oT, "oT", NC=2048)
        wt = [P.t(f"wC{i}", [128, 16, 512], BF16) for i in range(2)]
        mg = [P.t(f"mgC{i}", [128, 3, 512], F32) for i in range(2)]
        acc = [P.t(f"accC{i}", [128, 512], F32) for i in range(2)]
        tmpc = P.t("tmpC", [128, 512], F32)
        ups = ((nsa_up, 0, 8), (gla_up, 8, 4), (rw_up, 12, 4))
        it = 0
        for nb in range(D // 512 if D >= 512 else 1):
            cw = min(512, D)
            c0 = nb * 512
            wb = wt[nb % 2]
            for (wsrc, k0, nk) in ups:
                S.dma("pool", wb[:, k0:k0 + nk, 0:cw], wsrc[l].rearrange("(ko ki) n -> ki ko n", ki=128)[:, :, c0:c0 + cw],
                      writes=[("wC", nb % 2, k0)])
            for tt in range(NTA):
                b = it % 2
                for j in range(3):
                    S.dma("sp", mg[b][:, j, 0:cw], proj[tt * 128:(tt + 1) * 128, C_MG + j * D + c0:C_MG + j * D + c0 + cw],
                          writes=[("mgC", b, j)])
                    S.op("act", lambda e, b=b, j=j: e.activation(out=mg[b][:, j, 0:cw], in_=mg[b][:, j, 0:cw], func=AF.Sigmoid),
                         reads=[("mgC", b, j)], writes=[("mgC", b, j)])
                for j, (wsrc, k0, nk) in enumerate(ups):
                    ps = PS[j + 3 * (it % 2)]
                    pk = ("ps", j + 3 * (it % 2))
                    for k in range(k0, k0 + nk):
                        S.op("pe", lambda e, ps=ps, k=k, tt=tt, wb=wb, k0=k0, nk=nk: e.matmul(
                            ps[:, 0:cw], lhsT=oT[:, k, tt * 128:(tt + 1) * 128], rhs=wb[:, k, 0:cw],
                            start=(k == k0), stop=(k == k0 + nk - 1)),
                            reads=[("oT", "hT", tt, k // 4), ("wC", nb % 2, k0)], writes=[pk])
                    if j == 0:
                        S.op("dve", lambda e, b=b, ps=ps: e.tensor_tensor(out=acc[b][:, 0:cw], in0=mg[b][:, 0, 0:cw], in1=ps[:, 0:cw],
                                                                           op=ALU.mult), reads=[("mgC", b, 0), pk], writes=[("accC", b)])
                    else:
                        S.op("dve", lambda e, b=b, ps=ps, j=j: e.tensor_tensor(out=tmpc[:, 0:cw], in0=mg[b][:, j, 0:cw], in1=ps[:, 0:cw],
                                                                                op=ALU.mult), reads=[("mgC", b, j), pk], writes=["tmpC"])
                        S.op("dve", lambda e, b=b: e.tensor_tensor(out=acc[b][:, 0:cw], in0=acc[b][:, 0:cw], in1=tmpc[:, 0:cw], op=ALU.add),
                             reads=["tmpC", ("accC", b)], writes=[("accC", b)])
                S.store("sp", mrg[tt * 128:(tt + 1) * 128, c0:c0 + cw], acc[b][:, 0:cw], reads=[("accC", b)])
                it += 1
        S.barrier()
        P.release()

        P = Pool(nc)
        mT = P.t("mT", [128, KD, TA], BF16)
        rmsnorm_T(P, mrg, None, mT, "mT", NC=D)
        wt = [P.t(f"wD{i}", [128, KD, 512], BF16) for i in range(2)]
        xr = [P.t(f"xD{i}", [128, 512], F32) for i in range(3)]
        w_v = w_out[l].rearrange("(ko ki) n -> ki ko n", ki=128)
        it = 0
        for nb in range(D // 512 if D >= 512 else 1):
            cw = min(512, D)
            c0 = nb * 512
            wb = wt[nb % 2]
            S.dma("pool", wb[:, :, 0:cw], w_v[:, :, c0:c0 + cw], writes=[("wD", nb % 2)])
            for tt in range(NTA):
                ps = PS[it % 4]
                xb_ = xr[it % 3]
                S.dma("sp", xb_[:, 0:cw], xa[tt * 128:(tt + 1) * 128, c0:c0 + cw], writes=[("xD", it % 3)])
                for k in range(KD):
                    S.op("pe", lambda e, ps=ps, k=k, tt=tt, wb=wb: e.matmul(
                        ps[:, 0:cw], lhsT=mT[:, k, tt * 128:(tt + 1) * 128], rhs=wb[:, k, 0:cw],
                        start=(k == 0), stop=(k == KD - 1)),
                        reads=[("mT", "hT", tt, k // 4), ("wD", nb % 2)], writes=[("ps", it % 4)])
                S.op("dve", lambda e, ps=ps, xb_=xb_: e.tensor_tensor(out=xb_[:, 0:cw], in0=xb_[:, 0:cw], in1=ps[:, 0:cw], op=ALU.add),
                     reads=[("ps", it % 4), ("xD", it % 3)], writes=[("xD", it % 3)])
                S.store("sp", xm[tt * 128:(tt + 1) * 128, c0:c0 + cw], xb_[:, 0:cw], reads=[("xD", it % 3)])
                it += 1
        S.barrier()
        P.release()

        P = Pool(nc)
        h2T = P.t("h2T", [128, KD, TA], BF16)
        rmsnorm_T(P, xm, bcast_row(norm2_g[l], D), h2T, "n2")
        wt = [P.t(f"wE{i}", [128, KD, 512], BF16) for i in range(2)]
        hst = [P.t(f"hstE{i}", [128, 4, 4, 128], BF16) for i in range(2)]
        rl = [P.t(f"rlE{i}", [128, 512], F32) for i in range(2)]
        w_v = mlp_w1[l].rearrange("(ko ki) n -> ki ko n", ki=128)
        chunks = [(c * 4, min(4, NTA - c * 4)) for c in range((NTA + 3) // 4)]
        it = 0
        ic = 0
        for fb in range(DFF // 512):
            wb = wt[fb % 2]
            S.dma("pool", wb[:], w_v[:, :, fb * 512:(fb + 1) * 512], writes=[("wE", fb % 2)])
            for (t0, ntl) in chunks:
                hs = hst[ic % 2]
                ntok = ntl * 128
                for fs in range(4):
                    ps = PS[it % 4]
                    for k in range(KD):
                        S.op("pe", lambda e, ps=ps, k=k, wb=wb, fs=fs, t0=t0, ntok=ntok: e.matmul(
                            ps[:, 0:ntok], lhsT=wb[:, k, fs * 128:(fs + 1) * 128], rhs=h2T[:, k, t0 * 128:t0 * 128 + ntok],
                            start=(k == 0), stop=(k == KD - 1)),
                            reads=[("n2", "hT", t0 + j, k // 4) for j in range(ntl)] + [("wE", fb % 2)], writes=[("ps", it % 4)])
                    r_ = rl[it % 2]
                    S.op("act", lambda e, ps=ps, r_=r_, ntok=ntok: e.activation(out=r_[:, 0:ntok], in_=ps[:, 0:ntok], func=AF.Relu),
                         reads=[("ps", it % 4)], writes=[("rlE", it % 2)])
                    S.op("dve", lambda e, hs=hs, r_=r_, fs=fs, ntl=ntl, ntok=ntok: e.tensor_tensor(
                        out=hs[:, 0:ntl, fs, :], in0=r_[:, 0:ntok].rearrange("p (a t) -> p a t", a=ntl),
                        in1=r_[:, 0:ntok].rearrange("p (a t) -> p a t", a=ntl), op=ALU.mult),
                        reads=[("rlE", it % 2)], writes=[("hstE", ic % 2, fs)])
                    it += 1
                for j in range(ntl):
                    S.store("sp", hidS[t0 + j, :, fb * 4:(fb + 1) * 4, :], hs[:, j, :, :], reads=[("hstE", ic % 2, fs) for fs in range(4)])
                ic += 1
        S.barrier()
        P.release()

        P = Pool(nc)
        NFO = DFF // 128
        FW = 512 if D >= 512 else 256
        wt = [P.t(f"wF{i}", [128, NFO, FW], BF16) for i in range(2)]
        ht = [P.t(f"htF{i}", [128, NFO, 128], BF16) for i in range(2)]
        xr = [P.t(f"xF{i}", [128, FW], F32) for i in range(3)]
        w_v = mlp_w2[l].rearrange("(fo fi) n -> fi fo n", fi=128)
        it = 0
        for db in range(D // FW):
            c0 = db * FW
            wb = wt[db % 2]
            for hf_ in range(2):
                S.dma("pool", wb[:, hf_ * (NFO // 2):(hf_ + 1) * (NFO // 2), :], w_v[:, hf_ * (NFO // 2):(hf_ + 1) * (NFO // 2), c0:c0 + FW],
                      writes=[("wF", db % 2, hf_)])
            for tt in range(NTA):
                hb = ht[it % 2]
                S.dma("sp", hb[:], hidS[tt], writes=[("htF", it % 2)])
                xb_ = xr[it % 3]
                S.dma("sp", xb_[:], xm[tt * 128:(tt + 1) * 128, c0:c0 + FW], writes=[("xF", it % 3)])
                ps = PS[it % 4]
                for fo in range(NFO):
                    S.op("pe", lambda e, ps=ps, fo=fo, hb=hb, wb=wb: e.matmul(
                        ps[:, 0:FW], lhsT=hb[:, fo, :], rhs=wb[:, fo, :], start=(fo == 0), stop=(fo == NFO - 1)),
                        reads=[("htF", it % 2), ("wF", db % 2, fo // (NFO // 2))], writes=[("ps", it % 4)])
                S.op("dve", lambda e, ps=ps, xb_=xb_: e.tensor_tensor(out=xb_[:], in0=xb_[:], in1=ps[:, 0:FW], op=ALU.add),
                     reads=[("ps", it % 4), ("xF", it % 3)], writes=[("xF", it % 3)])
                S.store("sp", xa[tt * 128:(tt + 1) * 128, c0:c0 + FW], xb_[:], reads=[("xF", it % 3)])
                it += 1
        S.barrier()
        P.release()
        S.store("sp", xa[T + 1:TA, :], zeros_f[0:127, 0:D])
        S.barrier()

    S.barrier()
    S.dma("sp", y_prompt, xa[0:T, :])
    S.dma("sp", y_sample, xa[T:T + 1, :])
    S.barrier()
    S.emit_all()
    global LAST_S
    LAST_S = S
    return nc, declared


FULL_CFG = dict(D=2048, T=2048, PAST=16384, NPOOL=1280, L=2)
WEIGHTS = ["norm1_g", "norm2_g", "w_in", "nsa_qk_g", "cmp_pos", "cmp_w1", "cmp_w2", "nsa_up", "gla_a2", "gla_a_b",
           "gla_norm_g", "gla_up", "rw_mu", "rw_w0", "rw_w2", "rw_a0", "rw_a2", "rw_g2", "rw_kk", "rw_ka", "rw_rk",
           "rw_ln_g", "rw_ln_b", "rw_up", "w_out", "mlp_w1", "mlp_w2"]


def input_names(nc):
    names = []
    for a in nc.allocations:
        pass
    return names


def make_in_maps(inp, cfg, ncores, declared):
    B = inp["x_prompt"].shape[0]
    SB = inp["x_sample"].shape[0]
    L = cfg["L"]
    maps = []
    f = lambda a: np.ascontiguousarray(a)
    shared = {}
    for k in WEIGHTS:
        if k in declared:
            shared[k] = f(inp[k])
    if "cache_cmp" in declared:
        shared["cache_cmp"] = f(inp["cache_cmp_kv"].reshape(L, -1, 512))
    if "cache_sel" in declared:
        shared["cache_sel"] = f(inp["cache_sel_kv"].reshape(L, -1, 512))
    for c in range(ncores):
        m = dict(shared)
        sc = c % SB
        m["x_prompt"] = f(inp["x_prompt"][c % B])
        m["x_sample"] = f(inp["x_sample"][sc])
        if "cache_win" in declared:
            m["cache_win"] = f(inp["cache_win_kv"][:, sc].reshape(L, -1, 512))
        if "state_gla" in declared:
            m["state_gla"] = f(inp["state_gla"][:, sc])
        if "state_rwkv" in declared:
            m["state_rwkv"] = f(inp["state_rwkv"][:, sc])
        if "state_shift" in declared:
            m["state_shift"] = f(inp["state_rwkv_shift"][:, sc])
        if "page_table" in declared:
            m["page_table"] = f(inp["page_table"][sc].astype(np.int32))
        maps.append({k: v for k, v in m.items() if k in declared})
    return maps


def assemble(res, cfg, B, SB):
    L, T, D = cfg["L"], cfg["T"], cfg["D"]
    WK = min(512, T)
    g = lambda c, k, shp: np.asarray(res[c][k], dtype=np.float32).reshape(shp) if k in res[c] else np.zeros(shp, np.float32)
    y_p = np.stack([g(b, "y_prompt", (T, D)) for b in range(B)])
    y_s = np.stack([g(c, "y_sample", (1, D)) for c in range(SB)])
    pk = lambda k, n: np.stack([g(b, k, (L, n, 2, 4, 64)) for b in range(B)], axis=1)
    sk = lambda k: np.stack([g(c, k, (L, 1, 2, 4, 64)) for c in range(SB)], axis=1)
    p_gla = np.stack([g(b, "p_gla", (L, 4, 64, 128)) for b in range(B)], axis=1)
    p_rw = np.stack([g(b, "p_rw", (L, 8, 64, 64)) for b in range(B)], axis=1)
    p_sh = np.stack([g(b, "p_sh", (L, RWP)) for b in range(B)], axis=1)
    s_gla = np.stack([g(c, "s_gla", (L, 4, 64, 128)) for c in range(SB)], axis=1)
    s_rw = np.stack([g(c, "s_rw", (L, 8, 64, 64)) for c in range(SB)], axis=1)
    s_sh = np.stack([g(c, "s_sh", (L, RWP)) for c in range(SB)], axis=1)
    return (y_p, y_s, pk("p_cmp", T), pk("p_sel", T), pk("p_win", WK), p_gla, p_rw, p_sh,
            sk("s_cmp"), sk("s_sel"), sk("s_win"), s_gla, s_rw, s_sh)


_CACHE = {}


def kernel(**inputs):
    cfg = FULL_CFG
    if "nc" not in _CACHE:
        _CACHE["nc"] = build(cfg)
    nc, declared = _CACHE["nc"]
    inp = {k: np.asarray(v) for k, v in inputs.items()}
    maps = make_in_maps(inp, cfg, 8, declared)
    res = run_bass_kernel_spmd(nc, maps, core_ids=list(range(8)))
    return assemble(res.results, cfg, inp["x_prompt"].shape[0], inp["x_sample"].shape[0])
```

```python
import os
import numpy as np
import concourse.bass as bass
import concourse.mybir as mybir
from concourse.bass_utils import run_bass_kernel_spmd

F32 = mybir.dt.float32
BF16 = mybir.dt.bfloat16
I32 = mybir.dt.int32
AF = mybir.ActivationFunctionType
ALU = mybir.AluOpType
AX = mybir.AxisListType

NQ = 1024
C_Q, C_CMP, C_SEL, C_WIN, C_GATE = 0, 1024, 1536, 2048, 2560
C_GQ, C_GK, C_GV, C_GA, C_GG, C_RW, C_MG = 2608, 2864, 3120, 3632, 3648, 4160, 5856
RWP = 1696
EPS = 1e-6


import types


def _freeze(fn):
    if fn.__closure__ is None:
        return fn
    cells = []
    for c in fn.__closure__:
        try:
            cells.append(types.CellType(c.cell_contents))
        except ValueError:
            cells.append(c)
    return types.FunctionType(fn.__code__, fn.__globals__, fn.__name__, fn.__defaults__, tuple(cells))


class Sched:
    def __init__(self, nc, n_dma_sems=8):
        self.nc = nc
        self.eng = {"pe": nc.tensor, "act": nc.scalar, "dve": nc.vector, "pool": nc.gpsimd, "sp": nc.sync}
        self.prog = {e: [] for e in self.eng}
        self.sem = {e: nc.alloc_semaphore("c_" + e) for e in self.eng}
        self.cnt = {e: 0 for e in self.eng}
        self.dsem = {e: [nc.alloc_semaphore(f"d_{e}{i}") for i in range(n_dma_sems)] for e in ("sp", "pool", "act")}
        self.dcnt = {}
        self.semobj = {}
        for e in self.dsem:
            for s in self.dsem[e]:
                self.dcnt[id(s)] = 0
                self.semobj[id(s)] = s
        for e in self.sem:
            self.semobj[id(self.sem[e])] = self.sem[e]
        self.drr = {e: 0 for e in self.dsem}
        self.seen = {e: {} for e in self.eng}
        self.bufs = {}
        self.n_wait = 0
        self.pending = []
        self.max_pending = 2
        self.cap = None

    def _deps(self, e, reads, writes):
        waits = {}

        def need(tok, pe_ok=False):
            if tok is None:
                return
            sem, val, src = tok
            if e == "pe" and src == "pe" and pe_ok:
                return
            k = id(sem)
            if self.seen[e].get(k, 0) >= val:
                return
            if waits.get(k, 0) < val:
                waits[k] = val

        for b in reads:
            st = self.bufs.get(b)
            if st:
                need(st["w"])
        for b in writes:
            st = self.bufs.get(b)
            if st:
                need(st["w"], True)
                for r in st["r"]:
                    need(r, True)
        for k, v in waits.items():
            self.seen[e][k] = v
        self.n_wait += len(waits)
        return [(self.semobj[k], v) for k, v in waits.items()]

    def _commit(self, tok, reads, writes):
        for b in reads:
            st = self.bufs.setdefault(b, {"w": None, "r": []})
            st["r"].append(tok)
            if len(st["r"]) > 40:
                best = {}
                for t in st["r"]:
                    k = id(t[0])
                    if k not in best or best[k][1] < t[1]:
                        best[k] = t
                st["r"] = list(best.values())
        for b in writes:
            self.bufs[b] = {"w": tok, "r": []}

    def _flush_one(self):
        q, out, in_, reads, writes, indirect, kw = self.pending.pop(0)
        self.dma(q, out, in_, reads=reads, writes=writes, indirect=indirect, _nocheck=True, **kw)

    def _flush_conflicts(self, reads, writes):
        if not self.pending:
            return
        ws, rs = set(writes), set(reads)
        idx = -1
        for i, p in enumerate(self.pending):
            pr, pw = set(p[3]), set(p[4])
            if (ws & (pr | pw)) or (rs & pw):
                idx = i
        for _ in range(idx + 1):
            self._flush_one()

    def flush(self):
        while self.pending:
            self._flush_one()

    def store(self, q, out, in_, **kw):
        self.dma(q, out, in_, defer=True, **kw)

    def replay(self, items):
        for it in items:
            if it[0] == "op":
                self.op(it[1], it[2], reads=it[3], writes=it[4])
            else:
                _, q, out, in_, r, w, ind, defer, kw = it
                self.dma(q, out, in_, reads=r, writes=w, indirect=ind, defer=defer, **kw)

    def op(self, e, fn, reads=(), writes=()):
        fn = _freeze(fn)
        if self.cap is not None:
            self.cap.append(("op", e, fn, tuple(reads), tuple(writes)))
            return
        self._flush_conflicts(reads, writes)
        waits = self._deps(e, reads, writes)
        self.cnt[e] += 1
        sem = self.sem[e]
        tok = (sem, self.cnt[e], e)

        def emit(eng):
            for s, v in waits:
                eng.wait_ge(s, v)
            fn(eng).then_inc(sem, 1)

        self.prog[e].append(emit)
        self._commit(tok, reads, writes)

    def dma(self, q, out, in_, reads=(), writes=(), indirect=None, defer=False, _nocheck=False, **kw):
        if self.cap is not None:
            self.cap.append(("dma", q, out, in_, tuple(reads), tuple(writes), indirect, defer, kw))
            return
        if defer:
            self.pending.append((q, out, in_, tuple(reads), tuple(writes), indirect, kw))
            while len(self.pending) > self.max_pending:
                self._flush_one()
            return
        if not _nocheck:
            self._flush_conflicts(reads, writes)
        waits = self._deps(q, reads, writes)
        i = self.drr[q]
        self.drr[q] = (i + 1) % len(self.dsem[q])
        s = self.dsem[q][i]
        prev = self.dcnt[id(s)]
        self.dcnt[id(s)] = prev + 16
        tok = (s, prev + 16, "dma")
        if prev > 0 and self.seen[q].get(id(s), 0) < prev:
            waits.append((s, prev))
            self.seen[q][id(s)] = prev

        def emit(eng):
            for s_, v in waits:
                eng.wait_ge(s_, v)
            if indirect is not None:
                eng.indirect_dma_start(out=out, out_offset=None, in_=in_, in_offset=indirect, **kw).then_inc(s, 16)
            else:
                eng.dma_start(out=out, in_=in_, **kw).then_inc(s, 16)

        self.prog[q].append(emit)
        self._commit(tok, reads, writes)

    def barrier(self):
        self.flush()
        allw = []
        for e in self.dsem:
            for s in self.dsem[e]:
                if self.dcnt[id(s)] > 0:
                    allw.append((s, self.dcnt[id(s)]))
        for e in self.eng:
            if self.cnt[e] > 0:
                allw.append((self.sem[e], self.cnt[e]))
        for e in self.eng:
            waits = [(s, v) for (s, v) in allw if s is not self.sem[e] and self.seen[e].get(id(s), 0) < v]
            for s, v in waits:
                self.seen[e][id(s)] = v

            def emit(eng, waits=waits):
                for s_, v in waits:
                    eng.wait_ge(s_, v)

            self.prog[e].append(emit)
        self.bufs = {}

    def emit_all(self):
        self.flush()
        with self.nc.Block() as block:
            for e, deco in (("sp", block.sync), ("act", block.scalar), ("dve", block.vector),
                            ("pool", block.gpsimd), ("pe", block.tensor)):
                prog = self.prog[e]

                def body(eng, prog=prog):
                    for f in prog:
                        f(eng)

                deco(body)


class Pool:
    def __init__(self, nc):
        self.nc = nc
        self.stack = []

    uid = [0]

    def t(self, name, shape, dt):
        Pool.uid[0] += 1
        g = self.nc.sbuf_tensor(f"{name}_u{Pool.uid[0]}", list(shape), dt)
        h = g.__enter__()
        self.stack.append(g)
        assert self.nc.sbuf_bytes_remaining >= 0, f"SBUF overflow allocating {name}: {self.nc.sbuf_bytes_remaining}"
        return h

    def release(self):
        while self.stack:
            self.stack.pop().__exit__(None, None, None)


def build(cfg, dbg=False):
    D, T, PAST, NPOOL, L = cfg["D"], cfg["T"], cfg["PAST"], cfg["NPOOL"], cfg["L"]
    DFF = 4 * D
    NIN = C_MG + 3 * D
    NT = T // 128
    NTA = NT + 1
    TA = NTA * 128
    KD = D // 128
    WKEEP = min(512, T)
    NPG = PAST // 128

    nc = bass.Bass("TRN2", target_bir_lowering=False)
    S = Sched(nc)

    declared = set()

    def din(name, shape, dt=F32):
        declared.add(name)
        return nc.dram_tensor(name, list(shape), dt, kind="ExternalInput").ap()

    def dout(name, shape):
        return nc.dram_tensor(name, list(shape), F32, kind="ExternalOutput").ap()

    def dscr(name, shape, dt=F32):
        return nc.dram_tensor(name, list(shape), dt, kind=("ExternalOutput" if dbg else "Internal")).ap()

    x_prompt = din("x_prompt", [T, D])
    x_sample = din("x_sample", [1, D])
    norm1_g = din("norm1_g", [L, D])
    norm2_g = din("norm2_g", [L, D])
    w_in = din("w_in", [L, D, NIN])
    nsa_qk_g = din("nsa_qk_g", [L, 4, 64])
    nsa_up = din("nsa_up", [L, 1024, D])
    gla_up = din("gla_up", [L, 512, D])
    rw_up = din("rw_up", [L, 512, D])
    w_out = din("w_out", [L, D, D])
    mlp_w1 = din("mlp_w1", [L, D, DFF])
    mlp_w2 = din("mlp_w2", [L, DFF, D])
    cache_cmp = din("cache_cmp", [L * NPOOL * 128, 512])
    cache_sel = din("cache_sel", [L * NPOOL * 128, 512])
    cache_win = din("cache_win", [L, 512, 512])
    page_table = din("page_table", [NPG], I32)
    cmp_pos = din("cmp_pos", [L, 2, 64, 64])
    cmp_w1 = din("cmp_w1", [L, 2, 4096, 128])
    cmp_w2 = din("cmp_w2", [L, 2, 128, 64])
    gla_a2 = din("gla_a2", [L, 16, 256])
    gla_a_b = din("gla_a_b", [L, 256])
    gla_norm_g = din("gla_norm_g", [L, 128])
    state_gla = din("state_gla", [L, 4, 64, 128])
    rw_mu = din("rw_mu", [L, RWP])
    rw_w0 = din("rw_w0", [L, 512])
    rw_w2 = din("rw_w2", [L, 32, 512])
    rw_a0 = din("rw_a0", [L, 512])
    rw_a2 = din("rw_a2", [L, 32, 512])
    rw_g2 = din("rw_g2", [L, 96, 512])
    rw_kk = din("rw_kk", [L, 512])
    rw_ka = din("rw_ka", [L, 512])
    rw_rk = din("rw_rk", [L, 8, 64])
    rw_ln_g = din("rw_ln_g", [L, 512])
    rw_ln_b = din("rw_ln_b", [L, 512])
    state_rwkv = din("state_rwkv", [L, 8, 64, 64])
    state_shift = din("state_shift", [L, RWP])
    y_prompt = dout("y_prompt", [T, D])
    y_sample = dout("y_sample", [1, D])
    p_cmp_o = dout("p_cmp", [L, T, 512])
    p_sel_o = dout("p_sel", [L, T, 512])
    p_win_o = dout("p_win", [L, WKEEP, 512])
    p_sh_o = dout("p_sh", [L, RWP])
    s_cmp_o = dout("s_cmp", [L, 1, 512])
    s_sel_o = dout("s_sel", [L, 1, 512])
    s_win_o = dout("s_win", [L, 1, 512])
    s_sh_o = dout("s_sh", [L, RWP])
    p_gla_o = dout("p_gla", [L, 4, 64, 128])
    p_rw_o = dout("p_rw", [L, 8, 64, 64])
    s_rw_o = dout("s_rw", [L, 8, 64, 64])
    s_gla_o = dout("s_gla", [L, 4, 64, 128])
    omix_l0 = dout("omix_l0", [TA, 2048]) if dbg else None
    dbg_sel = dout("dbg_sel", [1, 4 * (PAST // 64)]) if dbg else None
    dbg_srow = dout("dbg_srow", [1, 4 * (PAST // 64)]) if dbg else None
    dbg_o = dout("dbg_o", [4, 3 * 256]) if dbg else None
    dbg_gs = dout("dbg_gs", [4, 12]) if dbg else None
    xa = dscr("xa", [TA, D])
    xm = dscr("xm", [TA, D])
    proj = dscr("proj", [TA, NIN])
    omix = dscr("omix", [TA, 2048])
    hidS = dscr("hidS", [NTA, 128, DFF // 128, 128], BF16)
    mrg = dscr("mrg", [TA, D])

    ident_b = nc.alloc_sbuf_tensor("ident_b", [128, 128], BF16)
    ident_f = nc.alloc_sbuf_tensor("ident_f", [128, 128], F32)
    zeros_f = nc.alloc_sbuf_tensor("zeros_f", [128, 2048], F32)
    PS = [nc.alloc_psum_tensor(f"ps{i}", [128, 512], F32) for i in range(6)]
    PB = [nc.alloc_psum_tensor(f"pb{i}", [128, 1024], BF16) for i in range(2)]

    S.op("pool", lambda e: e.memset(ident_f[:], 1.0), writes=["ident_f"])
    S.op("pool", lambda e: e.affine_select(out=ident_f[:], in_=ident_f[:], pattern=[[-1, 128]], compare_op=ALU.is_equal,
                                           fill=0.0, base=0, channel_multiplier=1), reads=["ident_f"], writes=["ident_f"])
    S.op("dve", lambda e: e.tensor_copy(out=ident_b[:], in_=ident_f[:]), reads=["ident_f"], writes=["ident_b"])
    S.op("dve", lambda e: e.memset(zeros_f[:], 0.0), writes=["zeros_f"])
    tri_f = nc.alloc_sbuf_tensor("tri_f", [64, 64], F32)
    ones_f = nc.alloc_sbuf_tensor("ones_f", [128, 64], F32)
    S.op("pool", lambda e: e.memset(ones_f[:], 1.0), writes=["ones_f"])
    trs_f = nc.alloc_sbuf_tensor("trs_f", [64, 64], F32)
    trl_f = nc.alloc_sbuf_tensor("trl_f", [64, 64], F32)
    S.op("pool", lambda e: e.memset(trs_f[:], 1.0), writes=["trs_f"])
    S.op("pool", lambda e: e.affine_select(out=trs_f[:], in_=trs_f[:], pattern=[[1, 64]], compare_op=ALU.is_gt,
                                           fill=0.0, base=0, channel_multiplier=-1), reads=["trs_f"], writes=["trs_f"])
    S.op("pool", lambda e: e.memset(trl_f[:], 1.0), writes=["trl_f"])
    S.op("pool", lambda e: e.affine_select(out=trl_f[:], in_=trl_f[:], pattern=[[-1, 64]], compare_op=ALU.is_gt,
                                           fill=0.0, base=0, channel_multiplier=1), reads=["trl_f"], writes=["trl_f"])
    S.op("pool", lambda e: e.memset(tri_f[:], 1.0), writes=["tri_f"])
    S.op("pool", lambda e: e.affine_select(out=tri_f[:], in_=tri_f[:], pattern=[[1, 64]], compare_op=ALU.is_ge,
                                           fill=0.0, base=0, channel_multiplier=-1), reads=["tri_f"], writes=["tri_f"])
    NBP = T // 64
    SLOPES = [2.0 ** (-8.0 * (h + 1) / 16) for h in range(16)]
    BIGN = -30000.0
    ci = [0]

    def iota_f(shape, pattern, base, cm):
        ci[0] += 1
        ti = nc.alloc_sbuf_tensor(f"iota_i{ci[0]}", list(shape), I32)
        tf = nc.alloc_sbuf_tensor(f"iota_f{ci[0]}", list(shape), F32)
        S.op("pool", lambda e: e.iota(ti[:], pattern=pattern, base=base, channel_multiplier=cm), writes=[("iota", ci[0])])
        S.op("dve", lambda e: e.tensor_copy(out=tf[:], in_=ti[:]), reads=[("iota", ci[0])], writes=[("iotaf", ci[0])])
        return tf

    Dc = iota_f([128, NBP], [[-64, NBP]], -63, 1)
    Dblk_i = iota_f([128, NBP], [[-64, NBP]], 0, 1)
    KDt = iota_f([128, NT + 1], [[-128, NT + 1]], -64, 1)
    QK_ = iota_f([128, 128], [[-1, 128]], 0, 1)
    S.barrier()
    negsl = nc.alloc_sbuf_tensor("negsl", [128, 16, NBP], F32)
    for h in range(16):
        S.op("pool", lambda e, h=h: e.memset(negsl[:, h, :], -SLOPES[h]), writes=[("negsl", h)])
    bcol = nc.alloc_sbuf_tensor("bcol", [128, 16, NT + 1], F32)
    for h in range(16):
        S.op("dve", lambda e, h=h: e.tensor_scalar(out=bcol[:, h, :], in0=KDt[:], scalar1=SLOPES[h], scalar2=None, op0=ALU.mult),
             writes=[("bcol", h)])
    Mc_b = nc.alloc_sbuf_tensor("Mc_b", [128, 128], BF16)
    Mw_b = nc.alloc_sbuf_tensor("Mw_b", [128, 128], BF16)
    S.op("dve", lambda e: e.tensor_scalar(out=Mc_b[:], in0=QK_[:], scalar1=0.0, scalar2=BIGN, op0=ALU.is_gt, op1=ALU.mult), writes=["Mc_b"])
    S.op("dve", lambda e: e.tensor_scalar(out=Mw_b[:], in0=QK_[:], scalar1=0.0, scalar2=BIGN, op0=ALU.is_le, op1=ALU.mult), writes=["Mw_b"])
    E_all = nc.alloc_sbuf_tensor("E_all", [max(NBP, 2), T], BF16)
    S.op("pool", lambda e: e.memset(E_all[:], 1.0), writes=["E_all"])
    S.op("pool", lambda e: e.affine_select(out=E_all[:], in_=E_all[:], pattern=[[1, T]], compare_op=ALU.is_ge, fill=0.0, base=0,
                                           channel_multiplier=-64), reads=["E_all"], writes=["E_all"])
    S.op("pool", lambda e: e.affine_select(out=E_all[:], in_=E_all[:], pattern=[[-1, T]], compare_op=ALU.is_ge, fill=0.0, base=63,
                                           channel_multiplier=64), reads=["E_all"], writes=["E_all"])
    NBS = PAST // 64
    SEGT = min(NT, NPG)
    SEGB = 2 * SEGT
    NSEG = NPG // SEGT
    pt_i = nc.alloc_sbuf_tensor("pt_i", [128, NPG], I32)
    pt_f = nc.alloc_sbuf_tensor("pt_f", [128, NPG], F32)
    idx_f = nc.alloc_sbuf_tensor("idx_f", [128, L, NPG], F32)
    idx_i = nc.alloc_sbuf_tensor("idx_i", [128, L, NPG], I32)
    pcol = iota_f([128, 1], [[0, 1]], 0, 1)
    slg_raw = iota_f([4, 4], [[4, 4]], 1, 1)
    dcs = iota_f([4, NBS], [[-64, NBS]], PAST - 63, 0)
    tb = iota_f([128, NPG], [[-128, NPG]], PAST, -1)
    tbw = iota_f([128, 4], [[-128, 4]], 512, -1)
    S.barrier()
    S.dma("sp", pt_i[:], page_table.partition_broadcast(128), writes=["pt_i"])
    S.op("dve", lambda e: e.tensor_copy(out=pt_f[:], in_=pt_i[:]), reads=["pt_i"], writes=["pt_f"])
    for l_ in range(L):
        S.op("dve", lambda e, l_=l_: e.tensor_scalar(out=idx_f[:, l_, :], in0=pt_f[:], scalar1=float(l_ * NPOOL), scalar2=128.0,
                                                    op0=ALU.add, op1=ALU.mult), reads=["pt_f"], writes=[("idx_f", l_)])
        S.op("dve", lambda e, l_=l_: e.tensor_scalar(out=idx_f[:, l_, :], in0=idx_f[:, l_, :], scalar1=pcol[:, 0:1], scalar2=None, op0=ALU.add),
             reads=[("idx_f", l_)], writes=[("idx_f", l_)])
        S.op("dve", lambda e, l_=l_: e.tensor_copy(out=idx_i[:, l_, :], in_=idx_f[:, l_, :]), reads=[("idx_f", l_)], writes=[("idx_i", l_)])
    slg = nc.alloc_sbuf_tensor("slg", [4, 4], F32)
    S.op("act", lambda e: e.activation(out=slg[:], in_=slg_raw[:], func=AF.Exp, scale=-0.34657359027997264), writes=["slg"])
    S.op("dve", lambda e: e.tensor_scalar(out=slg[:], in0=slg[:], scalar1=-1.0, scalar2=None, op0=ALU.mult), reads=["slg"], writes=["slg"])
    negsl16 = nc.alloc_sbuf_tensor("negsl16", [128, 16], F32)
    for h in range(16):
        S.op("pool", lambda e, h=h: e.memset(negsl16[:, h:h + 1], -SLOPES[h]), writes=[("negsl16", h)])
    half_sel = nc.alloc_sbuf_tensor("half_sel", [1, 2, 128], BF16)
    S.op("pool", lambda e: e.memset(half_sel[:], 0.0), writes=["half_sel"])
    S.op("pool", lambda e: e.memset(half_sel[0:1, 0, 0:64], 1.0), reads=["half_sel"], writes=["half_sel"])
    S.op("pool", lambda e: e.memset(half_sel[0:1, 1, 64:128], 1.0), reads=["half_sel"], writes=["half_sel"])
    mw0 = nc.alloc_sbuf_tensor("mw0", [128, 1], F32)
    S.op("dve", lambda e: e.tensor_scalar(out=mw0[:], in0=ident_f[:, 0:1], scalar1=BIGN, scalar2=None, op0=ALU.mult), writes=["mw0"])
    S.barrier()

    S.store("sp", xa[0:T, :], x_prompt)
    S.store("sp", xa[T:T + 1, :], x_sample)
    S.store("sp", xa[T + 1:TA, :], zeros_f[0:127, 0:D])
    S.barrier()

    evac_rr = [0]

    def evac(out, in_, reads, writes):
        evac_rr[0] ^= 1
        if evac_rr[0]:
            S.op("act", lambda e: e.copy(out=out, in_=in_), reads=reads, writes=writes)
        else:
            S.op("dve", lambda e: e.tensor_copy(out=out, in_=in_), reads=reads, writes=writes)

    def rmsnorm_T(P, src, g_row, hT, tag, NC=None):
        NC = NC or D
        KD = NC // 128
        D_ = NC
        xt = [P.t(f"{tag}_x{i}", [128, NC], F32) for i in range(2)]
        xb = [P.t(f"{tag}_xb{i}", [128, NC], BF16) for i in range(2)]
        if g_row is not None:
            gb = P.t(tag + "_g", [128, NC], F32)
            S.dma("sp", gb[:], g_row, writes=[tag + "g"])
            sq = P.t(tag + "_sq", [128, NC], F32)
            ss = P.t(tag + "_ss", [128, 2 * NTA], F32)
        for tt in range(NTA):
            b = tt % 2
            S.dma("sp", xt[b][:], src[tt * 128:(tt + 1) * 128, 0:NC], writes=[(tag, "x", b)])
            if g_row is None:
                S.op("act" if tt % 2 else "dve", (lambda e, b=b: e.copy(out=xb[b][:], in_=xt[b][:])) if tt % 2 else
                     (lambda e, b=b: e.tensor_copy(out=xb[b][:], in_=xt[b][:])), reads=[(tag, "x", b)], writes=[(tag, "xb", b)])
            if g_row is not None:
              S.op("act", lambda e, b=b, tt=tt: e.activation(out=sq[:], in_=xt[b][:], func=AF.Square,
                                                         accum_out=ss[:, 2 * tt:2 * tt + 1]),
                 reads=[(tag, "x", b)], writes=[(tag, "sq"), (tag, "ss", tt)])
            if g_row is not None:
              S.op("dve", lambda e, tt=tt: e.tensor_scalar(out=ss[:, 2 * tt + 1:2 * tt + 2], in0=ss[:, 2 * tt:2 * tt + 1],
                                                          scalar1=1.0 / D_, scalar2=EPS, op0=ALU.mult, op1=ALU.add),
                   reads=[(tag, "ss", tt)], writes=[(tag, "r0", tt)])
              S.op("act", lambda e, tt=tt: e.activation(out=ss[:, 2 * tt + 1:2 * tt + 2], in_=ss[:, 2 * tt + 1:2 * tt + 2], func=AF.Ln),
                   reads=[(tag, "r0", tt)], writes=[(tag, "r0b", tt)])
              S.op("act", lambda e, tt=tt: e.activation(out=ss[:, 2 * tt + 1:2 * tt + 2], in_=ss[:, 2 * tt + 1:2 * tt + 2], func=AF.Exp,
                                                       scale=-0.5),
                   reads=[(tag, "r0b", tt)], writes=[(tag, "r1", tt)])
              S.op("dve", lambda e, b=b, tt=tt: e.scalar_tensor_tensor(out=xb[b][:], in0=xt[b][:], scalar=ss[:, 2 * tt + 1:2 * tt + 2],
                                                                      in1=gb[:], op0=ALU.mult, op1=ALU.mult),
                   reads=[(tag, "x", b), (tag, "r1", tt), tag + "g"], writes=[(tag, "xb", b)])
            for k4 in range(KD // 4 if KD >= 4 else 1):
                nk = min(4, KD)
                pb = PB[k4 % 2]
                for kk in range(nk):
                    k = k4 * 4 + kk
                    S.op("pe", lambda e, b=b, k=k, kk=kk, pb=pb: e.transpose(out=pb[:, kk * 128:(kk + 1) * 128],
                                                                         in_=xb[b][:, k * 128:(k + 1) * 128], identity=ident_b[:]),
                         reads=[(tag, "xb", b)], writes=[("pb", k4 % 2)])
                evac(hT[:, k4 * 4:k4 * 4 + nk, tt * 128:(tt + 1) * 128],
                     pb[:, 0:nk * 128].rearrange("p (k t) -> p k t", k=nk),
                     reads=[("pb", k4 % 2)], writes=[(tag, "hT", tt, k4)])

    def bcast_row(ap_row, n):
        return ap_row.partition_broadcast(128)

    NCH = T // 64 + 1

    def gla_mixer(l):
        P = Pool(nc)
        NCG = 1552
        pg = [P.t(f"pgG{i}", [64, NCG], F32) for i in range(2)]
        a2 = P.t("a2G", [16, 256], F32)
        ab = P.t("abG", [64, 256], F32)
        gng = P.t("gngG", [64, 128], F32)
        S.dma("sp", a2[:], gla_a2[l], writes=["a2G"])
        S.dma("sp", ab[:], gla_a_b[l].partition_broadcast(64), writes=["abG"])
        S.dma("sp", gng[:], gla_norm_g[l].partition_broadcast(64), writes=["gngG"])
        aT = P.t("aTG", [16, 64], F32)
        la = P.t("laG", [64, 256], F32)
        cum = P.t("cumG", [64, 256], F32)
        e_q = P.t("eqG", [64, 256], F32)
        e_k = P.t("ekG", [64, 256], F32)
        e_l = P.t("elG", [64, 256], F32)
        qd = P.t("qdG", [64, 256], BF16)
        kd = P.t("kdG", [64, 256], BF16)
        kl = P.t("klG", [64, 256], BF16)
        vb = P.t("vbG", [64, 512], BF16)
        qkT = P.t("qkTG", [64, 8, 64], BF16)
        att = P.t("attG", [64, 4, 64], BF16)
        St = P.t("StG", [64, 4, 128], F32)
        Sb = P.t("SbG", [64, 4, 128], BF16)
        ecol = P.t("ecolG", [64, 4], F32)
        og = P.t("ogG", [64, 4, 128], F32)
        o2 = P.t("o2G", [64, 4, 128], F32)
        sg = P.t("sgG", [64, 512], F32)
        ssg = P.t("ssgG", [64, 16], F32)
        S.op("dve", lambda e: e.memset(St[:], 0.0), writes=["StG"])
        S.op("dve", lambda e: e.memset(Sb[:], 0.0), writes=["SbG"])
        QO, KO, VO, AO, GO = 0, 256, 512, 1024, 1040
        for c in range(NCH):
            is_s = (c == NCH - 1)
            r0 = c * 64
            p = pg[c % 2]
            pk = ("pgG", c % 2)
            if is_s:
                S.store("sp", p_gla_o[l].rearrange("h d v -> d h v"), St[:], reads=["StG"])
                S.dma("sp", St[:], state_gla[l].rearrange("h d v -> d h v"), writes=["StG"])
                S.op("dve", lambda e: e.tensor_copy(out=Sb[:], in_=St[:]), reads=["StG"], writes=["SbG"])
            S.dma("sp", p[:], proj[r0:r0 + 64, C_GQ:C_GQ + NCG], writes=[pk])
            S.op("pe", lambda e, p=p: e.transpose(out=PS[0][0:16, 0:64], in_=p[:, AO:AO + 16], identity=ident_f[0:64, 0:64]),
                 reads=[pk], writes=[("ps", 0)])
            S.op("act", lambda e: e.copy(out=aT[:], in_=PS[0][0:16, 0:64]), reads=[("ps", 0)], writes=["aTG"])
            S.op("pe", lambda e: e.matmul(PS[1][0:64, 0:256], lhsT=aT[:], rhs=a2[:], start=True, stop=True),
                 reads=["aTG", "a2G"], writes=[("ps", 1)])
            S.op("dve", lambda e: e.tensor_tensor(out=la[:], in0=PS[1][0:64, 0:256], in1=ab[:], op=ALU.add),
                 reads=[("ps", 1), "abG"], writes=["laG"])
            S.op("act", lambda e: e.activation(out=la[:], in_=la[:], func=AF.Exp, scale=-1.0), reads=["laG"], writes=["laG"])
            S.op("act", lambda e: e.activation(out=la[:], in_=la[:], func=AF.Ln, bias=1.0), reads=["laG"], writes=["laG"])
            mcol = ident_f[0:64, 0:1] if is_s else ones_f[0:64, 0:1]
            S.op("dve", lambda e, mcol=mcol: e.tensor_scalar(out=la[:], in0=la[:], scalar1=-1.0 / 16, scalar2=mcol,
                                                            op0=ALU.mult, op1=ALU.mult), reads=["laG"], writes=["laG"])
            S.op("pe", lambda e: e.matmul(PS[2][0:64, 0:256], lhsT=tri_f[:], rhs=la[:], start=True, stop=True),
                 reads=["laG"], writes=[("ps", 2)])
            S.op("pe", lambda e: e.matmul(PS[3][0:64, 0:256], lhsT=ones_f[0:64, 0:64], rhs=la[:], start=True, stop=True),
                 reads=["laG"], writes=[("ps", 3)])
            for h in range(4):
                S.op("pe", lambda e, h=h: e.matmul(PS[4][0:64, h:h + 1], lhsT=la[:, h * 64:(h + 1) * 64], rhs=ones_f[0:64, 0:1],
                                                   start=True, stop=True), reads=["laG"], writes=[("ps", 4)])
            S.op("act", lambda e: e.activation(out=ecol[:], in_=PS[4][0:64, 0:4], func=AF.Exp), reads=[("ps", 4)], writes=["ecolG"])
            S.op("dve", lambda e: e.tensor_copy(out=cum[:], in_=PS[2][0:64, 0:256]), reads=[("ps", 2)], writes=["cumG"])
            S.op("act", lambda e: e.activation(out=e_q[:], in_=cum[:], func=AF.Exp), reads=["cumG"], writes=["eqG"])
            S.op("act", lambda e: e.activation(out=e_k[:], in_=cum[:], func=AF.Exp, scale=-1.0), reads=["cumG"], writes=["ekG"])
            S.op("dve", lambda e: e.tensor_tensor(out=e_l[:], in0=PS[3][0:64, 0:256], in1=cum[:], op=ALU.subtract),
                 reads=[("ps", 3), "cumG"], writes=["elG"])
            S.op("act", lambda e: e.activation(out=e_l[:], in_=e_l[:], func=AF.Exp), reads=["elG"], writes=["elG"])
            S.op("dve", lambda e, p=p: e.scalar_tensor_tensor(out=qd[:], in0=p[:, QO:QO + 256], scalar=0.125, in1=e_q[:],
                                                              op0=ALU.mult, op1=ALU.mult), reads=[pk, "eqG"], writes=["qdG"])
            S.op("dve", lambda e, p=p: e.tensor_tensor(out=kd[:], in0=p[:, KO:KO + 256], in1=e_k[:], op=ALU.mult),
                 reads=[pk, "ekG"], writes=["kdG"])
            S.op("dve", lambda e, p=p: e.tensor_tensor(out=kl[:], in0=p[:, KO:KO + 256], in1=e_l[:], op=ALU.mult),
                 reads=[pk, "elG"], writes=["klG"])
            S.op("act", lambda e, p=p: e.copy(out=vb[:], in_=p[:, VO:VO + 512]), reads=[pk], writes=["vbG"])
            for h in range(4):
                S.op("pe", lambda e, h=h: e.transpose(out=PB[0][0:64, h * 64:(h + 1) * 64], in_=qd[:, h * 64:(h + 1) * 64],
                                                      identity=ident_b[0:64, 0:64]), reads=["qdG"], writes=[("pb", 0)])
                S.op("pe", lambda e, h=h: e.transpose(out=PB[0][0:64, (4 + h) * 64:(5 + h) * 64], in_=kd[:, h * 64:(h + 1) * 64],
                                                      identity=ident_b[0:64, 0:64]), reads=["kdG"], writes=[("pb", 0)])
            S.op("dve", lambda e: e.tensor_copy(out=qkT[:].rearrange("p a t -> p (a t)"), in_=PB[0][0:64, 0:512]),
                 reads=[("pb", 0)], writes=["qkTG"])
            for h in range(4):
                S.op("pe", lambda e, h=h: e.matmul(PS[5][0:64, h * 64:(h + 1) * 64], lhsT=qkT[:, 4 + h, :], rhs=qkT[:, h, :],
                                                   start=True, stop=True), reads=["qkTG"], writes=[("ps", 5)])
            S.op("dve", lambda e: e.tensor_tensor(out=att[:], in0=PS[5][0:64, 0:256].rearrange("p (h t) -> p h t", h=4),
                                                  in1=tri_f[:].unsqueeze(1).to_broadcast([64, 4, 64]), op=ALU.mult),
                 reads=[("ps", 5)], writes=["attG"])
            for h in range(4):
                S.op("pe", lambda e, h=h: e.matmul(PS[0][0:64, h * 128:(h + 1) * 128], lhsT=att[:, h, :], rhs=vb[:, h * 128:(h + 1) * 128],
                                                   start=True, stop=False), reads=["attG", "vbG"], writes=[("ps", 0)])
                S.op("pe", lambda e, h=h: e.matmul(PS[0][0:64, h * 128:(h + 1) * 128], lhsT=qkT[:, h, :], rhs=Sb[:, h, :],
                                                   start=False, stop=True), reads=["qkTG", "SbG"], writes=[("ps", 0)])
            S.op("act", lambda e: e.copy(out=og[:].rearrange("p h v -> p (h v)"), in_=PS[0][0:64, 0:512]), reads=[("ps", 0)], writes=["ogG"])
            for h in range(4):
                S.op("pe", lambda e, h=h: e.matmul(PS[1][0:64, h * 128:(h + 1) * 128], lhsT=kl[:, h * 64:(h + 1) * 64],
                                                   rhs=vb[:, h * 128:(h + 1) * 128], start=True, stop=True),
                     reads=["klG", "vbG"], writes=[("ps", 1)])
            for h in range(4):
                S.op("dve", lambda e, h=h: e.scalar_tensor_tensor(out=St[:, h, :], in0=St[:, h, :], scalar=ecol[:, h:h + 1],
                                                                  in1=PS[1][0:64, h * 128:(h + 1) * 128], op0=ALU.mult, op1=ALU.add),
                     reads=["StG", "ecolG", ("ps", 1)], writes=["StG"])
            S.op("dve", lambda e: e.tensor_copy(out=Sb[:], in_=St[:]), reads=["StG"], writes=["SbG"])
            S.op("dve", lambda e: e.tensor_tensor(out=o2[:], in0=og[:], in1=og[:], op=ALU.mult), reads=["ogG"], writes=["o2G"])
            S.op("dve", lambda e: e.tensor_reduce(out=ssg[:, 0:4], in_=o2[:], axis=AX.X, op=ALU.add), reads=["o2G"], writes=["ssg0"])
            S.op("dve", lambda e: e.tensor_scalar(out=ssg[:, 4:8], in0=ssg[:, 0:4], scalar1=1.0 / 128, scalar2=EPS,
                                                  op0=ALU.mult, op1=ALU.add), reads=["ssg0"], writes=["ssg1"])
            S.op("act", lambda e: e.activation(out=ssg[:, 8:12], in_=ssg[:, 4:8], func=AF.Ln), reads=["ssg1"], writes=["ssg2"])
            S.op("act", lambda e: e.activation(out=ssg[:, 12:16], in_=ssg[:, 8:12], func=AF.Exp, scale=-0.5), reads=["ssg2"], writes=["ssg3"])
            S.op("dve", lambda e: e.tensor_tensor(out=o2[:], in0=og[:], in1=ssg[:, 12:16].unsqueeze(2).to_broadcast([64, 4, 128]),
                                                  op=ALU.mult), reads=["ogG", "ssg3"], writes=["o2G"])
            S.op("dve", lambda e: e.tensor_tensor(out=o2[:], in0=o2[:], in1=gng[:].unsqueeze(1).to_broadcast([64, 4, 128]),
                                                  op=ALU.mult), reads=["o2G", "gngG"], writes=["o2G"])
            S.op("act", lambda e, p=p: e.activation(out=sg[:], in_=p[:, GO:GO + 512], func=AF.Sigmoid), reads=[pk], writes=["sgG"])
            S.op("dve", lambda e, p=p: e.tensor_tensor(out=sg[:], in0=sg[:], in1=p[:, GO:GO + 512], op=ALU.mult),
                 reads=["sgG", pk], writes=["sgG"])
            S.op("dve", lambda e: e.tensor_tensor(out=o2[:].rearrange("p h v -> p (h v)"), in0=o2[:].rearrange("p h v -> p (h v)"),
                                                  in1=sg[:], op=ALU.mult), reads=["o2G", "sgG"], writes=["o2G"])
            S.store("sp", omix[r0:r0 + 64, 1024:1536], o2[:].rearrange("p h v -> p (h v)"), reads=["o2G"])
        S.store("sp", s_gla_o[l].rearrange("h d v -> d h v"), St[:], reads=["StG"])
        S.barrier()
        P.release()


    def rw_mixer(l):
        P = Pool(nc)
        H8 = lambda ap: ap.rearrange("p (h d) -> p h d", h=8)
        cnt = [0]

        def T_(name, shape=(64, 512), dt=F32):
            return P.t(name + "R", list(shape), dt)

        rwt = [T_(f"rwt{i}", (64, RWP)) for i in range(2)]
        sht = [T_(f"sht{i}", (64, RWP)) for i in range(2)]
        mu = T_("mu", (64, RWP)); S.dma("sp", mu[:], rw_mu[l].partition_broadcast(64), writes=["muR"])
        cb = {}
        for nm, src in (("w0", rw_w0), ("a0", rw_a0), ("kkw", rw_kk), ("ka", rw_ka), ("lng", rw_ln_g), ("lnb", rw_ln_b)):
            cb[nm] = T_(nm)
            S.dma("sp", cb[nm][:], src[l].partition_broadcast(64), writes=[nm + "R"])
        rk = T_("rk"); S.dma("sp", rk[:], rw_rk[l].rearrange("h d -> (h d)").partition_broadcast(64), writes=["rkR"])
        w2 = T_("w2", (32, 512)); S.dma("sp", w2[:], rw_w2[l], writes=["w2R"])
        a2 = T_("a2", (32, 512)); S.dma("sp", a2[:], rw_a2[l], writes=["a2R"])
        g2 = T_("g2", (96, 512)); S.dma("sp", g2[:], rw_g2[l], writes=["g2R"])
        xm = T_("xm", (64, RWP))
        sm = T_("sm", (64, 160))
        smT = T_("smT", (96, 3, 64))
        names = ["lw", "av", "gv", "kk", "kp", "tmp", "cum", "E1", "E2", "E3", "E4", "Rt", "At", "Bt", "Kt", "Bh", "Kh", "bv",
                 "vv", "bon", "oo", "o2"]
        BFN = ("Rt", "At", "Bt", "Kt", "Bh", "Kh")
        t = {n: (T_(n, (64, 512), BF16) if n in BFN else T_(n)) for n in names}
        vvb = T_("vvb", (64, 512), BF16)
        ssr = T_("ssr", (64, 64))
        gcol = T_("gcol", (64, 8))
        TT = {n: T_(n + "T", (64, 8, 64), BF16) for n in ("Rt", "At", "Bt", "Kt")}
        mats = {n: T_(n, (64, 8, 64), BF16) for n in ("N", "N2", "AK", "KR", "BR", "Q", "MAT", "LV", "U")}
        H = T_("H", (64, 8, 64))
        Hb = T_("Hb", (64, 8, 64), BF16)
        Hv = T_("Hv", (64, 8, 64))
        lvl = P.t("lvlR", [64, 5, 8, 64], BF16)
        S.op("dve", lambda e: e.memset(H[:], 0.0), writes=["HR"])
        S.op("dve", lambda e: e.memset(Hb[:], 0.0), writes=["HbR"])

        def op3(eng, out, in0, in1, op, rk_, wk_):
            S.op(eng, lambda e: e.tensor_tensor(out=out, in0=in0, in1=in1, op=op), reads=rk_, writes=wk_)

        def store_state(dst):
            for h in range(8):
                S.op("pe", lambda e, h=h: e.transpose(out=PS[0][0:64, h * 64:(h + 1) * 64], in_=H[:, h, :], identity=ident_f[0:64, 0:64]),
                     reads=["HR"], writes=[("ps", 0)])
            S.op("act", lambda e: e.copy(out=Hv[:].rearrange("p h k -> p (h k)"), in_=PS[0][0:64, 0:512]), reads=[("ps", 0)], writes=["HvR"])
            S.store("sp", dst.rearrange("h v k -> v h k"), Hv[:], reads=["HvR"])

        def load_state(src):
            S.dma("sp", Hv[:], src.rearrange("h v k -> v h k"), writes=["HvR"])
            for h in range(8):
                S.op("pe", lambda e, h=h: e.transpose(out=PS[0][0:64, h * 64:(h + 1) * 64], in_=Hv[:, h, :], identity=ident_f[0:64, 0:64]),
                     reads=["HvR"], writes=[("ps", 0)])
            S.op("act", lambda e: e.copy(out=H[:].rearrange("p h k -> p (h k)"), in_=PS[0][0:64, 0:512]), reads=[("ps", 0)], writes=["HR"])
            S.op("dve", lambda e: e.tensor_copy(out=Hb[:], in_=H[:]), reads=["HR"], writes=["HbR"])

        def mm8(ps_i, lhs_fn, rhs_fn, rk_, start=True, stop=True):
            for h in range(8):
                S.op("pe", lambda e, h=h: e.matmul(PS[ps_i][0:64, h * 64:(h + 1) * 64], lhsT=lhs_fn(h), rhs=rhs_fn(h), start=start, stop=stop),
                     reads=rk_, writes=[("ps", ps_i)])

        def ps8(i):
            return PS[i][0:64, 0:512].rearrange("p (h d) -> p h d", h=8)

        xm2 = [xm, T_("xmB", (64, RWP))]
        gv2 = [t["gv"], T_("gvB")]
        bon2 = [t["bon"], T_("bonB")]
        vvb2 = [vvb, T_("vvbB", (64, 512), BF16)]

        def P0(c):
            is_s = (c == NCH - 1)
            r0 = c * 64
            b = c % 2
            rwb, shb = rwt[b], sht[b]
            kr, ks = ("rwtR", b), ("shtR", b)
            xm = xm2[b]
            kx = ("xmR", b)
            S.dma("sp", rwb[:], proj[r0:r0 + 64, C_RW:C_RW + RWP], writes=[kr])
            if c == 0:
                S.dma("sp", shb[0:1, :], zeros_f[0:1, 0:RWP], writes=[(ks, 0)])
            elif is_s:
                S.dma("sp", shb[0:1, :], state_shift[l:l + 1, :], writes=[(ks, 0)])
            else:
                S.dma("sp", shb[0:1, :], proj[r0 - 1:r0, C_RW:C_RW + RWP], writes=[(ks, 0)])
            S.dma("sp", shb[1:64, :], proj[r0:r0 + 63, C_RW:C_RW + RWP], writes=[(ks, 1)])
            rsh = [kr, (ks, 0), (ks, 1)]
            op3("pool", xm[:], shb[:], rwb[:], ALU.subtract, rsh, [kx])
            op3("pool", xm[:], xm[:], mu[:], ALU.mult, [kx, "muR"], [kx])
            op3("pool", xm[:], xm[:], rwb[:], ALU.add, [kx, kr], [kx])
            S.op("act", lambda e: e.activation(out=sm[:, 0:32], in_=xm[:, 1536:1568], func=AF.Tanh), reads=[kx], writes=[("smR", 0)])
            S.op("act", lambda e: e.copy(out=sm[:, 32:64], in_=xm[:, 1568:1600]), reads=[kx], writes=[("smR", 1)])
            S.op("act", lambda e: e.activation(out=sm[:, 64:160], in_=xm[:, 1600:1696], func=AF.Sigmoid), reads=[kx], writes=[("smR", 2)])

        def P1p(c):
            is_s = (c == NCH - 1)
            b = c % 2
            gv = gv2[b]
            for i, (o0, n) in enumerate(((0, 32), (32, 32), (64, 96))):
                S.op("pe", lambda e, i=i, o0=o0, n=n: e.transpose(out=PS[0][0:n, i * 64:(i + 1) * 64], in_=sm[:, o0:o0 + n],
                                                                  identity=ident_f[0:64, 0:64]), reads=[("smR", i)], writes=[("ps", 0)])
                S.op("act", lambda e, i=i, n=n: e.copy(out=smT[0:n, i, :], in_=PS[0][0:n, i * 64:(i + 1) * 64]),
                     reads=[("ps", 0)], writes=[("smTR", i)])
            for i, (wsb, n, wk) in enumerate(((w2, 32, "w2R"), (a2, 32, "a2R"), (g2, 96, "g2R"))):
                S.op("pe", lambda e, i=i, wsb=wsb, n=n: e.matmul(PS[1 + i][0:64, 0:512], lhsT=smT[0:n, i, :], rhs=wsb[:], start=True, stop=True),
                     reads=[("smTR", i), wk], writes=[("ps", 1 + i)])
            mcol = ident_f[0:64, 0:1] if is_s else ones_f[0:64, 0:1]
            op3("dve", t["lw"][:], PS[1][0:64, 0:512], cb["w0"][:], ALU.add, [("ps", 1), "w0R"], ["lwR"])
            S.op("act", lambda e: e.activation(out=t["lw"][:], in_=t["lw"][:], func=AF.Sigmoid), reads=["lwR"], writes=["lwR"])
            S.op("dve", lambda e, mcol=mcol: e.tensor_scalar(out=t["lw"][:], in0=t["lw"][:], scalar1=-0.6065306597, scalar2=mcol,
                                                            op0=ALU.mult, op1=ALU.mult), reads=["lwR"], writes=["lwR"])
            op3("dve", t["av"][:], PS[2][0:64, 0:512], cb["a0"][:], ALU.add, [("ps", 2), "a0R"], ["avR"])
            S.op("act", lambda e: e.activation(out=t["av"][:], in_=t["av"][:], func=AF.Sigmoid), reads=["avR"], writes=["avR"])
            S.op("act", lambda e: e.copy(out=gv[:], in_=PS[3][0:64, 0:512]), reads=[("ps", 3)], writes=[("gvR", b)])

        def P1b(c):
            is_s = (c == NCH - 1)
            b = c % 2
            xm = xm2[b]
            kx = ("xmR", b)
            vvb = vvb2[b]
            bon = bon2[b]
            mcol = ident_f[0:64, 0:1] if is_s else ones_f[0:64, 0:1]
            r_, k_, v_ = xm[:, 0:512], xm[:, 512:1024], xm[:, 1024:1536]
            op3("dve", t["kk"][:], k_, cb["kkw"][:], ALU.mult, [kx, "kkwR"], ["kkR"])
            op3("dve", t["tmp"][:], t["kk"][:], t["kk"][:], ALU.mult, ["kkR"], ["tmpR"])
            S.op("dve", lambda e: e.tensor_reduce(out=ssr[:, 0:8], in_=H8(t["tmp"][:]), axis=AX.X, op=ALU.add), reads=["tmpR"], writes=["ss0R"])
            S.op("dve", lambda e: e.tensor_scalar(out=ssr[:, 8:16], in0=ssr[:, 0:8], scalar1=1e-24, scalar2=None, op0=ALU.max),
                 reads=["ss0R"], writes=["ss1R"])
            S.op("act", lambda e: e.activation(out=ssr[:, 16:24], in_=ssr[:, 8:16], func=AF.Ln), reads=["ss1R"], writes=["ss2R"])
            S.op("act", lambda e: e.activation(out=ssr[:, 24:32], in_=ssr[:, 16:24], func=AF.Exp, scale=-0.5), reads=["ss2R"], writes=["ss3R"])
            S.op("dve", lambda e, mcol=mcol: e.tensor_scalar(out=ssr[:, 24:32], in0=ssr[:, 24:32], scalar1=mcol, scalar2=None, op0=ALU.mult),
                 reads=["ss3R"], writes=["ss3R"])
            op3("dve", H8(t["kk"][:]), H8(t["kk"][:]), ssr[:, 24:32].unsqueeze(2).to_broadcast([64, 8, 64]), ALU.mult, ["kkR", "ss3R"], ["kkR"])
            S.op("dve", lambda e: e.scalar_tensor_tensor(out=t["kp"][:], in0=t["av"][:], scalar=-1.0, in1=cb["ka"][:], op0=ALU.add, op1=ALU.mult),
                 reads=["avR", "kaR"], writes=["kpR"])
            S.op("dve", lambda e: e.scalar_tensor_tensor(out=t["kp"][:], in0=t["kp"][:], scalar=1.0, in1=k_, op0=ALU.add, op1=ALU.mult),
                 reads=["kpR", kx], writes=["kpR"])
            S.op("pool", lambda e, mcol=mcol: e.tensor_scalar(out=t["kp"][:], in0=t["kp"][:], scalar1=mcol, scalar2=None, op0=ALU.mult),
                 reads=["kpR"], writes=["kpR"])
            S.op("pool", lambda e, mcol=mcol: e.tensor_scalar(out=t["vv"][:], in0=v_, scalar1=mcol, scalar2=None, op0=ALU.mult),
                 reads=[kx], writes=["vvR"])
            S.op("act", lambda e: e.copy(out=vvb[:], in_=t["vv"][:]), reads=["vvR"], writes=[("vvbR", b)])
            op3("dve", t["bv"][:], t["kk"][:], t["av"][:], ALU.mult, ["kkR", "avR"], ["bvR"])
            op3("dve", t["tmp"][:], r_, t["kp"][:], ALU.mult, [kx, "kpR"], ["tmpR"])
            op3("dve", t["tmp"][:], t["tmp"][:], rk[:], ALU.mult, ["tmpR", "rkR"], ["tmpR"])
            S.op("dve", lambda e: e.tensor_reduce(out=ssr[:, 32:40], in_=H8(t["tmp"][:]), axis=AX.X, op=ALU.add), reads=["tmpR"], writes=["ss4R"])
            op3("dve", H8(bon[:]), H8(t["vv"][:]), ssr[:, 32:40].unsqueeze(2).to_broadcast([64, 8, 64]), ALU.mult, ["vvR", "ss4R"], [("bonR", b)])

        def P2(c):
            b = c % 2
            xm = xm2[b]
            kx = ("xmR", b)
            r_ = xm[:, 0:512]
            S.op("pe", lambda e: e.matmul(PS[4][0:64, 0:512], lhsT=tri_f[:], rhs=t["lw"][:], start=True, stop=True), reads=["lwR"], writes=[("ps", 4)])
            S.op("pe", lambda e: e.matmul(PS[5][0:64, 0:512], lhsT=ones_f[0:64, 0:64], rhs=t["lw"][:], start=True, stop=True),
                 reads=["lwR"], writes=[("ps", 5)])
            for h in range(8):
                S.op("pe", lambda e, h=h: e.matmul(PS[0][0:64, h:h + 1], lhsT=t["lw"][:, h * 64:(h + 1) * 64], rhs=ones_f[0:64, 0:1],
                                                   start=True, stop=True), reads=["lwR"], writes=[("ps", 0)])
            S.op("act", lambda e: e.activation(out=gcol[:], in_=PS[0][0:64, 0:8], func=AF.Exp), reads=[("ps", 0)], writes=["gcolR"])
            S.op("dve", lambda e: e.tensor_copy(out=t["cum"][:], in_=PS[4][0:64, 0:512]), reads=[("ps", 4)], writes=["cumR"])
            S.op("act", lambda e: e.activation(out=t["E1"][:], in_=t["cum"][:], func=AF.Exp), reads=["cumR"], writes=["E1R"])
            S.op("act", lambda e: e.activation(out=t["E2"][:], in_=t["cum"][:], func=AF.Exp, scale=-1.0), reads=["cumR"], writes=["E2R"])
            op3("dve", t["E3"][:], t["cum"][:], t["lw"][:], ALU.subtract, ["cumR", "lwR"], ["E3R"])
            S.op("act", lambda e: e.activation(out=t["E3"][:], in_=t["E3"][:], func=AF.Exp), reads=["E3R"], writes=["E3R"])
            op3("dve", t["E4"][:], PS[5][0:64, 0:512], t["cum"][:], ALU.subtract, [("ps", 5), "cumR"], ["E4R"])
            S.op("act", lambda e: e.activation(out=t["E4"][:], in_=t["E4"][:], func=AF.Exp), reads=["E4R"], writes=["E4R"])
            op3("dve", t["Rt"][:], r_, t["E1"][:], ALU.mult, [kx, "E1R"], ["RtR"])
            S.op("dve", lambda e: e.scalar_tensor_tensor(out=t["At"][:], in0=t["kk"][:], scalar=-1.0, in1=t["E3"][:], op0=ALU.mult, op1=ALU.mult),
                 reads=["kkR", "E3R"], writes=["AtR"])
            op3("pool", t["Bt"][:], t["bv"][:], t["E2"][:], ALU.mult, ["bvR", "E2R"], ["BtR"])
            op3("pool", t["Kt"][:], t["kp"][:], t["E2"][:], ALU.mult, ["kpR", "E2R"], ["KtR"])
            op3("pool", t["Bh"][:], t["bv"][:], t["E4"][:], ALU.mult, ["bvR", "E4R"], ["BhR"])
            op3("pool", t["Kh"][:], t["kp"][:], t["E4"][:], ALU.mult, ["kpR", "E4R"], ["KhR"])

        def P3(c):
            r0 = c * 64
            b = c % 2
            vvb = vvb2[b]
            bon = bon2[b]
            gv = gv2[b]
            for i, n in enumerate(("Rt", "At", "Bt", "Kt")):
                pbt, po_ = PB[i // 2], (i % 2) * 512
                for h in range(8):
                    S.op("pe", lambda e, h=h, n=n, pbt=pbt, po_=po_: e.transpose(out=pbt[0:64, po_ + h * 64:po_ + (h + 1) * 64],
                                                                               in_=t[n][:, h * 64:(h + 1) * 64], identity=ident_b[0:64, 0:64]),
                         reads=[n + "R"], writes=[("pb", i // 2)])
                evac(TT[n][:].rearrange("p h d -> p (h d)"), pbt[0:64, po_:po_ + 512], reads=[("pb", i // 2)], writes=[n + "TR"])
            hh = lambda m: (lambda h: m[:, h, :])
            for (nm, lh, rh, msk, psi) in (("N", "Bt", "At", trs_f, 5), ("Lm", "At", "Bt", trl_f, 0), ("AK", "Kt", "At", trs_f, 1),
                                           ("KR", "Kt", "Rt", tri_f, 2), ("BR", "Bt", "Rt", tri_f, 3)):
                mm8(psi, hh(TT[lh]), hh(TT[rh]), [lh + "TR", rh + "TR"])
                dst_, dk_ = (lvl[:, 0], ("lvlR", 0)) if nm == "Lm" else (mats[nm][:], nm + "R")
                op3("dve", dst_, ps8(psi), msk[:].unsqueeze(1).to_broadcast([64, 8, 64]), ALU.mult, [("ps", psi)], [dk_])
            Nb = [mats["N"], mats["N2"]]
            Nk = ["NR", "N2R"]
            for i in range(5):
                cur, nxt = i % 2, (i + 1) % 2
                mm8(4, (lambda h, i=i: lvl[:, i, h, :]), hh(Nb[cur]), [Nk[cur], ("lvlR", i)])
                if i < 4:
                    mm8(5, hh(Nb[cur]), (lambda h, i=i: lvl[:, i, h, :]), [Nk[cur], ("lvlR", i)])
                S.op("act", lambda e, nxt=nxt: e.copy(out=Nb[nxt][:], in_=ps8(4)), reads=[("ps", 4)], writes=[Nk[nxt]])
                if i < 4:
                    S.op("dve", lambda e, i=i: e.tensor_copy(out=lvl[:, i + 1], in_=ps8(5)), reads=[("ps", 5)], writes=[("lvlR", i + 1)])
            op3("dve", mats["Q"][:], Nb[1][:], ident_f[0:64, 0:64].unsqueeze(1).to_broadcast([64, 8, 64]), ALU.add, [Nk[1]], ["QR"])
            for i in (4, 3, 2, 1, 0):
                mm8(4, (lambda h, i=i: lvl[:, i, h, :]), hh(mats["Q"]), [("lvlR", i), "QR"])
                op3("dve", mats["Q"][:], mats["Q"][:], ps8(4), ALU.add, ["QR", ("ps", 4)], ["QR"])
            mm8(5, (lambda h: t["At"][:, h * 64:(h + 1) * 64]), hh(mats["Q"]), ["AtR", "QR"])
            evac(mats["MAT"][:], ps8(5), reads=[("ps", 5)], writes=["MATR"])
            mm8(0, hh(mats["AK"]), (lambda h: vvb[:, h * 64:(h + 1) * 64]), ["AKR", ("vvbR", b)])
            evac(mats["LV"][:], ps8(0), reads=[("ps", 0)], writes=["LVR"])
            for h in range(8):
                S.op("pe", lambda e, h=h: e.matmul(PS[1][0:64, h * 64:(h + 1) * 64], lhsT=mats["Q"][:, h, :], rhs=mats["LV"][:, h, :],
                                                   start=True, stop=False), reads=["QR", "LVR"], writes=[("ps", 1)])
                S.op("pe", lambda e, h=h: e.matmul(PS[1][0:64, h * 64:(h + 1) * 64], lhsT=mats["MAT"][:, h, :], rhs=Hb[:, h, :],
                                                   start=False, stop=True), reads=["MATR", "HbR"], writes=[("ps", 1)])
            evac(mats["U"][:], ps8(1), reads=[("ps", 1)], writes=["UR"])
            for h in range(8):
                sl = slice(h * 64, (h + 1) * 64)
                S.op("pe", lambda e, h=h, sl=sl: e.matmul(PS[2][0:64, sl], lhsT=TT["Rt"][:, h, :], rhs=Hb[:, h, :], start=True, stop=False),
                     reads=["RtTR", "HbR"], writes=[("ps", 2)])
                S.op("pe", lambda e, h=h, sl=sl: e.matmul(PS[2][0:64, sl], lhsT=mats["BR"][:, h, :], rhs=mats["U"][:, h, :], start=False, stop=False),
                     reads=["BRR", "UR"], writes=[("ps", 2)])
                S.op("pe", lambda e, h=h, sl=sl: e.matmul(PS[2][0:64, sl], lhsT=mats["KR"][:, h, :], rhs=vvb[:, sl], start=False, stop=True),
                     reads=["KRR", ("vvbR", b)], writes=[("ps", 2)])
            S.op("act", lambda e: e.copy(out=t["oo"][:], in_=PS[2][0:64, 0:512]), reads=[("ps", 2)], writes=["ooR"])
            for h in range(8):
                sl = slice(h * 64, (h + 1) * 64)
                S.op("pe", lambda e, h=h, sl=sl: e.matmul(PS[3][0:64, sl], lhsT=t["Bh"][:, sl], rhs=mats["U"][:, h, :], start=True, stop=False),
                     reads=["BhR", "UR"], writes=[("ps", 3)])
                S.op("pe", lambda e, h=h, sl=sl: e.matmul(PS[3][0:64, sl], lhsT=t["Kh"][:, sl], rhs=vvb[:, sl], start=False, stop=True),
                     reads=["KhR", ("vvbR", b)], writes=[("ps", 3)])
            op3("dve", H[:], H[:], gcol[:].unsqueeze(2).to_broadcast([64, 8, 64]), ALU.mult, ["HR", "gcolR"], ["HR"])
            op3("dve", H[:], H[:], ps8(3), ALU.add, ["HR", ("ps", 3)], ["HR"])
            S.op("act", lambda e: e.copy(out=Hb[:], in_=H[:]), reads=["HR"], writes=["HbR"])
            S.op("dve", lambda e: e.tensor_reduce(out=ssr[:, 40:48], in_=H8(t["oo"][:]), axis=AX.X, op=ALU.add), reads=["ooR"], writes=["ss5R"])
            S.op("dve", lambda e: e.tensor_scalar(out=ssr[:, 40:48], in0=ssr[:, 40:48], scalar1=-1.0 / 64, scalar2=None, op0=ALU.mult),
                 reads=["ss5R"], writes=["ss5R"])
            op3("dve", H8(t["oo"][:]), H8(t["oo"][:]), ssr[:, 40:48].unsqueeze(2).to_broadcast([64, 8, 64]), ALU.add, ["ooR", "ss5R"], ["ooR"])
            op3("dve", t["o2"][:], t["oo"][:], t["oo"][:], ALU.mult, ["ooR"], ["o2R"])
            S.op("dve", lambda e: e.tensor_reduce(out=ssr[:, 48:56], in_=H8(t["o2"][:]), axis=AX.X, op=ALU.add), reads=["o2R"], writes=["ss6R"])
            S.op("dve", lambda e: e.tensor_scalar(out=ssr[:, 48:56], in0=ssr[:, 48:56], scalar1=1.0 / 64, scalar2=64e-5, op0=ALU.mult, op1=ALU.add),
                 reads=["ss6R"], writes=["ss6R"])
            S.op("act", lambda e: e.activation(out=ssr[:, 48:56], in_=ssr[:, 48:56], func=AF.Ln), reads=["ss6R"], writes=["ss6R"])
            S.op("act", lambda e: e.activation(out=ssr[:, 56:64], in_=ssr[:, 48:56], func=AF.Exp, scale=-0.5), reads=["ss6R"], writes=["ss7R"])
            op3("dve", H8(t["oo"][:]), H8(t["oo"][:]), ssr[:, 56:64].unsqueeze(2).to_broadcast([64, 8, 64]), ALU.mult, ["ooR", "ss7R"], ["ooR"])
            op3("pool", t["oo"][:], t["oo"][:], cb["lng"][:], ALU.mult, ["ooR", "lngR"], ["ooR"])
            op3("pool", t["oo"][:], t["oo"][:], cb["lnb"][:], ALU.add, ["ooR", "lnbR"], ["ooR"])
            op3("pool", t["oo"][:], t["oo"][:], bon[:], ALU.add, ["ooR", ("bonR", b)], ["ooR"])
            op3("pool", t["oo"][:], t["oo"][:], gv[:], ALU.mult, ["ooR", ("gvR", b)], ["ooR"])
            S.store("sp", omix[r0:r0 + 64, 1536:2048], t["oo"][:], reads=["ooR"])

        def merged(a, b_):
            out = []
            i = j = 0
            while i < len(a) or j < len(b_):
                if j >= len(b_) or (i < len(a) and i * len(b_) <= j * len(a)):
                    out.append(a[i]); i += 1
                else:
                    out.append(b_[j]); j += 1
            return out

        P0(0)
        P1p(0)
        P1b(0)
        for c in range(NCH):
            if c == NCH - 1:
                store_state(p_rw_o[l])
                load_state(state_rwkv[l])
            if c + 1 < NCH:
                P0(c + 1)
            P2(c)
            if c + 1 < NCH:
                P1p(c + 1)
                S.cap = []
                P1b(c + 1)
                sa = S.cap
                S.cap = []
                P3(c)
                sb_ = S.cap
                S.cap = None
                S.replay(merged(sb_, sa))
            else:
                P3(c)
        store_state(s_rw_o[l])
        S.barrier()
        P.release()


    def nsa_mixer(l):
        LP = Pool(nc)
        kcT_p = LP.t("kcT_p", [64, 4, NBP], BF16)
        vc_p = LP.t("vc_p", [max(NBP, 2), 4, 64], BF16)
        qkg = LP.t("qkgN", [128, 4, 64], F32)
        S.dma("sp", qkg[:].rearrange("p a d -> p (a d)"), nsa_qk_g[l].rearrange("a d -> (a d)").partition_broadcast(128), writes=["qkgN"])
        tmpN = LP.t("tmpN", [128, 16, 64], F32)
        ssN = LP.t("ssN", [128, 64], F32)
        kcT_s = LP.t("kcT_s", [64, 4, NBS], BF16)
        vc_s = LP.t("vc_s", [SEGB, NSEG, 4, 64], BF16)
        q_s = LP.t("q_s", [64, 16], BF16)
        k_s = LP.t("k_s", [64, 8], BF16)
        v_s = LP.t("v_s", [1, 8, 65], BF16)
        g_s = LP.t("g_s", [1, 48], F32)

        def headnormN(out3, in3, g_ap, H, rk, wk, sc=1.0, np_=128):
            S.op("dve", lambda e: e.tensor_tensor(out=tmpN[0:np_, 0:H, :], in0=in3, in1=in3, op=ALU.mult), reads=rk, writes=["tmpN"])
            S.op("dve", lambda e: e.tensor_reduce(out=ssN[0:np_, 0:H], in_=tmpN[0:np_, 0:H, :], axis=AX.X, op=ALU.add), reads=["tmpN"], writes=["ssN0"])
            S.op("dve", lambda e: e.tensor_scalar(out=ssN[0:np_, 16:16 + H], in0=ssN[0:np_, 0:H], scalar1=1.0 / 64, scalar2=EPS,
                                                  op0=ALU.mult, op1=ALU.add), reads=["ssN0"], writes=["ssN1"])
            S.op("act", lambda e: e.activation(out=ssN[0:np_, 48:48 + H], in_=ssN[0:np_, 16:16 + H], func=AF.Ln), reads=["ssN1"], writes=["ssN1b"])
            S.op("act", lambda e: e.activation(out=ssN[0:np_, 32:32 + H], in_=ssN[0:np_, 48:48 + H], func=AF.Exp, scale=-0.5,
                                               bias=float(np.log(sc))), reads=["ssN1b"], writes=["ssN2"])
            S.op("dve", lambda e: e.tensor_tensor(out=tmpN[0:np_, 0:H, :], in0=in3,
                                                  in1=ssN[0:np_, 32:32 + H].unsqueeze(2).to_broadcast([np_, H, 64]), op=ALU.mult),
                 reads=rk + ["ssN2"], writes=["tmpN"])
            S.op("dve", lambda e: e.tensor_tensor(out=out3, in0=tmpN[0:np_, 0:H, :],
                                                  in1=g_ap.unsqueeze(1).to_broadcast([np_, H, 64]), op=ALU.mult),
                 reads=["tmpN", "qkgN"], writes=wk)

        P = Pool(nc)
        w1 = P.t("w1N", [64, 2, 64, 128], BF16)
        w2 = P.t("w2N", [128, 2, 64], BF16)
        for kv in range(2):
            S.dma("pool", w1[:, kv], cmp_w1[l, kv].rearrange("(pos d) h -> d pos h", d=64), writes=[("w1N", kv)])
            S.dma("pool", w2[:, kv, :], cmp_w2[l, kv], writes=[("w2N", kv)])
        posb = P.t("posbN", [128, 2, 64], F32)
        for hf in range(2):
            S.dma("sp", posb[hf * 64:(hf + 1) * 64], cmp_pos[l].rearrange("kv pos d -> pos kv d"), writes=[("posbN", hf)])
        zT = P.t("zTN", [64, 8, T], BF16)
        ct = [P.t(f"ctN{i}", [128, 512], F32) for i in range(2)]
        zb = [P.t(f"zbN{i}", [128, 512], BF16) for i in range(2)]
        hx = [P.t(f"hxN{i}", [128, 4 * NBP], F32) for i in range(4)]
        hb = P.t("hbN", [128, 4 * NBP], BF16)
        kc = P.t("kcN", [max(NBP, 2), 4, 64], F32)
        kcb = P.t("kcbN", [max(NBP, 2), 4, 64], BF16)

        def compress_segment(load_fn, ntiles, kT_out, v_out):
            nb = ntiles * 2
            for tt in range(ntiles):
                b = tt % 2
                load_fn(tt, ct[b], ("ctN", b))
                S.op("dve", lambda e, b=b: e.tensor_tensor(out=zb[b][:].rearrange("p (kv g d) -> p kv g d", kv=2, g=4),
                                                           in0=ct[b][:].rearrange("p (kv g d) -> p kv g d", kv=2, g=4),
                                                           in1=posb[:].unsqueeze(2).to_broadcast([128, 2, 4, 64]), op=ALU.add),
                     reads=[("ctN", b), ("posbN", 0), ("posbN", 1)], writes=[("zbN", b)])
                for a in range(8):
                    S.op("pe", lambda e, a=a, b=b: e.transpose(out=PB[b][0:64, a * 128:(a + 1) * 128], in_=zb[b][:, a * 64:(a + 1) * 64],
                                                             identity=ident_b[:]), reads=[("zbN", b)], writes=[("pb", b)])
                evac(zT[:, :, tt * 128:(tt + 1) * 128], PB[b][0:64, 0:1024].rearrange("p (a t) -> p a t", a=8),
                     reads=[("pb", b)], writes=[("zTN", tt)])
            zr = [("zTN", tt) for tt in range(ntiles)]
            for kv in range(2):
                ps = PS[kv]
                for pos in range(64):
                    S.op("pe", lambda e, kv=kv, pos=pos, ps=ps: e.matmul(
                        ps[:, 0:4 * nb].rearrange("p (g n) -> p g n", g=4), lhsT=w1[:, kv, pos, :],
                        rhs=zT[:, kv * 4:(kv + 1) * 4, bass.DynSlice(pos, nb, step=64)] if False else
                        zT[:, kv * 4:(kv + 1) * 4, 0:nb * 64].rearrange("p g (n s) -> p g n s", s=64)[:, :, :, pos],
                        start=(pos == 0), stop=(pos == 63)), reads=zr + [("w1N", kv)], writes=[("ps", kv)])
                n4 = 4 * nb
                x_, x2, u_, th = hx[0], hx[1], hx[2], hx[3]
                S.op("act", lambda e, ps=ps: e.copy(out=x_[:, 0:n4], in_=ps[:, 0:n4]), reads=[("ps", kv)], writes=["hx0"])
                S.op("dve", lambda e: e.tensor_tensor(out=x2[:, 0:n4], in0=x_[:, 0:n4], in1=x_[:, 0:n4], op=ALU.mult), reads=["hx0"], writes=["hx1"])
                S.op("dve", lambda e: e.tensor_scalar(out=x2[:, 0:n4], in0=x2[:, 0:n4], scalar1=0.044715, scalar2=1.0, op0=ALU.mult, op1=ALU.add),
                     reads=["hx1"], writes=["hx1"])
                S.op("dve", lambda e: e.tensor_tensor(out=u_[:, 0:n4], in0=x2[:, 0:n4], in1=x_[:, 0:n4], op=ALU.mult), reads=["hx1", "hx0"], writes=["hx2"])
                S.op("act", lambda e: e.activation(out=th[:, 0:n4], in_=u_[:, 0:n4], func=AF.Tanh, scale=0.7978845608), reads=["hx2"], writes=["hx3"])
                S.op("dve", lambda e: e.scalar_tensor_tensor(out=th[:, 0:n4], in0=th[:, 0:n4], scalar=1.0, in1=x_[:, 0:n4], op0=ALU.add, op1=ALU.mult),
                     reads=["hx3", "hx0"], writes=["hx3"])
                S.op("dve", lambda e: e.tensor_scalar(out=hb[:, 0:n4], in0=th[:, 0:n4], scalar1=0.5, scalar2=None, op0=ALU.mult),
                     reads=["hx3"], writes=["hbN"])
                for g in range(4):
                    S.op("pe", lambda e, g=g, kv=kv: e.matmul(PS[2][0:nb, g * 64:(g + 1) * 64], lhsT=hb[:, g * nb:(g + 1) * nb], rhs=w2[:, kv, :],
                                                             start=True, stop=True), reads=["hbN", ("w2N", kv)], writes=[("ps", 2)])
                if kv == 0:
                    S.op("act", lambda e: e.copy(out=kc[0:nb].rearrange("p g d -> p (g d)"), in_=PS[2][0:nb, 0:256]), reads=[("ps", 2)], writes=["kcN"])
                    headnormN(kcb[0:nb], kc[0:nb], qkg[0:nb, 1, :], 4, ["kcN"], ["kcbN"], np_=nb)
                    for g in range(4):
                        S.op("pe", lambda e, g=g: e.transpose(out=PB[0][0:64, g * nb:(g + 1) * nb], in_=kcb[0:nb, g, :], identity=ident_b[0:nb, 0:nb]),
                             reads=["kcbN"], writes=[("pb", 0)])
                    S.op("dve", lambda e: e.tensor_copy(out=kT_out, in_=PB[0][0:64, 0:4 * nb].rearrange("p (g n) -> p g n", g=4)),
                         reads=[("pb", 0)], writes=["kcT"])
                else:
                    S.op("act", lambda e: e.copy(out=v_out, in_=PS[2][0:nb, 0:256].rearrange("p (g d) -> p g d", g=4)), reads=[("ps", 2)], writes=["vc"])

        def load_prompt_cmp(tt, dst, key):
            S.store("sp", dst[:], proj[tt * 128:(tt + 1) * 128, C_CMP:C_CMP + 512], writes=[key])

        compress_segment(load_prompt_cmp, NT, kcT_p[:], vc_p[0:NBP])
        for sgi in range(NSEG):
            def load_cache_cmp(tt, dst, key, sgi=sgi):
                j = sgi * SEGT + tt
                S.dma("pool", dst[:], cache_cmp, reads=[("idx_i", l)], writes=[key],
                      indirect=bass.IndirectOffsetOnAxis(ap=idx_i[:, l, j:j + 1], axis=0))
            compress_segment(load_cache_cmp, SEGT, kcT_s[:, :, sgi * SEGB:(sgi + 1) * SEGB], vc_s[0:SEGB, sgi])
        S.barrier()
        P.release()

        P = Pool(nc)
        qT = P.t("qTN", [64, 16, TA], BF16)
        kT = P.t("kTN", [64, 8, TA], BF16)
        Va = P.t("VaN", [128, NTA, 8, 65], BF16)
        gts = P.t("gtsN", [128, NTA, 48], F32)
        S.op("pool", lambda e: e.memset(Va[:], 1.0), writes=["VaN"])
        ptN = [P.t(f"ptN{i}", [128, C_GQ], F32) for i in range(1)]
        qn = [P.t(f"qnN{i}", [128, 16, 64], BF16) for i in range(2)]
        kn = [P.t(f"knN{i}", [128, 8, 64], BF16) for i in range(2)]
        for tt in range(NTA):
            b = tt % 2
            p_ = ptN[0]
            pk = ("ptN", 0)
            S.dma("sp", p_[:], proj[tt * 128:(tt + 1) * 128, 0:C_GQ], writes=[pk])
            headnormN(qn[b][:], p_[:, 0:1024].rearrange("p (h d) -> p h d", h=16), qkg[:, 0, :], 16, [pk], [("qnN", b)], sc=0.125)
            headnormN(kn[b][:, 0:4], p_[:, C_SEL:C_SEL + 256].rearrange("p (h d) -> p h d", h=4), qkg[:, 2, :], 4, [pk], [("knN", b, 0)])
            headnormN(kn[b][:, 4:8], p_[:, C_WIN:C_WIN + 256].rearrange("p (h d) -> p h d", h=4), qkg[:, 3, :], 4, [pk], [("knN", b, 1)])
            S.op("act", lambda e, p_=p_, tt=tt: e.copy(out=Va[:, tt, 0:4, 0:64], in_=p_[:, C_SEL + 256:C_SEL + 512].rearrange("p (g d) -> p g d", g=4)),
                 reads=[pk, "VaN"], writes=[("VaN", tt, 0)])
            S.op("act", lambda e, p_=p_, tt=tt: e.copy(out=Va[:, tt, 4:8, 0:64], in_=p_[:, C_WIN + 256:C_WIN + 512].rearrange("p (g d) -> p g d", g=4)),
                 reads=[pk, "VaN"], writes=[("VaN", tt, 1)])
            S.op("act", lambda e, p_=p_, tt=tt: e.activation(out=gts[:, tt, :], in_=p_[:, C_GATE:C_GATE + 48], func=AF.Sigmoid),
                 reads=[pk], writes=[("gtsN", tt)])
            for half in range(2):
                for a in range(8):
                    S.op("pe", lambda e, a=a, b=b, half=half: e.transpose(out=PB[half][0:64, a * 128:(a + 1) * 128], in_=qn[b][:, half * 8 + a, :],
                                                                        identity=ident_b[:]), reads=[("qnN", b)], writes=[("pb", half)])
                evac(qT[:, half * 8:(half + 1) * 8, tt * 128:(tt + 1) * 128], PB[half][0:64, 0:1024].rearrange("p (a t) -> p a t", a=8),
                     reads=[("pb", half)], writes=[("qTN", tt, half)])
            for a in range(8):
                S.op("pe", lambda e, a=a, b=b: e.transpose(out=PB[0][0:64, a * 128:(a + 1) * 128], in_=kn[b][:, a, :], identity=ident_b[:]),
                     reads=[("knN", b, 0), ("knN", b, 1)], writes=[("pb", 0)])
            evac(kT[:, :, tt * 128:(tt + 1) * 128], PB[0][0:64, 0:1024].rearrange("p (a t) -> p a t", a=8), reads=[("pb", 0)], writes=[("kTN", tt)])

        S.op("dve", lambda e: e.tensor_copy(out=q_s[:], in_=qT[:, :, T]), reads=[("qTN", NT, 0), ("qTN", NT, 1)], writes=["q_s"])
        S.op("dve", lambda e: e.tensor_copy(out=k_s[:], in_=kT[:, :, T]), reads=[("kTN", NT)], writes=["k_s"])
        S.op("dve", lambda e: e.tensor_copy(out=v_s[:], in_=Va[0:1, NT, :, :]), reads=[("VaN", NT, 0), ("VaN", NT, 1), "VaN"], writes=["v_s"])
        S.op("dve", lambda e: e.tensor_copy(out=g_s[:], in_=gts[0:1, NT, :]), reads=[("gtsN", NT)], writes=["g_s"])
        bc = P.t("bcN", [128, 16, NBP], F32)
        dI = P.t("dIN", [128, NBP], F32)
        pen = P.t("penN", [128, NBP], F32)
        ec = P.t("ecN", [128, 16, NBP], F32)
        pb16 = P.t("pb16N", [128, 16, NBP], BF16)
        sm = P.t("smN", [128, 64], F32)
        scg = P.t("scgN", [128, 4, NBP], F32)
        adj = P.t("adjN", [128, NBP], F32)
        adj2 = P.t("adj2N", [128, NBP], F32)
        vld = P.t("vldN", [128, NBP], F32)
        m8 = P.t("m8N", [128, 4, 16], F32)
        scw = P.t("scwN", [128, 4, NBP], F32)
        selp = P.t("selpN", [128, 4, NBP], BF16)
        penT = P.t("penTN", [max(NBP, 2), 4, 128], BF16)
        pT = P.t("pTN", [max(NBP, 2), 16, 128], BF16)
        oacc = P.t("oaccN", [128, 16, 64], F32)
        otmp = P.t("otmpN", [128, 4, 64], F32)
        wv = P.t("wvN", [128, 8], F32)
        PT = [P.t(f"PTN{i}", [128, 4, 128], BF16) for i in range(2)]
        sti = [0]

        def attend(i, g, kbase, vbase, tiles, use_pen, gate_idx):
            acc = PS[4]
            nj = len(tiles)
            S.op("dve", lambda e: e.memset(acc[:, 0:260], 0.0), writes=[("ps", 4)])
            base = sti[0]
            sti[0] += nj

            def qk(jn):
                j = tiles[jn]
                sb = (base + jn) % 2
                st = PS[2 + sb]
                stk = ("ps", 2 + sb)
                dl = i - j
                extra = []
                if use_pen:
                    extra.append((E_all[0:NBP, j * 128:(j + 1) * 128], penT[0:NBP, g, :].unsqueeze(1).to_broadcast([NBP, 4, 128]), ["penTN"]))
                if dl == 0:
                    extra.append((ident_b[:], Mc_b[:].unsqueeze(1).to_broadcast([128, 4, 128]), []))
                if (not use_pen) and dl == 4:
                    extra.append((ident_b[:], Mw_b[:].unsqueeze(1).to_broadcast([128, 4, 128]), []))
                ne = len(extra)
                for r in range(4):
                    h = 4 * g + r
                    sl = slice(r * 128, (r + 1) * 128)
                    S.op("pe", lambda e, st=st, sl=sl, j=j, h=h, r=r, ne=ne: e.matmul(
                        st[:, sl], lhsT=kT[:, kbase + g, j * 128:(j + 1) * 128], rhs=qT[:, h, i * 128:(i + 1) * 128],
                        start=(r == 0), stop=(ne == 0 and r == 3), skip_group_check=True),
                        reads=[("kTN", j), ("qTN", i, h // 8)], writes=[stk])
                for xi, (lh, rh, rk_) in enumerate(extra):
                    S.op("pe", lambda e, st=st, lh=lh, rh=rh, last=(xi == ne - 1): e.matmul(
                        st[:, 0:512].rearrange("p (r q) -> p r q", r=4), lhsT=lh, rhs=rh, start=False, stop=last, skip_group_check=True),
                        reads=rk_, writes=[stk])

            def ex_pv(jn):
                j = tiles[jn]
                sb = (base + jn) % 2
                st = PS[2 + sb]
                stk = ("ps", 2 + sb)
                ptile = PT[sb]
                dl = i - j
                for r in range(4):
                    h = 4 * g + r
                    S.op("act", lambda e, st=st, ptile=ptile, r=r, h=h, dl=dl: e.activation(
                        out=ptile[:, r, :], in_=st[:, r * 128:(r + 1) * 128], func=AF.Exp, bias=bcol[:, h, dl:dl + 1], scale=1.0),
                        reads=[stk], writes=[("PTN", sb, r)])
                for r in range(4):
                    S.op("pe", lambda e, ptile=ptile, r=r, j=j, jn=jn: e.matmul(
                        acc[:, r * 65:(r + 1) * 65], lhsT=ptile[:, r, :], rhs=Va[:, j, vbase + g, :], start=False, stop=(jn == nj - 1),
                        skip_group_check=True),
                        reads=[("PTN", sb, r), ("VaN", j, vbase // 4), "VaN"], writes=[("ps", 4)])

            qk(0)
            for jn in range(nj):
                if jn + 1 < nj:
                    qk(jn + 1)
                ex_pv(jn)
            a3 = acc[:, 0:260].rearrange("p (r c) -> p r c", r=4)
            S.op("dve", lambda e: e.reciprocal(out=wv[:, 0:4], in_=a3[:, :, 64]), reads=[("ps", 4)], writes=["wv0"])
            S.op("dve", lambda e: e.tensor_tensor(out=wv[:, 4:8], in0=wv[:, 0:4], in1=gts[:, i, gate_idx * 16 + 4 * g:gate_idx * 16 + 4 * g + 4], op=ALU.mult),
                 reads=["wv0", ("gtsN", i)], writes=["wv1"])
            S.op("dve", lambda e: e.tensor_tensor(out=otmp[:], in0=a3[:, :, 0:64], in1=wv[:, 4:8].unsqueeze(2).to_broadcast([128, 4, 64]), op=ALU.mult),
                 reads=[("ps", 4), "wv1"], writes=["otmpN"])
            S.op("dve", lambda e: e.tensor_tensor(out=oacc[:, 4 * g:4 * g + 4, :], in0=oacc[:, 4 * g:4 * g + 4, :], in1=otmp[:], op=ALU.add),
                 reads=["otmpN", "oaccN"], writes=["oaccN"])

        for i in range(NT):
            qr = [("qTN", i, 0), ("qTN", i, 1)]
            for h in range(16):
                S.op("pe", lambda e, h=h: e.matmul(PS[0][:, h * NBP:(h + 1) * NBP], lhsT=qT[:, h, i * 128:(i + 1) * 128], rhs=kcT_p[:, h // 4, :],
                                                   start=True, stop=True), reads=qr + ["kcT"], writes=[("ps", 0)])
            S.op("dve", lambda e: e.tensor_scalar(out=dI[:], in0=Dc[:], scalar1=float(128 * i), scalar2=None, op0=ALU.add), writes=["dIN"])
            S.op("dve", lambda e: e.tensor_scalar(out=pen[:], in0=dI[:], scalar1=0.0, scalar2=BIGN, op0=ALU.is_lt, op1=ALU.mult), reads=["dIN"], writes=["penN"])
            S.op("dve", lambda e: e.tensor_tensor(out=bc[:], in0=negsl[:], in1=dI[:].unsqueeze(1).to_broadcast([128, 16, NBP]), op=ALU.mult),
                 reads=["dIN"], writes=["bcN"])
            S.op("dve", lambda e: e.tensor_tensor(out=bc[:], in0=bc[:], in1=pen[:].unsqueeze(1).to_broadcast([128, 16, NBP]), op=ALU.add),
                 reads=["bcN", "penN"], writes=["bcN"])
            S.op("dve", lambda e: e.tensor_tensor(out=ec[:], in0=PS[0][:, 0:16 * NBP].rearrange("p (h n) -> p h n", h=16), in1=bc[:], op=ALU.add),
                 reads=[("ps", 0), "bcN"], writes=["ecN"])
            S.op("act", lambda e: e.activation(out=ec[:], in_=ec[:], func=AF.Exp), reads=["ecN"], writes=["ecN"])
            S.op("dve", lambda e: e.tensor_reduce(out=sm[:, 0:16], in_=ec[:], axis=AX.X, op=ALU.add), reads=["ecN"], writes=["sm0"])
            S.op("dve", lambda e: e.tensor_scalar(out=sm[:, 16:32], in0=sm[:, 0:16], scalar1=1e-30, scalar2=None, op0=ALU.max), reads=["sm0"], writes=["sm1"])
            S.op("dve", lambda e: e.reciprocal(out=sm[:, 32:48], in_=sm[:, 16:32]), reads=["sm1"], writes=["sm2"])
            S.op("dve", lambda e: e.tensor_tensor(out=ec[:], in0=ec[:], in1=sm[:, 32:48].unsqueeze(2).to_broadcast([128, 16, NBP]), op=ALU.mult),
                 reads=["ecN", "sm2"], writes=["ecN"])
            S.op("act", lambda e: e.copy(out=pb16[:], in_=ec[:]), reads=["ecN"], writes=["pb16N"])
            S.op("dve", lambda e: e.tensor_reduce(out=scg[:], in_=ec[:].rearrange("p (g r) n -> p g n r", g=4), axis=AX.X, op=ALU.add),
                 reads=["ecN"], writes=["scgN"])
            S.op("dve", lambda e: e.tensor_scalar(out=adj[:], in0=Dblk_i[:], scalar1=float(128 * i), scalar2=None, op0=ALU.add), writes=["adjN"])
            S.op("dve", lambda e: e.tensor_scalar(out=vld[:], in0=adj[:], scalar1=0.0, scalar2=None, op0=ALU.is_ge), reads=["adjN"], writes=["vldN"])
            S.op("dve", lambda e: e.tensor_scalar(out=adj2[:], in0=adj[:], scalar1=128.0, scalar2=None, op0=ALU.is_lt), reads=["adjN"], writes=["adj2N"])
            S.op("dve", lambda e: e.tensor_tensor(out=adj2[:], in0=adj2[:], in1=vld[:], op=ALU.mult), reads=["adj2N", "vldN"], writes=["adj2N"])
            S.op("dve", lambda e: e.tensor_scalar(out=adj2[:, 0:1], in0=adj2[:, 0:1], scalar1=1.0, scalar2=None, op0=ALU.max), reads=["adj2N"], writes=["adj2N"])
            S.op("dve", lambda e: e.tensor_scalar(out=adj[:], in0=vld[:], scalar1=-1.0, scalar2=2.0e4, op0=ALU.add, op1=ALU.mult), reads=["vldN"], writes=["adjN"])
            S.op("dve", lambda e: e.scalar_tensor_tensor(out=adj[:], in0=adj2[:], scalar=1.0e4, in1=adj[:], op0=ALU.mult, op1=ALU.add),
                 reads=["adj2N", "adjN"], writes=["adjN"])
            S.op("dve", lambda e: e.tensor_tensor(out=scg[:], in0=scg[:], in1=adj[:].unsqueeze(1).to_broadcast([128, 4, NBP]), op=ALU.add),
                 reads=["scgN", "adjN"], writes=["scgN"])
            if NBP > 16:
                for g in range(4):
                    S.op("dve", lambda e, g=g: e.max(out=m8[:, g, 0:8], in_=scg[:, g, :]), reads=["scgN"], writes=[("m8N", g)])
                    S.op("dve", lambda e, g=g: e.match_replace(out=scw[:, g, :], in_to_replace=m8[:, g, 0:8], in_values=scg[:, g, :], imm_value=-1.0e9),
                         reads=["scgN", ("m8N", g)], writes=[("scwN", g)])
                    S.op("dve", lambda e, g=g: e.max(out=m8[:, g, 8:16], in_=scw[:, g, :]), reads=[("scwN", g)], writes=[("m8bN", g)])
                    S.op("dve", lambda e, g=g: e.tensor_scalar(out=scw[:, g, :], in0=scg[:, g, :], scalar1=m8[:, g, 15:16], scalar2=None, op0=ALU.is_ge),
                         reads=["scgN", ("m8bN", g), ("scwN", g)], writes=[("scwN", g)])
                S.op("dve", lambda e: e.tensor_tensor(out=scw[:], in0=scw[:], in1=vld[:].unsqueeze(1).to_broadcast([128, 4, NBP]), op=ALU.mult),
                     reads=[("scwN", g) for g in range(4)] + ["vldN"], writes=["scwA"])
            else:
                S.op("dve", lambda e: e.tensor_copy(out=scw[:], in_=vld[:].unsqueeze(1).to_broadcast([128, 4, NBP])), reads=["vldN"], writes=["scwA"])
            S.op("dve", lambda e: e.tensor_scalar(out=selp[:], in0=scw[:], scalar1=-1.0, scalar2=-BIGN, op0=ALU.add, op1=ALU.mult),
                 reads=["scwA"], writes=["selpN"])
            for g in range(4):
                S.op("pe", lambda e, g=g: e.transpose(out=PB[0][0:NBP, g * 128:(g + 1) * 128], in_=selp[:, g, :], identity=ident_b[:]),
                     reads=["selpN"], writes=[("pb", 0)])
            S.op("act", lambda e: e.copy(out=penT[0:NBP], in_=PB[0][0:NBP, 0:512].rearrange("p (g t) -> p g t", g=4)), reads=[("pb", 0)], writes=["penTN"])
            for half in range(2):
                for a in range(8):
                    S.op("pe", lambda e, a=a, half=half: e.transpose(out=PB[1][0:NBP, a * 128:(a + 1) * 128], in_=pb16[:, half * 8 + a, :], identity=ident_b[:]),
                         reads=["pb16N"], writes=[("pb", 1)])
                S.op("dve", lambda e, half=half: e.tensor_copy(out=pT[0:NBP, half * 8:(half + 1) * 8, :],
                                                              in_=PB[1][0:NBP, 0:1024].rearrange("p (a t) -> p a t", a=8)),
                     reads=[("pb", 1)], writes=[("pTN", half)])
            for half in range(2):
                for a in range(8):
                    h = half * 8 + a
                    S.op("pe", lambda e, a=a, h=h, half=half: e.matmul(PS[half][:, a * 64:(a + 1) * 64], lhsT=pT[0:NBP, h, :], rhs=vc_p[0:NBP, h // 4, :],
                                                                       start=True, stop=True), reads=[("pTN", half), "vc"], writes=[("ps", half)])
                S.op("dve", lambda e, half=half: e.tensor_tensor(out=oacc[:, half * 8:(half + 1) * 8, :],
                                                                in0=PS[half][:, 0:512].rearrange("p (a d) -> p a d", a=8),
                                                                in1=gts[:, i, half * 8:(half + 1) * 8].unsqueeze(2).to_broadcast([128, 8, 64]), op=ALU.mult),
                     reads=[("ps", half), ("gtsN", i), "oaccN"], writes=["oaccN"])
            NSA_BR = int(os.environ.get("NSA_BR", "7"))
            if not (NSA_BR & 1):
                S.op("dve", lambda e: e.memset(oacc[:], 0.0), reads=["oaccN"], writes=["oaccN"])
            for g in range(4):
                if NSA_BR & 2:
                    attend(i, g, 0, 0, list(range(0, i + 1)), True, 1)
                if NSA_BR & 4:
                    attend(i, g, 4, 4, list(range(max(0, i - 4), i + 1)), False, 2)
            S.store("sp", omix[i * 128:(i + 1) * 128, 0:1024], oacc[:].rearrange("p h d -> p (h d)"), reads=["oaccN"])
        S.barrier()
        P.release()

        P = Pool(nc)
        bS = P.t("bS3", [4, NBS], F32)
        eS = P.t("eS3", [4, NBS], F32)
        sm3 = P.t("sm3", [4, 8], F32)
        srow = P.t("srow3", [1, 4, NBS], F32)
        srw = P.t("srw3", [1, 4, NBS], F32)
        m83 = P.t("m83", [1, 4, 16], F32)
        penr = P.t("penr3", [1, 4, NBS], BF16)
        penE = P.t("penE3", [128, NPG, 4], F32)
        pTs = P.t("pTs3", [SEGB, NSEG, 4], BF16)
        ocmp = P.t("ocmp3", [4, 4, 64], F32)
        osw = P.t("osw3", [4, 2, 4, 64], F32)
        gS = P.t("gS3", [4, 12], F32)
        wr = P.t("wr3", [4, 16], F32)
        pg = [P.t(f"pg3{i}", [128, 512], F32) for i in range(3)]
        kTs = [P.t(f"kTs3{i}", [64, 4, 128], BF16) for i in range(2)]
        Vs = [P.t(f"Vs3{i}", [128, 4, 65], BF16) for i in range(2)]
        b16 = [P.t(f"b163{i}", [128, 16], F32) for i in range(2)]
        sc16 = [P.t(f"sc163{i}", [128, 16], F32) for i in range(2)]
        PTs = [P.t(f"PTs3{i}", [128, 16], BF16) for i in range(2)]
        for i in range(2):
            S.op("pool", lambda e, i=i: e.memset(Vs[i][:], 1.0), writes=[("Vs3", i)])
        for bg in range(12):
            S.op("pe", lambda e, bg=bg: e.matmul(PS[5][0:4, bg:bg + 1], lhsT=g_s[0:1, (bg // 4) * 16 + (bg % 4) * 4:(bg // 4) * 16 + (bg % 4) * 4 + 4],
                                                 rhs=ones_f[0:1, 0:1], start=True, stop=True), reads=["g_s"], writes=[("ps", 5)])
        S.op("act", lambda e: e.copy(out=gS[:], in_=PS[5][0:4, 0:12]), reads=[("ps", 5)], writes=["gS3"])
        for g in range(4):
            S.op("pe", lambda e, g=g: e.matmul(PS[0][0:4, 0:NBS], lhsT=q_s[:, 4 * g:4 * g + 4], rhs=kcT_s[:, g, :], start=True, stop=True),
                 reads=["q_s", "kcT"], writes=[("ps", 0)])
            S.op("dve", lambda e, g=g: e.tensor_scalar(out=bS[:], in0=dcs[:], scalar1=slg[:, g:g + 1], scalar2=None, op0=ALU.mult), writes=["bS3"])
            S.op("dve", lambda e: e.tensor_tensor(out=eS[:], in0=PS[0][0:4, 0:NBS], in1=bS[:], op=ALU.add), reads=[("ps", 0), "bS3"], writes=["eS3"])
            S.op("act", lambda e: e.activation(out=eS[:], in_=eS[:], func=AF.Exp), reads=["eS3"], writes=["eS3"])
            S.op("dve", lambda e: e.tensor_reduce(out=sm3[:, 0:1], in_=eS[:], axis=AX.X, op=ALU.add), reads=["eS3"], writes=["sm30"])
            S.op("dve", lambda e: e.tensor_scalar(out=sm3[:, 1:2], in0=sm3[:, 0:1], scalar1=1e-30, scalar2=None, op0=ALU.max), reads=["sm30"], writes=["sm31"])
            S.op("dve", lambda e: e.reciprocal(out=sm3[:, 2:3], in_=sm3[:, 1:2]), reads=["sm31"], writes=["sm32"])
            S.op("dve", lambda e: e.tensor_scalar(out=eS[:], in0=eS[:], scalar1=sm3[:, 2:3], scalar2=None, op0=ALU.mult), reads=["eS3", "sm32"], writes=["eS3"])
            S.op("pe", lambda e: e.matmul(PS[1][0:1, 0:NBS], lhsT=ones_f[0:4, 0:1], rhs=eS[:], start=True, stop=True), reads=["eS3"], writes=[("ps", 1)])
            S.op("act", lambda e, g=g: e.copy(out=srow[0:1, g, :], in_=PS[1][0:1, 0:NBS]), reads=[("ps", 1)], writes=[("srow3", g)])
            for sg in range(NSEG):
                S.op("pe", lambda e, sg=sg: e.transpose(out=PS[2][0:SEGB, sg * 4:(sg + 1) * 4], in_=eS[0:4, sg * SEGB:(sg + 1) * SEGB],
                                                        identity=ident_f[0:4, 0:4]), reads=["eS3"], writes=[("ps", 2)])
            S.op("dve", lambda e: e.tensor_copy(out=pTs[:].rearrange("p s r -> p (s r)"), in_=PS[2][0:SEGB, 0:NSEG * 4]), reads=[("ps", 2)], writes=["pTs3"])
            for sg in range(NSEG):
                S.op("pe", lambda e, sg=sg, g=g: e.matmul(PS[3][0:4, g * 64:(g + 1) * 64], lhsT=pTs[:, sg, :], rhs=vc_s[0:SEGB, sg, g, :],
                                                         start=(sg == 0), stop=(sg == NSEG - 1)), reads=["pTs3", "vc"], writes=[("ps", 3)])
        S.op("act", lambda e: e.copy(out=ocmp[:].rearrange("p g d -> p (g d)"), in_=PS[3][0:4, 0:256]), reads=[("ps", 3)], writes=["ocmp3"])
        sr = [("srow3", g) for g in range(4)]
        S.op("dve", lambda e: e.memset(srow[0:1, :, 0:1], 1.0e4), reads=sr, writes=["srowA"])
        S.op("dve", lambda e: e.memset(srow[0:1, :, NBS - 1:NBS], 1.0e4), reads=sr, writes=["srowB"])
        srk = sr + ["srowA", "srowB"]
        for g in range(4):
            S.op("dve", lambda e, g=g: e.max(out=m83[0:1, g, 0:8], in_=srow[0:1, g, :]), reads=srk, writes=[("m83", g)])
            S.op("dve", lambda e, g=g: e.match_replace(out=srw[0:1, g, :], in_to_replace=m83[0:1, g, 0:8], in_values=srow[0:1, g, :], imm_value=-1.0e9),
                 reads=srk + [("m83", g)], writes=[("srw3", g)])
            S.op("dve", lambda e, g=g: e.max(out=m83[0:1, g, 8:16], in_=srw[0:1, g, :]), reads=[("srw3", g)], writes=[("m83b", g)])
            S.op("dve", lambda e, g=g: e.tensor_scalar(out=srw[0:1, g, :], in0=srow[0:1, g, :], scalar1=m83[0:1, g, 14:15], scalar2=None, op0=ALU.is_ge),
                 reads=srk + [("m83b", g), ("srw3", g)], writes=[("srw3", g)])
        S.op("dve", lambda e: e.tensor_scalar(out=penr[:], in0=srw[:], scalar1=-1.0, scalar2=-BIGN, op0=ALU.add, op1=ALU.mult),
             reads=[("srw3", g) for g in range(4)], writes=["penr3"])
        for g in range(4):
            for k in range(2):
                S.op("pe", lambda e, g=g, k=k: e.matmul(PS[4][:, g * NPG:(g + 1) * NPG], lhsT=half_sel[0:1, k, :],
                                                       rhs=penr[0:1, g, :].rearrange("p (j k) -> p j k", k=2)[:, :, k],
                                                       start=(k == 0), stop=(k == 1)), reads=["penr3"], writes=[("ps", 4)])
        S.op("dve", lambda e: e.tensor_copy(out=penE[:], in_=PS[4][:, 0:4 * NPG].rearrange("p (g j) -> p j g", g=4)), reads=[("ps", 4)], writes=["penE3"])

        def key_pass(br, ntile, load_fn, bias_tab, use_pen, knew, vnew):
            acc = PS[5]
            S.op("dve", lambda e: e.memset(acc[0:4, 0:260], 0.0), writes=[("ps", 5)])
            for j in range(ntile):
                b2, b3 = j % 2, j % 3
                load_fn(j, pg[b3], ("pg3", b3))
                for g in range(4):
                    S.op("pe", lambda e, g=g, b3=b3: e.transpose(out=PS[b2][0:64, g * 128:(g + 1) * 128], in_=pg[b3][:, g * 64:(g + 1) * 64], identity=ident_f[:]),
                         reads=[("pg3", b3)], writes=[("ps", b2)])
                evac(kTs[b2][:].rearrange("p g t -> p (g t)"), PS[b2][0:64, 0:512], reads=[("ps", b2)], writes=[("kTs3", b2)])
                S.op("act", lambda e, b2=b2, b3=b3: e.copy(out=Vs[b2][:, :, 0:64], in_=pg[b3][:, 256:512].rearrange("p (g d) -> p g d", g=4)),
                     reads=[("pg3", b3), ("Vs3", b2)], writes=[("Vs3v", b2)])
                for g in range(4):
                    S.op("pe", lambda e, g=g, b2=b2: e.matmul(PS[2 + b2][:, g * 4:(g + 1) * 4], lhsT=kTs[b2][:, g, :], rhs=q_s[:, 4 * g:4 * g + 4],
                                                             start=True, stop=True), reads=[("kTs3", b2), "q_s"], writes=[("ps", 2 + b2)])
                S.op("dve", lambda e, b2=b2, j=j: e.tensor_scalar(out=b16[b2][:], in0=negsl16[:], scalar1=bias_tab[:, j:j + 1], scalar2=None, op0=ALU.mult),
                     writes=[("b163", b2)])
                if (not use_pen) and j == 0:
                    S.op("dve", lambda e, b2=b2: e.tensor_scalar(out=b16[b2][:], in0=b16[b2][:], scalar1=mw0[:, 0:1], scalar2=None, op0=ALU.add),
                         reads=[("b163", b2)], writes=[("b163", b2)])
                S.op("dve", lambda e, b2=b2: e.tensor_tensor(out=sc16[b2][:], in0=PS[2 + b2][:, 0:16], in1=b16[b2][:], op=ALU.add),
                     reads=[("ps", 2 + b2), ("b163", b2)], writes=[("sc163", b2)])
                if use_pen:
                    S.op("dve", lambda e, b2=b2, j=j: e.tensor_tensor(out=sc16[b2][:].rearrange("p (g r) -> p g r", g=4),
                                                                       in0=sc16[b2][:].rearrange("p (g r) -> p g r", g=4),
                                                                       in1=penE[:, j, :].unsqueeze(2).to_broadcast([128, 4, 4]), op=ALU.add),
                         reads=[("sc163", b2), "penE3"], writes=[("sc163", b2)])
                S.op("act", lambda e, b2=b2: e.activation(out=PTs[b2][:], in_=sc16[b2][:], func=AF.Exp), reads=[("sc163", b2)], writes=[("PTs3", b2)])
                for g in range(4):
                    S.op("pe", lambda e, g=g, b2=b2: e.matmul(acc[0:4, g * 65:(g + 1) * 65], lhsT=PTs[b2][:, 4 * g:4 * g + 4], rhs=Vs[b2][:, g, :],
                                                             start=False, stop=False, skip_group_check=True),
                         reads=[("PTs3", b2), ("Vs3v", b2), ("Vs3", b2)], writes=[("ps", 5)])
            for g in range(4):
                S.op("pe", lambda e, g=g: e.matmul(PS[2][0:1, g * 4:(g + 1) * 4], lhsT=k_s[:, knew + g:knew + g + 1], rhs=q_s[:, 4 * g:4 * g + 4],
                                                   start=True, stop=True), reads=["k_s", "q_s"], writes=[("ps", 2)])
            S.op("act", lambda e: e.activation(out=PTs[0][0:1, :], in_=PS[2][0:1, 0:16], func=AF.Exp), reads=[("ps", 2)], writes=[("PTs3", 0)])
            for g in range(4):
                S.op("pe", lambda e, g=g: e.matmul(acc[0:4, g * 65:(g + 1) * 65], lhsT=PTs[0][0:1, 4 * g:4 * g + 4], rhs=v_s[0:1, vnew + g, :],
                                                   start=False, stop=True, skip_group_check=True), reads=[("PTs3", 0), "v_s"], writes=[("ps", 5)])
            a3 = acc[0:4, 0:260].rearrange("p (g c) -> p g c", g=4)
            S.op("dve", lambda e: e.reciprocal(out=wr[:, 0:4], in_=a3[:, :, 64]), reads=[("ps", 5)], writes=["wr0"])
            S.op("dve", lambda e: e.tensor_tensor(out=wr[:, 4:8], in0=wr[:, 0:4], in1=gS[:, (1 + br) * 4:(2 + br) * 4], op=ALU.mult),
                 reads=["wr0", "gS3"], writes=["wr1"])
            S.op("dve", lambda e: e.tensor_tensor(out=osw[:, br], in0=a3[:, :, 0:64], in1=wr[:, 4:8].unsqueeze(2).to_broadcast([4, 4, 64]), op=ALU.mult),
                 reads=[("ps", 5), "wr1"], writes=[("osw3", br)])

        def load_sel(j, dst, key):
            S.dma("pool", dst[:], cache_sel, reads=[("idx_i", l)], writes=[key],
                  indirect=bass.IndirectOffsetOnAxis(ap=idx_i[:, l, j:j + 1], axis=0))

        def load_win(j, dst, key):
            S.store("sp", dst[:], cache_win[l, j * 128:(j + 1) * 128, :], writes=[key])

        key_pass(0, NPG, load_sel, tb, True, 0, 0)
        key_pass(1, 4, load_win, tbw, False, 4, 4)
        if dbg and l == 0:
            S.dma("sp", dbg_sel, srw[:].rearrange("p g n -> p (g n)"), reads=[("srw3", g) for g in range(4)])
            S.dma("sp", dbg_srow, srow[:].rearrange("p g n -> p (g n)"), reads=srk)
            S.dma("sp", dbg_o[:, 0:256], ocmp[:].rearrange("p g d -> p (g d)"), reads=["ocmp3"])
            S.dma("sp", dbg_o[:, 256:768], osw[:].rearrange("p b g d -> p (b g d)"), reads=[("osw3", 0), ("osw3", 1)])
            S.dma("sp", dbg_gs, gS[:], reads=["gS3"])
        S.op("dve", lambda e: e.tensor_tensor(out=ocmp[:], in0=ocmp[:], in1=gS[:, 0:4].unsqueeze(2).to_broadcast([4, 4, 64]), op=ALU.mult),
             reads=["ocmp3", "gS3"], writes=["ocmp3"])
        NSA_BR3 = int(os.environ.get("NSA_BR", "7"))
        if not (NSA_BR3 & 1):
            S.op("dve", lambda e: e.memset(ocmp[:], 0.0), reads=["ocmp3"], writes=["ocmp3"])
        if NSA_BR3 & 2:
            S.op("dve", lambda e: e.tensor_tensor(out=ocmp[:], in0=ocmp[:], in1=osw[:, 0], op=ALU.add), reads=["ocmp3", ("osw3", 0)], writes=["ocmp3"])
        if NSA_BR3 & 4:
            S.op("dve", lambda e: e.tensor_tensor(out=ocmp[:], in0=ocmp[:], in1=osw[:, 1], op=ALU.add), reads=["ocmp3", ("osw3", 1)], writes=["ocmp3"])
        S.store("sp", omix[T:T + 1, 0:1024].rearrange("o (g r d) -> (o r) g d", g=4, r=4), ocmp[:], reads=["ocmp3"])
        S.barrier()
        P.release()
        LP.release()

    def mixers(l):
        for tt in range(NTA):
            S.store("sp", omix[tt * 128:(tt + 1) * 128, :], zeros_f[:, 0:2048])
        S.barrier()
        gla_mixer(l)
        rw_mixer(l)
        nsa_mixer(l)
        if dbg and l == 0:
            S.dma("sp", omix_l0, omix)
            S.barrier()

    for l in range(L):
        P = Pool(nc)
        hT = P.t("hT", [128, KD, TA], BF16)
        rmsnorm_T(P, xa, bcast_row(norm1_g[l], D), hT, f"n1")
        wt = [P.t(f"wA{i}", [128, KD, 512], BF16) for i in range(2)]
        stg = [P.t(f"stgA{i}", [128, 512], F32) for i in range(3)]
        w_in_v = w_in[l].rearrange("(ko ki) n -> ki ko n", ki=128)
        ncb = (NIN + 511) // 512
        it = 0
        for cb in range(ncb):
            c0 = cb * 512
            cw = min(512, NIN - c0)
            wb = wt[cb % 2]
            S.dma("pool", wb[:, :, 0:cw], w_in_v[:, :, c0:c0 + cw], writes=[("wA", cb % 2)])
            for tt in range(NTA):
                ps = PS[it % 4]
                for k in range(KD):
                    S.op("pe", lambda e, ps=ps, k=k, tt=tt, wb=wb, cw=cw: e.matmul(
                        ps[:, 0:cw], lhsT=hT[:, k, tt * 128:(tt + 1) * 128], rhs=wb[:, k, 0:cw],
                        start=(k == 0), stop=(k == KD - 1)),
                        reads=[("n1", "hT", tt, k // 4), ("wA", cb % 2)], writes=[("ps", it % 4)])
                sg = stg[it % 3]
                evac(sg[:, 0:cw], ps[:, 0:cw], reads=[("ps", it % 4)], writes=[("stgA", it % 3)])
                S.store("sp", proj[tt * 128:(tt + 1) * 128, c0:c0 + cw], sg[:, 0:cw], reads=[("stgA", it % 3)])
                it += 1
        S.barrier()
        P.release()

        P = Pool(nc)
        NB = C_MG
        pt = [P.t(f"ptB{i}", [128, NB], F32) for i in range(2)]
        qkg = P.t("qkg", [128, 4, 64], F32)
        S.dma("sp", qkg[:].rearrange("p a d -> p (a d)"), nsa_qk_g[l].rearrange("a d -> (a d)").partition_broadcast(128),
              writes=["qkg"])
        tmpB = P.t("tmpB", [128, 16, 64], F32)
        ssB = P.t("ssB", [128, 64], F32)
        rows = [P.t(f"rowsB{i}", [128, 2, 512], F32) for i in range(2)]

        def headnorm(out3, in3, g_ap, H, rk, wk, sc=1.0):
            S.op("dve", lambda e: e.tensor_tensor(out=tmpB[:, 0:H, :], in0=in3, in1=in3, op=ALU.mult),
                 reads=rk, writes=["tmpB"])
            S.op("dve", lambda e: e.tensor_reduce(out=ssB[:, 0:H], in_=tmpB[:, 0:H, :], axis=AX.X, op=ALU.add),
                 reads=["tmpB"], writes=["ssB0"])
            S.op("dve", lambda e: e.tensor_scalar(out=ssB[:, 16:16 + H], in0=ssB[:, 0:H], scalar1=1.0 / 64, scalar2=EPS,
                                                  op0=ALU.mult, op1=ALU.add), reads=["ssB0"], writes=["ssB1"])
            S.op("act", lambda e: e.activation(out=ssB[:, 48:48 + H], in_=ssB[:, 16:16 + H], func=AF.Ln),
                 reads=["ssB1"], writes=["ssB1b"])
            S.op("act", lambda e: e.activation(out=ssB[:, 32:32 + H], in_=ssB[:, 48:48 + H], func=AF.Exp, scale=-0.5,
                                               bias=float(np.log(sc))), reads=["ssB1b"], writes=["ssB2"])
            S.op("dve", lambda e: e.tensor_tensor(out=tmpB[:, 0:H, :], in0=in3,
                                                  in1=ssB[:, 32:32 + H].unsqueeze(2).to_broadcast([128, H, 64]), op=ALU.mult),
                 reads=rk + ["ssB2"], writes=["tmpB"])
            S.op("dve", lambda e: e.tensor_tensor(out=out3, in0=tmpB[:, 0:H, :],
                                                  in1=g_ap.unsqueeze(1).to_broadcast([128, H, 64]), op=ALU.mult),
                 reads=["tmpB", "qkg"], writes=wk)

        for tt in range(NTA):
            b = tt % 2
            ptb = pt[b]
            rw_ = rows[b]
            S.dma("sp", ptb[:], proj[tt * 128:(tt + 1) * 128, 0:NB], writes=[("ptB", b)])
            is_s = (tt == NT)
            if not is_s:
                S.store("sp", p_cmp_o[l, tt * 128:(tt + 1) * 128, :], ptb[:, C_CMP:C_CMP + 512], reads=[("ptB", b)])
            else:
                S.store("sp", s_cmp_o[l], ptb[0:1, C_CMP:C_CMP + 512], reads=[("ptB", b)])
            for wi, (c0, gi, po, so) in enumerate(((C_SEL, 2, p_sel_o, s_sel_o), (C_WIN, 3, p_win_o, s_win_o))):
                headnorm(rw_[:, wi, 0:256].rearrange("p (h d) -> p h d", h=4),
                         ptb[:, c0:c0 + 256].rearrange("p (h d) -> p h d", h=4), qkg[:, gi, :], 4,
                         [("ptB", b)], [("rowsB", b, wi, "k")])
                S.op("act", lambda e, rw_=rw_, ptb=ptb, wi=wi, c0=c0: e.copy(out=rw_[:, wi, 256:512], in_=ptb[:, c0 + 256:c0 + 512]),
                     reads=[("ptB", b)], writes=[("rowsB", b, wi, "v")])
                rk = [("rowsB", b, wi, "k"), ("rowsB", b, wi, "v")]
                if is_s:
                    S.store("sp", so[l], rw_[0:1, wi, :], reads=rk)
                elif wi == 0:
                    S.store("sp", po[l, tt * 128:(tt + 1) * 128, :], rw_[:, wi, :], reads=rk)
                else:
                    t0w = tt * 128 - (T - WKEEP)
                    if t0w >= 0:
                        S.store("sp", po[l, t0w:t0w + 128, :], rw_[:, wi, :], reads=rk)
            if tt == NT - 1:
                S.store("sp", p_sh_o[l:l + 1, :], ptb[127:128, C_RW:C_RW + RWP], reads=[("ptB", b)])
            if is_s:
                S.store("sp", s_sh_o[l:l + 1, :], ptb[0:1, C_RW:C_RW + RWP], reads=[("ptB", b)])
        S.barrier()
        P.release()

        mixers(l)

        P = Pool(nc)
        oT = P.t("oT", [128, 16, TA], BF16)
        rmsnorm_T(P, omix, None, oT, "oT", NC=2048)
        wt = [P.t(f"wC{i}", [128, 16, 512], BF16) for i in range(2)]
        mg = [P.t(f"mgC{i}", [128, 3, 512], F32) for i in range(2)]
        acc = [P.t(f"accC{i}", [128, 512], F32) for i in range(2)]
        tmpc = P.t("tmpC", [128, 512], F32)
        ups = ((nsa_up, 0, 8), (gla_up, 8, 4), (rw_up, 12, 4))
        it = 0
        for nb in range(D // 512 if D >= 512 else 1):
            cw = min(512, D)
            c0 = nb * 512
            wb = wt[nb % 2]
            for (wsrc, k0, nk) in ups:
                S.dma("pool", wb[:, k0:k0 + nk, 0:cw], wsrc[l].rearrange("(ko ki) n -> ki ko n", ki=128)[:, :, c0:c0 + cw],
                      writes=[("wC", nb % 2, k0)])
            for tt in range(NTA):
                b = it % 2
                for j in range(3):
                    S.dma("sp", mg[b][:, j, 0:cw], proj[tt * 128:(tt + 1) * 128, C_MG + j * D + c0:C_MG + j * D + c0 + cw],
                          writes=[("mgC", b, j)])
                    S.op("act", lambda e, b=b, j=j: e.activation(out=mg[b][:, j, 0:cw], in_=mg[b][:, j, 0:cw], func=AF.Sigmoid),
                         reads=[("mgC", b, j)], writes=[("mgC", b, j)])
                for j, (wsrc, k0, nk) in enumerate(ups):
                    ps = PS[j + 3 * (it % 2)]
                    pk = ("ps", j + 3 * (it % 2))
                    for k in range(k0, k0 + nk):
                        S.op("pe", lambda e, ps=ps, k=k, tt=tt, wb=wb, k0=k0, nk=nk: e.matmul(
                            ps[:, 0:cw], lhsT=oT[:, k, tt * 128:(tt + 1) * 128], rhs=wb[:, k, 0:cw],
                            start=(k == k0), stop=(k == k0 + nk - 1)),
                            reads=[("oT", "hT", tt, k // 4), ("wC", nb % 2, k0)], writes=[pk])
                    if j == 0:
                        S.op("dve", lambda e, b=b, ps=ps: e.tensor_tensor(out=acc[b][:, 0:cw], in0=mg[b][:, 0, 0:cw], in1=ps[:, 0:cw],
                                                                           op=ALU.mult), reads=[("mgC", b, 0), pk], writes=[("accC", b)])
                    else:
                        S.op("dve", lambda e, b=b, ps=ps, j=j: e.tensor_tensor(out=tmpc[:, 0:cw], in0=mg[b][:, j, 0:cw], in1=ps[:, 0:cw],
                                                                                op=ALU.mult), reads=[("mgC", b, j), pk], writes=["tmpC"])
                        S.op("dve", lambda e, b=b: e.tensor_tensor(out=acc[b][:, 0:cw], in0=acc[b][:, 0:cw], in1=tmpc[:, 0:cw], op=ALU.add),
                             reads=["tmpC", ("accC", b)], writes=[("accC", b)])
                S.store("sp", mrg[tt * 128:(tt + 1) * 128, c0:c0 + cw], acc[b][:, 0:cw], reads=[("accC", b)])
                it += 1
        S.barrier()
        P.release()

        P = Pool(nc)
        mT = P.t("mT", [128, KD, TA], BF16)
        rmsnorm_T(P, mrg, None, mT, "mT", NC=D)
        wt = [P.t(f"wD{i}", [128, KD, 512], BF16) for i in range(2)]
        xr = [P.t(f"xD{i}", [128, 512], F32) for i in range(3)]
        w_v = w_out[l].rearrange("(ko ki) n -> ki ko n", ki=128)
        it = 0
        for nb in range(D // 512 if D >= 512 else 1):
            cw = min(512, D)
            c0 = nb * 512
            wb = wt[nb % 2]
            S.dma("pool", wb[:, :, 0:cw], w_v[:, :, c0:c0 + cw], writes=[("wD", nb % 2)])
            for tt in range(NTA):
                ps = PS[it % 4]
                xb_ = xr[it % 3]
                S.dma("sp", xb_[:, 0:cw], xa[tt * 128:(tt + 1) * 128, c0:c0 + cw], writes=[("xD", it % 3)])
                for k in range(KD):
                    S.op("pe", lambda e, ps=ps, k=k, tt=tt, wb=wb: e.matmul(
                        ps[:, 0:cw], lhsT=mT[:, k, tt * 128:(tt + 1) * 128], rhs=wb[:, k, 0:cw],
                        start=(k == 0), stop=(k == KD - 1)),
                        reads=[("mT", "hT", tt, k // 4), ("wD", nb % 2)], writes=[("ps", it % 4)])
                S.op("dve", lambda e, ps=ps, xb_=xb_: e.tensor_tensor(out=xb_[:, 0:cw], in0=xb_[:, 0:cw], in1=ps[:, 0:cw], op=ALU.add),
                     reads=[("ps", it % 4), ("xD", it % 3)], writes=[("xD", it % 3)])
                S.store("sp", xm[tt * 128:(tt + 1) * 128, c0:c0 + cw], xb_[:, 0:cw], reads=[("xD", it % 3)])
                it += 1
        S.barrier()
        P.release()

        P = Pool(nc)
        h2T = P.t("h2T", [128, KD, TA], BF16)
        rmsnorm_T(P, xm, bcast_row(norm2_g[l], D), h2T, "n2")
        wt = [P.t(f"wE{i}", [128, KD, 512], BF16) for i in range(2)]
        hst = [P.t(f"hstE{i}", [128, 4, 4, 128], BF16) for i in range(2)]
        rl = [P.t(f"rlE{i}", [128, 512], F32) for i in range(2)]
        w_v = mlp_w1[l].rearrange("(ko ki) n -> ki ko n", ki=128)
        chunks = [(c * 4, min(4, NTA - c * 4)) for c in range((NTA + 3) // 4)]
        it = 0
        ic = 0
        for fb in range(DFF // 512):
            wb = wt[fb % 2]
            S.dma("pool", wb[:], w_v[:, :, fb * 512:(fb + 1) * 512], writes=[("wE", fb % 2)])
            for (t0, ntl) in chunks:
                hs = hst[ic % 2]
                ntok = ntl * 128
                for fs in range(4):
                    ps = PS[it % 4]
                    for k in range(KD):
                        S.op("pe", lambda e, ps=ps, k=k, wb=wb, fs=fs, t0=t0, ntok=ntok: e.matmul(
                            ps[:, 0:ntok], lhsT=wb[:, k, fs * 128:(fs + 1) * 128], rhs=h2T[:, k, t0 * 128:t0 * 128 + ntok],
                            start=(k == 0), stop=(k == KD - 1)),
                            reads=[("n2", "hT", t0 + j, k // 4) for j in range(ntl)] + [("wE", fb % 2)], writes=[("ps", it % 4)])
                    r_ = rl[it % 2]
                    S.op("act", lambda e, ps=ps, r_=r_, ntok=ntok: e.activation(out=r_[:, 0:ntok], in_=ps[:, 0:ntok], func=AF.Relu),
                         reads=[("ps", it % 4)], writes=[("rlE", it % 2)])
                    S.op("dve", lambda e, hs=hs, r_=r_, fs=fs, ntl=ntl, ntok=ntok: e.tensor_tensor(
                        out=hs[:, 0:ntl, fs, :], in0=r_[:, 0:ntok].rearrange("p (a t) -> p a t", a=ntl),
                        in1=r_[:, 0:ntok].rearrange("p (a t) -> p a t", a=ntl), op=ALU.mult),
                        reads=[("rlE", it % 2)], writes=[("hstE", ic % 2, fs)])
                    it += 1
                for j in range(ntl):
                    S.store("sp", hidS[t0 + j, :, fb * 4:(fb + 1) * 4, :], hs[:, j, :, :], reads=[("hstE", ic % 2, fs) for fs in range(4)])
                ic += 1
        S.barrier()
        P.release()

        P = Pool(nc)
        NFO = DFF // 128
        FW = 512 if D >= 512 else 256
        wt = [P.t(f"wF{i}", [128, NFO, FW], BF16) for i in range(2)]
        ht = [P.t(f"htF{i}", [128, NFO, 128], BF16) for i in range(2)]
        xr = [P.t(f"xF{i}", [128, FW], F32) for i in range(3)]
        w_v = mlp_w2[l].rearrange("(fo fi) n -> fi fo n", fi=128)
        it = 0
        for db in range(D // FW):
            c0 = db * FW
            wb = wt[db % 2]
            for hf_ in range(2):
                S.dma("pool", wb[:, hf_ * (NFO // 2):(hf_ + 1) * (NFO // 2), :], w_v[:, hf_ * (NFO // 2):(hf_ + 1) * (NFO // 2), c0:c0 + FW],
                      writes=[("wF", db % 2, hf_)])
            for tt in range(NTA):
                hb = ht[it % 2]
                S.dma("sp", hb[:], hidS[tt], writes=[("htF", it % 2)])
                xb_ = xr[it % 3]
                S.dma("sp", xb_[:], xm[tt * 128:(tt + 1) * 128, c0:c0 + FW], writes=[("xF", it % 3)])
                ps = PS[it % 4]
                for fo in range(NFO):
                    S.op("pe", lambda e, ps=ps, fo=fo, hb=hb, wb=wb: e.matmul(
                        ps[:, 0:FW], lhsT=hb[:, fo, :], rhs=wb[:, fo, :], start=(fo == 0), stop=(fo == NFO - 1)),
                        reads=[("htF", it % 2), ("wF", db % 2, fo // (NFO // 2))], writes=[("ps", it % 4)])
                S.op("dve", lambda e, ps=ps, xb_=xb_: e.tensor_tensor(out=xb_[:], in0=xb_[:], in1=ps[:, 0:FW], op=ALU.add),
                     reads=[("ps", it % 4), ("xF", it % 3)], writes=[("xF", it % 3)])
                S.store("sp", xa[tt * 128:(tt + 1) * 128, c0:c0 + FW], xb_[:], reads=[("xF", it % 3)])
                it += 1
        S.barrier()
        P.release()
        S.store("sp", xa[T + 1:TA, :], zeros_f[0:127, 0:D])
        S.barrier()

    S.barrier()
    S.dma("sp", y_prompt, xa[0:T, :])
    S.dma("sp", y_sample, xa[T:T + 1, :])
    S.barrier()
    S.emit_all()
    global LAST_S
    LAST_S = S
    return nc, declared


FULL_CFG = dict(D=2048, T=2048, PAST=16384, NPOOL=1280, L=2)
WEIGHTS = ["norm1_g", "norm2_g", "w_in", "nsa_qk_g", "cmp_pos", "cmp_w1", "cmp_w2", "nsa_up", "gla_a2", "gla_a_b",
           "gla_norm_g", "gla_up", "rw_mu", "rw_w0", "rw_w2", "rw_a0", "rw_a2", "rw_g2", "rw_kk", "rw_ka", "rw_rk",
           "rw_ln_g", "rw_ln_b", "rw_up", "w_out", "mlp_w1", "mlp_w2"]


def input_names(nc):
    names = []
    for a in nc.allocations:
        pass
    return names


def make_in_maps(inp, cfg, ncores, declared):
    B = inp["x_prompt"].shape[0]
    SB = inp["x_sample"].shape[0]
    L = cfg["L"]
    maps = []
    f = lambda a: np.ascontiguousarray(a)
    shared = {}
    for k in WEIGHTS:
        if k in declared:
            shared[k] = f(inp[k])
    if "cache_cmp" in declared:
        shared["cache_cmp"] = f(inp["cache_cmp_kv"].reshape(L, -1, 512))
    if "cache_sel" in declared:
        shared["cache_sel"] = f(inp["cache_sel_kv"].reshape(L, -1, 512))
    for c in range(ncores):
        m = dict(shared)
        sc = c % SB
        m["x_prompt"] = f(inp["x_prompt"][c % B])
        m["x_sample"] = f(inp["x_sample"][sc])
        if "cache_win" in declared:
            m["cache_win"] = f(inp["cache_win_kv"][:, sc].reshape(L, -1, 512))
        if "state_gla" in declared:
            m["state_gla"] = f(inp["state_gla"][:, sc])
        if "state_rwkv" in declared:
            m["state_rwkv"] = f(inp["state_rwkv"][:, sc])
        if "state_shift" in declared:
            m["state_shift"] = f(inp["state_rwkv_shift"][:, sc])
        if "page_table" in declared:
            m["page_table"] = f(inp["page_table"][sc].astype(np.int32))
        maps.append({k: v for k, v in m.items() if k in declared})
    return maps


def assemble(res, cfg, B, SB):
    L, T, D = cfg["L"], cfg["T"], cfg["D"]
    WK = min(512, T)
    g = lambda c, k, shp: np.asarray(res[c][k], dtype=np.float32).reshape(shp) if k in res[c] else np.zeros(shp, np.float32)
    y_p = np.stack([g(b, "y_prompt", (T, D)) for b in range(B)])
    y_s = np.stack([g(c, "y_sample", (1, D)) for c in range(SB)])
    pk = lambda k, n: np.stack([g(b, k, (L, n, 2, 4, 64)) for b in range(B)], axis=1)
    sk = lambda k: np.stack([g(c, k, (L, 1, 2, 4, 64)) for c in range(SB)], axis=1)
    p_gla = np.stack([g(b, "p_gla", (L, 4, 64, 128)) for b in range(B)], axis=1)
    p_rw = np.stack([g(b, "p_rw", (L, 8, 64, 64)) for b in range(B)], axis=1)
    p_sh = np.stack([g(b, "p_sh", (L, RWP)) for b in range(B)], axis=1)
    s_gla = np.stack([g(c, "s_gla", (L, 4, 64, 128)) for c in range(SB)], axis=1)
    s_rw = np.stack([g(c, "s_rw", (L, 8, 64, 64)) for c in range(SB)], axis=1)
    s_sh = np.stack([g(c, "s_sh", (L, RWP)) for c in range(SB)], axis=1)
    return (y_p, y_s, pk("p_cmp", T), pk("p_sel", T), pk("p_win", WK), p_gla, p_rw, p_sh,
            sk("s_cmp"), sk("s_sel"), sk("s_win"), s_gla, s_rw, s_sh)


_CACHE = {}


def kernel(**inputs):
    cfg = FULL_CFG
    if "nc" not in _CACHE:
        _CACHE["nc"] = build(cfg)
    nc, declared = _CACHE["nc"]
    inp = {k: np.asarray(v) for k, v in inputs.items()}
    maps = make_in_maps(inp, cfg, 8, declared)
    res = run_bass_kernel_spmd(nc, maps, core_ids=list(range(8)))
    return assemble(res.results, cfg, inp["x_prompt"].shape[0], inp["x_sample"].shape[0])
```

```python
import os
import numpy as np
import concourse.bass as bass
import concourse.mybir as mybir
from concourse.bass_utils import run_bass_kernel_spmd

F32 = mybir.dt.float32
BF16 = mybir.dt.bfloat16
I32 = mybir.dt.int32
AF = mybir.ActivationFunctionType
ALU = mybir.AluOpType
AX = mybir.AxisListType

NQ = 1024
C_Q, C_CMP, C_SEL, C_WIN, C_GATE = 0, 1024, 1536, 2048, 2560
C_GQ, C_GK, C_GV, C_GA, C_GG, C_RW, C_MG = 2608, 2864, 3120, 3632, 3648, 4160, 5856
RWP = 1696
EPS = 1e-6


import types


def _freeze(fn):
    if fn.__closure__ is None:
        return fn
    cells = []
    for c in fn.__closure__:
        try:
            cells.append(types.CellType(c.cell_contents))
        except ValueError:
            cells.append(c)
    return types.FunctionType(fn.__code__, fn.__globals__, fn.__name__, fn.__defaults__, tuple(cells))


class Sched:
    def __init__(self, nc, n_dma_sems=8):
        self.nc = nc
        self.eng = {"pe": nc.tensor, "act": nc.scalar, "dve": nc.vector, "pool": nc.gpsimd, "sp": nc.sync}
        self.prog = {e: [] for e in self.eng}
        self.sem = {e: nc.alloc_semaphore("c_" + e) for e in self.eng}
        self.cnt = {e: 0 for e in self.eng}
        self.dsem = {e: [nc.alloc_semaphore(f"d_{e}{i}") for i in range(n_dma_sems)] for e in ("sp", "pool", "act")}
        self.dcnt = {}
        self.semobj = {}
        for e in self.dsem:
            for s in self.dsem[e]:
                self.dcnt[id(s)] = 0
                self.semobj[id(s)] = s
        for e in self.sem:
            self.semobj[id(self.sem[e])] = self.sem[e]
        self.drr = {e: 0 for e in self.dsem}
        self.seen = {e: {} for e in self.eng}
        self.bufs = {}
        self.n_wait = 0
        self.pending = []
        self.max_pending = 2
        self.cap = None

    def _deps(self, e, reads, writes):
        waits = {}

        def need(tok, pe_ok=False):
            if tok is None:
                return
            sem, val, src = tok
            if e == "pe" and src == "pe" and pe_ok:
                return
            k = id(sem)
            if self.seen[e].get(k, 0) >= val:
                return
            if waits.get(k, 0) < val:
                waits[k] = val

        for b in reads:
            st = self.bufs.get(b)
            if st:
                need(st["w"])
        for b in writes:
            st = self.bufs.get(b)
            if st:
                need(st["w"], True)
                for r in st["r"]:
                    need(r, True)
        for k, v in waits.items():
            self.seen[e][k] = v
        self.n_wait += len(waits)
        return [(self.semobj[k], v) for k, v in waits.items()]

    def _commit(self, tok, reads, writes):
        for b in reads:
            st = self.bufs.setdefault(b, {"w": None, "r": []})
            st["r"].append(tok)
            if len(st["r"]) > 40:
                best = {}
                for t in st["r"]:
                    k = id(t[0])
                    if k not in best or best[k][1] < t[1]:
                        best[k] = t
                st["r"] = list(best.values())
        for b in writes:
            self.bufs[b] = {"w": tok, "r": []}

    def _flush_one(self):
        q, out, in_, reads, writes, indirect, kw = self.pending.pop(0)
        self.dma(q, out, in_, reads=reads, writes=writes, indirect=indirect, _nocheck=True, **kw)

    def _flush_conflicts(self, reads, writes):
        if not self.pending:
            return
        ws, rs = set(writes), set(reads)
        idx = -1
        for i, p in enumerate(self.pending):
            pr, pw = set(p[3]), set(p[4])
            if (ws & (pr | pw)) or (rs & pw):
                idx = i
        for _ in range(idx + 1):
            self._flush_one()

    def flush(self):
        while self.pending:
            self._flush_one()

    def store(self, q, out, in_, **kw):
        self.dma(q, out, in_, defer=True, **kw)

    def sp(self):
        if self.cap is not None:
            self.cap.append(("sp",))

    def replay(self, items):
        for it in items:
            if it[0] == "sp":
                self.sp()
            elif it[0] == "op":
                self.op(it[1], it[2], reads=it[3], writes=it[4])
            else:
                _, q, out, in_, r, w, ind, defer, kw = it
                self.dma(q, out, in_, reads=r, writes=w, indirect=ind, defer=defer, **kw)

    def op(self, e, fn, reads=(), writes=()):
        fn = _freeze(fn)
        if self.cap is not None:
            self.cap.append(("op", e, fn, tuple(reads), tuple(writes)))
            return
        self._flush_conflicts(reads, writes)
        waits = self._deps(e, reads, writes)
        self.cnt[e] += 1
        sem = self.sem[e]
        tok = (sem, self.cnt[e], e)

        def emit(eng):
            for s, v in waits:
                eng.wait_ge(s, v)
            fn(eng).then_inc(sem, 1)

        self.prog[e].append(emit)
        self._commit(tok, reads, writes)

    def dma(self, q, out, in_, reads=(), writes=(), indirect=None, defer=False, _nocheck=False, **kw):
        if self.cap is not None:
            self.cap.append(("dma", q, out, in_, tuple(reads), tuple(writes), indirect, defer, kw))
            return
        if defer:
            self.pending.append((q, out, in_, tuple(reads), tuple(writes), indirect, kw))
            while len(self.pending) > self.max_pending:
                self._flush_one()
            return
        if not _nocheck:
            self._flush_conflicts(reads, writes)
        waits = self._deps(q, reads, writes)
        i = self.drr[q]
        self.drr[q] = (i + 1) % len(self.dsem[q])
        s = self.dsem[q][i]
        prev = self.dcnt[id(s)]
        self.dcnt[id(s)] = prev + 16
        tok = (s, prev + 16, "dma")
        if prev > 0 and self.seen[q].get(id(s), 0) < prev:
            waits.append((s, prev))
            self.seen[q][id(s)] = prev

        def emit(eng):
            for s_, v in waits:
                eng.wait_ge(s_, v)
            if indirect is not None:
                eng.indirect_dma_start(out=out, out_offset=None, in_=in_, in_offset=indirect, **kw).then_inc(s, 16)
            else:
                eng.dma_start(out=out, in_=in_, **kw).then_inc(s, 16)

        self.prog[q].append(emit)
        self._commit(tok, reads, writes)

    def barrier(self):
        self.flush()
        allw = []
        for e in self.dsem:
            for s in self.dsem[e]:
                if self.dcnt[id(s)] > 0:
                    allw.append((s, self.dcnt[id(s)]))
        for e in self.eng:
            if self.cnt[e] > 0:
                allw.append((self.sem[e], self.cnt[e]))
        for e in self.eng:
            waits = [(s, v) for (s, v) in allw if s is not self.sem[e] and self.seen[e].get(id(s), 0) < v]
            for s, v in waits:
                self.seen[e][id(s)] = v

            def emit(eng, waits=waits):
                for s_, v in waits:
                    eng.wait_ge(s_, v)

            self.prog[e].append(emit)
        self.bufs = {}

    def emit_all(self):
        self.flush()
        with self.nc.Block() as block:
            for e, deco in (("sp", block.sync), ("act", block.scalar), ("dve", block.vector),
                            ("pool", block.gpsimd), ("pe", block.tensor)):
                prog = self.prog[e]

                def body(eng, prog=prog):
                    for f in prog:
                        f(eng)

                deco(body)


class Pool:
    def __init__(self, nc):
        self.nc = nc
        self.stack = []

    uid = [0]

    def t(self, name, shape, dt):
        Pool.uid[0] += 1
        g = self.nc.sbuf_tensor(f"{name}_u{Pool.uid[0]}", list(shape), dt)
        h = g.__enter__()
        self.stack.append(g)
        assert self.nc.sbuf_bytes_remaining >= 0, f"SBUF overflow allocating {name}: {self.nc.sbuf_bytes_remaining}"
        return h

    def release(self):
        while self.stack:
            self.stack.pop().__exit__(None, None, None)


def build(cfg, dbg=False):
    D, T, PAST, NPOOL, L = cfg["D"], cfg["T"], cfg["PAST"], cfg["NPOOL"], cfg["L"]
    DFF = 4 * D
    NIN = C_MG + 3 * D
    NT = T // 128
    NTA = NT + 1
    TA = NTA * 128
    KD = D // 128
    WKEEP = min(512, T)
    NPG = PAST // 128

    nc = bass.Bass("TRN2", target_bir_lowering=False)
    S = Sched(nc)

    declared = set()

    def din(name, shape, dt=F32):
        declared.add(name)
        return nc.dram_tensor(name, list(shape), dt, kind="ExternalInput").ap()

    def dout(name, shape):
        return nc.dram_tensor(name, list(shape), F32, kind="ExternalOutput").ap()

    def dscr(name, shape, dt=F32):
        return nc.dram_tensor(name, list(shape), dt, kind=("ExternalOutput" if dbg else "Internal")).ap()

    x_prompt = din("x_prompt", [T, D])
    x_sample = din("x_sample", [1, D])
    norm1_g = din("norm1_g", [L, D])
    norm2_g = din("norm2_g", [L, D])
    w_in = din("w_in", [L, D, NIN])
    nsa_qk_g = din("nsa_qk_g", [L, 4, 64])
    nsa_up = din("nsa_up", [L, 1024, D])
    gla_up = din("gla_up", [L, 512, D])
    rw_up = din("rw_up", [L, 512, D])
    w_out = din("w_out", [L, D, D])
    mlp_w1 = din("mlp_w1", [L, D, DFF])
    mlp_w2 = din("mlp_w2", [L, DFF, D])
    cache_cmp = din("cache_cmp", [L * NPOOL * 128, 512])
    cache_sel = din("cache_sel", [L * NPOOL * 128, 512])
    cache_win = din("cache_win", [L, 512, 512])
    page_table = din("page_table", [NPG], I32)
    cmp_pos = din("cmp_pos", [L, 2, 64, 64])
    cmp_w1 = din("cmp_w1", [L, 2, 4096, 128])
    cmp_w2 = din("cmp_w2", [L, 2, 128, 64])
    gla_a2 = din("gla_a2", [L, 16, 256])
    gla_a_b = din("gla_a_b", [L, 256])
    gla_norm_g = din("gla_norm_g", [L, 128])
    state_gla = din("state_gla", [L, 4, 64, 128])
    rw_mu = din("rw_mu", [L, RWP])
    rw_w0 = din("rw_w0", [L, 512])
    rw_w2 = din("rw_w2", [L, 32, 512])
    rw_a0 = din("rw_a0", [L, 512])
    rw_a2 = din("rw_a2", [L, 32, 512])
    rw_g2 = din("rw_g2", [L, 96, 512])
    rw_kk = din("rw_kk", [L, 512])
    rw_ka = din("rw_ka", [L, 512])
    rw_rk = din("rw_rk", [L, 8, 64])
    rw_ln_g = din("rw_ln_g", [L, 512])
    rw_ln_b = din("rw_ln_b", [L, 512])
    state_rwkv = din("state_rwkv", [L, 8, 64, 64])
    state_shift = din("state_shift", [L, RWP])
    y_prompt = dout("y_prompt", [T, D])
    y_sample = dout("y_sample", [1, D])
    p_cmp_o = dout("p_cmp", [L, T, 512])
    p_sel_o = dout("p_sel", [L, T, 512])
    p_win_o = dout("p_win", [L, WKEEP, 512])
    p_sh_o = dout("p_sh", [L, RWP])
    s_cmp_o = dout("s_cmp", [L, 1, 512])
    s_sel_o = dout("s_sel", [L, 1, 512])
    s_win_o = dout("s_win", [L, 1, 512])
    s_sh_o = dout("s_sh", [L, RWP])
    p_gla_o = dout("p_gla", [L, 4, 64, 128])
    p_rw_o = dout("p_rw", [L, 8, 64, 64])
    s_rw_o = dout("s_rw", [L, 8, 64, 64])
    s_gla_o = dout("s_gla", [L, 4, 64, 128])
    omix_l0 = dout("omix_l0", [TA, 2048]) if dbg else None
    dbg_sel = dout("dbg_sel", [1, 4 * (PAST // 64)]) if dbg else None
    dbg_srow = dout("dbg_srow", [1, 4 * (PAST // 64)]) if dbg else None
    dbg_o = dout("dbg_o", [4, 3 * 256]) if dbg else None
    dbg_gs = dout("dbg_gs", [4, 12]) if dbg else None
    xa = dscr("xa", [TA, D])
    xm = dscr("xm", [TA, D])
    proj = dscr("proj", [TA, NIN])
    omix = dscr("omix", [TA, 2048])
    hidS = dscr("hidS", [NTA, 128, DFF // 128, 128], BF16)
    mrg = dscr("mrg", [TA, D])

    ident_b = nc.alloc_sbuf_tensor("ident_b", [128, 128], BF16)
    ident_f = nc.alloc_sbuf_tensor("ident_f", [128, 128], F32)
    zeros_f = nc.alloc_sbuf_tensor("zeros_f", [128, 2048], F32)
    PS = [nc.alloc_psum_tensor(f"ps{i}", [128, 512], F32) for i in range(6)]
    PB = [nc.alloc_psum_tensor(f"pb{i}", [128, 1024], BF16) for i in range(2)]

    S.op("pool", lambda e: e.memset(ident_f[:], 1.0), writes=["ident_f"])
    S.op("pool", lambda e: e.affine_select(out=ident_f[:], in_=ident_f[:], pattern=[[-1, 128]], compare_op=ALU.is_equal,
                                           fill=0.0, base=0, channel_multiplier=1), reads=["ident_f"], writes=["ident_f"])
    S.op("dve", lambda e: e.tensor_copy(out=ident_b[:], in_=ident_f[:]), reads=["ident_f"], writes=["ident_b"])
    S.op("dve", lambda e: e.memset(zeros_f[:], 0.0), writes=["zeros_f"])
    tri_f = nc.alloc_sbuf_tensor("tri_f", [64, 64], F32)
    ones_f = nc.alloc_sbuf_tensor("ones_f", [128, 64], F32)
    S.op("pool", lambda e: e.memset(ones_f[:], 1.0), writes=["ones_f"])
    trs_f = nc.alloc_sbuf_tensor("trs_f", [64, 64], F32)
    trl_f = nc.alloc_sbuf_tensor("trl_f", [64, 64], F32)
    S.op("pool", lambda e: e.memset(trs_f[:], 1.0), writes=["trs_f"])
    S.op("pool", lambda e: e.affine_select(out=trs_f[:], in_=trs_f[:], pattern=[[1, 64]], compare_op=ALU.is_gt,
                                           fill=0.0, base=0, channel_multiplier=-1), reads=["trs_f"], writes=["trs_f"])
    S.op("pool", lambda e: e.memset(trl_f[:], 1.0), writes=["trl_f"])
    S.op("pool", lambda e: e.affine_select(out=trl_f[:], in_=trl_f[:], pattern=[[-1, 64]], compare_op=ALU.is_gt,
                                           fill=0.0, base=0, channel_multiplier=1), reads=["trl_f"], writes=["trl_f"])
    S.op("pool", lambda e: e.memset(tri_f[:], 1.0), writes=["tri_f"])
    S.op("pool", lambda e: e.affine_select(out=tri_f[:], in_=tri_f[:], pattern=[[1, 64]], compare_op=ALU.is_ge,
                                           fill=0.0, base=0, channel_multiplier=-1), reads=["tri_f"], writes=["tri_f"])
    NBP = T // 64
    SLOPES = [2.0 ** (-8.0 * (h + 1) / 16) for h in range(16)]
    BIGN = -30000.0
    ci = [0]

    def iota_f(shape, pattern, base, cm):
        ci[0] += 1
        ti = nc.alloc_sbuf_tensor(f"iota_i{ci[0]}", list(shape), I32)
        tf = nc.alloc_sbuf_tensor(f"iota_f{ci[0]}", list(shape), F32)
        S.op("pool", lambda e: e.iota(ti[:], pattern=pattern, base=base, channel_multiplier=cm), writes=[("iota", ci[0])])
        S.op("dve", lambda e: e.tensor_copy(out=tf[:], in_=ti[:]), reads=[("iota", ci[0])], writes=[("iotaf", ci[0])])
        return tf

    Dc = iota_f([128, NBP], [[-64, NBP]], -63, 1)
    Dblk_i = iota_f([128, NBP], [[-64, NBP]], 0, 1)
    KDt = iota_f([128, NT + 1], [[-128, NT + 1]], -64, 1)
    QK_ = iota_f([128, 128], [[-1, 128]], 0, 1)
    S.barrier()
    negsl = nc.alloc_sbuf_tensor("negsl", [128, 16, NBP], F32)
    for h in range(16):
        S.op("pool", lambda e, h=h: e.memset(negsl[:, h, :], -SLOPES[h]), writes=[("negsl", h)])
    bcol = nc.alloc_sbuf_tensor("bcol", [128, 16, NT + 1], F32)
    for h in range(16):
        S.op("dve", lambda e, h=h: e.tensor_scalar(out=bcol[:, h, :], in0=KDt[:], scalar1=SLOPES[h], scalar2=None, op0=ALU.mult),
             writes=[("bcol", h)])
    Mc_b = nc.alloc_sbuf_tensor("Mc_b", [128, 128], BF16)
    Mw_b = nc.alloc_sbuf_tensor("Mw_b", [128, 128], BF16)
    S.op("dve", lambda e: e.tensor_scalar(out=Mc_b[:], in0=QK_[:], scalar1=0.0, scalar2=BIGN, op0=ALU.is_gt, op1=ALU.mult), writes=["Mc_b"])
    S.op("dve", lambda e: e.tensor_scalar(out=Mw_b[:], in0=QK_[:], scalar1=0.0, scalar2=BIGN, op0=ALU.is_le, op1=ALU.mult), writes=["Mw_b"])
    E_all = nc.alloc_sbuf_tensor("E_all", [max(NBP, 2), T], BF16)
    S.op("pool", lambda e: e.memset(E_all[:], 1.0), writes=["E_all"])
    S.op("pool", lambda e: e.affine_select(out=E_all[:], in_=E_all[:], pattern=[[1, T]], compare_op=ALU.is_ge, fill=0.0, base=0,
                                           channel_multiplier=-64), reads=["E_all"], writes=["E_all"])
    S.op("pool", lambda e: e.affine_select(out=E_all[:], in_=E_all[:], pattern=[[-1, T]], compare_op=ALU.is_ge, fill=0.0, base=63,
                                           channel_multiplier=64), reads=["E_all"], writes=["E_all"])
    NBS = PAST // 64
    SEGT = min(NT, NPG)
    SEGB = 2 * SEGT
    NSEG = NPG // SEGT
    pt_i = nc.alloc_sbuf_tensor("pt_i", [128, NPG], I32)
    pt_f = nc.alloc_sbuf_tensor("pt_f", [128, NPG], F32)
    idx_f = nc.alloc_sbuf_tensor("idx_f", [128, L, NPG], F32)
    idx_i = nc.alloc_sbuf_tensor("idx_i", [128, L, NPG], I32)
    pcol = iota_f([128, 1], [[0, 1]], 0, 1)
    slg_raw = iota_f([4, 4], [[4, 4]], 1, 1)
    dcs = iota_f([4, NBS], [[-64, NBS]], PAST - 63, 0)
    tb = iota_f([128, NPG], [[-128, NPG]], PAST, -1)
    tbw = iota_f([128, 4], [[-128, 4]], 512, -1)
    S.barrier()
    S.dma("sp", pt_i[:], page_table.partition_broadcast(128), writes=["pt_i"])
    S.op("dve", lambda e: e.tensor_copy(out=pt_f[:], in_=pt_i[:]), reads=["pt_i"], writes=["pt_f"])
    for l_ in range(L):
        S.op("dve", lambda e, l_=l_: e.tensor_scalar(out=idx_f[:, l_, :], in0=pt_f[:], scalar1=float(l_ * NPOOL), scalar2=128.0,
                                                    op0=ALU.add, op1=ALU.mult), reads=["pt_f"], writes=[("idx_f", l_)])
        S.op("dve", lambda e, l_=l_: e.tensor_scalar(out=idx_f[:, l_, :], in0=idx_f[:, l_, :], scalar1=pcol[:, 0:1], scalar2=None, op0=ALU.add),
             reads=[("idx_f", l_)], writes=[("idx_f", l_)])
        S.op("dve", lambda e, l_=l_: e.tensor_copy(out=idx_i[:, l_, :], in_=idx_f[:, l_, :]), reads=[("idx_f", l_)], writes=[("idx_i", l_)])
    slg = nc.alloc_sbuf_tensor("slg", [4, 4], F32)
    S.op("act", lambda e: e.activation(out=slg[:], in_=slg_raw[:], func=AF.Exp, scale=-0.34657359027997264), writes=["slg"])
    S.op("dve", lambda e: e.tensor_scalar(out=slg[:], in0=slg[:], scalar1=-1.0, scalar2=None, op0=ALU.mult), reads=["slg"], writes=["slg"])
    negsl16 = nc.alloc_sbuf_tensor("negsl16", [128, 16], F32)
    for h in range(16):
        S.op("pool", lambda e, h=h: e.memset(negsl16[:, h:h + 1], -SLOPES[h]), writes=[("negsl16", h)])
    half_sel = nc.alloc_sbuf_tensor("half_sel", [1, 2, 128], BF16)
    S.op("pool", lambda e: e.memset(half_sel[:], 0.0), writes=["half_sel"])
    S.op("pool", lambda e: e.memset(half_sel[0:1, 0, 0:64], 1.0), reads=["half_sel"], writes=["half_sel"])
    S.op("pool", lambda e: e.memset(half_sel[0:1, 1, 64:128], 1.0), reads=["half_sel"], writes=["half_sel"])
    mw0 = nc.alloc_sbuf_tensor("mw0", [128, 1], F32)
    S.op("dve", lambda e: e.tensor_scalar(out=mw0[:], in0=ident_f[:, 0:1], scalar1=BIGN, scalar2=None, op0=ALU.mult), writes=["mw0"])
    S.barrier()

    S.store("sp", xa[0:T, :], x_prompt)
    S.store("sp", xa[T:T + 1, :], x_sample)
    S.store("sp", xa[T + 1:TA, :], zeros_f[0:127, 0:D])
    S.barrier()

    evac_rr = [0]

    def evac(out, in_, reads, writes):
        evac_rr[0] ^= 1
        if evac_rr[0]:
            S.op("act", lambda e: e.copy(out=out, in_=in_), reads=reads, writes=writes)
        else:
            S.op("dve", lambda e: e.tensor_copy(out=out, in_=in_), reads=reads, writes=writes)

    def rmsnorm_T(P, src, g_row, hT, tag, NC=None):
        NC = NC or D
        KD = NC // 128
        D_ = NC
        xt = [P.t(f"{tag}_x{i}", [128, NC], F32) for i in range(2)]
        xb = [P.t(f"{tag}_xb{i}", [128, NC], BF16) for i in range(2)]
        if g_row is not None:
            gb = P.t(tag + "_g", [128, NC], F32)
            S.dma("sp", gb[:], g_row, writes=[tag + "g"])
            sq = P.t(tag + "_sq", [128, NC], F32)
            ss = P.t(tag + "_ss", [128, 2 * NTA], F32)
        for tt in range(NTA):
            b = tt % 2
            S.dma("sp", xt[b][:], src[tt * 128:(tt + 1) * 128, 0:NC], writes=[(tag, "x", b)])
            if g_row is None:
                S.op("act" if tt % 2 else "dve", (lambda e, b=b: e.copy(out=xb[b][:], in_=xt[b][:])) if tt % 2 else
                     (lambda e, b=b: e.tensor_copy(out=xb[b][:], in_=xt[b][:])), reads=[(tag, "x", b)], writes=[(tag, "xb", b)])
            if g_row is not None:
              S.op("act", lambda e, b=b, tt=tt: e.activation(out=sq[:], in_=xt[b][:], func=AF.Square,
                                                         accum_out=ss[:, 2 * tt:2 * tt + 1]),
                 reads=[(tag, "x", b)], writes=[(tag, "sq"), (tag, "ss", tt)])
            if g_row is not None:
              S.op("dve", lambda e, tt=tt: e.tensor_scalar(out=ss[:, 2 * tt + 1:2 * tt + 2], in0=ss[:, 2 * tt:2 * tt + 1],
                                                          scalar1=1.0 / D_, scalar2=EPS, op0=ALU.mult, op1=ALU.add),
                   reads=[(tag, "ss", tt)], writes=[(tag, "r0", tt)])
              S.op("act", lambda e, tt=tt: e.activation(out=ss[:, 2 * tt + 1:2 * tt + 2], in_=ss[:, 2 * tt + 1:2 * tt + 2], func=AF.Ln),
                   reads=[(tag, "r0", tt)], writes=[(tag, "r0b", tt)])
              S.op("act", lambda e, tt=tt: e.activation(out=ss[:, 2 * tt + 1:2 * tt + 2], in_=ss[:, 2 * tt + 1:2 * tt + 2], func=AF.Exp,
                                                       scale=-0.5),
                   reads=[(tag, "r0b", tt)], writes=[(tag, "r1", tt)])
              S.op("dve", lambda e, b=b, tt=tt: e.scalar_tensor_tensor(out=xb[b][:], in0=xt[b][:], scalar=ss[:, 2 * tt + 1:2 * tt + 2],
                                                                      in1=gb[:], op0=ALU.mult, op1=ALU.mult),
                   reads=[(tag, "x", b), (tag, "r1", tt), tag + "g"], writes=[(tag, "xb", b)])
            for k4 in range(KD // 4 if KD >= 4 else 1):
                nk = min(4, KD)
                pb = PB[k4 % 2]
                for kk in range(nk):
                    k = k4 * 4 + kk
                    S.op("pe", lambda e, b=b, k=k, kk=kk, pb=pb: e.transpose(out=pb[:, kk * 128:(kk + 1) * 128],
                                                                         in_=xb[b][:, k * 128:(k + 1) * 128], identity=ident_b[:]),
                         reads=[(tag, "xb", b)], writes=[("pb", k4 % 2)])
                evac(hT[:, k4 * 4:k4 * 4 + nk, tt * 128:(tt + 1) * 128],
                     pb[:, 0:nk * 128].rearrange("p (k t) -> p k t", k=nk),
                     reads=[("pb", k4 % 2)], writes=[(tag, "hT", tt, k4)])

    def bcast_row(ap_row, n):
        return ap_row.partition_broadcast(128)

    NCH = T // 64 + 1

    def gla_mixer(l):
        P = Pool(nc)
        NCG = 1552
        pg = [P.t(f"pgG{i}", [64, NCG], F32) for i in range(2)]
        a2 = P.t("a2G", [16, 256], F32)
        ab = P.t("abG", [64, 256], F32)
        gng = P.t("gngG", [64, 128], F32)
        S.dma("sp", a2[:], gla_a2[l], writes=["a2G"])
        S.dma("sp", ab[:], gla_a_b[l].partition_broadcast(64), writes=["abG"])
        S.dma("sp", gng[:], gla_norm_g[l].partition_broadcast(64), writes=["gngG"])
        aT = P.t("aTG", [16, 64], F32)
        la = P.t("laG", [64, 256], F32)
        cum = P.t("cumG", [64, 256], F32)
        e_q = P.t("eqG", [64, 256], F32)
        e_k = P.t("ekG", [64, 256], F32)
        e_l = P.t("elG", [64, 256], F32)
        qd = P.t("qdG", [64, 256], BF16)
        kd = P.t("kdG", [64, 256], BF16)
        kl = P.t("klG", [64, 256], BF16)
        vb = P.t("vbG", [64, 512], BF16)
        qkT = P.t("qkTG", [64, 8, 64], BF16)
        att = P.t("attG", [64, 4, 64], BF16)
        St = P.t("StG", [64, 4, 128], F32)
        Sb = P.t("SbG", [64, 4, 128], BF16)
        ecol = P.t("ecolG", [64, 4], F32)
        og = P.t("ogG", [64, 4, 128], F32)
        o2 = P.t("o2G", [64, 4, 128], F32)
        sg = P.t("sgG", [64, 512], F32)
        ssg = P.t("ssgG", [64, 16], F32)
        S.op("dve", lambda e: e.memset(St[:], 0.0), writes=["StG"])
        S.op("dve", lambda e: e.memset(Sb[:], 0.0), writes=["SbG"])
        QO, KO, VO, AO, GO = 0, 256, 512, 1024, 1040
        def chunk(c):
            is_s = (c == NCH - 1)
            r0 = c * 64
            p = pg[c % 2]
            pk = ("pgG", c % 2)
            if is_s:
                S.store("sp", p_gla_o[l].rearrange("h d v -> d h v"), St[:], reads=["StG"])
                S.dma("sp", St[:], state_gla[l].rearrange("h d v -> d h v"), writes=["StG"])
                S.op("dve", lambda e: e.tensor_copy(out=Sb[:], in_=St[:]), reads=["StG"], writes=["SbG"])
            S.dma("sp", p[:], proj[r0:r0 + 64, C_GQ:C_GQ + NCG], writes=[pk])
            S.op("pe", lambda e, p=p: e.transpose(out=PS[0][0:16, 0:64], in_=p[:, AO:AO + 16], identity=ident_f[0:64, 0:64]),
                 reads=[pk], writes=[("ps", 0)])
            S.op("act", lambda e: e.copy(out=aT[:], in_=PS[0][0:16, 0:64]), reads=[("ps", 0)], writes=["aTG"])
            S.op("pe", lambda e: e.matmul(PS[1][0:64, 0:256], lhsT=aT[:], rhs=a2[:], start=True, stop=True),
                 reads=["aTG", "a2G"], writes=[("ps", 1)])
            S.op("dve", lambda e: e.tensor_tensor(out=la[:], in0=PS[1][0:64, 0:256], in1=ab[:], op=ALU.add),
                 reads=[("ps", 1), "abG"], writes=["laG"])
            S.op("act", lambda e: e.activation(out=la[:], in_=la[:], func=AF.Exp, scale=-1.0), reads=["laG"], writes=["laG"])
            S.op("act", lambda e: e.activation(out=la[:], in_=la[:], func=AF.Ln, bias=1.0), reads=["laG"], writes=["laG"])
            mcol = ident_f[0:64, 0:1] if is_s else ones_f[0:64, 0:1]
            S.op("dve", lambda e, mcol=mcol: e.tensor_scalar(out=la[:], in0=la[:], scalar1=-1.0 / 16, scalar2=mcol,
                                                            op0=ALU.mult, op1=ALU.mult), reads=["laG"], writes=["laG"])
            S.sp()
            S.op("pe", lambda e: e.matmul(PS[2][0:64, 0:256], lhsT=tri_f[:], rhs=la[:], start=True, stop=True),
                 reads=["laG"], writes=[("ps", 2)])
            S.op("pe", lambda e: e.matmul(PS[3][0:64, 0:256], lhsT=ones_f[0:64, 0:64], rhs=la[:], start=True, stop=True),
                 reads=["laG"], writes=[("ps", 3)])
            for h in range(4):
                S.op("pe", lambda e, h=h: e.matmul(PS[4][0:64, h:h + 1], lhsT=la[:, h * 64:(h + 1) * 64], rhs=ones_f[0:64, 0:1],
                                                   start=True, stop=True), reads=["laG"], writes=[("ps", 4)])
            S.op("act", lambda e: e.activation(out=ecol[:], in_=PS[4][0:64, 0:4], func=AF.Exp), reads=[("ps", 4)], writes=["ecolG"])
            S.op("dve", lambda e: e.tensor_copy(out=cum[:], in_=PS[2][0:64, 0:256]), reads=[("ps", 2)], writes=["cumG"])
            S.op("act", lambda e: e.activation(out=e_q[:], in_=cum[:], func=AF.Exp), reads=["cumG"], writes=["eqG"])
            S.op("act", lambda e: e.activation(out=e_k[:], in_=cum[:], func=AF.Exp, scale=-1.0), reads=["cumG"], writes=["ekG"])
            S.op("dve", lambda e: e.tensor_tensor(out=e_l[:], in0=PS[3][0:64, 0:256], in1=cum[:], op=ALU.subtract),
                 reads=[("ps", 3), "cumG"], writes=["elG"])
            S.op("act", lambda e: e.activation(out=e_l[:], in_=e_l[:], func=AF.Exp), reads=["elG"], writes=["elG"])
            S.op("dve", lambda e, p=p: e.scalar_tensor_tensor(out=qd[:], in0=p[:, QO:QO + 256], scalar=0.125, in1=e_q[:],
                                                              op0=ALU.mult, op1=ALU.mult), reads=[pk, "eqG"], writes=["qdG"])
            S.op("dve", lambda e, p=p: e.tensor_tensor(out=kd[:], in0=p[:, KO:KO + 256], in1=e_k[:], op=ALU.mult),
                 reads=[pk, "ekG"], writes=["kdG"])
            S.op("dve", lambda e, p=p: e.tensor_tensor(out=kl[:], in0=p[:, KO:KO + 256], in1=e_l[:], op=ALU.mult),
                 reads=[pk, "elG"], writes=["klG"])
            S.op("act", lambda e, p=p: e.copy(out=vb[:], in_=p[:, VO:VO + 512]), reads=[pk], writes=["vbG"])
            S.sp()
            for h in range(4):
                S.op("pe", lambda e, h=h: e.transpose(out=PB[0][0:64, h * 64:(h + 1) * 64], in_=qd[:, h * 64:(h + 1) * 64],
                                                      identity=ident_b[0:64, 0:64]), reads=["qdG"], writes=[("pb", 0)])
                S.op("pe", lambda e, h=h: e.transpose(out=PB[0][0:64, (4 + h) * 64:(5 + h) * 64], in_=kd[:, h * 64:(h + 1) * 64],
                                                      identity=ident_b[0:64, 0:64]), reads=["kdG"], writes=[("pb", 0)])
            S.op("dve", lambda e: e.tensor_copy(out=qkT[:].rearrange("p a t -> p (a t)"), in_=PB[0][0:64, 0:512]),
                 reads=[("pb", 0)], writes=["qkTG"])
            S.sp()
            for h in range(4):
                S.op("pe", lambda e, h=h: e.matmul(PS[5][0:64, h * 64:(h + 1) * 64], lhsT=qkT[:, 4 + h, :], rhs=qkT[:, h, :],
                                                   start=True, stop=True), reads=["qkTG"], writes=[("ps", 5)])
            S.op("dve", lambda e: e.tensor_tensor(out=att[:], in0=PS[5][0:64, 0:256].rearrange("p (h t) -> p h t", h=4),
                                                  in1=tri_f[:].unsqueeze(1).to_broadcast([64, 4, 64]), op=ALU.mult),
                 reads=[("ps", 5)], writes=["attG"])
            S.sp()
            for h in range(4):
                S.op("pe", lambda e, h=h: e.matmul(PS[0][0:64, h * 128:(h + 1) * 128], lhsT=att[:, h, :], rhs=vb[:, h * 128:(h + 1) * 128],
                                                   start=True, stop=False), reads=["attG", "vbG"], writes=[("ps", 0)])
                S.op("pe", lambda e, h=h: e.matmul(PS[0][0:64, h * 128:(h + 1) * 128], lhsT=qkT[:, h, :], rhs=Sb[:, h, :],
                                                   start=False, stop=True), reads=["qkTG", "SbG"], writes=[("ps", 0)])
            S.op("act", lambda e: e.copy(out=og[:].rearrange("p h v -> p (h v)"), in_=PS[0][0:64, 0:512]), reads=[("ps", 0)], writes=["ogG"])
            S.sp()
            for h in range(4):
                S.op("pe", lambda e, h=h: e.matmul(PS[1][0:64, h * 128:(h + 1) * 128], lhsT=kl[:, h * 64:(h + 1) * 64],
                                                   rhs=vb[:, h * 128:(h + 1) * 128], start=True, stop=True),
                     reads=["klG", "vbG"], writes=[("ps", 1)])
            for h in range(4):
                S.op("dve", lambda e, h=h: e.scalar_tensor_tensor(out=St[:, h, :], in0=St[:, h, :], scalar=ecol[:, h:h + 1],
                                                                  in1=PS[1][0:64, h * 128:(h + 1) * 128], op0=ALU.mult, op1=ALU.add),
                     reads=["StG", "ecolG", ("ps", 1)], writes=["StG"])
            S.op("dve", lambda e: e.tensor_copy(out=Sb[:], in_=St[:]), reads=["StG"], writes=["SbG"])
            S.sp()
            S.op("dve", lambda e: e.tensor_tensor(out=o2[:], in0=og[:], in1=og[:], op=ALU.mult), reads=["ogG"], writes=["o2G"])
            S.op("dve", lambda e: e.tensor_reduce(out=ssg[:, 0:4], in_=o2[:], axis=AX.X, op=ALU.add), reads=["o2G"], writes=["ssg0"])
            S.op("dve", lambda e: e.tensor_scalar(out=ssg[:, 4:8], in0=ssg[:, 0:4], scalar1=1.0 / 128, scalar2=EPS,
                                                  op0=ALU.mult, op1=ALU.add), reads=["ssg0"], writes=["ssg1"])
            S.op("act", lambda e: e.activation(out=ssg[:, 8:12], in_=ssg[:, 4:8], func=AF.Ln), reads=["ssg1"], writes=["ssg2"])
            S.op("act", lambda e: e.activation(out=ssg[:, 12:16], in_=ssg[:, 8:12], func=AF.Exp, scale=-0.5), reads=["ssg2"], writes=["ssg3"])
            S.op("dve", lambda e: e.tensor_tensor(out=o2[:], in0=og[:], in1=ssg[:, 12:16].unsqueeze(2).to_broadcast([64, 4, 128]),
                                                  op=ALU.mult), reads=["ogG", "ssg3"], writes=["o2G"])
            S.op("dve", lambda e: e.tensor_tensor(out=o2[:], in0=o2[:], in1=gng[:].unsqueeze(1).to_broadcast([64, 4, 128]),
                                                  op=ALU.mult), reads=["o2G", "gngG"], writes=["o2G"])
            S.op("act", lambda e, p=p: e.activation(out=sg[:], in_=p[:, GO:GO + 512], func=AF.Sigmoid), reads=[pk], writes=["sgG"])
            S.op("dve", lambda e, p=p: e.tensor_tensor(out=sg[:], in0=sg[:], in1=p[:, GO:GO + 512], op=ALU.mult),
                 reads=["sgG", pk], writes=["sgG"])
            S.op("dve", lambda e: e.tensor_tensor(out=o2[:].rearrange("p h v -> p (h v)"), in0=o2[:].rearrange("p h v -> p (h v)"),
                                                  in1=sg[:], op=ALU.mult), reads=["o2G", "sgG"], writes=["o2G"])
            S.store("sp", omix[r0:r0 + 64, 1024:1536], o2[:].rearrange("p h v -> p (h v)"), reads=["o2G"])
        def fin():
            S.store("sp", s_gla_o[l].rearrange("h d v -> d h v"), St[:], reads=["StG"])
        return P, chunk, fin


    def rw_mixer(l):
        P = Pool(nc)
        H8 = lambda ap: ap.rearrange("p (h d) -> p h d", h=8)
        cnt = [0]

        def T_(name, shape=(64, 512), dt=F32):
            return P.t(name + "R", list(shape), dt)

        rwt = [T_(f"rwt{i}", (64, RWP)) for i in range(2)]
        sht = [T_(f"sht{i}", (64, RWP)) for i in range(2)]
        mu = T_("mu", (64, RWP)); S.dma("sp", mu[:], rw_mu[l].partition_broadcast(64), writes=["muR"])
        cb = {}
        for nm, src in (("w0", rw_w0), ("a0", rw_a0), ("kkw", rw_kk), ("ka", rw_ka), ("lng", rw_ln_g), ("lnb", rw_ln_b)):
            cb[nm] = T_(nm)
            S.dma("sp", cb[nm][:], src[l].partition_broadcast(64), writes=[nm + "R"])
        rk = T_("rk"); S.dma("sp", rk[:], rw_rk[l].rearrange("h d -> (h d)").partition_broadcast(64), writes=["rkR"])
        w2 = T_("w2", (32, 512)); S.dma("sp", w2[:], rw_w2[l], writes=["w2R"])
        a2 = T_("a2", (32, 512)); S.dma("sp", a2[:], rw_a2[l], writes=["a2R"])
        g2 = T_("g2", (96, 512)); S.dma("sp", g2[:], rw_g2[l], writes=["g2R"])
        xm = T_("xm", (64, RWP))
        sm = T_("sm", (64, 160))
        smT = T_("smT", (96, 3, 64))
        names = ["lw", "av", "gv", "kk", "kp", "tmp", "cum", "E1", "E2", "E3", "E4", "Rt", "At", "Bt", "Kt", "Bh", "Kh", "bv",
                 "vv", "bon", "oo", "o2"]
        BFN = ("Rt", "At", "Bt", "Kt", "Bh", "Kh")
        t = {n: (T_(n, (64, 512), BF16) if n in BFN else T_(n)) for n in names}
        vvb = T_("vvb", (64, 512), BF16)
        ssr = T_("ssr", (64, 64))
        gcol = T_("gcol", (64, 8))
        TT = {n: T_(n + "T", (64, 8, 64), BF16) for n in ("Rt", "At", "Bt", "Kt")}
        mats = {n: T_(n, (64, 8, 64), BF16) for n in ("N", "N2", "AK", "KR", "BR", "Q", "MAT", "LV", "U")}
        H = T_("H", (64, 8, 64))
        Hb = T_("Hb", (64, 8, 64), BF16)
        Hv = T_("Hv", (64, 8, 64))
        lvl = P.t("lvlR", [64, 5, 8, 64], BF16)
        S.op("dve", lambda e: e.memset(H[:], 0.0), writes=["HR"])
        S.op("dve", lambda e: e.memset(Hb[:], 0.0), writes=["HbR"])

        def op3(eng, out, in0, in1, op, rk_, wk_):
            S.op(eng, lambda e: e.tensor_tensor(out=out, in0=in0, in1=in1, op=op), reads=rk_, writes=wk_)

        def store_state(dst):
            for h in range(8):
                S.op("pe", lambda e, h=h: e.transpose(out=PS[0][0:64, h * 64:(h + 1) * 64], in_=H[:, h, :], identity=ident_f[0:64, 0:64]),
                     reads=["HR"], writes=[("ps", 0)])
            S.op("act", lambda e: e.copy(out=Hv[:].rearrange("p h k -> p (h k)"), in_=PS[0][0:64, 0:512]), reads=[("ps", 0)], writes=["HvR"])
            S.store("sp", dst.rearrange("h v k -> v h k"), Hv[:], reads=["HvR"])

        def load_state(src):
            S.dma("sp", Hv[:], src.rearrange("h v k -> v h k"), writes=["HvR"])
            for h in range(8):
                S.op("pe", lambda e, h=h: e.transpose(out=PS[0][0:64, h * 64:(h + 1) * 64], in_=Hv[:, h, :], identity=ident_f[0:64, 0:64]),
                     reads=["HvR"], writes=[("ps", 0)])
            S.op("act", lambda e: e.copy(out=H[:].rearrange("p h k -> p (h k)"), in_=PS[0][0:64, 0:512]), reads=[("ps", 0)], writes=["HR"])
            S.op("dve", lambda e: e.tensor_copy(out=Hb[:], in_=H[:]), reads=["HR"], writes=["HbR"])

        def mm8(ps_i, lhs_fn, rhs_fn, rk_, start=True, stop=True):
            for h in range(8):
                S.op("pe", lambda e, h=h: e.matmul(PS[ps_i][0:64, h * 64:(h + 1) * 64], lhsT=lhs_fn(h), rhs=rhs_fn(h), start=start, stop=stop),
                     reads=rk_, writes=[("ps", ps_i)])

        def ps8(i):
            return PS[i][0:64, 0:512].rearrange("p (h d) -> p h d", h=8)

        xm2 = [xm, T_("xmB", (64, RWP))]
        gv2 = [t["gv"], T_("gvB")]
        bon2 = [t["bon"], T_("bonB")]
        vvb2 = [vvb, T_("vvbB", (64, 512), BF16)]

        def P0(c):
            is_s = (c == NCH - 1)
            r0 = c * 64
            b = c % 2
            rwb, shb = rwt[b], sht[b]
            kr, ks = ("rwtR", b), ("shtR", b)
            xm = xm2[b]
            kx = ("xmR", b)
            S.dma("sp", rwb[:], proj[r0:r0 + 64, C_RW:C_RW + RWP], writes=[kr])
            if c == 0:
                S.dma("sp", shb[0:1, :], zeros_f[0:1, 0:RWP], writes=[(ks, 0)])
            elif is_s:
                S.dma("sp", shb[0:1, :], state_shift[l:l + 1, :], writes=[(ks, 0)])
            else:
                S.dma("sp", shb[0:1, :], proj[r0 - 1:r0, C_RW:C_RW + RWP], writes=[(ks, 0)])
            S.dma("sp", shb[1:64, :], proj[r0:r0 + 63, C_RW:C_RW + RWP], writes=[(ks, 1)])
            rsh = [kr, (ks, 0), (ks, 1)]
            op3("pool", xm[:], shb[:], rwb[:], ALU.subtract, rsh, [kx])
            op3("pool", xm[:], xm[:], mu[:], ALU.mult, [kx, "muR"], [kx])
            op3("pool", xm[:], xm[:], rwb[:], ALU.add, [kx, kr], [kx])
            S.op("act", lambda e: e.activation(out=sm[:, 0:32], in_=xm[:, 1536:1568], func=AF.Tanh), reads=[kx], writes=[("smR", 0)])
            S.op("act", lambda e: e.copy(out=sm[:, 32:64], in_=xm[:, 1568:1600]), reads=[kx], writes=[("smR", 1)])
            S.op("act", lambda e: e.activation(out=sm[:, 64:160], in_=xm[:, 1600:1696], func=AF.Sigmoid), reads=[kx], writes=[("smR", 2)])

        def P1p(c):
            is_s = (c == NCH - 1)
            b = c % 2
            gv = gv2[b]
            for i, (o0, n) in enumerate(((0, 32), (32, 32), (64, 96))):
                S.op("pe", lambda e, i=i, o0=o0, n=n: e.transpose(out=PS[0][0:n, i * 64:(i + 1) * 64], in_=sm[:, o0:o0 + n],
                                                                  identity=ident_f[0:64, 0:64]), reads=[("smR", i)], writes=[("ps", 0)])
                S.op("act", lambda e, i=i, n=n: e.copy(out=smT[0:n, i, :], in_=PS[0][0:n, i * 64:(i + 1) * 64]),
                     reads=[("ps", 0)], writes=[("smTR", i)])
            for i, (wsb, n, wk) in enumerate(((w2, 32, "w2R"), (a2, 32, "a2R"), (g2, 96, "g2R"))):
                S.op("pe", lambda e, i=i, wsb=wsb, n=n: e.matmul(PS[1 + i][0:64, 0:512], lhsT=smT[0:n, i, :], rhs=wsb[:], start=True, stop=True),
                     reads=[("smTR", i), wk], writes=[("ps", 1 + i)])
            mcol = ident_f[0:64, 0:1] if is_s else ones_f[0:64, 0:1]
            op3("dve", t["lw"][:], PS[1][0:64, 0:512], cb["w0"][:], ALU.add, [("ps", 1), "w0R"], ["lwR"])
            S.op("act", lambda e: e.activation(out=t["lw"][:], in_=t["lw"][:], func=AF.Sigmoid), reads=["lwR"], writes=["lwR"])
            S.op("dve", lambda e, mcol=mcol: e.tensor_scalar(out=t["lw"][:], in0=t["lw"][:], scalar1=-0.6065306597, scalar2=mcol,
                                                            op0=ALU.mult, op1=ALU.mult), reads=["lwR"], writes=["lwR"])
            op3("dve", t["av"][:], PS[2][0:64, 0:512], cb["a0"][:], ALU.add, [("ps", 2), "a0R"], ["avR"])
            S.op("act", lambda e: e.activation(out=t["av"][:], in_=t["av"][:], func=AF.Sigmoid), reads=["avR"], writes=["avR"])
            S.op("act", lambda e: e.copy(out=gv[:], in_=PS[3][0:64, 0:512]), reads=[("ps", 3)], writes=[("gvR", b)])

        def P1b(c):
            is_s = (c == NCH - 1)
            b = c % 2
            xm = xm2[b]
            kx = ("xmR", b)
            vvb = vvb2[b]
            bon = bon2[b]
            mcol = ident_f[0:64, 0:1] if is_s else ones_f[0:64, 0:1]
            r_, k_, v_ = xm[:, 0:512], xm[:, 512:1024], xm[:, 1024:1536]
            op3("dve", t["kk"][:], k_, cb["kkw"][:], ALU.mult, [kx, "kkwR"], ["kkR"])
            op3("dve", t["tmp"][:], t["kk"][:], t["kk"][:], ALU.mult, ["kkR"], ["tmpR"])
            S.op("dve", lambda e: e.tensor_reduce(out=ssr[:, 0:8], in_=H8(t["tmp"][:]), axis=AX.X, op=ALU.add), reads=["tmpR"], writes=["ss0R"])
            S.op("dve", lambda e: e.tensor_scalar(out=ssr[:, 8:16], in0=ssr[:, 0:8], scalar1=1e-24, scalar2=None, op0=ALU.max),
                 reads=["ss0R"], writes=["ss1R"])
            S.op("act", lambda e: e.activation(out=ssr[:, 16:24], in_=ssr[:, 8:16], func=AF.Ln), reads=["ss1R"], writes=["ss2R"])
            S.op("act", lambda e: e.activation(out=ssr[:, 24:32], in_=ssr[:, 16:24], func=AF.Exp, scale=-0.5), reads=["ss2R"], writes=["ss3R"])
            S.op("dve", lambda e, mcol=mcol: e.tensor_scalar(out=ssr[:, 24:32], in0=ssr[:, 24:32], scalar1=mcol, scalar2=None, op0=ALU.mult),
                 reads=["ss3R"], writes=["ss3R"])
            op3("dve", H8(t["kk"][:]), H8(t["kk"][:]), ssr[:, 24:32].unsqueeze(2).to_broadcast([64, 8, 64]), ALU.mult, ["kkR", "ss3R"], ["kkR"])
            S.op("dve", lambda e: e.scalar_tensor_tensor(out=t["kp"][:], in0=t["av"][:], scalar=-1.0, in1=cb["ka"][:], op0=ALU.add, op1=ALU.mult),
                 reads=["avR", "kaR"], writes=["kpR"])
            S.op("dve", lambda e: e.scalar_tensor_tensor(out=t["kp"][:], in0=t["kp"][:], scalar=1.0, in1=k_, op0=ALU.add, op1=ALU.mult),
                 reads=["kpR", kx], writes=["kpR"])
            S.op("pool", lambda e, mcol=mcol: e.tensor_scalar(out=t["kp"][:], in0=t["kp"][:], scalar1=mcol, scalar2=None, op0=ALU.mult),
                 reads=["kpR"], writes=["kpR"])
            S.op("pool", lambda e, mcol=mcol: e.tensor_scalar(out=t["vv"][:], in0=v_, scalar1=mcol, scalar2=None, op0=ALU.mult),
                 reads=[kx], writes=["vvR"])
            S.op("act", lambda e: e.copy(out=vvb[:], in_=t["vv"][:]), reads=["vvR"], writes=[("vvbR", b)])
            op3("dve", t["bv"][:], t["kk"][:], t["av"][:], ALU.mult, ["kkR", "avR"], ["bvR"])
            op3("dve", t["tmp"][:], r_, t["kp"][:], ALU.mult, [kx, "kpR"], ["tmpR"])
            op3("dve", t["tmp"][:], t["tmp"][:], rk[:], ALU.mult, ["tmpR", "rkR"], ["tmpR"])
            S.op("dve", lambda e: e.tensor_reduce(out=ssr[:, 32:40], in_=H8(t["tmp"][:]), axis=AX.X, op=ALU.add), reads=["tmpR"], writes=["ss4R"])
            op3("dve", H8(bon[:]), H8(t["vv"][:]), ssr[:, 32:40].unsqueeze(2).to_broadcast([64, 8, 64]), ALU.mult, ["vvR", "ss4R"], [("bonR", b)])

        def P2(c):
            b = c % 2
            xm = xm2[b]
            kx = ("xmR", b)
            r_ = xm[:, 0:512]
            S.op("pe", lambda e: e.matmul(PS[4][0:64, 0:512], lhsT=tri_f[:], rhs=t["lw"][:], start=True, stop=True), reads=["lwR"], writes=[("ps", 4)])
            S.op("pe", lambda e: e.matmul(PS[5][0:64, 0:512], lhsT=ones_f[0:64, 0:64], rhs=t["lw"][:], start=True, stop=True),
                 reads=["lwR"], writes=[("ps", 5)])
            for h in range(8):
                S.op("pe", lambda e, h=h: e.matmul(PS[0][0:64, h:h + 1], lhsT=t["lw"][:, h * 64:(h + 1) * 64], rhs=ones_f[0:64, 0:1],
                                                   start=True, stop=True), reads=["lwR"], writes=[("ps", 0)])
            S.op("act", lambda e: e.activation(out=gcol[:], in_=PS[0][0:64, 0:8], func=AF.Exp), reads=[("ps", 0)], writes=["gcolR"])
            S.op("dve", lambda e: e.tensor_copy(out=t["cum"][:], in_=PS[4][0:64, 0:512]), reads=[("ps", 4)], writes=["cumR"])
            S.op("act", lambda e: e.activation(out=t["E1"][:], in_=t["cum"][:], func=AF.Exp), reads=["cumR"], writes=["E1R"])
            S.op("act", lambda e: e.activation(out=t["E2"][:], in_=t["cum"][:], func=AF.Exp, scale=-1.0), reads=["cumR"], writes=["E2R"])
            op3("dve", t["E3"][:], t["cum"][:], t["lw"][:], ALU.subtract, ["cumR", "lwR"], ["E3R"])
            S.op("act", lambda e: e.activation(out=t["E3"][:], in_=t["E3"][:], func=AF.Exp), reads=["E3R"], writes=["E3R"])
            op3("dve", t["E4"][:], PS[5][0:64, 0:512], t["cum"][:], ALU.subtract, [("ps", 5), "cumR"], ["E4R"])
            S.op("act", lambda e: e.activation(out=t["E4"][:], in_=t["E4"][:], func=AF.Exp), reads=["E4R"], writes=["E4R"])
            op3("dve", t["Rt"][:], r_, t["E1"][:], ALU.mult, [kx, "E1R"], ["RtR"])
            S.op("dve", lambda e: e.scalar_tensor_tensor(out=t["At"][:], in0=t["kk"][:], scalar=-1.0, in1=t["E3"][:], op0=ALU.mult, op1=ALU.mult),
                 reads=["kkR", "E3R"], writes=["AtR"])
            op3("pool", t["Bt"][:], t["bv"][:], t["E2"][:], ALU.mult, ["bvR", "E2R"], ["BtR"])
            op3("pool", t["Kt"][:], t["kp"][:], t["E2"][:], ALU.mult, ["kpR", "E2R"], ["KtR"])
            op3("pool", t["Bh"][:], t["bv"][:], t["E4"][:], ALU.mult, ["bvR", "E4R"], ["BhR"])
            op3("pool", t["Kh"][:], t["kp"][:], t["E4"][:], ALU.mult, ["kpR", "E4R"], ["KhR"])

        def P3(c):
            r0 = c * 64
            b = c % 2
            vvb = vvb2[b]
            bon = bon2[b]
            gv = gv2[b]
            def tr(i, n):
                pbt, po_ = PB[i // 2], (i % 2) * 512
                for h in range(8):
                    S.op("pe", lambda e, h=h, n=n, pbt=pbt, po_=po_: e.transpose(out=pbt[0:64, po_ + h * 64:po_ + (h + 1) * 64],
                                                                               in_=t[n][:, h * 64:(h + 1) * 64], identity=ident_b[0:64, 0:64]),
                         reads=[n + "R"], writes=[("pb", i // 2)])
                evac(TT[n][:].rearrange("p h d -> p (h d)"), pbt[0:64, po_:po_ + 512], reads=[("pb", i // 2)], writes=[n + "TR"])
                S.sp()
            hh = lambda m: (lambda h: m[:, h, :])

            def prod(nm, lh, rh, msk, psi):
                mm8(psi, hh(TT[lh]), hh(TT[rh]), [lh + "TR", rh + "TR"])
                dst_, dk_ = (lvl[:, 0], ("lvlR", 0)) if nm == "Lm" else (mats[nm][:], nm + "R")
                op3("dve", dst_, ps8(psi), msk[:].unsqueeze(1).to_broadcast([64, 8, 64]), ALU.mult, [("ps", psi)], [dk_])
                S.sp()

            def lv_():
                mm8(0, hh(mats["AK"]), (lambda h: vvb[:, h * 64:(h + 1) * 64]), ["AKR", ("vvbR", b)])
                evac(mats["LV"][:], ps8(0), reads=[("ps", 0)], writes=["LVR"])
                S.sp()

            tr(1, "At")
            tr(2, "Bt")
            prod("N", "Bt", "At", trs_f, 5)
            prod("Lm", "At", "Bt", trl_f, 0)
            tr(0, "Rt")
            tr(3, "Kt")
            fill = [lambda: prod("AK", "Kt", "At", trs_f, 1), lambda: prod("KR", "Kt", "Rt", tri_f, 2),
                    lambda: prod("BR", "Bt", "Rt", tri_f, 3), lv_, None]
            Nb = [mats["N"], mats["N2"]]
            Nk = ["NR", "N2R"]
            for i in range(5):
                cur, nxt = i % 2, (i + 1) % 2
                mm8(4, (lambda h, i=i: lvl[:, i, h, :]), hh(Nb[cur]), [Nk[cur], ("lvlR", i)])
                if i < 4:
                    mm8(5, hh(Nb[cur]), (lambda h, i=i: lvl[:, i, h, :]), [Nk[cur], ("lvlR", i)])
                S.op("act", lambda e, nxt=nxt: e.copy(out=Nb[nxt][:], in_=ps8(4)), reads=[("ps", 4)], writes=[Nk[nxt]])
                if i < 4:
                    S.op("dve", lambda e, i=i: e.tensor_copy(out=lvl[:, i + 1], in_=ps8(5)), reads=[("ps", 5)], writes=[("lvlR", i + 1)])
                S.sp()
                if fill[i] is not None:
                    fill[i]()
            op3("dve", mats["Q"][:], Nb[1][:], ident_f[0:64, 0:64].unsqueeze(1).to_broadcast([64, 8, 64]), ALU.add, [Nk[1]], ["QR"])
            for i in (4, 3, 2, 1, 0):
                mm8(4, (lambda h, i=i: lvl[:, i, h, :]), hh(mats["Q"]), [("lvlR", i), "QR"])
                op3("dve", mats["Q"][:], mats["Q"][:], ps8(4), ALU.add, ["QR", ("ps", 4)], ["QR"])
                S.sp()
            mm8(5, (lambda h: t["At"][:, h * 64:(h + 1) * 64]), hh(mats["Q"]), ["AtR", "QR"])
            evac(mats["MAT"][:], ps8(5), reads=[("ps", 5)], writes=["MATR"])
            S.sp()
            for h in range(8):
                S.op("pe", lambda e, h=h: e.matmul(PS[1][0:64, h * 64:(h + 1) * 64], lhsT=mats["Q"][:, h, :], rhs=mats["LV"][:, h, :],
                                                   start=True, stop=False), reads=["QR", "LVR"], writes=[("ps", 1)])
                S.op("pe", lambda e, h=h: e.matmul(PS[1][0:64, h * 64:(h + 1) * 64], lhsT=mats["MAT"][:, h, :], rhs=Hb[:, h, :],
                                                   start=False, stop=True), reads=["MATR", "HbR"], writes=[("ps", 1)])
            evac(mats["U"][:], ps8(1), reads=[("ps", 1)], writes=["UR"])
            S.sp()
            for h in range(8):
                sl = slice(h * 64, (h + 1) * 64)
                S.op("pe", lambda e, h=h, sl=sl: e.matmul(PS[2][0:64, sl], lhsT=TT["Rt"][:, h, :], rhs=Hb[:, h, :], start=True, stop=False),
                     reads=["RtTR", "HbR"], writes=[("ps", 2)])
                S.op("pe", lambda e, h=h, sl=sl: e.matmul(PS[2][0:64, sl], lhsT=mats["BR"][:, h, :], rhs=mats["U"][:, h, :], start=False, stop=False),
                     reads=["BRR", "UR"], writes=[("ps", 2)])
                S.op("pe", lambda e, h=h, sl=sl: e.matmul(PS[2][0:64, sl], lhsT=mats["KR"][:, h, :], rhs=vvb[:, sl], start=False, stop=True),
                     reads=["KRR", ("vvbR", b)], writes=[("ps", 2)])
            S.op("act", lambda e: e.copy(out=t["oo"][:], in_=PS[2][0:64, 0:512]), reads=[("ps", 2)], writes=["ooR"])
            S.sp()
            for h in range(8):
                sl = slice(h * 64, (h + 1) * 64)
                S.op("pe", lambda e, h=h, sl=sl: e.matmul(PS[3][0:64, sl], lhsT=t["Bh"][:, sl], rhs=mats["U"][:, h, :], start=True, stop=False),
                     reads=["BhR", "UR"], writes=[("ps", 3)])
                S.op("pe", lambda e, h=h, sl=sl: e.matmul(PS[3][0:64, sl], lhsT=t["Kh"][:, sl], rhs=vvb[:, sl], start=False, stop=True),
                     reads=["KhR", ("vvbR", b)], writes=[("ps", 3)])
            op3("dve", H[:], H[:], gcol[:].unsqueeze(2).to_broadcast([64, 8, 64]), ALU.mult, ["HR", "gcolR"], ["HR"])
            op3("dve", H[:], H[:], ps8(3), ALU.add, ["HR", ("ps", 3)], ["HR"])
            S.sp()
            S.op("act", lambda e: e.copy(out=Hb[:], in_=H[:]), reads=["HR"], writes=["HbR"])
            S.op("dve", lambda e: e.tensor_reduce(out=ssr[:, 40:48], in_=H8(t["oo"][:]), axis=AX.X, op=ALU.add), reads=["ooR"], writes=["ss5R"])
            S.op("dve", lambda e: e.tensor_scalar(out=ssr[:, 40:48], in0=ssr[:, 40:48], scalar1=-1.0 / 64, scalar2=None, op0=ALU.mult),
                 reads=["ss5R"], writes=["ss5R"])
            op3("dve", H8(t["oo"][:]), H8(t["oo"][:]), ssr[:, 40:48].unsqueeze(2).to_broadcast([64, 8, 64]), ALU.add, ["ooR", "ss5R"], ["ooR"])
            op3("dve", t["o2"][:], t["oo"][:], t["oo"][:], ALU.mult, ["ooR"], ["o2R"])
            S.op("dve", lambda e: e.tensor_reduce(out=ssr[:, 48:56], in_=H8(t["o2"][:]), axis=AX.X, op=ALU.add), reads=["o2R"], writes=["ss6R"])
            S.op("dve", lambda e: e.tensor_scalar(out=ssr[:, 48:56], in0=ssr[:, 48:56], scalar1=1.0 / 64, scalar2=64e-5, op0=ALU.mult, op1=ALU.add),
                 reads=["ss6R"], writes=["ss6R"])
            S.op("act", lambda e: e.activation(out=ssr[:, 48:56], in_=ssr[:, 48:56], func=AF.Ln), reads=["ss6R"], writes=["ss6R"])
            S.op("act", lambda e: e.activation(out=ssr[:, 56:64], in_=ssr[:, 48:56], func=AF.Exp, scale=-0.5), reads=["ss6R"], writes=["ss7R"])
            op3("dve", H8(t["oo"][:]), H8(t["oo"][:]), ssr[:, 56:64].unsqueeze(2).to_broadcast([64, 8, 64]), ALU.mult, ["ooR", "ss7R"], ["ooR"])
            op3("pool", t["oo"][:], t["oo"][:], cb["lng"][:], ALU.mult, ["ooR", "lngR"], ["ooR"])
            op3("pool", t["oo"][:], t["oo"][:], cb["lnb"][:], ALU.add, ["ooR", "lnbR"], ["ooR"])
            op3("pool", t["oo"][:], t["oo"][:], bon[:], ALU.add, ["ooR", ("bonR", b)], ["ooR"])
            op3("pool", t["oo"][:], t["oo"][:], gv[:], ALU.mult, ["ooR", ("gvR", b)], ["ooR"])
            S.store("sp", omix[r0:r0 + 64, 1536:2048], t["oo"][:], reads=["ooR"])

        def merged(a, b_):
            out = []
            i = j = 0
            while i < len(a) or j < len(b_):
                if j >= len(b_) or (i < len(a) and i * len(b_) <= j * len(a)):
                    out.append(a[i]); i += 1
                else:
                    out.append(b_[j]); j += 1
            return out

        P0(0)
        P1p(0)
        P1b(0)
        def chunk(c):
            if c == NCH - 1:
                store_state(p_rw_o[l])
                load_state(state_rwkv[l])
                S.sp()
            if c + 1 < NCH:
                P0(c + 1)
                S.sp()
            P2(c)
            S.sp()
            if c + 1 < NCH:
                P1p(c + 1)
                S.sp()
                outer = S.cap
                S.cap = []
                P1b(c + 1)
                sa = S.cap
                S.cap = []
                P3(c)
                sb_ = S.cap
                S.cap = outer
                S.replay(merged(sb_, sa))
            else:
                P3(c)
        def fin():
            store_state(s_rw_o[l])
        return P, chunk, fin


    def nsa_mixer(l):
        LP = Pool(nc)
        kcT_p = LP.t("kcT_p", [64, 4, NBP], BF16)
        vc_p = LP.t("vc_p", [max(NBP, 2), 4, 64], BF16)
        qkg = LP.t("qkgN", [128, 4, 64], F32)
        S.dma("sp", qkg[:].rearrange("p a d -> p (a d)"), nsa_qk_g[l].rearrange("a d -> (a d)").partition_broadcast(128), writes=["qkgN"])
        tmpN = LP.t("tmpN", [128, 16, 64], F32)
        ssN = LP.t("ssN", [128, 64], F32)
        kcT_s = LP.t("kcT_s", [64, 4, NBS], BF16)
        vc_s = LP.t("vc_s", [SEGB, NSEG, 4, 64], BF16)
        q_s = LP.t("q_s", [64, 16], BF16)
        k_s = LP.t("k_s", [64, 8], BF16)
        v_s = LP.t("v_s", [1, 8, 65], BF16)
        g_s = LP.t("g_s", [1, 48], F32)

        def headnormN(out3, in3, g_ap, H, rk, wk, sc=1.0, np_=128):
            S.op("dve", lambda e: e.tensor_tensor(out=tmpN[0:np_, 0:H, :], in0=in3, in1=in3, op=ALU.mult), reads=rk, writes=["tmpN"])
            S.op("dve", lambda e: e.tensor_reduce(out=ssN[0:np_, 0:H], in_=tmpN[0:np_, 0:H, :], axis=AX.X, op=ALU.add), reads=["tmpN"], writes=["ssN0"])
            S.op("dve", lambda e: e.tensor_scalar(out=ssN[0:np_, 16:16 + H], in0=ssN[0:np_, 0:H], scalar1=1.0 / 64, scalar2=EPS,
                                                  op0=ALU.mult, op1=ALU.add), reads=["ssN0"], writes=["ssN1"])
            S.op("act", lambda e: e.activation(out=ssN[0:np_, 48:48 + H], in_=ssN[0:np_, 16:16 + H], func=AF.Ln), reads=["ssN1"], writes=["ssN1b"])
            S.op("act", lambda e: e.activation(out=ssN[0:np_, 32:32 + H], in_=ssN[0:np_, 48:48 + H], func=AF.Exp, scale=-0.5,
                                               bias=float(np.log(sc))), reads=["ssN1b"], writes=["ssN2"])
            S.op("dve", lambda e: e.tensor_tensor(out=tmpN[0:np_, 0:H, :], in0=in3,
                                                  in1=ssN[0:np_, 32:32 + H].unsqueeze(2).to_broadcast([np_, H, 64]), op=ALU.mult),
                 reads=rk + ["ssN2"], writes=["tmpN"])
            S.op("dve", lambda e: e.tensor_tensor(out=out3, in0=tmpN[0:np_, 0:H, :],
                                                  in1=g_ap.unsqueeze(1).to_broadcast([np_, H, 64]), op=ALU.mult),
                 reads=["tmpN", "qkgN"], writes=wk)

        P = Pool(nc)
        w1 = P.t("w1N", [64, 2, 64, 128], BF16)
        w2 = P.t("w2N", [128, 2, 64], BF16)
        for kv in range(2):
            S.dma("pool", w1[:, kv], cmp_w1[l, kv].rearrange("(pos d) h -> d pos h", d=64), writes=[("w1N", kv)])
            S.dma("pool", w2[:, kv, :], cmp_w2[l, kv], writes=[("w2N", kv)])
        posb = P.t("posbN", [128, 2, 64], F32)
        for hf in range(2):
            S.dma("sp", posb[hf * 64:(hf + 1) * 64], cmp_pos[l].rearrange("kv pos d -> pos kv d"), writes=[("posbN", hf)])
        zT = P.t("zTN", [64, 8, T], BF16)
        ct = [P.t(f"ctN{i}", [128, 512], F32) for i in range(2)]
        zb = [P.t(f"zbN{i}", [128, 512], BF16) for i in range(2)]
        hx = [P.t(f"hxN{i}", [128, 4 * NBP], F32) for i in range(4)]
        hb = P.t("hbN", [128, 4 * NBP], BF16)
        kc = P.t("kcN", [max(NBP, 2), 4, 64], F32)
        kcb = P.t("kcbN", [max(NBP, 2), 4, 64], BF16)

        def compress_segment(load_fn, ntiles, kT_out, v_out):
            nb = ntiles * 2
            for tt in range(ntiles):
                b = tt % 2
                load_fn(tt, ct[b], ("ctN", b))
                S.op("dve", lambda e, b=b: e.tensor_tensor(out=zb[b][:].rearrange("p (kv g d) -> p kv g d", kv=2, g=4),
                                                           in0=ct[b][:].rearrange("p (kv g d) -> p kv g d", kv=2, g=4),
                                                           in1=posb[:].unsqueeze(2).to_broadcast([128, 2, 4, 64]), op=ALU.add),
                     reads=[("ctN", b), ("posbN", 0), ("posbN", 1)], writes=[("zbN", b)])
                for a in range(8):
                    S.op("pe", lambda e, a=a, b=b: e.transpose(out=PB[b][0:64, a * 128:(a + 1) * 128], in_=zb[b][:, a * 64:(a + 1) * 64],
                                                             identity=ident_b[:]), reads=[("zbN", b)], writes=[("pb", b)])
                evac(zT[:, :, tt * 128:(tt + 1) * 128], PB[b][0:64, 0:1024].rearrange("p (a t) -> p a t", a=8),
                     reads=[("pb", b)], writes=[("zTN", tt)])
            zr = [("zTN", tt) for tt in range(ntiles)]
            for kv in range(2):
                ps = PS[kv]
                for pos in range(64):
                    S.op("pe", lambda e, kv=kv, pos=pos, ps=ps: e.matmul(
                        ps[:, 0:4 * nb].rearrange("p (g n) -> p g n", g=4), lhsT=w1[:, kv, pos, :],
                        rhs=zT[:, kv * 4:(kv + 1) * 4, bass.DynSlice(pos, nb, step=64)] if False else
                        zT[:, kv * 4:(kv + 1) * 4, 0:nb * 64].rearrange("p g (n s) -> p g n s", s=64)[:, :, :, pos],
                        start=(pos == 0), stop=(pos == 63)), reads=zr + [("w1N", kv)], writes=[("ps", kv)])
                n4 = 4 * nb
                x_, x2, u_, th = hx[0], hx[1], hx[2], hx[3]
                S.op("act", lambda e, ps=ps: e.copy(out=x_[:, 0:n4], in_=ps[:, 0:n4]), reads=[("ps", kv)], writes=["hx0"])
                S.op("dve", lambda e: e.tensor_tensor(out=x2[:, 0:n4], in0=x_[:, 0:n4], in1=x_[:, 0:n4], op=ALU.mult), reads=["hx0"], writes=["hx1"])
                S.op("dve", lambda e: e.tensor_scalar(out=x2[:, 0:n4], in0=x2[:, 0:n4], scalar1=0.044715, scalar2=1.0, op0=ALU.mult, op1=ALU.add),
                     reads=["hx1"], writes=["hx1"])
                S.op("dve", lambda e: e.tensor_tensor(out=u_[:, 0:n4], in0=x2[:, 0:n4], in1=x_[:, 0:n4], op=ALU.mult), reads=["hx1", "hx0"], writes=["hx2"])
                S.op("act", lambda e: e.activation(out=th[:, 0:n4], in_=u_[:, 0:n4], func=AF.Tanh, scale=0.7978845608), reads=["hx2"], writes=["hx3"])
                S.op("dve", lambda e: e.scalar_tensor_tensor(out=th[:, 0:n4], in0=th[:, 0:n4], scalar=1.0, in1=x_[:, 0:n4], op0=ALU.add, op1=ALU.mult),
                     reads=["hx3", "hx0"], writes=["hx3"])
                S.op("dve", lambda e: e.tensor_scalar(out=hb[:, 0:n4], in0=th[:, 0:n4], scalar1=0.5, scalar2=None, op0=ALU.mult),
                     reads=["hx3"], writes=["hbN"])
                for g in range(4):
                    S.op("pe", lambda e, g=g, kv=kv: e.matmul(PS[2][0:nb, g * 64:(g + 1) * 64], lhsT=hb[:, g * nb:(g + 1) * nb], rhs=w2[:, kv, :],
                                                             start=True, stop=True), reads=["hbN", ("w2N", kv)], writes=[("ps", 2)])
                if kv == 0:
                    S.op("act", lambda e: e.copy(out=kc[0:nb].rearrange("p g d -> p (g d)"), in_=PS[2][0:nb, 0:256]), reads=[("ps", 2)], writes=["kcN"])
                    headnormN(kcb[0:nb], kc[0:nb], qkg[0:nb, 1, :], 4, ["kcN"], ["kcbN"], np_=nb)
                    for g in range(4):
                        S.op("pe", lambda e, g=g: e.transpose(out=PB[0][0:64, g * nb:(g + 1) * nb], in_=kcb[0:nb, g, :], identity=ident_b[0:nb, 0:nb]),
                             reads=["kcbN"], writes=[("pb", 0)])
                    S.op("dve", lambda e: e.tensor_copy(out=kT_out, in_=PB[0][0:64, 0:4 * nb].rearrange("p (g n) -> p g n", g=4)),
                         reads=[("pb", 0)], writes=["kcT"])
                else:
                    S.op("act", lambda e: e.copy(out=v_out, in_=PS[2][0:nb, 0:256].rearrange("p (g d) -> p g d", g=4)), reads=[("ps", 2)], writes=["vc"])

        def load_prompt_cmp(tt, dst, key):
            S.store("sp", dst[:], proj[tt * 128:(tt + 1) * 128, C_CMP:C_CMP + 512], writes=[key])

        compress_segment(load_prompt_cmp, NT, kcT_p[:], vc_p[0:NBP])
        for sgi in range(NSEG):
            def load_cache_cmp(tt, dst, key, sgi=sgi):
                j = sgi * SEGT + tt
                S.dma("pool", dst[:], cache_cmp, reads=[("idx_i", l)], writes=[key],
                      indirect=bass.IndirectOffsetOnAxis(ap=idx_i[:, l, j:j + 1], axis=0))
            compress_segment(load_cache_cmp, SEGT, kcT_s[:, :, sgi * SEGB:(sgi + 1) * SEGB], vc_s[0:SEGB, sgi])
        S.barrier()
        P.release()

        P = Pool(nc)
        qT = P.t("qTN", [64, 16, TA], BF16)
        kT = P.t("kTN", [64, 8, TA], BF16)
        Va = P.t("VaN", [128, NTA, 8, 65], BF16)
        gts = P.t("gtsN", [128, NTA, 48], F32)
        S.op("pool", lambda e: e.memset(Va[:], 1.0), writes=["VaN"])
        ptN = [P.t(f"ptN{i}", [128, C_GQ], F32) for i in range(1)]
        qn = [P.t(f"qnN{i}", [128, 16, 64], BF16) for i in range(2)]
        kn = [P.t(f"knN{i}", [128, 8, 64], BF16) for i in range(2)]
        for tt in range(NTA):
            b = tt % 2
            p_ = ptN[0]
            pk = ("ptN", 0)
            S.dma("sp", p_[:], proj[tt * 128:(tt + 1) * 128, 0:C_GQ], writes=[pk])
            headnormN(qn[b][:], p_[:, 0:1024].rearrange("p (h d) -> p h d", h=16), qkg[:, 0, :], 16, [pk], [("qnN", b)], sc=0.125)
            headnormN(kn[b][:, 0:4], p_[:, C_SEL:C_SEL + 256].rearrange("p (h d) -> p h d", h=4), qkg[:, 2, :], 4, [pk], [("knN", b, 0)])
            headnormN(kn[b][:, 4:8], p_[:, C_WIN:C_WIN + 256].rearrange("p (h d) -> p h d", h=4), qkg[:, 3, :], 4, [pk], [("knN", b, 1)])
            S.op("act", lambda e, p_=p_, tt=tt: e.copy(out=Va[:, tt, 0:4, 0:64], in_=p_[:, C_SEL + 256:C_SEL + 512].rearrange("p (g d) -> p g d", g=4)),
                 reads=[pk, "VaN"], writes=[("VaN", tt, 0)])
            S.op("act", lambda e, p_=p_, tt=tt: e.copy(out=Va[:, tt, 4:8, 0:64], in_=p_[:, C_WIN + 256:C_WIN + 512].rearrange("p (g d) -> p g d", g=4)),
                 reads=[pk, "VaN"], writes=[("VaN", tt, 1)])
            S.op("act", lambda e, p_=p_, tt=tt: e.activation(out=gts[:, tt, :], in_=p_[:, C_GATE:C_GATE + 48], func=AF.Sigmoid),
                 reads=[pk], writes=[("gtsN", tt)])
            for half in range(2):
                for a in range(8):
                    S.op("pe", lambda e, a=a, b=b, half=half: e.transpose(out=PB[half][0:64, a * 128:(a + 1) * 128], in_=qn[b][:, half * 8 + a, :],
                                                                        identity=ident_b[:]), reads=[("qnN", b)], writes=[("pb", half)])
                evac(qT[:, half * 8:(half + 1) * 8, tt * 128:(tt + 1) * 128], PB[half][0:64, 0:1024].rearrange("p (a t) -> p a t", a=8),
                     reads=[("pb", half)], writes=[("qTN", tt, half)])
            for a in range(8):
                S.op("pe", lambda e, a=a, b=b: e.transpose(out=PB[0][0:64, a * 128:(a + 1) * 128], in_=kn[b][:, a, :], identity=ident_b[:]),
                     reads=[("knN", b, 0), ("knN", b, 1)], writes=[("pb", 0)])
            evac(kT[:, :, tt * 128:(tt + 1) * 128], PB[0][0:64, 0:1024].rearrange("p (a t) -> p a t", a=8), reads=[("pb", 0)], writes=[("kTN", tt)])

        S.op("dve", lambda e: e.tensor_copy(out=q_s[:], in_=qT[:, :, T]), reads=[("qTN", NT, 0), ("qTN", NT, 1)], writes=["q_s"])
        S.op("dve", lambda e: e.tensor_copy(out=k_s[:], in_=kT[:, :, T]), reads=[("kTN", NT)], writes=["k_s"])
        S.op("dve", lambda e: e.tensor_copy(out=v_s[:], in_=Va[0:1, NT, :, :]), reads=[("VaN", NT, 0), ("VaN", NT, 1), "VaN"], writes=["v_s"])
        S.op("dve", lambda e: e.tensor_copy(out=g_s[:], in_=gts[0:1, NT, :]), reads=[("gtsN", NT)], writes=["g_s"])
        bc = P.t("bcN", [128, 16, NBP], F32)
        dI = P.t("dIN", [128, NBP], F32)
        pen = P.t("penN", [128, NBP], F32)
        ec = P.t("ecN", [128, 16, NBP], F32)
        pb16 = P.t("pb16N", [128, 16, NBP], BF16)
        sm = P.t("smN", [128, 64], F32)
        scg = P.t("scgN", [128, 4, NBP], F32)
        adj = P.t("adjN", [128, NBP], F32)
        adj2 = P.t("adj2N", [128, NBP], F32)
        vld = P.t("vldN", [128, NBP], F32)
        m8 = P.t("m8N", [128, 4, 16], F32)
        scw = P.t("scwN", [128, 4, NBP], F32)
        selp = P.t("selpN", [128, 4, NBP], BF16)
        penT = P.t("penTN", [max(NBP, 2), 4, 128], BF16)
        pT = P.t("pTN", [max(NBP, 2), 16, 128], BF16)
        oacc = P.t("oaccN", [128, 16, 64], F32)
        otmp = P.t("otmpN", [128, 4, 64], F32)
        wv = P.t("wvN", [128, 8], F32)
        PT = [P.t(f"PTN{i}", [128, 4, 128], BF16) for i in range(2)]
        sti = [0]

        def attend(i, g, kbase, vbase, tiles, use_pen, gate_idx):
            acc = PS[4]
            nj = len(tiles)
            S.op("dve", lambda e: e.memset(acc[:, 0:260], 0.0), writes=[("ps", 4)])
            base = sti[0]
            sti[0] += nj

            def qk(jn):
                j = tiles[jn]
                sb = (base + jn) % 2
                st = PS[2 + sb]
                stk = ("ps", 2 + sb)
                dl = i - j
                extra = []
                if use_pen:
                    extra.append((E_all[0:NBP, j * 128:(j + 1) * 128], penT[0:NBP, g, :].unsqueeze(1).to_broadcast([NBP, 4, 128]), ["penTN"]))
                if dl == 0:
                    extra.append((ident_b[:], Mc_b[:].unsqueeze(1).to_broadcast([128, 4, 128]), []))
                if (not use_pen) and dl == 4:
                    extra.append((ident_b[:], Mw_b[:].unsqueeze(1).to_broadcast([128, 4, 128]), []))
                ne = len(extra)
                for r in range(4):
                    h = 4 * g + r
                    sl = slice(r * 128, (r + 1) * 128)
                    S.op("pe", lambda e, st=st, sl=sl, j=j, h=h, r=r, ne=ne: e.matmul(
                        st[:, sl], lhsT=kT[:, kbase + g, j * 128:(j + 1) * 128], rhs=qT[:, h, i * 128:(i + 1) * 128],
                        start=(r == 0), stop=(ne == 0 and r == 3), skip_group_check=True),
                        reads=[("kTN", j), ("qTN", i, h // 8)], writes=[stk])
                for xi, (lh, rh, rk_) in enumerate(extra):
                    S.op("pe", lambda e, st=st, lh=lh, rh=rh, last=(xi == ne - 1): e.matmul(
                        st[:, 0:512].rearrange("p (r q) -> p r q", r=4), lhsT=lh, rhs=rh, start=False, stop=last, skip_group_check=True),
                        reads=rk_, writes=[stk])

            def ex_pv(jn):
                j = tiles[jn]
                sb = (base + jn) % 2
                st = PS[2 + sb]
                stk = ("ps", 2 + sb)
                ptile = PT[sb]
                dl = i - j
                for r in range(4):
                    h = 4 * g + r
                    S.op("act", lambda e, st=st, ptile=ptile, r=r, h=h, dl=dl: e.activation(
                        out=ptile[:, r, :], in_=st[:, r * 128:(r + 1) * 128], func=AF.Exp, bias=bcol[:, h, dl:dl + 1], scale=1.0),
                        reads=[stk], writes=[("PTN", sb, r)])
                for r in range(4):
                    S.op("pe", lambda e, ptile=ptile, r=r, j=j, jn=jn: e.matmul(
                        acc[:, r * 65:(r + 1) * 65], lhsT=ptile[:, r, :], rhs=Va[:, j, vbase + g, :], start=False, stop=(jn == nj - 1),
                        skip_group_check=True),
                        reads=[("PTN", sb, r), ("VaN", j, vbase // 4), "VaN"], writes=[("ps", 4)])

            qk(0)
            for jn in range(nj):
                if jn + 1 < nj:
                    qk(jn + 1)
                ex_pv(jn)
            a3 = acc[:, 0:260].rearrange("p (r c) -> p r c", r=4)
            S.op("dve", lambda e: e.reciprocal(out=wv[:, 0:4], in_=a3[:, :, 64]), reads=[("ps", 4)], writes=["wv0"])
            S.op("dve", lambda e: e.tensor_tensor(out=wv[:, 4:8], in0=wv[:, 0:4], in1=gts[:, i, gate_idx * 16 + 4 * g:gate_idx * 16 + 4 * g + 4], op=ALU.mult),
                 reads=["wv0", ("gtsN", i)], writes=["wv1"])
            S.op("dve", lambda e: e.tensor_tensor(out=otmp[:], in0=a3[:, :, 0:64], in1=wv[:, 4:8].unsqueeze(2).to_broadcast([128, 4, 64]), op=ALU.mult),
                 reads=[("ps", 4), "wv1"], writes=["otmpN"])
            S.op("dve", lambda e: e.tensor_tensor(out=oacc[:, 4 * g:4 * g + 4, :], in0=oacc[:, 4 * g:4 * g + 4, :], in1=otmp[:], op=ALU.add),
                 reads=["otmpN", "oaccN"], writes=["oaccN"])

        for i in range(NT):
            qr = [("qTN", i, 0), ("qTN", i, 1)]
            for h in range(16):
                S.op("pe", lambda e, h=h: e.matmul(PS[0][:, h * NBP:(h + 1) * NBP], lhsT=qT[:, h, i * 128:(i + 1) * 128], rhs=kcT_p[:, h // 4, :],
                                                   start=True, stop=True), reads=qr + ["kcT"], writes=[("ps", 0)])
            S.op("dve", lambda e: e.tensor_scalar(out=dI[:], in0=Dc[:], scalar1=float(128 * i), scalar2=None, op0=ALU.add), writes=["dIN"])
            S.op("dve", lambda e: e.tensor_scalar(out=pen[:], in0=dI[:], scalar1=0.0, scalar2=BIGN, op0=ALU.is_lt, op1=ALU.mult), reads=["dIN"], writes=["penN"])
            S.op("dve", lambda e: e.tensor_tensor(out=bc[:], in0=negsl[:], in1=dI[:].unsqueeze(1).to_broadcast([128, 16, NBP]), op=ALU.mult),
                 reads=["dIN"], writes=["bcN"])
            S.op("dve", lambda e: e.tensor_tensor(out=bc[:], in0=bc[:], in1=pen[:].unsqueeze(1).to_broadcast([128, 16, NBP]), op=ALU.add),
                 reads=["bcN", "penN"], writes=["bcN"])
            S.op("dve", lambda e: e.tensor_tensor(out=ec[:], in0=PS[0][:, 0:16 * NBP].rearrange("p (h n) -> p h n", h=16), in1=bc[:], op=ALU.add),
                 reads=[("ps", 0), "bcN"], writes=["ecN"])
            S.op("act", lambda e: e.activation(out=ec[:], in_=ec[:], func=AF.Exp), reads=["ecN"], writes=["ecN"])
            S.op("dve", lambda e: e.tensor_reduce(out=sm[:, 0:16], in_=ec[:], axis=AX.X, op=ALU.add), reads=["ecN"], writes=["sm0"])
            S.op("dve", lambda e: e.tensor_scalar(out=sm[:, 16:32], in0=sm[:, 0:16], scalar1=1e-30, scalar2=None, op0=ALU.max), reads=["sm0"], writes=["sm1"])
            S.op("dve", lambda e: e.reciprocal(out=sm[:, 32:48], in_=sm[:, 16:32]), reads=["sm1"], writes=["sm2"])
            S.op("dve", lambda e: e.tensor_tensor(out=ec[:], in0=ec[:], in1=sm[:, 32:48].unsqueeze(2).to_broadcast([128, 16, NBP]), op=ALU.mult),
                 reads=["ecN", "sm2"], writes=["ecN"])
            S.op("act", lambda e: e.copy(out=pb16[:], in_=ec[:]), reads=["ecN"], writes=["pb16N"])
            S.op("dve", lambda e: e.tensor_reduce(out=scg[:], in_=ec[:].rearrange("p (g r) n -> p g n r", g=4), axis=AX.X, op=ALU.add),
                 reads=["ecN"], writes=["scgN"])
            S.op("dve", lambda e: e.tensor_scalar(out=adj[:], in0=Dblk_i[:], scalar1=float(128 * i), scalar2=None, op0=ALU.add), writes=["adjN"])
            S.op("dve", lambda e: e.tensor_scalar(out=vld[:], in0=adj[:], scalar1=0.0, scalar2=None, op0=ALU.is_ge), reads=["adjN"], writes=["vldN"])
            S.op("dve", lambda e: e.tensor_scalar(out=adj2[:], in0=adj[:], scalar1=128.0, scalar2=None, op0=ALU.is_lt), reads=["adjN"], writes=["adj2N"])
            S.op("dve", lambda e: e.tensor_tensor(out=adj2[:], in0=adj2[:], in1=vld[:], op=ALU.mult), reads=["adj2N", "vldN"], writes=["adj2N"])
            S.op("dve", lambda e: e.tensor_scalar(out=adj2[:, 0:1], in0=adj2[:, 0:1], scalar1=1.0, scalar2=None, op0=ALU.max), reads=["adj2N"], writes=["adj2N"])
            S.op("dve", lambda e: e.tensor_scalar(out=adj[:], in0=vld[:], scalar1=-1.0, scalar2=2.0e4, op0=ALU.add, op1=ALU.mult), reads=["vldN"], writes=["adjN"])
            S.op("dve", lambda e: e.scalar_tensor_tensor(out=adj[:], in0=adj2[:], scalar=1.0e4, in1=adj[:], op0=ALU.mult, op1=ALU.add),
                 reads=["adj2N", "adjN"], writes=["adjN"])
            S.op("dve", lambda e: e.tensor_tensor(out=scg[:], in0=scg[:], in1=adj[:].unsqueeze(1).to_broadcast([128, 4, NBP]), op=ALU.add),
                 reads=["scgN", "adjN"], writes=["scgN"])
            if NBP > 16:
                for g in range(4):
                    S.op("dve", lambda e, g=g: e.max(out=m8[:, g, 0:8], in_=scg[:, g, :]), reads=["scgN"], writes=[("m8N", g)])
                    S.op("dve", lambda e, g=g: e.match_replace(out=scw[:, g, :], in_to_replace=m8[:, g, 0:8], in_values=scg[:, g, :], imm_value=-1.0e9),
                         reads=["scgN", ("m8N", g)], writes=[("scwN", g)])
                    S.op("dve", lambda e, g=g: e.max(out=m8[:, g, 8:16], in_=scw[:, g, :]), reads=[("scwN", g)], writes=[("m8bN", g)])
                    S.op("dve", lambda e, g=g: e.tensor_scalar(out=scw[:, g, :], in0=scg[:, g, :], scalar1=m8[:, g, 15:16], scalar2=None, op0=ALU.is_ge),
                         reads=["scgN", ("m8bN", g), ("scwN", g)], writes=[("scwN", g)])
                S.op("dve", lambda e: e.tensor_tensor(out=scw[:], in0=scw[:], in1=vld[:].unsqueeze(1).to_broadcast([128, 4, NBP]), op=ALU.mult),
                     reads=[("scwN", g) for g in range(4)] + ["vldN"], writes=["scwA"])
            else:
                S.op("dve", lambda e: e.tensor_copy(out=scw[:], in_=vld[:].unsqueeze(1).to_broadcast([128, 4, NBP])), reads=["vldN"], writes=["scwA"])
            S.op("dve", lambda e: e.tensor_scalar(out=selp[:], in0=scw[:], scalar1=-1.0, scalar2=-BIGN, op0=ALU.add, op1=ALU.mult),
                 reads=["scwA"], writes=["selpN"])
            for g in range(4):
                S.op("pe", lambda e, g=g: e.transpose(out=PB[0][0:NBP, g * 128:(g + 1) * 128], in_=selp[:, g, :], identity=ident_b[:]),
                     reads=["selpN"], writes=[("pb", 0)])
            S.op("act", lambda e: e.copy(out=penT[0:NBP], in_=PB[0][0:NBP, 0:512].rearrange("p (g t) -> p g t", g=4)), reads=[("pb", 0)], writes=["penTN"])
            for half in range(2):
                for a in range(8):
                    S.op("pe", lambda e, a=a, half=half: e.transpose(out=PB[1][0:NBP, a * 128:(a + 1) * 128], in_=pb16[:, half * 8 + a, :], identity=ident_b[:]),
                         reads=["pb16N"], writes=[("pb", 1)])
                S.op("dve", lambda e, half=half: e.tensor_copy(out=pT[0:NBP, half * 8:(half + 1) * 8, :],
                                                              in_=PB[1][0:NBP, 0:1024].rearrange("p (a t) -> p a t", a=8)),
                     reads=[("pb", 1)], writes=[("pTN", half)])
            for half in range(2):
                for a in range(8):
                    h = half * 8 + a
                    S.op("pe", lambda e, a=a, h=h, half=half: e.matmul(PS[half][:, a * 64:(a + 1) * 64], lhsT=pT[0:NBP, h, :], rhs=vc_p[0:NBP, h // 4, :],
                                                                       start=True, stop=True), reads=[("pTN", half), "vc"], writes=[("ps", half)])
                S.op("dve", lambda e, half=half: e.tensor_tensor(out=oacc[:, half * 8:(half + 1) * 8, :],
                                                                in0=PS[half][:, 0:512].rearrange("p (a d) -> p a d", a=8),
                                                                in1=gts[:, i, half * 8:(half + 1) * 8].unsqueeze(2).to_broadcast([128, 8, 64]), op=ALU.mult),
                     reads=[("ps", half), ("gtsN", i), "oaccN"], writes=["oaccN"])
            NSA_BR = int(os.environ.get("NSA_BR", "7"))
            if not (NSA_BR & 1):
                S.op("dve", lambda e: e.memset(oacc[:], 0.0), reads=["oaccN"], writes=["oaccN"])
            for g in range(4):
                if NSA_BR & 2:
                    attend(i, g, 0, 0, list(range(0, i + 1)), True, 1)
                if NSA_BR & 4:
                    attend(i, g, 4, 4, list(range(max(0, i - 4), i + 1)), False, 2)
            S.store("sp", omix[i * 128:(i + 1) * 128, 0:1024], oacc[:].rearrange("p h d -> p (h d)"), reads=["oaccN"])
        S.barrier()
        P.release()

        P = Pool(nc)
        bS = P.t("bS3", [4, NBS], F32)
        eS = P.t("eS3", [4, NBS], F32)
        sm3 = P.t("sm3", [4, 8], F32)
        srow = P.t("srow3", [1, 4, NBS], F32)
        srw = P.t("srw3", [1, 4, NBS], F32)
        m83 = P.t("m83", [1, 4, 16], F32)
        penr = P.t("penr3", [1, 4, NBS], BF16)
        penE = P.t("penE3", [128, NPG, 4], F32)
        pTs = P.t("pTs3", [SEGB, NSEG, 4], BF16)
        ocmp = P.t("ocmp3", [4, 4, 64], F32)
        osw = P.t("osw3", [4, 2, 4, 64], F32)
        gS = P.t("gS3", [4, 12], F32)
        wr = P.t("wr3", [4, 16], F32)
        pg = [P.t(f"pg3{i}", [128, 512], F32) for i in range(3)]
        kTs = [P.t(f"kTs3{i}", [64, 4, 128], BF16) for i in range(2)]
        Vs = [P.t(f"Vs3{i}", [128, 4, 65], BF16) for i in range(2)]
        b16 = [P.t(f"b163{i}", [128, 16], F32) for i in range(2)]
        sc16 = [P.t(f"sc163{i}", [128, 16], F32) for i in range(2)]
        PTs = [P.t(f"PTs3{i}", [128, 16], BF16) for i in range(2)]
        for i in range(2):
            S.op("pool", lambda e, i=i: e.memset(Vs[i][:], 1.0), writes=[("Vs3", i)])
        for bg in range(12):
            S.op("pe", lambda e, bg=bg: e.matmul(PS[5][0:4, bg:bg + 1], lhsT=g_s[0:1, (bg // 4) * 16 + (bg % 4) * 4:(bg // 4) * 16 + (bg % 4) * 4 + 4],
                                                 rhs=ones_f[0:1, 0:1], start=True, stop=True), reads=["g_s"], writes=[("ps", 5)])
        S.op("act", lambda e: e.copy(out=gS[:], in_=PS[5][0:4, 0:12]), reads=[("ps", 5)], writes=["gS3"])
        for g in range(4):
            S.op("pe", lambda e, g=g: e.matmul(PS[0][0:4, 0:NBS], lhsT=q_s[:, 4 * g:4 * g + 4], rhs=kcT_s[:, g, :], start=True, stop=True),
                 reads=["q_s", "kcT"], writes=[("ps", 0)])
            S.op("dve", lambda e, g=g: e.tensor_scalar(out=bS[:], in0=dcs[:], scalar1=slg[:, g:g + 1], scalar2=None, op0=ALU.mult), writes=["bS3"])
            S.op("dve", lambda e: e.tensor_tensor(out=eS[:], in0=PS[0][0:4, 0:NBS], in1=bS[:], op=ALU.add), reads=[("ps", 0), "bS3"], writes=["eS3"])
            S.op("act", lambda e: e.activation(out=eS[:], in_=eS[:], func=AF.Exp), reads=["eS3"], writes=["eS3"])
            S.op("dve", lambda e: e.tensor_reduce(out=sm3[:, 0:1], in_=eS[:], axis=AX.X, op=ALU.add), reads=["eS3"], writes=["sm30"])
            S.op("dve", lambda e: e.tensor_scalar(out=sm3[:, 1:2], in0=sm3[:, 0:1], scalar1=1e-30, scalar2=None, op0=ALU.max), reads=["sm30"], writes=["sm31"])
            S.op("dve", lambda e: e.reciprocal(out=sm3[:, 2:3], in_=sm3[:, 1:2]), reads=["sm31"], writes=["sm32"])
            S.op("dve", lambda e: e.tensor_scalar(out=eS[:], in0=eS[:], scalar1=sm3[:, 2:3], scalar2=None, op0=ALU.mult), reads=["eS3", "sm32"], writes=["eS3"])
            S.op("pe", lambda e: e.matmul(PS[1][0:1, 0:NBS], lhsT=ones_f[0:4, 0:1], rhs=eS[:], start=True, stop=True), reads=["eS3"], writes=[("ps", 1)])
            S.op("act", lambda e, g=g: e.copy(out=srow[0:1, g, :], in_=PS[1][0:1, 0:NBS]), reads=[("ps", 1)], writes=[("srow3", g)])
            for sg in range(NSEG):
                S.op("pe", lambda e, sg=sg: e.transpose(out=PS[2][0:SEGB, sg * 4:(sg + 1) * 4], in_=eS[0:4, sg * SEGB:(sg + 1) * SEGB],
                                                        identity=ident_f[0:4, 0:4]), reads=["eS3"], writes=[("ps", 2)])
            S.op("dve", lambda e: e.tensor_copy(out=pTs[:].rearrange("p s r -> p (s r)"), in_=PS[2][0:SEGB, 0:NSEG * 4]), reads=[("ps", 2)], writes=["pTs3"])
            for sg in range(NSEG):
                S.op("pe", lambda e, sg=sg, g=g: e.matmul(PS[3][0:4, g * 64:(g + 1) * 64], lhsT=pTs[:, sg, :], rhs=vc_s[0:SEGB, sg, g, :],
                                                         start=(sg == 0), stop=(sg == NSEG - 1)), reads=["pTs3", "vc"], writes=[("ps", 3)])
        S.op("act", lambda e: e.copy(out=ocmp[:].rearrange("p g d -> p (g d)"), in_=PS[3][0:4, 0:256]), reads=[("ps", 3)], writes=["ocmp3"])
        sr = [("srow3", g) for g in range(4)]
        S.op("dve", lambda e: e.memset(srow[0:1, :, 0:1], 1.0e4), reads=sr, writes=["srowA"])
        S.op("dve", lambda e: e.memset(srow[0:1, :, NBS - 1:NBS], 1.0e4), reads=sr, writes=["srowB"])
        srk = sr + ["srowA", "srowB"]
        for g in range(4):
            S.op("dve", lambda e, g=g: e.max(out=m83[0:1, g, 0:8], in_=srow[0:1, g, :]), reads=srk, writes=[("m83", g)])
            S.op("dve", lambda e, g=g: e.match_replace(out=srw[0:1, g, :], in_to_replace=m83[0:1, g, 0:8], in_values=srow[0:1, g, :], imm_value=-1.0e9),
                 reads=srk + [("m83", g)], writes=[("srw3", g)])
            S.op("dve", lambda e, g=g: e.max(out=m83[0:1, g, 8:16], in_=srw[0:1, g, :]), reads=[("srw3", g)], writes=[("m83b", g)])
            S.op("dve", lambda e, g=g: e.tensor_scalar(out=srw[0:1, g, :], in0=srow[0:1, g, :], scalar1=m83[0:1, g, 14:15], scalar2=None, op0=ALU.is_ge),
                 reads=srk + [("m83b", g), ("srw3", g)], writes=[("srw3", g)])
        S.op("dve", lambda e: e.tensor_scalar(out=penr[:], in0=srw[:], scalar1=-1.0, scalar2=-BIGN, op0=ALU.add, op1=ALU.mult),
             reads=[("srw3", g) for g in range(4)], writes=["penr3"])
        for g in range(4):
            for k in range(2):
                S.op("pe", lambda e, g=g, k=k: e.matmul(PS[4][:, g * NPG:(g + 1) * NPG], lhsT=half_sel[0:1, k, :],
                                                       rhs=penr[0:1, g, :].rearrange("p (j k) -> p j k", k=2)[:, :, k],
                                                       start=(k == 0), stop=(k == 1)), reads=["penr3"], writes=[("ps", 4)])
        S.op("dve", lambda e: e.tensor_copy(out=penE[:], in_=PS[4][:, 0:4 * NPG].rearrange("p (g j) -> p j g", g=4)), reads=[("ps", 4)], writes=["penE3"])

        def key_pass(br, ntile, load_fn, bias_tab, use_pen, knew, vnew):
            acc = PS[5]
            S.op("dve", lambda e: e.memset(acc[0:4, 0:260], 0.0), writes=[("ps", 5)])
            for j in range(ntile):
                b2, b3 = j % 2, j % 3
                load_fn(j, pg[b3], ("pg3", b3))
                for g in range(4):
                    S.op("pe", lambda e, g=g, b3=b3: e.transpose(out=PS[b2][0:64, g * 128:(g + 1) * 128], in_=pg[b3][:, g * 64:(g + 1) * 64], identity=ident_f[:]),
                         reads=[("pg3", b3)], writes=[("ps", b2)])
                evac(kTs[b2][:].rearrange("p g t -> p (g t)"), PS[b2][0:64, 0:512], reads=[("ps", b2)], writes=[("kTs3", b2)])
                S.op("act", lambda e, b2=b2, b3=b3: e.copy(out=Vs[b2][:, :, 0:64], in_=pg[b3][:, 256:512].rearrange("p (g d) -> p g d", g=4)),
                     reads=[("pg3", b3), ("Vs3", b2)], writes=[("Vs3v", b2)])
                for g in range(4):
                    S.op("pe", lambda e, g=g, b2=b2: e.matmul(PS[2 + b2][:, g * 4:(g + 1) * 4], lhsT=kTs[b2][:, g, :], rhs=q_s[:, 4 * g:4 * g + 4],
                                                             start=True, stop=True), reads=[("kTs3", b2), "q_s"], writes=[("ps", 2 + b2)])
                S.op("dve", lambda e, b2=b2, j=j: e.tensor_scalar(out=b16[b2][:], in0=negsl16[:], scalar1=bias_tab[:, j:j + 1], scalar2=None, op0=ALU.mult),
                     writes=[("b163", b2)])
                if (not use_pen) and j == 0:
                    S.op("dve", lambda e, b2=b2: e.tensor_scalar(out=b16[b2][:], in0=b16[b2][:], scalar1=mw0[:, 0:1], scalar2=None, op0=ALU.add),
                         reads=[("b163", b2)], writes=[("b163", b2)])
                S.op("dve", lambda e, b2=b2: e.tensor_tensor(out=sc16[b2][:], in0=PS[2 + b2][:, 0:16], in1=b16[b2][:], op=ALU.add),
                     reads=[("ps", 2 + b2), ("b163", b2)], writes=[("sc163", b2)])
                if use_pen:
                    S.op("dve", lambda e, b2=b2, j=j: e.tensor_tensor(out=sc16[b2][:].rearrange("p (g r) -> p g r", g=4),
                                                                       in0=sc16[b2][:].rearrange("p (g r) -> p g r", g=4),
                                                                       in1=penE[:, j, :].unsqueeze(2).to_broadcast([128, 4, 4]), op=ALU.add),
                         reads=[("sc163", b2), "penE3"], writes=[("sc163", b2)])
                S.op("act", lambda e, b2=b2: e.activation(out=PTs[b2][:], in_=sc16[b2][:], func=AF.Exp), reads=[("sc163", b2)], writes=[("PTs3", b2)])
                for g in range(4):
                    S.op("pe", lambda e, g=g, b2=b2: e.matmul(acc[0:4, g * 65:(g + 1) * 65], lhsT=PTs[b2][:, 4 * g:4 * g + 4], rhs=Vs[b2][:, g, :],
                                                             start=False, stop=False, skip_group_check=True),
                         reads=[("PTs3", b2), ("Vs3v", b2), ("Vs3", b2)], writes=[("ps", 5)])
            for g in range(4):
                S.op("pe", lambda e, g=g: e.matmul(PS[2][0:1, g * 4:(g + 1) * 4], lhsT=k_s[:, knew + g:knew + g + 1], rhs=q_s[:, 4 * g:4 * g + 4],
                                                   start=True, stop=True), reads=["k_s", "q_s"], writes=[("ps", 2)])
            S.op("act", lambda e: e.activation(out=PTs[0][0:1, :], in_=PS[2][0:1, 0:16], func=AF.Exp), reads=[("ps", 2)], writes=[("PTs3", 0)])
            for g in range(4):
                S.op("pe", lambda e, g=g: e.matmul(acc[0:4, g * 65:(g + 1) * 65], lhsT=PTs[0][0:1, 4 * g:4 * g + 4], rhs=v_s[0:1, vnew + g, :],
                                                   start=False, stop=True, skip_group_check=True), reads=[("PTs3", 0), "v_s"], writes=[("ps", 5)])
            a3 = acc[0:4, 0:260].rearrange("p (g c) -> p g c", g=4)
            S.op("dve", lambda e: e.reciprocal(out=wr[:, 0:4], in_=a3[:, :, 64]), reads=[("ps", 5)], writes=["wr0"])
            S.op("dve", lambda e: e.tensor_tensor(out=wr[:, 4:8], in0=wr[:, 0:4], in1=gS[:, (1 + br) * 4:(2 + br) * 4], op=ALU.mult),
                 reads=["wr0", "gS3"], writes=["wr1"])
            S.op("dve", lambda e: e.tensor_tensor(out=osw[:, br], in0=a3[:, :, 0:64], in1=wr[:, 4:8].unsqueeze(2).to_broadcast([4, 4, 64]), op=ALU.mult),
                 reads=[("ps", 5), "wr1"], writes=[("osw3", br)])

        def load_sel(j, dst, key):
            S.dma("pool", dst[:], cache_sel, reads=[("idx_i", l)], writes=[key],
                  indirect=bass.IndirectOffsetOnAxis(ap=idx_i[:, l, j:j + 1], axis=0))

        def load_win(j, dst, key):
            S.store("sp", dst[:], cache_win[l, j * 128:(j + 1) * 128, :], writes=[key])

        key_pass(0, NPG, load_sel, tb, True, 0, 0)
        key_pass(1, 4, load_win, tbw, False, 4, 4)
        if dbg and l == 0:
            S.dma("sp", dbg_sel, srw[:].rearrange("p g n -> p (g n)"), reads=[("srw3", g) for g in range(4)])
            S.dma("sp", dbg_srow, srow[:].rearrange("p g n -> p (g n)"), reads=srk)
            S.dma("sp", dbg_o[:, 0:256], ocmp[:].rearrange("p g d -> p (g d)"), reads=["ocmp3"])
            S.dma("sp", dbg_o[:, 256:768], osw[:].rearrange("p b g d -> p (b g d)"), reads=[("osw3", 0), ("osw3", 1)])
            S.dma("sp", dbg_gs, gS[:], reads=["gS3"])
        S.op("dve", lambda e: e.tensor_tensor(out=ocmp[:], in0=ocmp[:], in1=gS[:, 0:4].unsqueeze(2).to_broadcast([4, 4, 64]), op=ALU.mult),
             reads=["ocmp3", "gS3"], writes=["ocmp3"])
        NSA_BR3 = int(os.environ.get("NSA_BR", "7"))
        if not (NSA_BR3 & 1):
            S.op("dve", lambda e: e.memset(ocmp[:], 0.0), reads=["ocmp3"], writes=["ocmp3"])
        if NSA_BR3 & 2:
            S.op("dve", lambda e: e.tensor_tensor(out=ocmp[:], in0=ocmp[:], in1=osw[:, 0], op=ALU.add), reads=["ocmp3", ("osw3", 0)], writes=["ocmp3"])
        if NSA_BR3 & 4:
            S.op("dve", lambda e: e.tensor_tensor(out=ocmp[:], in0=ocmp[:], in1=osw[:, 1], op=ALU.add), reads=["ocmp3", ("osw3", 1)], writes=["ocmp3"])
        S.store("sp", omix[T:T + 1, 0:1024].rearrange("o (g r d) -> (o r) g d", g=4, r=4), ocmp[:], reads=["ocmp3"])
        S.barrier()
        P.release()
        LP.release()

    def mixers(l):
        for tt in range(NTA):
            S.store("sp", omix[tt * 128:(tt + 1) * 128, :], zeros_f[:, 0:2048])
        S.barrier()
        Pg, gch, gfin = gla_mixer(l)
        Pr, rch, rfin = rw_mixer(l)

        def segs(items):
            out = [[]]
            for it_ in items:
                if it_[0] == "sp":
                    if out[-1]:
                        out.append([])
                else:
                    out[-1].append(it_)
            return [s_ for s_ in out if s_]

        for c in range(NCH):
            S.cap = []
            rch(c)
            A = segs(S.cap)
            S.cap = []
            gch(c)
            B = segs(S.cap)
            S.cap = None
            na, nb_ = sum(len(x) for x in A), sum(len(x) for x in B)
            i = j = ca = cb_ = 0
            while i < len(A) or j < len(B):
                if j >= len(B) or (i < len(A) and ca * nb_ <= cb_ * na):
                    S.replay(A[i]); ca += len(A[i]); i += 1
                else:
                    S.replay(B[j]); cb_ += len(B[j]); j += 1
        gfin()
        rfin()
        S.barrier()
        Pr.release()
        Pg.release()
        nsa_mixer(l)
        if dbg and l == 0:
            S.dma("sp", omix_l0, omix)
            S.barrier()

    for l in range(L):
        P = Pool(nc)
        hT = P.t("hT", [128, KD, TA], BF16)
        rmsnorm_T(P, xa, bcast_row(norm1_g[l], D), hT, f"n1")
        wt = [P.t(f"wA{i}", [128, KD, 512], BF16) for i in range(2)]
        stg = [P.t(f"stgA{i}", [128, 512], F32) for i in range(3)]
        w_in_v = w_in[l].rearrange("(ko ki) n -> ki ko n", ki=128)
        ncb = (NIN + 511) // 512
        it = 0
        for cb in range(ncb):
            c0 = cb * 512
            cw = min(512, NIN - c0)
            wb = wt[cb % 2]
            S.dma("pool", wb[:, :, 0:cw], w_in_v[:, :, c0:c0 + cw], writes=[("wA", cb % 2)])
            for tt in range(NTA):
                ps = PS[it % 4]
                for k in range(KD):
                    S.op("pe", lambda e, ps=ps, k=k, tt=tt, wb=wb, cw=cw: e.matmul(
                        ps[:, 0:cw], lhsT=hT[:, k, tt * 128:(tt + 1) * 128], rhs=wb[:, k, 0:cw],
                        start=(k == 0), stop=(k == KD - 1)),
                        reads=[("n1", "hT", tt, k // 4), ("wA", cb % 2)], writes=[("ps", it % 4)])
                sg = stg[it % 3]
                evac(sg[:, 0:cw], ps[:, 0:cw], reads=[("ps", it % 4)], writes=[("stgA", it % 3)])
                S.store("sp", proj[tt * 128:(tt + 1) * 128, c0:c0 + cw], sg[:, 0:cw], reads=[("stgA", it % 3)])
                it += 1
        S.barrier()
        P.release()

        P = Pool(nc)
        NB = C_MG
        pt = [P.t(f"ptB{i}", [128, NB], F32) for i in range(2)]
        qkg = P.t("qkg", [128, 4, 64], F32)
        S.dma("sp", qkg[:].rearrange("p a d -> p (a d)"), nsa_qk_g[l].rearrange("a d -> (a d)").partition_broadcast(128),
              writes=["qkg"])
        tmpB = P.t("tmpB", [128, 16, 64], F32)
        ssB = P.t("ssB", [128, 64], F32)
        rows = [P.t(f"rowsB{i}", [128, 2, 512], F32) for i in range(2)]

        def headnorm(out3, in3, g_ap, H, rk, wk, sc=1.0):
            S.op("dve", lambda e: e.tensor_tensor(out=tmpB[:, 0:H, :], in0=in3, in1=in3, op=ALU.mult),
                 reads=rk, writes=["tmpB"])
            S.op("dve", lambda e: e.tensor_reduce(out=ssB[:, 0:H], in_=tmpB[:, 0:H, :], axis=AX.X, op=ALU.add),
                 reads=["tmpB"], writes=["ssB0"])
            S.op("dve", lambda e: e.tensor_scalar(out=ssB[:, 16:16 + H], in0=ssB[:, 0:H], scalar1=1.0 / 64, scalar2=EPS,
                                                  op0=ALU.mult, op1=ALU.add), reads=["ssB0"], writes=["ssB1"])
            S.op("act", lambda e: e.activation(out=ssB[:, 48:48 + H], in_=ssB[:, 16:16 + H], func=AF.Ln),
                 reads=["ssB1"], writes=["ssB1b"])
            S.op("act", lambda e: e.activation(out=ssB[:, 32:32 + H], in_=ssB[:, 48:48 + H], func=AF.Exp, scale=-0.5,
                                               bias=float(np.log(sc))), reads=["ssB1b"], writes=["ssB2"])
            S.op("dve", lambda e: e.tensor_tensor(out=tmpB[:, 0:H, :], in0=in3,
                                                  in1=ssB[:, 32:32 + H].unsqueeze(2).to_broadcast([128, H, 64]), op=ALU.mult),
                 reads=rk + ["ssB2"], writes=["tmpB"])
            S.op("dve", lambda e: e.tensor_tensor(out=out3, in0=tmpB[:, 0:H, :],
                                                  in1=g_ap.unsqueeze(1).to_broadcast([128, H, 64]), op=ALU.mult),
                 reads=["tmpB", "qkg"], writes=wk)

        for tt in range(NTA):
            b = tt % 2
            ptb = pt[b]
            rw_ = rows[b]
            S.dma("sp", ptb[:], proj[tt * 128:(tt + 1) * 128, 0:NB], writes=[("ptB", b)])
            is_s = (tt == NT)
            if not is_s:
                S.store("sp", p_cmp_o[l, tt * 128:(tt + 1) * 128, :], ptb[:, C_CMP:C_CMP + 512], reads=[("ptB", b)])
            else:
                S.store("sp", s_cmp_o[l], ptb[0:1, C_CMP:C_CMP + 512], reads=[("ptB", b)])
            for wi, (c0, gi, po, so) in enumerate(((C_SEL, 2, p_sel_o, s_sel_o), (C_WIN, 3, p_win_o, s_win_o))):
                headnorm(rw_[:, wi, 0:256].rearrange("p (h d) -> p h d", h=4),
                         ptb[:, c0:c0 + 256].rearrange("p (h d) -> p h d", h=4), qkg[:, gi, :], 4,
                         [("ptB", b)], [("rowsB", b, wi, "k")])
                S.op("act", lambda e, rw_=rw_, ptb=ptb, wi=wi, c0=c0: e.copy(out=rw_[:, wi, 256:512], in_=ptb[:, c0 + 256:c0 + 512]),
                     reads=[("ptB", b)], writes=[("rowsB", b, wi, "v")])
                rk = [("rowsB", b, wi, "k"), ("rowsB", b, wi, "v")]
                if is_s:
                    S.store("sp", so[l], rw_[0:1, wi, :], reads=rk)
                elif wi == 0:
                    S.store("sp", po[l, tt * 128:(tt + 1) * 128, :], rw_[:, wi, :], reads=rk)
                else:
                    t0w = tt * 128 - (T - WKEEP)
                    if t0w >= 0:
                        S.store("sp", po[l, t0w:t0w + 128, :], rw_[:, wi, :], reads=rk)
            if tt == NT - 1:
                S.store("sp", p_sh_o[l:l + 1, :], ptb[127:128, C_RW:C_RW + RWP], reads=[("ptB", b)])
            if is_s:
                S.store("sp", s_sh_o[l:l + 1, :], ptb[0:1, C_RW:C_RW + RWP], reads=[("ptB", b)])
        S.barrier()
        P.release()

        mixers(l)

        P = Pool(nc)
        oT = P.t("oT", [128, 16, TA], BF16)
        rmsnorm_T(P, omix, None, oT, "oT", NC=2048)
        wt = [P.t(f"wC{i}", [128, 16, 512], BF16) for i in range(2)]
        mg = [P.t(f"mgC{i}", [128, 3, 512], F32) for i in range(2)]
        acc = [P.t(f"accC{i}", [128, 512], F32) for i in range(2)]
        tmpc = P.t("tmpC", [128, 512], F32)
        ups = ((nsa_up, 0, 8), (gla_up, 8, 4), (rw_up, 12, 4))
        it = 0
        for nb in range(D // 512 if D >= 512 else 1):
            cw = min(512, D)
            c0 = nb * 512
            wb = wt[nb % 2]
            for (wsrc, k0, nk) in ups:
                S.dma("pool", wb[:, k0:k0 + nk, 0:cw], wsrc[l].rearrange("(ko ki) n -> ki ko n", ki=128)[:, :, c0:c0 + cw],
                      writes=[("wC", nb % 2, k0)])
            for tt in range(NTA):
                b = it % 2
                for j in range(3):
                    S.dma("sp", mg[b][:, j, 0:cw], proj[tt * 128:(tt + 1) * 128, C_MG + j * D + c0:C_MG + j * D + c0 + cw],
                          writes=[("mgC", b, j)])
                    S.op("act", lambda e, b=b, j=j: e.activation(out=mg[b][:, j, 0:cw], in_=mg[b][:, j, 0:cw], func=AF.Sigmoid),
                         reads=[("mgC", b, j)], writes=[("mgC", b, j)])
                for j, (wsrc, k0, nk) in enumerate(ups):
                    ps = PS[j + 3 * (it % 2)]
                    pk = ("ps", j + 3 * (it % 2))
                    for k in range(k0, k0 + nk):
                        S.op("pe", lambda e, ps=ps, k=k, tt=tt, wb=wb, k0=k0, nk=nk: e.matmul(
                            ps[:, 0:cw], lhsT=oT[:, k, tt * 128:(tt + 1) * 128], rhs=wb[:, k, 0:cw],
                            start=(k == k0), stop=(k == k0 + nk - 1)),
                            reads=[("oT", "hT", tt, k // 4), ("wC", nb % 2, k0)], writes=[pk])
                    if j == 0:
                        S.op("dve", lambda e, b=b, ps=ps: e.tensor_tensor(out=acc[b][:, 0:cw], in0=mg[b][:, 0, 0:cw], in1=ps[:, 0:cw],
                                                                           op=ALU.mult), reads=[("mgC", b, 0), pk], writes=[("accC", b)])
                    else:
                        S.op("dve", lambda e, b=b, ps=ps, j=j: e.tensor_tensor(out=tmpc[:, 0:cw], in0=mg[b][:, j, 0:cw], in1=ps[:, 0:cw],
                                                                                op=ALU.mult), reads=[("mgC", b, j), pk], writes=["tmpC"])
                        S.op("dve", lambda e, b=b: e.tensor_tensor(out=acc[b][:, 0:cw], in0=acc[b][:, 0:cw], in1=tmpc[:, 0:cw], op=ALU.add),
                             reads=["tmpC", ("accC", b)], writes=[("accC", b)])
                S.store("sp", mrg[tt * 128:(tt + 1) * 128, c0:c0 + cw], acc[b][:, 0:cw], reads=[("accC", b)])
                it += 1
        S.barrier()
        P.release()

        P = Pool(nc)
        mT = P.t("mT", [128, KD, TA], BF16)
        rmsnorm_T(P, mrg, None, mT, "mT", NC=D)
        wt = [P.t(f"wD{i}", [128, KD, 512], BF16) for i in range(2)]
        xr = [P.t(f"xD{i}", [128, 512], F32) for i in range(3)]
        w_v = w_out[l].rearrange("(ko ki) n -> ki ko n", ki=128)
        it = 0
        for nb in range(D // 512 if D >= 512 else 1):
            cw = min(512, D)
            c0 = nb * 512
            wb = wt[nb % 2]
            S.dma("pool", wb[:, :, 0:cw], w_v[:, :, c0:c0 + cw], writes=[("wD", nb % 2)])
            for tt in range(NTA):
                ps = PS[it % 4]
                xb_ = xr[it % 3]
                S.dma("sp", xb_[:, 0:cw], xa[tt * 128:(tt + 1) * 128, c0:c0 + cw], writes=[("xD", it % 3)])
                for k in range(KD):
                    S.op("pe", lambda e, ps=ps, k=k, tt=tt, wb=wb: e.matmul(
                        ps[:, 0:cw], lhsT=mT[:, k, tt * 128:(tt + 1) * 128], rhs=wb[:, k, 0:cw],
                        start=(k == 0), stop=(k == KD - 1)),
                        reads=[("mT", "hT", tt, k // 4), ("wD", nb % 2)], writes=[("ps", it % 4)])
                S.op("dve", lambda e, ps=ps, xb_=xb_: e.tensor_tensor(out=xb_[:, 0:cw], in0=xb_[:, 0:cw], in1=ps[:, 0:cw], op=ALU.add),
                     reads=[("ps", it % 4), ("xD", it % 3)], writes=[("xD", it % 3)])
                S.store("sp", xm[tt * 128:(tt + 1) * 128, c0:c0 + cw], xb_[:, 0:cw], reads=[("xD", it % 3)])
                it += 1
        S.barrier()
        P.release()

        P = Pool(nc)
        h2T = P.t("h2T", [128, KD, TA], BF16)
        rmsnorm_T(P, xm, bcast_row(norm2_g[l], D), h2T, "n2")
        wt = [P.t(f"wE{i}", [128, KD, 512], BF16) for i in range(2)]
        hst = [P.t(f"hstE{i}", [128, 4, 4, 128], BF16) for i in range(2)]
        rl = [P.t(f"rlE{i}", [128, 512], F32) for i in range(2)]
        w_v = mlp_w1[l].rearrange("(ko ki) n -> ki ko n", ki=128)
        chunks = [(c * 4, min(4, NTA - c * 4)) for c in range((NTA + 3) // 4)]
        it = 0
        ic = 0
        for fb in range(DFF // 512):
            wb = wt[fb % 2]
            S.dma("pool", wb[:], w_v[:, :, fb * 512:(fb + 1) * 512], writes=[("wE", fb % 2)])
            for (t0, ntl) in chunks:
                hs = hst[ic % 2]
                ntok = ntl * 128
                for fs in range(4):
                    ps = PS[it % 4]
                    for k in range(KD):
                        S.op("pe", lambda e, ps=ps, k=k, wb=wb, fs=fs, t0=t0, ntok=ntok: e.matmul(
                            ps[:, 0:ntok], lhsT=wb[:, k, fs * 128:(fs + 1) * 128], rhs=h2T[:, k, t0 * 128:t0 * 128 + ntok],
                            start=(k == 0), stop=(k == KD - 1)),
                            reads=[("n2", "hT", t0 + j, k // 4) for j in range(ntl)] + [("wE", fb % 2)], writes=[("ps", it % 4)])
                    r_ = rl[it % 2]
                    S.op("act", lambda e, ps=ps, r_=r_, ntok=ntok: e.activation(out=r_[:, 0:ntok], in_=ps[:, 0:ntok], func=AF.Relu),
                         reads=[("ps", it % 4)], writes=[("rlE", it % 2)])
                    S.op("dve", lambda e, hs=hs, r_=r_, fs=fs, ntl=ntl, ntok=ntok: e.tensor_tensor(
                        out=hs[:, 0:ntl, fs, :], in0=r_[:, 0:ntok].rearrange("p (a t) -> p a t", a=ntl),
                        in1=r_[:, 0:ntok].rearrange("p (a t) -> p a t", a=ntl), op=ALU.mult),
                        reads=[("rlE", it % 2)], writes=[("hstE", ic % 2, fs)])
                    it += 1
                for j in range(ntl):
                    S.store("sp", hidS[t0 + j, :, fb * 4:(fb + 1) * 4, :], hs[:, j, :, :], reads=[("hstE", ic % 2, fs) for fs in range(4)])
                ic += 1
        S.barrier()
        P.release()

        P = Pool(nc)
        NFO = DFF // 128
        FW = 512 if D >= 512 else 256
        wt = [P.t(f"wF{i}", [128, NFO, FW], BF16) for i in range(2)]
        ht = [P.t(f"htF{i}", [128, NFO, 128], BF16) for i in range(2)]
        xr = [P.t(f"xF{i}", [128, FW], F32) for i in range(3)]
        w_v = mlp_w2[l].rearrange("(fo fi) n -> fi fo n", fi=128)
        it = 0
        for db in range(D // FW):
            c0 = db * FW
            wb = wt[db % 2]
            for hf_ in range(2):
                S.dma("pool", wb[:, hf_ * (NFO // 2):(hf_ + 1) * (NFO // 2), :], w_v[:, hf_ * (NFO // 2):(hf_ + 1) * (NFO // 2), c0:c0 + FW],
                      writes=[("wF", db % 2, hf_)])
            for tt in range(NTA):
                hb = ht[it % 2]
                S.dma("sp", hb[:], hidS[tt], writes=[("htF", it % 2)])
                xb_ = xr[it % 3]
                S.dma("sp", xb_[:], xm[tt * 128:(tt + 1) * 128, c0:c0 + FW], writes=[("xF", it % 3)])
                ps = PS[it % 4]
                for fo in range(NFO):
                    S.op("pe", lambda e, ps=ps, fo=fo, hb=hb, wb=wb: e.matmul(
                        ps[:, 0:FW], lhsT=hb[:, fo, :], rhs=wb[:, fo, :], start=(fo == 0), stop=(fo == NFO - 1)),
                        reads=[("htF", it % 2), ("wF", db % 2, fo // (NFO // 2))], writes=[("ps", it % 4)])
                S.op("dve", lambda e, ps=ps, xb_=xb_: e.tensor_tensor(out=xb_[:], in0=xb_[:], in1=ps[:, 0:FW], op=ALU.add),
                     reads=[("ps", it % 4), ("xF", it % 3)], writes=[("xF", it % 3)])
                S.store("sp", xa[tt * 128:(tt + 1) * 128, c0:c0 + FW], xb_[:], reads=[("xF", it % 3)])
                it += 1
        S.barrier()
        P.release()
        S.store("sp", xa[T + 1:TA, :], zeros_f[0:127, 0:D])
        S.barrier()

    S.barrier()
    S.dma("sp", y_prompt, xa[0:T, :])
    S.dma("sp", y_sample, xa[T:T + 1, :])
    S.barrier()
    S.emit_all()
    global LAST_S
    LAST_S = S
    return nc, declared


FULL_CFG = dict(D=2048, T=2048, PAST=16384, NPOOL=1280, L=2)
WEIGHTS = ["norm1_g", "norm2_g", "w_in", "nsa_qk_g", "cmp_pos", "cmp_w1", "cmp_w2", "nsa_up", "gla_a2", "gla_a_b",
           "gla_norm_g", "gla_up", "rw_mu", "rw_w0", "rw_w2", "rw_a0", "rw_a2", "rw_g2", "rw_kk", "rw_ka", "rw_rk",
           "rw_ln_g", "rw_ln_b", "rw_up", "w_out", "mlp_w1", "mlp_w2"]


def input_names(nc):
    names = []
    for a in nc.allocations:
        pass
    return names


def make_in_maps(inp, cfg, ncores, declared):
    B = inp["x_prompt"].shape[0]
    SB = inp["x_sample"].shape[0]
    L = cfg["L"]
    maps = []
    f = lambda a: np.ascontiguousarray(a)
    shared = {}
    for k in WEIGHTS:
        if k in declared:
            shared[k] = f(inp[k])
    if "cache_cmp" in declared:
        shared["cache_cmp"] = f(inp["cache_cmp_kv"].reshape(L, -1, 512))
    if "cache_sel" in declared:
        shared["cache_sel"] = f(inp["cache_sel_kv"].reshape(L, -1, 512))
    for c in range(ncores):
        m = dict(shared)
        sc = c % SB
        m["x_prompt"] = f(inp["x_prompt"][c % B])
        m["x_sample"] = f(inp["x_sample"][sc])
        if "cache_win" in declared:
            m["cache_win"] = f(inp["cache_win_kv"][:, sc].reshape(L, -1, 512))
        if "state_gla" in declared:
            m["state_gla"] = f(inp["state_gla"][:, sc])
        if "state_rwkv" in declared:
            m["state_rwkv"] = f(inp["state_rwkv"][:, sc])
        if "state_shift" in declared:
            m["state_shift"] = f(inp["state_rwkv_shift"][:, sc])
        if "page_table" in declared:
            m["page_table"] = f(inp["page_table"][sc].astype(np.int32))
        maps.append({k: v for k, v in m.items() if k in declared})
    return maps


def assemble(res, cfg, B, SB):
    L, T, D = cfg["L"], cfg["T"], cfg["D"]
    WK = min(512, T)
    g = lambda c, k, shp: np.asarray(res[c][k], dtype=np.float32).reshape(shp) if k in res[c] else np.zeros(shp, np.float32)
    y_p = np.stack([g(b, "y_prompt", (T, D)) for b in range(B)])
    y_s = np.stack([g(c, "y_sample", (1, D)) for c in range(SB)])
    pk = lambda k, n: np.stack([g(b, k, (L, n, 2, 4, 64)) for b in range(B)], axis=1)
    sk = lambda k: np.stack([g(c, k, (L, 1, 2, 4, 64)) for c in range(SB)], axis=1)
    p_gla = np.stack([g(b, "p_gla", (L, 4, 64, 128)) for b in range(B)], axis=1)
    p_rw = np.stack([g(b, "p_rw", (L, 8, 64, 64)) for b in range(B)], axis=1)
    p_sh = np.stack([g(b, "p_sh", (L, RWP)) for b in range(B)], axis=1)
    s_gla = np.stack([g(c, "s_gla", (L, 4, 64, 128)) for c in range(SB)], axis=1)
    s_rw = np.stack([g(c, "s_rw", (L, 8, 64, 64)) for c in range(SB)], axis=1)
    s_sh = np.stack([g(c, "s_sh", (L, RWP)) for c in range(SB)], axis=1)
    return (y_p, y_s, pk("p_cmp", T), pk("p_sel", T), pk("p_win", WK), p_gla, p_rw, p_sh,
            sk("s_cmp"), sk("s_sel"), sk("s_win"), s_gla, s_rw, s_sh)


_CACHE = {}


def kernel(**inputs):
    cfg = FULL_CFG
    if "nc" not in _CACHE:
        _CACHE["nc"] = build(cfg)
    nc, declared = _CACHE["nc"]
    inp = {k: np.asarray(v) for k, v in inputs.items()}
    maps = make_in_maps(inp, cfg, 8, declared)
    res = run_bass_kernel_spmd(nc, maps, core_ids=list(range(8)))
    return assemble(res.results, cfg, inp["x_prompt"].shape[0], inp["x_sample"].shape[0])
```

```python
import os
import numpy as np
import concourse.bass as bass
import concourse.mybir as mybir
from concourse.bass_utils import run_bass_kernel_spmd

F32 = mybir.dt.float32
BF16 = mybir.dt.bfloat16
I32 = mybir.dt.int32
AF = mybir.ActivationFunctionType
ALU = mybir.AluOpType
AX = mybir.AxisListType

NQ = 1024
C_Q, C_CMP, C_SEL, C_WIN, C_GATE = 0, 1024, 1536, 2048, 2560
C_GQ, C_GK, C_GV, C_GA, C_GG, C_RW, C_MG = 2608, 2864, 3120, 3632, 3648, 4160, 5856
RWP = 1696
EPS = 1e-6


import types


def _freeze(fn):
    if fn.__closure__ is None:
        return fn
    cells = []
    for c in fn.__closure__:
        try:
            cells.append(types.CellType(c.cell_contents))
        except ValueError:
            cells.append(c)
    return types.FunctionType(fn.__code__, fn.__globals__, fn.__name__, fn.__defaults__, tuple(cells))


class Sched:
    def __init__(self, nc, n_dma_sems=8):
        self.nc = nc
        self.eng = {"pe": nc.tensor, "act": nc.scalar, "dve": nc.vector, "pool": nc.gpsimd, "sp": nc.sync}
        self.prog = {e: [] for e in self.eng}
        self.sem = {e: nc.alloc_semaphore("c_" + e) for e in self.eng}
        self.cnt = {e: 0 for e in self.eng}
        self.dsem = {e: [nc.alloc_semaphore(f"d_{e}{i}") for i in range(n_dma_sems)] for e in ("sp", "pool", "act")}
        self.dcnt = {}
        self.semobj = {}
        for e in self.dsem:
            for s in self.dsem[e]:
                self.dcnt[id(s)] = 0
                self.semobj[id(s)] = s
        for e in self.sem:
            self.semobj[id(self.sem[e])] = self.sem[e]
        self.drr = {e: 0 for e in self.dsem}
        self.seen = {e: {} for e in self.eng}
        self.bufs = {}
        self.n_wait = 0
        self.pending = []
        self.max_pending = 2
        self.cap = None

    def _deps(self, e, reads, writes):
        waits = {}

        def need(tok, pe_ok=False):
            if tok is None:
                return
            sem, val, src = tok
            if e == "pe" and src == "pe" and pe_ok:
                return
            k = id(sem)
            if self.seen[e].get(k, 0) >= val:
                return
            if waits.get(k, 0) < val:
                waits[k] = val

        for b in reads:
            st = self.bufs.get(b)
            if st:
                need(st["w"])
        for b in writes:
            st = self.bufs.get(b)
            if st:
                need(st["w"], True)
                for r in st["r"]:
                    need(r, True)
        for k, v in waits.items():
            self.seen[e][k] = v
        self.n_wait += len(waits)
        return [(self.semobj[k], v) for k, v in waits.items()]

    def _commit(self, tok, reads, writes):
        for b in reads:
            st = self.bufs.setdefault(b, {"w": None, "r": []})
            st["r"].append(tok)
            if len(st["r"]) > 40:
                best = {}
                for t in st["r"]:
                    k = id(t[0])
                    if k not in best or best[k][1] < t[1]:
                        best[k] = t
                st["r"] = list(best.values())
        for b in writes:
            self.bufs[b] = {"w": tok, "r": []}

    def _flush_one(self):
        q, out, in_, reads, writes, indirect, kw = self.pending.pop(0)
        self.dma(q, out, in_, reads=reads, writes=writes, indirect=indirect, _nocheck=True, **kw)

    def _flush_conflicts(self, reads, writes):
        if not self.pending:
            return
        ws, rs = set(writes), set(reads)
        idx = -1
        for i, p in enumerate(self.pending):
            pr, pw = set(p[3]), set(p[4])
            if (ws & (pr | pw)) or (rs & pw):
                idx = i
        for _ in range(idx + 1):
            self._flush_one()

    def flush(self):
        while self.pending:
            self._flush_one()

    def store(self, q, out, in_, **kw):
        self.dma(q, out, in_, defer=True, **kw)

    def sp(self):
        if self.cap is not None:
            self.cap.append(("sp",))

    def replay(self, items):
        for it in items:
            if it[0] == "sp":
                self.sp()
            elif it[0] == "op":
                self.op(it[1], it[2], reads=it[3], writes=it[4])
            else:
                _, q, out, in_, r, w, ind, defer, kw = it
                self.dma(q, out, in_, reads=r, writes=w, indirect=ind, defer=defer, **kw)

    def op(self, e, fn, reads=(), writes=()):
        fn = _freeze(fn)
        if self.cap is not None:
            self.cap.append(("op", e, fn, tuple(reads), tuple(writes)))
            return
        self._flush_conflicts(reads, writes)
        waits = self._deps(e, reads, writes)
        self.cnt[e] += 1
        sem = self.sem[e]
        tok = (sem, self.cnt[e], e)

        def emit(eng):
            for s, v in waits:
                eng.wait_ge(s, v)
            fn(eng).then_inc(sem, 1)

        self.prog[e].append(emit)
        self._commit(tok, reads, writes)

    def dma(self, q, out, in_, reads=(), writes=(), indirect=None, defer=False, _nocheck=False, **kw):
        if self.cap is not None:
            self.cap.append(("dma", q, out, in_, tuple(reads), tuple(writes), indirect, defer, kw))
            return
        if defer:
            self.pending.append((q, out, in_, tuple(reads), tuple(writes), indirect, kw))
            while len(self.pending) > self.max_pending:
                self._flush_one()
            return
        if not _nocheck:
            self._flush_conflicts(reads, writes)
        waits = self._deps(q, reads, writes)
        i = self.drr[q]
        self.drr[q] = (i + 1) % len(self.dsem[q])
        s = self.dsem[q][i]
        prev = self.dcnt[id(s)]
        self.dcnt[id(s)] = prev + 16
        tok = (s, prev + 16, "dma")
        if prev > 0 and self.seen[q].get(id(s), 0) < prev:
            waits.append((s, prev))
            self.seen[q][id(s)] = prev

        def emit(eng):
            for s_, v in waits:
                eng.wait_ge(s_, v)
            if indirect is not None:
                eng.indirect_dma_start(out=out, out_offset=None, in_=in_, in_offset=indirect, **kw).then_inc(s, 16)
            else:
                eng.dma_start(out=out, in_=in_, **kw).then_inc(s, 16)

        self.prog[q].append(emit)
        self._commit(tok, reads, writes)

    def barrier(self):
        self.flush()
        allw = []
        for e in self.dsem:
            for s in self.dsem[e]:
                if self.dcnt[id(s)] > 0:
                    allw.append((s, self.dcnt[id(s)]))
        for e in self.eng:
            if self.cnt[e] > 0:
                allw.append((self.sem[e], self.cnt[e]))
        for e in self.eng:
            waits = [(s, v) for (s, v) in allw if s is not self.sem[e] and self.seen[e].get(id(s), 0) < v]
            for s, v in waits:
                self.seen[e][id(s)] = v

            def emit(eng, waits=waits):
                for s_, v in waits:
                    eng.wait_ge(s_, v)

            self.prog[e].append(emit)
        self.bufs = {}

    def emit_all(self):
        self.flush()
        with self.nc.Block() as block:
            for e, deco in (("sp", block.sync), ("act", block.scalar), ("dve", block.vector),
                            ("pool", block.gpsimd), ("pe", block.tensor)):
                prog = self.prog[e]

                def body(eng, prog=prog):
                    for f in prog:
                        f(eng)

                deco(body)


class Pool:
    def __init__(self, nc):
        self.nc = nc
        self.stack = []

    uid = [0]

    def t(self, name, shape, dt):
        Pool.uid[0] += 1
        g = self.nc.sbuf_tensor(f"{name}_u{Pool.uid[0]}", list(shape), dt)
        h = g.__enter__()
        self.stack.append(g)
        assert self.nc.sbuf_bytes_remaining >= 0, f"SBUF overflow allocating {name}: {self.nc.sbuf_bytes_remaining}"
        return h

    def release(self):
        while self.stack:
            self.stack.pop().__exit__(None, None, None)


def build(cfg, dbg=False):
    D, T, PAST, NPOOL, L = cfg["D"], cfg["T"], cfg["PAST"], cfg["NPOOL"], cfg["L"]
    DFF = 4 * D
    NIN = C_MG + 3 * D
    NT = T // 128
    NTA = NT + 1
    TA = NTA * 128
    KD = D // 128
    WKEEP = min(512, T)
    NPG = PAST // 128

    nc = bass.Bass("TRN2", target_bir_lowering=False)
    S = Sched(nc)

    declared = set()

    def din(name, shape, dt=F32):
        declared.add(name)
        return nc.dram_tensor(name, list(shape), dt, kind="ExternalInput").ap()

    def dout(name, shape):
        return nc.dram_tensor(name, list(shape), F32, kind="ExternalOutput").ap()

    def dscr(name, shape, dt=F32):
        return nc.dram_tensor(name, list(shape), dt, kind=("ExternalOutput" if dbg else "Internal")).ap()

    x_prompt = din("x_prompt", [T, D])
    x_sample = din("x_sample", [1, D])
    norm1_g = din("norm1_g", [L, D])
    norm2_g = din("norm2_g", [L, D])
    w_in = din("w_in", [L, D, NIN])
    nsa_qk_g = din("nsa_qk_g", [L, 4, 64])
    nsa_up = din("nsa_up", [L, 1024, D])
    gla_up = din("gla_up", [L, 512, D])
    rw_up = din("rw_up", [L, 512, D])
    w_out = din("w_out", [L, D, D])
    mlp_w1 = din("mlp_w1", [L, D, DFF])
    mlp_w2 = din("mlp_w2", [L, DFF, D])
    cache_cmp = din("cache_cmp", [L * NPOOL * 128, 512])
    cache_sel = din("cache_sel", [L * NPOOL * 128, 512])
    cache_win = din("cache_win", [L, 512, 512])
    page_table = din("page_table", [NPG], I32)
    cmp_pos = din("cmp_pos", [L, 2, 64, 64])
    cmp_w1 = din("cmp_w1", [L, 2, 4096, 128])
    cmp_w2 = din("cmp_w2", [L, 2, 128, 64])
    gla_a2 = din("gla_a2", [L, 16, 256])
    gla_a_b = din("gla_a_b", [L, 256])
    gla_norm_g = din("gla_norm_g", [L, 128])
    state_gla = din("state_gla", [L, 4, 64, 128])
    rw_mu = din("rw_mu", [L, RWP])
    rw_w0 = din("rw_w0", [L, 512])
    rw_w2 = din("rw_w2", [L, 32, 512])
    rw_a0 = din("rw_a0", [L, 512])
    rw_a2 = din("rw_a2", [L, 32, 512])
    rw_g2 = din("rw_g2", [L, 96, 512])
    rw_kk = din("rw_kk", [L, 512])
    rw_ka = din("rw_ka", [L, 512])
    rw_rk = din("rw_rk", [L, 8, 64])
    rw_ln_g = din("rw_ln_g", [L, 512])
    rw_ln_b = din("rw_ln_b", [L, 512])
    state_rwkv = din("state_rwkv", [L, 8, 64, 64])
    state_shift = din("state_shift", [L, RWP])
    y_prompt = dout("y_prompt", [T, D])
    y_sample = dout("y_sample", [1, D])
    p_cmp_o = dout("p_cmp", [L, T, 512])
    p_sel_o = dout("p_sel", [L, T, 512])
    p_win_o = dout("p_win", [L, WKEEP, 512])
    p_sh_o = dout("p_sh", [L, RWP])
    s_cmp_o = dout("s_cmp", [L, 1, 512])
    s_sel_o = dout("s_sel", [L, 1, 512])
    s_win_o = dout("s_win", [L, 1, 512])
    s_sh_o = dout("s_sh", [L, RWP])
    p_gla_o = dout("p_gla", [L, 4, 64, 128])
    p_rw_o = dout("p_rw", [L, 8, 64, 64])
    s_rw_o = dout("s_rw", [L, 8, 64, 64])
    s_gla_o = dout("s_gla", [L, 4, 64, 128])
    omix_l0 = dout("omix_l0", [TA, 2048]) if dbg else None
    dbg_sel = dout("dbg_sel", [1, 4 * (PAST // 64)]) if dbg else None
    dbg_srow = dout("dbg_srow", [1, 4 * (PAST // 64)]) if dbg else None
    dbg_o = dout("dbg_o", [4, 3 * 256]) if dbg else None
    dbg_gs = dout("dbg_gs", [4, 12]) if dbg else None
    xa = dscr("xa", [TA, D])
    xm = dscr("xm", [TA, D])
    proj = dscr("proj", [TA, NIN])
    omix = dscr("omix", [TA, 2048])
    hidS = dscr("hidS", [NTA, 128, DFF // 128, 128], BF16)
    mrg = dscr("mrg", [TA, D])

    ident_b = nc.alloc_sbuf_tensor("ident_b", [128, 128], BF16)
    ident_f = nc.alloc_sbuf_tensor("ident_f", [128, 128], F32)
    zeros_f = nc.alloc_sbuf_tensor("zeros_f", [128, 2048], F32)
    PS = [nc.alloc_psum_tensor(f"ps{i}", [128, 512], F32) for i in range(6)]
    PB = [nc.alloc_psum_tensor(f"pb{i}", [128, 1024], BF16) for i in range(2)]

    S.op("pool", lambda e: e.memset(ident_f[:], 1.0), writes=["ident_f"])
    S.op("pool", lambda e: e.affine_select(out=ident_f[:], in_=ident_f[:], pattern=[[-1, 128]], compare_op=ALU.is_equal,
                                           fill=0.0, base=0, channel_multiplier=1), reads=["ident_f"], writes=["ident_f"])
    S.op("dve", lambda e: e.tensor_copy(out=ident_b[:], in_=ident_f[:]), reads=["ident_f"], writes=["ident_b"])
    S.op("dve", lambda e: e.memset(zeros_f[:], 0.0), writes=["zeros_f"])
    tri_f = nc.alloc_sbuf_tensor("tri_f", [64, 64], F32)
    ones_f = nc.alloc_sbuf_tensor("ones_f", [128, 64], F32)
    S.op("pool", lambda e: e.memset(ones_f[:], 1.0), writes=["ones_f"])
    trs_f = nc.alloc_sbuf_tensor("trs_f", [64, 64], F32)
    trl_f = nc.alloc_sbuf_tensor("trl_f", [64, 64], F32)
    S.op("pool", lambda e: e.memset(trs_f[:], 1.0), writes=["trs_f"])
    S.op("pool", lambda e: e.affine_select(out=trs_f[:], in_=trs_f[:], pattern=[[1, 64]], compare_op=ALU.is_gt,
                                           fill=0.0, base=0, channel_multiplier=-1), reads=["trs_f"], writes=["trs_f"])
    S.op("pool", lambda e: e.memset(trl_f[:], 1.0), writes=["trl_f"])
    S.op("pool", lambda e: e.affine_select(out=trl_f[:], in_=trl_f[:], pattern=[[-1, 64]], compare_op=ALU.is_gt,
                                           fill=0.0, base=0, channel_multiplier=1), reads=["trl_f"], writes=["trl_f"])
    S.op("pool", lambda e: e.memset(tri_f[:], 1.0), writes=["tri_f"])
    S.op("pool", lambda e: e.affine_select(out=tri_f[:], in_=tri_f[:], pattern=[[1, 64]], compare_op=ALU.is_ge,
                                           fill=0.0, base=0, channel_multiplier=-1), reads=["tri_f"], writes=["tri_f"])
    NBP = T // 64
    SLOPES = [2.0 ** (-8.0 * (h + 1) / 16) for h in range(16)]
    BIGN = -30000.0
    ci = [0]

    def iota_f(shape, pattern, base, cm):
        ci[0] += 1
        ti = nc.alloc_sbuf_tensor(f"iota_i{ci[0]}", list(shape), I32)
        tf = nc.alloc_sbuf_tensor(f"iota_f{ci[0]}", list(shape), F32)
        S.op("pool", lambda e: e.iota(ti[:], pattern=pattern, base=base, channel_multiplier=cm), writes=[("iota", ci[0])])
        S.op("dve", lambda e: e.tensor_copy(out=tf[:], in_=ti[:]), reads=[("iota", ci[0])], writes=[("iotaf", ci[0])])
        return tf

    Dc = iota_f([128, NBP], [[-64, NBP]], -63, 1)
    Dblk_i = iota_f([128, NBP], [[-64, NBP]], 0, 1)
    KDt = iota_f([128, NT + 1], [[-128, NT + 1]], -64, 1)
    QK_ = iota_f([128, 128], [[-1, 128]], 0, 1)
    S.barrier()
    negsl = nc.alloc_sbuf_tensor("negsl", [128, 16, NBP], F32)
    for h in range(16):
        S.op("pool", lambda e, h=h: e.memset(negsl[:, h, :], -SLOPES[h]), writes=[("negsl", h)])
    bcol = nc.alloc_sbuf_tensor("bcol", [128, 16, NT + 1], F32)
    for h in range(16):
        S.op("dve", lambda e, h=h: e.tensor_scalar(out=bcol[:, h, :], in0=KDt[:], scalar1=SLOPES[h], scalar2=None, op0=ALU.mult),
             writes=[("bcol", h)])
    Mc_b = nc.alloc_sbuf_tensor("Mc_b", [128, 128], BF16)
    Mw_b = nc.alloc_sbuf_tensor("Mw_b", [128, 128], BF16)
    S.op("dve", lambda e: e.tensor_scalar(out=Mc_b[:], in0=QK_[:], scalar1=0.0, scalar2=BIGN, op0=ALU.is_gt, op1=ALU.mult), writes=["Mc_b"])
    S.op("dve", lambda e: e.tensor_scalar(out=Mw_b[:], in0=QK_[:], scalar1=0.0, scalar2=BIGN, op0=ALU.is_le, op1=ALU.mult), writes=["Mw_b"])
    E_all = nc.alloc_sbuf_tensor("E_all", [max(NBP, 2), T], BF16)
    S.op("pool", lambda e: e.memset(E_all[:], 1.0), writes=["E_all"])
    S.op("pool", lambda e: e.affine_select(out=E_all[:], in_=E_all[:], pattern=[[1, T]], compare_op=ALU.is_ge, fill=0.0, base=0,
                                           channel_multiplier=-64), reads=["E_all"], writes=["E_all"])
    S.op("pool", lambda e: e.affine_select(out=E_all[:], in_=E_all[:], pattern=[[-1, T]], compare_op=ALU.is_ge, fill=0.0, base=63,
                                           channel_multiplier=64), reads=["E_all"], writes=["E_all"])
    NBS = PAST // 64
    SEGT = min(NT, NPG)
    SEGB = 2 * SEGT
    NSEG = NPG // SEGT
    pt_i = nc.alloc_sbuf_tensor("pt_i", [128, NPG], I32)
    pt_f = nc.alloc_sbuf_tensor("pt_f", [128, NPG], F32)
    idx_f = nc.alloc_sbuf_tensor("idx_f", [128, L, NPG], F32)
    idx_i = nc.alloc_sbuf_tensor("idx_i", [128, L, NPG], I32)
    pcol = iota_f([128, 1], [[0, 1]], 0, 1)
    slg_raw = iota_f([4, 4], [[4, 4]], 1, 1)
    dcs = iota_f([4, NBS], [[-64, NBS]], PAST - 63, 0)
    tb = iota_f([128, NPG], [[-128, NPG]], PAST, -1)
    tbw = iota_f([128, 4], [[-128, 4]], 512, -1)
    S.barrier()
    S.dma("sp", pt_i[:], page_table.partition_broadcast(128), writes=["pt_i"])
    S.op("dve", lambda e: e.tensor_copy(out=pt_f[:], in_=pt_i[:]), reads=["pt_i"], writes=["pt_f"])
    for l_ in range(L):
        S.op("dve", lambda e, l_=l_: e.tensor_scalar(out=idx_f[:, l_, :], in0=pt_f[:], scalar1=float(l_ * NPOOL), scalar2=128.0,
                                                    op0=ALU.add, op1=ALU.mult), reads=["pt_f"], writes=[("idx_f", l_)])
        S.op("dve", lambda e, l_=l_: e.tensor_scalar(out=idx_f[:, l_, :], in0=idx_f[:, l_, :], scalar1=pcol[:, 0:1], scalar2=None, op0=ALU.add),
             reads=[("idx_f", l_)], writes=[("idx_f", l_)])
        S.op("dve", lambda e, l_=l_: e.tensor_copy(out=idx_i[:, l_, :], in_=idx_f[:, l_, :]), reads=[("idx_f", l_)], writes=[("idx_i", l_)])
    slg = nc.alloc_sbuf_tensor("slg", [4, 4], F32)
    S.op("act", lambda e: e.activation(out=slg[:], in_=slg_raw[:], func=AF.Exp, scale=-0.34657359027997264), writes=["slg"])
    S.op("dve", lambda e: e.tensor_scalar(out=slg[:], in0=slg[:], scalar1=-1.0, scalar2=None, op0=ALU.mult), reads=["slg"], writes=["slg"])
    negsl16 = nc.alloc_sbuf_tensor("negsl16", [128, 16], F32)
    for h in range(16):
        S.op("pool", lambda e, h=h: e.memset(negsl16[:, h:h + 1], -SLOPES[h]), writes=[("negsl16", h)])
    half_sel = nc.alloc_sbuf_tensor("half_sel", [1, 2, 128], BF16)
    S.op("pool", lambda e: e.memset(half_sel[:], 0.0), writes=["half_sel"])
    S.op("pool", lambda e: e.memset(half_sel[0:1, 0, 0:64], 1.0), reads=["half_sel"], writes=["half_sel"])
    S.op("pool", lambda e: e.memset(half_sel[0:1, 1, 64:128], 1.0), reads=["half_sel"], writes=["half_sel"])
    mw0 = nc.alloc_sbuf_tensor("mw0", [128, 1], F32)
    S.op("dve", lambda e: e.tensor_scalar(out=mw0[:], in0=ident_f[:, 0:1], scalar1=BIGN, scalar2=None, op0=ALU.mult), writes=["mw0"])
    S.barrier()

    S.store("sp", xa[0:T, :], x_prompt)
    S.store("sp", xa[T:T + 1, :], x_sample)
    S.store("sp", xa[T + 1:TA, :], zeros_f[0:127, 0:D])
    S.barrier()

    evac_rr = [0]

    def evac(out, in_, reads, writes):
        evac_rr[0] ^= 1
        if evac_rr[0]:
            S.op("act", lambda e: e.copy(out=out, in_=in_), reads=reads, writes=writes)
        else:
            S.op("dve", lambda e: e.tensor_copy(out=out, in_=in_), reads=reads, writes=writes)

    def rmsnorm_T(P, src, g_row, hT, tag, NC=None):
        NC = NC or D
        KD = NC // 128
        D_ = NC
        xt = [P.t(f"{tag}_x{i}", [128, NC], F32) for i in range(2)]
        xb = [P.t(f"{tag}_xb{i}", [128, NC], BF16) for i in range(2)]
        if g_row is not None:
            gb = P.t(tag + "_g", [128, NC], F32)
            S.dma("sp", gb[:], g_row, writes=[tag + "g"])
            sq = P.t(tag + "_sq", [128, NC], F32)
            ss = P.t(tag + "_ss", [128, 2 * NTA], F32)
        for tt in range(NTA):
            b = tt % 2
            S.dma("sp", xt[b][:], src[tt * 128:(tt + 1) * 128, 0:NC], writes=[(tag, "x", b)])
            if g_row is None:
                S.op("act" if tt % 2 else "dve", (lambda e, b=b: e.copy(out=xb[b][:], in_=xt[b][:])) if tt % 2 else
                     (lambda e, b=b: e.tensor_copy(out=xb[b][:], in_=xt[b][:])), reads=[(tag, "x", b)], writes=[(tag, "xb", b)])
            if g_row is not None:
              S.op("act", lambda e, b=b, tt=tt: e.activation(out=sq[:], in_=xt[b][:], func=AF.Square,
                                                         accum_out=ss[:, 2 * tt:2 * tt + 1]),
                 reads=[(tag, "x", b)], writes=[(tag, "sq"), (tag, "ss", tt)])
            if g_row is not None:
              S.op("dve", lambda e, tt=tt: e.tensor_scalar(out=ss[:, 2 * tt + 1:2 * tt + 2], in0=ss[:, 2 * tt:2 * tt + 1],
                                                          scalar1=1.0 / D_, scalar2=EPS, op0=ALU.mult, op1=ALU.add),
                   reads=[(tag, "ss", tt)], writes=[(tag, "r0", tt)])
              S.op("act", lambda e, tt=tt: e.activation(out=ss[:, 2 * tt + 1:2 * tt + 2], in_=ss[:, 2 * tt + 1:2 * tt + 2], func=AF.Ln),
                   reads=[(tag, "r0", tt)], writes=[(tag, "r0b", tt)])
              S.op("act", lambda e, tt=tt: e.activation(out=ss[:, 2 * tt + 1:2 * tt + 2], in_=ss[:, 2 * tt + 1:2 * tt + 2], func=AF.Exp,
                                                       scale=-0.5),
                   reads=[(tag, "r0b", tt)], writes=[(tag, "r1", tt)])
              S.op("dve", lambda e, b=b, tt=tt: e.scalar_tensor_tensor(out=xb[b][:], in0=xt[b][:], scalar=ss[:, 2 * tt + 1:2 * tt + 2],
                                                                      in1=gb[:], op0=ALU.mult, op1=ALU.mult),
                   reads=[(tag, "x", b), (tag, "r1", tt), tag + "g"], writes=[(tag, "xb", b)])
            for k4 in range(KD // 4 if KD >= 4 else 1):
                nk = min(4, KD)
                pb = PB[k4 % 2]
                for kk in range(nk):
                    k = k4 * 4 + kk
                    S.op("pe", lambda e, b=b, k=k, kk=kk, pb=pb: e.transpose(out=pb[:, kk * 128:(kk + 1) * 128],
                                                                         in_=xb[b][:, k * 128:(k + 1) * 128], identity=ident_b[:]),
                         reads=[(tag, "xb", b)], writes=[("pb", k4 % 2)])
                evac(hT[:, k4 * 4:k4 * 4 + nk, tt * 128:(tt + 1) * 128],
                     pb[:, 0:nk * 128].rearrange("p (k t) -> p k t", k=nk),
                     reads=[("pb", k4 % 2)], writes=[(tag, "hT", tt, k4)])

    def bcast_row(ap_row, n):
        return ap_row.partition_broadcast(128)

    NCH = T // 64 + 1

    def gla_mixer(l):
        P = Pool(nc)
        NCG = 1552
        pg = [P.t(f"pgG{i}", [64, NCG], F32) for i in range(2)]
        a2 = P.t("a2G", [16, 256], F32)
        ab = P.t("abG", [64, 256], F32)
        gng = P.t("gngG", [64, 128], F32)
        S.dma("sp", a2[:], gla_a2[l], writes=["a2G"])
        S.dma("sp", ab[:], gla_a_b[l].partition_broadcast(64), writes=["abG"])
        S.dma("sp", gng[:], gla_norm_g[l].partition_broadcast(64), writes=["gngG"])
        aT = P.t("aTG", [16, 64], F32)
        la = P.t("laG", [64, 256], F32)
        cum = P.t("cumG", [64, 256], F32)
        e_q = P.t("eqG", [64, 256], F32)
        e_k = P.t("ekG", [64, 256], F32)
        e_l = P.t("elG", [64, 256], F32)
        qd = P.t("qdG", [64, 256], BF16)
        kd = P.t("kdG", [64, 256], BF16)
        kl = P.t("klG", [64, 256], BF16)
        vb = P.t("vbG", [64, 512], BF16)
        qkT = P.t("qkTG", [64, 8, 64], BF16)
        att = P.t("attG", [64, 4, 64], BF16)
        St = P.t("StG", [64, 4, 128], F32)
        Sb = P.t("SbG", [64, 4, 128], BF16)
        ecol = P.t("ecolG", [64, 4], F32)
        og = P.t("ogG", [64, 4, 128], F32)
        o2 = P.t("o2G", [64, 4, 128], F32)
        sg = P.t("sgG", [64, 512], F32)
        ssg = P.t("ssgG", [64, 16], F32)
        S.op("dve", lambda e: e.memset(St[:], 0.0), writes=["StG"])
        S.op("dve", lambda e: e.memset(Sb[:], 0.0), writes=["SbG"])
        QO, KO, VO, AO, GO = 0, 256, 512, 1024, 1040
        def chunk(c):
            is_s = (c == NCH - 1)
            r0 = c * 64
            p = pg[c % 2]
            pk = ("pgG", c % 2)
            if is_s:
                S.store("sp", p_gla_o[l].rearrange("h d v -> d h v"), St[:], reads=["StG"])
                S.dma("sp", St[:], state_gla[l].rearrange("h d v -> d h v"), writes=["StG"])
                S.op("dve", lambda e: e.tensor_copy(out=Sb[:], in_=St[:]), reads=["StG"], writes=["SbG"])
            S.dma("sp", p[:], proj[r0:r0 + 64, C_GQ:C_GQ + NCG], writes=[pk])
            S.op("pe", lambda e, p=p: e.transpose(out=PS[0][0:16, 0:64], in_=p[:, AO:AO + 16], identity=ident_f[0:64, 0:64]),
                 reads=[pk], writes=[("ps", 0)])
            S.op("act", lambda e: e.copy(out=aT[:], in_=PS[0][0:16, 0:64]), reads=[("ps", 0)], writes=["aTG"])
            S.op("pe", lambda e: e.matmul(PS[1][0:64, 0:256], lhsT=aT[:], rhs=a2[:], start=True, stop=True),
                 reads=["aTG", "a2G"], writes=[("ps", 1)])
            S.op("dve", lambda e: e.tensor_tensor(out=la[:], in0=PS[1][0:64, 0:256], in1=ab[:], op=ALU.add),
                 reads=[("ps", 1), "abG"], writes=["laG"])
            S.op("act", lambda e: e.activation(out=la[:], in_=la[:], func=AF.Exp, scale=-1.0), reads=["laG"], writes=["laG"])
            S.op("act", lambda e: e.activation(out=la[:], in_=la[:], func=AF.Ln, bias=1.0), reads=["laG"], writes=["laG"])
            mcol = ident_f[0:64, 0:1] if is_s else ones_f[0:64, 0:1]
            S.op("dve", lambda e, mcol=mcol: e.tensor_scalar(out=la[:], in0=la[:], scalar1=-1.0 / 16, scalar2=mcol,
                                                            op0=ALU.mult, op1=ALU.mult), reads=["laG"], writes=["laG"])
            S.sp()
            S.op("pe", lambda e: e.matmul(PS[2][0:64, 0:256], lhsT=tri_f[:], rhs=la[:], start=True, stop=True),
                 reads=["laG"], writes=[("ps", 2)])
            S.op("pe", lambda e: e.matmul(PS[3][0:64, 0:256], lhsT=ones_f[0:64, 0:64], rhs=la[:], start=True, stop=True),
                 reads=["laG"], writes=[("ps", 3)])
            for h in range(4):
                S.op("pe", lambda e, h=h: e.matmul(PS[4][0:64, h:h + 1], lhsT=la[:, h * 64:(h + 1) * 64], rhs=ones_f[0:64, 0:1],
                                                   start=True, stop=True), reads=["laG"], writes=[("ps", 4)])
            S.op("act", lambda e: e.activation(out=ecol[:], in_=PS[4][0:64, 0:4], func=AF.Exp), reads=[("ps", 4)], writes=["ecolG"])
            S.op("dve", lambda e: e.tensor_copy(out=cum[:], in_=PS[2][0:64, 0:256]), reads=[("ps", 2)], writes=["cumG"])
            S.op("act", lambda e: e.activation(out=e_q[:], in_=cum[:], func=AF.Exp), reads=["cumG"], writes=["eqG"])
            S.op("act", lambda e: e.activation(out=e_k[:], in_=cum[:], func=AF.Exp, scale=-1.0), reads=["cumG"], writes=["ekG"])
            S.op("dve", lambda e: e.tensor_tensor(out=e_l[:], in0=PS[3][0:64, 0:256], in1=cum[:], op=ALU.subtract),
                 reads=[("ps", 3), "cumG"], writes=["elG"])
            S.op("act", lambda e: e.activation(out=e_l[:], in_=e_l[:], func=AF.Exp), reads=["elG"], writes=["elG"])
            S.op("dve", lambda e, p=p: e.scalar_tensor_tensor(out=qd[:], in0=p[:, QO:QO + 256], scalar=0.125, in1=e_q[:],
                                                              op0=ALU.mult, op1=ALU.mult), reads=[pk, "eqG"], writes=["qdG"])
            S.op("dve", lambda e, p=p: e.tensor_tensor(out=kd[:], in0=p[:, KO:KO + 256], in1=e_k[:], op=ALU.mult),
                 reads=[pk, "ekG"], writes=["kdG"])
            S.op("dve", lambda e, p=p: e.tensor_tensor(out=kl[:], in0=p[:, KO:KO + 256], in1=e_l[:], op=ALU.mult),
                 reads=[pk, "elG"], writes=["klG"])
            S.op("act", lambda e, p=p: e.copy(out=vb[:], in_=p[:, VO:VO + 512]), reads=[pk], writes=["vbG"])
            S.sp()
            for h in range(4):
                S.op("pe", lambda e, h=h: e.transpose(out=PB[0][0:64, h * 64:(h + 1) * 64], in_=qd[:, h * 64:(h + 1) * 64],
                                                      identity=ident_b[0:64, 0:64]), reads=["qdG"], writes=[("pb", 0)])
                S.op("pe", lambda e, h=h: e.transpose(out=PB[0][0:64, (4 + h) * 64:(5 + h) * 64], in_=kd[:, h * 64:(h + 1) * 64],
                                                      identity=ident_b[0:64, 0:64]), reads=["kdG"], writes=[("pb", 0)])
            S.op("dve", lambda e: e.tensor_copy(out=qkT[:].rearrange("p a t -> p (a t)"), in_=PB[0][0:64, 0:512]),
                 reads=[("pb", 0)], writes=["qkTG"])
            S.sp()
            for h in range(4):
                S.op("pe", lambda e, h=h: e.matmul(PS[5][0:64, h * 64:(h + 1) * 64], lhsT=qkT[:, 4 + h, :], rhs=qkT[:, h, :],
                                                   start=True, stop=True), reads=["qkTG"], writes=[("ps", 5)])
            S.op("dve", lambda e: e.tensor_tensor(out=att[:], in0=PS[5][0:64, 0:256].rearrange("p (h t) -> p h t", h=4),
                                                  in1=tri_f[:].unsqueeze(1).to_broadcast([64, 4, 64]), op=ALU.mult),
                 reads=[("ps", 5)], writes=["attG"])
            S.sp()
            for h in range(4):
                S.op("pe", lambda e, h=h: e.matmul(PS[0][0:64, h * 128:(h + 1) * 128], lhsT=att[:, h, :], rhs=vb[:, h * 128:(h + 1) * 128],
                                                   start=True, stop=False), reads=["attG", "vbG"], writes=[("ps", 0)])
                S.op("pe", lambda e, h=h: e.matmul(PS[0][0:64, h * 128:(h + 1) * 128], lhsT=qkT[:, h, :], rhs=Sb[:, h, :],
                                                   start=False, stop=True), reads=["qkTG", "SbG"], writes=[("ps", 0)])
            S.op("act", lambda e: e.copy(out=og[:].rearrange("p h v -> p (h v)"), in_=PS[0][0:64, 0:512]), reads=[("ps", 0)], writes=["ogG"])
            S.sp()
            for h in range(4):
                S.op("pe", lambda e, h=h: e.matmul(PS[1][0:64, h * 128:(h + 1) * 128], lhsT=kl[:, h * 64:(h + 1) * 64],
                                                   rhs=vb[:, h * 128:(h + 1) * 128], start=True, stop=True),
                     reads=["klG", "vbG"], writes=[("ps", 1)])
            for h in range(4):
                S.op("dve", lambda e, h=h: e.scalar_tensor_tensor(out=St[:, h, :], in0=St[:, h, :], scalar=ecol[:, h:h + 1],
                                                                  in1=PS[1][0:64, h * 128:(h + 1) * 128], op0=ALU.mult, op1=ALU.add),
                     reads=["StG", "ecolG", ("ps", 1)], writes=["StG"])
            S.op("dve", lambda e: e.tensor_copy(out=Sb[:], in_=St[:]), reads=["StG"], writes=["SbG"])
            S.sp()
            S.op("dve", lambda e: e.tensor_tensor(out=o2[:], in0=og[:], in1=og[:], op=ALU.mult), reads=["ogG"], writes=["o2G"])
            S.op("dve", lambda e: e.tensor_reduce(out=ssg[:, 0:4], in_=o2[:], axis=AX.X, op=ALU.add), reads=["o2G"], writes=["ssg0"])
            S.op("dve", lambda e: e.tensor_scalar(out=ssg[:, 4:8], in0=ssg[:, 0:4], scalar1=1.0 / 128, scalar2=EPS,
                                                  op0=ALU.mult, op1=ALU.add), reads=["ssg0"], writes=["ssg1"])
            S.op("act", lambda e: e.activation(out=ssg[:, 8:12], in_=ssg[:, 4:8], func=AF.Ln), reads=["ssg1"], writes=["ssg2"])
            S.op("act", lambda e: e.activation(out=ssg[:, 12:16], in_=ssg[:, 8:12], func=AF.Exp, scale=-0.5), reads=["ssg2"], writes=["ssg3"])
            S.op("dve", lambda e: e.tensor_tensor(out=o2[:], in0=og[:], in1=ssg[:, 12:16].unsqueeze(2).to_broadcast([64, 4, 128]),
                                                  op=ALU.mult), reads=["ogG", "ssg3"], writes=["o2G"])
            S.op("dve", lambda e: e.tensor_tensor(out=o2[:], in0=o2[:], in1=gng[:].unsqueeze(1).to_broadcast([64, 4, 128]),
                                                  op=ALU.mult), reads=["o2G", "gngG"], writes=["o2G"])
            S.op("act", lambda e, p=p: e.activation(out=sg[:], in_=p[:, GO:GO + 512], func=AF.Sigmoid), reads=[pk], writes=["sgG"])
            S.op("dve", lambda e, p=p: e.tensor_tensor(out=sg[:], in0=sg[:], in1=p[:, GO:GO + 512], op=ALU.mult),
                 reads=["sgG", pk], writes=["sgG"])
            S.op("dve", lambda e: e.tensor_tensor(out=o2[:].rearrange("p h v -> p (h v)"), in0=o2[:].rearrange("p h v -> p (h v)"),
                                                  in1=sg[:], op=ALU.mult), reads=["o2G", "sgG"], writes=["o2G"])
            S.store("sp", omix[r0:r0 + 64, 1024:1536], o2[:].rearrange("p h v -> p (h v)"), reads=["o2G"])
        def fin():
            S.store("sp", s_gla_o[l].rearrange("h d v -> d h v"), St[:], reads=["StG"])
        return P, chunk, fin


    def rw_mixer(l):
        P = Pool(nc)
        H8 = lambda ap: ap.rearrange("p (h d) -> p h d", h=8)
        cnt = [0]

        def T_(name, shape=(64, 512), dt=F32):
            return P.t(name + "R", list(shape), dt)

        rwt = [T_(f"rwt{i}", (64, RWP)) for i in range(2)]
        sht = [T_(f"sht{i}", (64, RWP)) for i in range(2)]
        mu = T_("mu", (64, RWP)); S.dma("sp", mu[:], rw_mu[l].partition_broadcast(64), writes=["muR"])
        cb = {}
        for nm, src in (("w0", rw_w0), ("a0", rw_a0), ("kkw", rw_kk), ("ka", rw_ka), ("lng", rw_ln_g), ("lnb", rw_ln_b)):
            cb[nm] = T_(nm)
            S.dma("sp", cb[nm][:], src[l].partition_broadcast(64), writes=[nm + "R"])
        rk = T_("rk"); S.dma("sp", rk[:], rw_rk[l].rearrange("h d -> (h d)").partition_broadcast(64), writes=["rkR"])
        w2 = T_("w2", (32, 512)); S.dma("sp", w2[:], rw_w2[l], writes=["w2R"])
        a2 = T_("a2", (32, 512)); S.dma("sp", a2[:], rw_a2[l], writes=["a2R"])
        g2 = T_("g2", (96, 512)); S.dma("sp", g2[:], rw_g2[l], writes=["g2R"])
        xm = T_("xm", (64, RWP))
        sm = T_("sm", (64, 160))
        smT = T_("smT", (96, 3, 64))
        names = ["lw", "av", "gv", "kk", "kp", "tmp", "cum", "E1", "E2", "E3", "E4", "Rt", "At", "Bt", "Kt", "Bh", "Kh", "bv",
                 "vv", "bon", "oo", "o2"]
        BFN = ("Rt", "At", "Bt", "Kt", "Bh", "Kh")
        t = {n: (T_(n, (64, 512), BF16) if n in BFN else T_(n)) for n in names}
        vvb = T_("vvb", (64, 512), BF16)
        ssr = T_("ssr", (64, 64))
        gcol = T_("gcol", (64, 8))
        TT = {n: T_(n + "T", (64, 8, 64), BF16) for n in ("Rt", "At", "Bt", "Kt")}
        mats = {n: T_(n, (64, 8, 64), BF16) for n in ("N", "N2", "AK", "KR", "BR", "Q", "MAT", "LV", "U")}
        H = T_("H", (64, 8, 64))
        Hb = T_("Hb", (64, 8, 64), BF16)
        Hv = T_("Hv", (64, 8, 64))
        lvl = P.t("lvlR", [64, 5, 8, 64], BF16)
        S.op("dve", lambda e: e.memset(H[:], 0.0), writes=["HR"])
        S.op("dve", lambda e: e.memset(Hb[:], 0.0), writes=["HbR"])

        def op3(eng, out, in0, in1, op, rk_, wk_):
            S.op(eng, lambda e: e.tensor_tensor(out=out, in0=in0, in1=in1, op=op), reads=rk_, writes=wk_)

        def store_state(dst):
            for h in range(8):
                S.op("pe", lambda e, h=h: e.transpose(out=PS[0][0:64, h * 64:(h + 1) * 64], in_=H[:, h, :], identity=ident_f[0:64, 0:64]),
                     reads=["HR"], writes=[("ps", 0)])
            S.op("act", lambda e: e.copy(out=Hv[:].rearrange("p h k -> p (h k)"), in_=PS[0][0:64, 0:512]), reads=[("ps", 0)], writes=["HvR"])
            S.store("sp", dst.rearrange("h v k -> v h k"), Hv[:], reads=["HvR"])

        def load_state(src):
            S.dma("sp", Hv[:], src.rearrange("h v k -> v h k"), writes=["HvR"])
            for h in range(8):
                S.op("pe", lambda e, h=h: e.transpose(out=PS[0][0:64, h * 64:(h + 1) * 64], in_=Hv[:, h, :], identity=ident_f[0:64, 0:64]),
                     reads=["HvR"], writes=[("ps", 0)])
            S.op("act", lambda e: e.copy(out=H[:].rearrange("p h k -> p (h k)"), in_=PS[0][0:64, 0:512]), reads=[("ps", 0)], writes=["HR"])
            S.op("dve", lambda e: e.tensor_copy(out=Hb[:], in_=H[:]), reads=["HR"], writes=["HbR"])

        def mm8(ps_i, lhs_fn, rhs_fn, rk_, start=True, stop=True):
            for h in range(8):
                S.op("pe", lambda e, h=h: e.matmul(PS[ps_i][0:64, h * 64:(h + 1) * 64], lhsT=lhs_fn(h), rhs=rhs_fn(h), start=start, stop=stop),
                     reads=rk_, writes=[("ps", ps_i)])

        def ps8(i):
            return PS[i][0:64, 0:512].rearrange("p (h d) -> p h d", h=8)

        xm2 = [xm, T_("xmB", (64, RWP))]
        gv2 = [t["gv"], T_("gvB")]
        bon2 = [t["bon"], T_("bonB")]
        vvb2 = [vvb, T_("vvbB", (64, 512), BF16)]

        def P0(c):
            is_s = (c == NCH - 1)
            r0 = c * 64
            b = c % 2
            rwb, shb = rwt[b], sht[b]
            kr, ks = ("rwtR", b), ("shtR", b)
            xm = xm2[b]
            kx = ("xmR", b)
            S.dma("sp", rwb[:], proj[r0:r0 + 64, C_RW:C_RW + RWP], writes=[kr])
            if c == 0:
                S.dma("sp", shb[0:1, :], zeros_f[0:1, 0:RWP], writes=[(ks, 0)])
            elif is_s:
                S.dma("sp", shb[0:1, :], state_shift[l:l + 1, :], writes=[(ks, 0)])
            else:
                S.dma("sp", shb[0:1, :], proj[r0 - 1:r0, C_RW:C_RW + RWP], writes=[(ks, 0)])
            S.dma("sp", shb[1:64, :], proj[r0:r0 + 63, C_RW:C_RW + RWP], writes=[(ks, 1)])
            rsh = [kr, (ks, 0), (ks, 1)]
            op3("pool", xm[:], shb[:], rwb[:], ALU.subtract, rsh, [kx])
            op3("pool", xm[:], xm[:], mu[:], ALU.mult, [kx, "muR"], [kx])
            op3("pool", xm[:], xm[:], rwb[:], ALU.add, [kx, kr], [kx])
            S.op("act", lambda e: e.activation(out=sm[:, 0:32], in_=xm[:, 1536:1568], func=AF.Tanh), reads=[kx], writes=[("smR", 0)])
            S.op("act", lambda e: e.copy(out=sm[:, 32:64], in_=xm[:, 1568:1600]), reads=[kx], writes=[("smR", 1)])
            S.op("act", lambda e: e.activation(out=sm[:, 64:160], in_=xm[:, 1600:1696], func=AF.Sigmoid), reads=[kx], writes=[("smR", 2)])

        def P1p(c):
            is_s = (c == NCH - 1)
            b = c % 2
            gv = gv2[b]
            for i, (o0, n) in enumerate(((0, 32), (32, 32), (64, 96))):
                S.op("pe", lambda e, i=i, o0=o0, n=n: e.transpose(out=PS[0][0:n, i * 64:(i + 1) * 64], in_=sm[:, o0:o0 + n],
                                                                  identity=ident_f[0:64, 0:64]), reads=[("smR", i)], writes=[("ps", 0)])
                S.op("act", lambda e, i=i, n=n: e.copy(out=smT[0:n, i, :], in_=PS[0][0:n, i * 64:(i + 1) * 64]),
                     reads=[("ps", 0)], writes=[("smTR", i)])
            for i, (wsb, n, wk) in enumerate(((w2, 32, "w2R"), (a2, 32, "a2R"), (g2, 96, "g2R"))):
                S.op("pe", lambda e, i=i, wsb=wsb, n=n: e.matmul(PS[1 + i][0:64, 0:512], lhsT=smT[0:n, i, :], rhs=wsb[:], start=True, stop=True),
                     reads=[("smTR", i), wk], writes=[("ps", 1 + i)])
            mcol = ident_f[0:64, 0:1] if is_s else ones_f[0:64, 0:1]
            op3("dve", t["lw"][:], PS[1][0:64, 0:512], cb["w0"][:], ALU.add, [("ps", 1), "w0R"], ["lwR"])
            S.op("act", lambda e: e.activation(out=t["lw"][:], in_=t["lw"][:], func=AF.Sigmoid), reads=["lwR"], writes=["lwR"])
            S.op("dve", lambda e, mcol=mcol: e.tensor_scalar(out=t["lw"][:], in0=t["lw"][:], scalar1=-0.6065306597, scalar2=mcol,
                                                            op0=ALU.mult, op1=ALU.mult), reads=["lwR"], writes=["lwR"])
            op3("dve", t["av"][:], PS[2][0:64, 0:512], cb["a0"][:], ALU.add, [("ps", 2), "a0R"], ["avR"])
            S.op("act", lambda e: e.activation(out=t["av"][:], in_=t["av"][:], func=AF.Sigmoid), reads=["avR"], writes=["avR"])
            S.op("act", lambda e: e.copy(out=gv[:], in_=PS[3][0:64, 0:512]), reads=[("ps", 3)], writes=[("gvR", b)])

        def P1b(c):
            is_s = (c == NCH - 1)
            b = c % 2
            xm = xm2[b]
            kx = ("xmR", b)
            vvb = vvb2[b]
            bon = bon2[b]
            mcol = ident_f[0:64, 0:1] if is_s else ones_f[0:64, 0:1]
            r_, k_, v_ = xm[:, 0:512], xm[:, 512:1024], xm[:, 1024:1536]
            op3("dve", t["kk"][:], k_, cb["kkw"][:], ALU.mult, [kx, "kkwR"], ["kkR"])
            op3("dve", t["tmp"][:], t["kk"][:], t["kk"][:], ALU.mult, ["kkR"], ["tmpR"])
            S.op("dve", lambda e: e.tensor_reduce(out=ssr[:, 0:8], in_=H8(t["tmp"][:]), axis=AX.X, op=ALU.add), reads=["tmpR"], writes=["ss0R"])
            S.op("dve", lambda e: e.tensor_scalar(out=ssr[:, 8:16], in0=ssr[:, 0:8], scalar1=1e-24, scalar2=None, op0=ALU.max),
                 reads=["ss0R"], writes=["ss1R"])
            S.op("act", lambda e: e.activation(out=ssr[:, 16:24], in_=ssr[:, 8:16], func=AF.Ln), reads=["ss1R"], writes=["ss2R"])
            S.op("act", lambda e: e.activation(out=ssr[:, 24:32], in_=ssr[:, 16:24], func=AF.Exp, scale=-0.5), reads=["ss2R"], writes=["ss3R"])
            S.op("dve", lambda e, mcol=mcol: e.tensor_scalar(out=ssr[:, 24:32], in0=ssr[:, 24:32], scalar1=mcol, scalar2=None, op0=ALU.mult),
                 reads=["ss3R"], writes=["ss3R"])
            op3("dve", H8(t["kk"][:]), H8(t["kk"][:]), ssr[:, 24:32].unsqueeze(2).to_broadcast([64, 8, 64]), ALU.mult, ["kkR", "ss3R"], ["kkR"])
            S.op("dve", lambda e: e.scalar_tensor_tensor(out=t["kp"][:], in0=t["av"][:], scalar=-1.0, in1=cb["ka"][:], op0=ALU.add, op1=ALU.mult),
                 reads=["avR", "kaR"], writes=["kpR"])
            S.op("dve", lambda e: e.scalar_tensor_tensor(out=t["kp"][:], in0=t["kp"][:], scalar=1.0, in1=k_, op0=ALU.add, op1=ALU.mult),
                 reads=["kpR", kx], writes=["kpR"])
            S.op("pool", lambda e, mcol=mcol: e.tensor_scalar(out=t["kp"][:], in0=t["kp"][:], scalar1=mcol, scalar2=None, op0=ALU.mult),
                 reads=["kpR"], writes=["kpR"])
            S.op("pool", lambda e, mcol=mcol: e.tensor_scalar(out=t["vv"][:], in0=v_, scalar1=mcol, scalar2=None, op0=ALU.mult),
                 reads=[kx], writes=["vvR"])
            S.op("act", lambda e: e.copy(out=vvb[:], in_=t["vv"][:]), reads=["vvR"], writes=[("vvbR", b)])
            op3("dve", t["bv"][:], t["kk"][:], t["av"][:], ALU.mult, ["kkR", "avR"], ["bvR"])
            op3("dve", t["tmp"][:], r_, t["kp"][:], ALU.mult, [kx, "kpR"], ["tmpR"])
            op3("dve", t["tmp"][:], t["tmp"][:], rk[:], ALU.mult, ["tmpR", "rkR"], ["tmpR"])
            S.op("dve", lambda e: e.tensor_reduce(out=ssr[:, 32:40], in_=H8(t["tmp"][:]), axis=AX.X, op=ALU.add), reads=["tmpR"], writes=["ss4R"])
            op3("dve", H8(bon[:]), H8(t["vv"][:]), ssr[:, 32:40].unsqueeze(2).to_broadcast([64, 8, 64]), ALU.mult, ["vvR", "ss4R"], [("bonR", b)])

        def P2(c):
            b = c % 2
            xm = xm2[b]
            kx = ("xmR", b)
            r_ = xm[:, 0:512]
            S.op("pe", lambda e: e.matmul(PS[4][0:64, 0:512], lhsT=tri_f[:], rhs=t["lw"][:], start=True, stop=True), reads=["lwR"], writes=[("ps", 4)])
            S.op("pe", lambda e: e.matmul(PS[5][0:64, 0:512], lhsT=ones_f[0:64, 0:64], rhs=t["lw"][:], start=True, stop=True),
                 reads=["lwR"], writes=[("ps", 5)])
            for h in range(8):
                S.op("pe", lambda e, h=h: e.matmul(PS[0][0:64, h:h + 1], lhsT=t["lw"][:, h * 64:(h + 1) * 64], rhs=ones_f[0:64, 0:1],
                                                   start=True, stop=True), reads=["lwR"], writes=[("ps", 0)])
            S.op("act", lambda e: e.activation(out=gcol[:], in_=PS[0][0:64, 0:8], func=AF.Exp), reads=[("ps", 0)], writes=["gcolR"])
            S.op("dve", lambda e: e.tensor_copy(out=t["cum"][:], in_=PS[4][0:64, 0:512]), reads=[("ps", 4)], writes=["cumR"])
            S.op("act", lambda e: e.activation(out=t["E1"][:], in_=t["cum"][:], func=AF.Exp), reads=["cumR"], writes=["E1R"])
            S.op("act", lambda e: e.activation(out=t["E2"][:], in_=t["cum"][:], func=AF.Exp, scale=-1.0), reads=["cumR"], writes=["E2R"])
            op3("dve", t["E3"][:], t["cum"][:], t["lw"][:], ALU.subtract, ["cumR", "lwR"], ["E3R"])
            S.op("act", lambda e: e.activation(out=t["E3"][:], in_=t["E3"][:], func=AF.Exp), reads=["E3R"], writes=["E3R"])
            op3("dve", t["E4"][:], PS[5][0:64, 0:512], t["cum"][:], ALU.subtract, [("ps", 5), "cumR"], ["E4R"])
            S.op("act", lambda e: e.activation(out=t["E4"][:], in_=t["E4"][:], func=AF.Exp), reads=["E4R"], writes=["E4R"])
            op3("dve", t["Rt"][:], r_, t["E1"][:], ALU.mult, [kx, "E1R"], ["RtR"])
            S.op("dve", lambda e: e.scalar_tensor_tensor(out=t["At"][:], in0=t["kk"][:], scalar=-1.0, in1=t["E3"][:], op0=ALU.mult, op1=ALU.mult),
                 reads=["kkR", "E3R"], writes=["AtR"])
            op3("pool", t["Bt"][:], t["bv"][:], t["E2"][:], ALU.mult, ["bvR", "E2R"], ["BtR"])
            op3("pool", t["Kt"][:], t["kp"][:], t["E2"][:], ALU.mult, ["kpR", "E2R"], ["KtR"])
            op3("pool", t["Bh"][:], t["bv"][:], t["E4"][:], ALU.mult, ["bvR", "E4R"], ["BhR"])
            op3("pool", t["Kh"][:], t["kp"][:], t["E4"][:], ALU.mult, ["kpR", "E4R"], ["KhR"])

        def P3(c):
            r0 = c * 64
            b = c % 2
            vvb = vvb2[b]
            bon = bon2[b]
            gv = gv2[b]
            for i, n in enumerate(("Rt", "At", "Bt", "Kt")):
                pbt, po_ = PB[i // 2], (i % 2) * 512
                for h in range(8):
                    S.op("pe", lambda e, h=h, n=n, pbt=pbt, po_=po_: e.transpose(out=pbt[0:64, po_ + h * 64:po_ + (h + 1) * 64],
                                                                               in_=t[n][:, h * 64:(h + 1) * 64], identity=ident_b[0:64, 0:64]),
                         reads=[n + "R"], writes=[("pb", i // 2)])
                evac(TT[n][:].rearrange("p h d -> p (h d)"), pbt[0:64, po_:po_ + 512], reads=[("pb", i // 2)], writes=[n + "TR"])
                S.sp()
            hh = lambda m: (lambda h: m[:, h, :])
            for (nm, lh, rh, msk, psi) in (("N", "Bt", "At", trs_f, 5), ("Lm", "At", "Bt", trl_f, 0), ("AK", "Kt", "At", trs_f, 1),
                                           ("KR", "Kt", "Rt", tri_f, 2), ("BR", "Bt", "Rt", tri_f, 3)):
                mm8(psi, hh(TT[lh]), hh(TT[rh]), [lh + "TR", rh + "TR"])
                dst_, dk_ = (lvl[:, 0], ("lvlR", 0)) if nm == "Lm" else (mats[nm][:], nm + "R")
                op3("dve", dst_, ps8(psi), msk[:].unsqueeze(1).to_broadcast([64, 8, 64]), ALU.mult, [("ps", psi)], [dk_])
                S.sp()
            Nb = [mats["N"], mats["N2"]]
            Nk = ["NR", "N2R"]
            for i in range(5):
                cur, nxt = i % 2, (i + 1) % 2
                mm8(4, (lambda h, i=i: lvl[:, i, h, :]), hh(Nb[cur]), [Nk[cur], ("lvlR", i)])
                if i < 4:
                    mm8(5, hh(Nb[cur]), (lambda h, i=i: lvl[:, i, h, :]), [Nk[cur], ("lvlR", i)])
                S.op("act", lambda e, nxt=nxt: e.copy(out=Nb[nxt][:], in_=ps8(4)), reads=[("ps", 4)], writes=[Nk[nxt]])
                if i < 4:
                    S.op("dve", lambda e, i=i: e.tensor_copy(out=lvl[:, i + 1], in_=ps8(5)), reads=[("ps", 5)], writes=[("lvlR", i + 1)])
                S.sp()
            op3("dve", mats["Q"][:], Nb[1][:], ident_f[0:64, 0:64].unsqueeze(1).to_broadcast([64, 8, 64]), ALU.add, [Nk[1]], ["QR"])
            for i in (4, 3, 2, 1, 0):
                mm8(4, (lambda h, i=i: lvl[:, i, h, :]), hh(mats["Q"]), [("lvlR", i), "QR"])
                op3("dve", mats["Q"][:], mats["Q"][:], ps8(4), ALU.add, ["QR", ("ps", 4)], ["QR"])
                S.sp()
            mm8(5, (lambda h: t["At"][:, h * 64:(h + 1) * 64]), hh(mats["Q"]), ["AtR", "QR"])
            evac(mats["MAT"][:], ps8(5), reads=[("ps", 5)], writes=["MATR"])
            S.sp()
            mm8(0, hh(mats["AK"]), (lambda h: vvb[:, h * 64:(h + 1) * 64]), ["AKR", ("vvbR", b)])
            evac(mats["LV"][:], ps8(0), reads=[("ps", 0)], writes=["LVR"])
            S.sp()
            for h in range(8):
                S.op("pe", lambda e, h=h: e.matmul(PS[1][0:64, h * 64:(h + 1) * 64], lhsT=mats["Q"][:, h, :], rhs=mats["LV"][:, h, :],
                                                   start=True, stop=False), reads=["QR", "LVR"], writes=[("ps", 1)])
                S.op("pe", lambda e, h=h: e.matmul(PS[1][0:64, h * 64:(h + 1) * 64], lhsT=mats["MAT"][:, h, :], rhs=Hb[:, h, :],
                                                   start=False, stop=True), reads=["MATR", "HbR"], writes=[("ps", 1)])
            evac(mats["U"][:], ps8(1), reads=[("ps", 1)], writes=["UR"])
            S.sp()
            for h in range(8):
                sl = slice(h * 64, (h + 1) * 64)
                S.op("pe", lambda e, h=h, sl=sl: e.matmul(PS[2][0:64, sl], lhsT=TT["Rt"][:, h, :], rhs=Hb[:, h, :], start=True, stop=False),
                     reads=["RtTR", "HbR"], writes=[("ps", 2)])
                S.op("pe", lambda e, h=h, sl=sl: e.matmul(PS[2][0:64, sl], lhsT=mats["BR"][:, h, :], rhs=mats["U"][:, h, :], start=False, stop=False),
                     reads=["BRR", "UR"], writes=[("ps", 2)])
                S.op("pe", lambda e, h=h, sl=sl: e.matmul(PS[2][0:64, sl], lhsT=mats["KR"][:, h, :], rhs=vvb[:, sl], start=False, stop=True),
                     reads=["KRR", ("vvbR", b)], writes=[("ps", 2)])
            S.op("act", lambda e: e.copy(out=t["oo"][:], in_=PS[2][0:64, 0:512]), reads=[("ps", 2)], writes=["ooR"])
            S.sp()
            for h in range(8):
                sl = slice(h * 64, (h + 1) * 64)
                S.op("pe", lambda e, h=h, sl=sl: e.matmul(PS[3][0:64, sl], lhsT=t["Bh"][:, sl], rhs=mats["U"][:, h, :], start=True, stop=False),
                     reads=["BhR", "UR"], writes=[("ps", 3)])
                S.op("pe", lambda e, h=h, sl=sl: e.matmul(PS[3][0:64, sl], lhsT=t["Kh"][:, sl], rhs=vvb[:, sl], start=False, stop=True),
                     reads=["KhR", ("vvbR", b)], writes=[("ps", 3)])
            op3("dve", H[:], H[:], gcol[:].unsqueeze(2).to_broadcast([64, 8, 64]), ALU.mult, ["HR", "gcolR"], ["HR"])
            op3("dve", H[:], H[:], ps8(3), ALU.add, ["HR", ("ps", 3)], ["HR"])
            S.sp()
            S.op("act", lambda e: e.copy(out=Hb[:], in_=H[:]), reads=["HR"], writes=["HbR"])
            S.op("dve", lambda e: e.tensor_reduce(out=ssr[:, 40:48], in_=H8(t["oo"][:]), axis=AX.X, op=ALU.add), reads=["ooR"], writes=["ss5R"])
            S.op("dve", lambda e: e.tensor_scalar(out=ssr[:, 40:48], in0=ssr[:, 40:48], scalar1=-1.0 / 64, scalar2=None, op0=ALU.mult),
                 reads=["ss5R"], writes=["ss5R"])
            op3("dve", H8(t["oo"][:]), H8(t["oo"][:]), ssr[:, 40:48].unsqueeze(2).to_broadcast([64, 8, 64]), ALU.add, ["ooR", "ss5R"], ["ooR"])
            op3("dve", t["o2"][:], t["oo"][:], t["oo"][:], ALU.mult, ["ooR"], ["o2R"])
            S.op("dve", lambda e: e.tensor_reduce(out=ssr[:, 48:56], in_=H8(t["o2"][:]), axis=AX.X, op=ALU.add), reads=["o2R"], writes=["ss6R"])
            S.op("dve", lambda e: e.tensor_scalar(out=ssr[:, 48:56], in0=ssr[:, 48:56], scalar1=1.0 / 64, scalar2=64e-5, op0=ALU.mult, op1=ALU.add),
                 reads=["ss6R"], writes=["ss6R"])
            S.op("act", lambda e: e.activation(out=ssr[:, 48:56], in_=ssr[:, 48:56], func=AF.Ln), reads=["ss6R"], writes=["ss6R"])
            S.op("act", lambda e: e.activation(out=ssr[:, 56:64], in_=ssr[:, 48:56], func=AF.Exp, scale=-0.5), reads=["ss6R"], writes=["ss7R"])
            op3("dve", H8(t["oo"][:]), H8(t["oo"][:]), ssr[:, 56:64].unsqueeze(2).to_broadcast([64, 8, 64]), ALU.mult, ["ooR", "ss7R"], ["ooR"])
            op3("pool", t["oo"][:], t["oo"][:], cb["lng"][:], ALU.mult, ["ooR", "lngR"], ["ooR"])
            op3("pool", t["oo"][:], t["oo"][:], cb["lnb"][:], ALU.add, ["ooR", "lnbR"], ["ooR"])
            op3("pool", t["oo"][:], t["oo"][:], bon[:], ALU.add, ["ooR", ("bonR", b)], ["ooR"])
            op3("pool", t["oo"][:], t["oo"][:], gv[:], ALU.mult, ["ooR", ("gvR", b)], ["ooR"])
            S.store("sp", omix[r0:r0 + 64, 1536:2048], t["oo"][:], reads=["ooR"])

        def merged(a, b_):
            out = []
            i = j = 0
            while i < len(a) or j < len(b_):
                if j >= len(b_) or (i < len(a) and i * len(b_) <= j * len(a)):
                    out.append(a[i]); i += 1
                else:
                    out.append(b_[j]); j += 1
            return out

        P0(0)
        P1p(0)
        P1b(0)
        def chunk(c):
            if c == NCH - 1:
                store_state(p_rw_o[l])
                load_state(state_rwkv[l])
                S.sp()
            if c + 1 < NCH:
                P0(c + 1)
                S.sp()
            P2(c)
            S.sp()
            if c + 1 < NCH:
                P1p(c + 1)
                S.sp()
                outer = S.cap
                S.cap = []
                P1b(c + 1)
                sa = S.cap
                S.cap = []
                P3(c)
                sb_ = S.cap
                S.cap = outer
                S.replay(merged(sb_, sa))
            else:
                P3(c)
        def fin():
            store_state(s_rw_o[l])
        return P, chunk, fin


    def nsa_mixer(l):
        LP = Pool(nc)
        kcT_p = LP.t("kcT_p", [64, 4, NBP], BF16)
        vc_p = LP.t("vc_p", [max(NBP, 2), 4, 64], BF16)
        qkg = LP.t("qkgN", [128, 4, 64], F32)
        S.dma("sp", qkg[:].rearrange("p a d -> p (a d)"), nsa_qk_g[l].rearrange("a d -> (a d)").partition_broadcast(128), writes=["qkgN"])
        tmpN = LP.t("tmpN", [128, 16, 64], F32)
        ssN = LP.t("ssN", [128, 64], F32)
        kcT_s = LP.t("kcT_s", [64, 4, NBS], BF16)
        vc_s = LP.t("vc_s", [SEGB, NSEG, 4, 64], BF16)
        q_s = LP.t("q_s", [64, 16], BF16)
        k_s = LP.t("k_s", [64, 8], BF16)
        v_s = LP.t("v_s", [1, 8, 65], BF16)
        g_s = LP.t("g_s", [1, 48], F32)

        def headnormN(out3, in3, g_ap, H, rk, wk, sc=1.0, np_=128):
            S.op("dve", lambda e: e.tensor_tensor(out=tmpN[0:np_, 0:H, :], in0=in3, in1=in3, op=ALU.mult), reads=rk, writes=["tmpN"])
            S.op("dve", lambda e: e.tensor_reduce(out=ssN[0:np_, 0:H], in_=tmpN[0:np_, 0:H, :], axis=AX.X, op=ALU.add), reads=["tmpN"], writes=["ssN0"])
            S.op("dve", lambda e: e.tensor_scalar(out=ssN[0:np_, 16:16 + H], in0=ssN[0:np_, 0:H], scalar1=1.0 / 64, scalar2=EPS,
                                                  op0=ALU.mult, op1=ALU.add), reads=["ssN0"], writes=["ssN1"])
            S.op("act", lambda e: e.activation(out=ssN[0:np_, 48:48 + H], in_=ssN[0:np_, 16:16 + H], func=AF.Ln), reads=["ssN1"], writes=["ssN1b"])
            S.op("act", lambda e: e.activation(out=ssN[0:np_, 32:32 + H], in_=ssN[0:np_, 48:48 + H], func=AF.Exp, scale=-0.5,
                                               bias=float(np.log(sc))), reads=["ssN1b"], writes=["ssN2"])
            S.op("dve", lambda e: e.tensor_tensor(out=tmpN[0:np_, 0:H, :], in0=in3,
                                                  in1=ssN[0:np_, 32:32 + H].unsqueeze(2).to_broadcast([np_, H, 64]), op=ALU.mult),
                 reads=rk + ["ssN2"], writes=["tmpN"])
            S.op("dve", lambda e: e.tensor_tensor(out=out3, in0=tmpN[0:np_, 0:H, :],
                                                  in1=g_ap.unsqueeze(1).to_broadcast([np_, H, 64]), op=ALU.mult),
                 reads=["tmpN", "qkgN"], writes=wk)

        P = Pool(nc)
        w1 = P.t("w1N", [64, 2, 64, 128], BF16)
        w2 = P.t("w2N", [128, 2, 64], BF16)
        for kv in range(2):
            S.dma("pool", w1[:, kv], cmp_w1[l, kv].rearrange("(pos d) h -> d pos h", d=64), writes=[("w1N", kv)])
            S.dma("pool", w2[:, kv, :], cmp_w2[l, kv], writes=[("w2N", kv)])
        posb = P.t("posbN", [128, 2, 64], F32)
        for hf in range(2):
            S.dma("sp", posb[hf * 64:(hf + 1) * 64], cmp_pos[l].rearrange("kv pos d -> pos kv d"), writes=[("posbN", hf)])
        zT = P.t("zTN", [64, 8, T], BF16)
        ct = [P.t(f"ctN{i}", [128, 512], F32) for i in range(2)]
        zb = [P.t(f"zbN{i}", [128, 512], BF16) for i in range(2)]
        hx = [P.t(f"hxN{i}", [128, 4 * NBP], F32) for i in range(4)]
        hb = P.t("hbN", [128, 4 * NBP], BF16)
        kc = P.t("kcN", [max(NBP, 2), 4, 64], F32)
        kcb = P.t("kcbN", [max(NBP, 2), 4, 64], BF16)

        def compress_segment(load_fn, ntiles, kT_out, v_out):
            nb = ntiles * 2
            for tt in range(ntiles):
                b = tt % 2
                load_fn(tt, ct[b], ("ctN", b))
                S.op("dve", lambda e, b=b: e.tensor_tensor(out=zb[b][:].rearrange("p (kv g d) -> p kv g d", kv=2, g=4),
                                                           in0=ct[b][:].rearrange("p (kv g d) -> p kv g d", kv=2, g=4),
                                                           in1=posb[:].unsqueeze(2).to_broadcast([128, 2, 4, 64]), op=ALU.add),
                     reads=[("ctN", b), ("posbN", 0), ("posbN", 1)], writes=[("zbN", b)])
                for a in range(8):
                    S.op("pe", lambda e, a=a, b=b: e.transpose(out=PB[b][0:64, a * 128:(a + 1) * 128], in_=zb[b][:, a * 64:(a + 1) * 64],
                                                             identity=ident_b[:]), reads=[("zbN", b)], writes=[("pb", b)])
                evac(zT[:, :, tt * 128:(tt + 1) * 128], PB[b][0:64, 0:1024].rearrange("p (a t) -> p a t", a=8),
                     reads=[("pb", b)], writes=[("zTN", tt)])
            zr = [("zTN", tt) for tt in range(ntiles)]
            for kv in range(2):
                ps = PS[kv]
                for pos in range(64):
                    S.op("pe", lambda e, kv=kv, pos=pos, ps=ps: e.matmul(
                        ps[:, 0:4 * nb].rearrange("p (g n) -> p g n", g=4), lhsT=w1[:, kv, pos, :],
                        rhs=zT[:, kv * 4:(kv + 1) * 4, bass.DynSlice(pos, nb, step=64)] if False else
                        zT[:, kv * 4:(kv + 1) * 4, 0:nb * 64].rearrange("p g (n s) -> p g n s", s=64)[:, :, :, pos],
                        start=(pos == 0), stop=(pos == 63)), reads=zr + [("w1N", kv)], writes=[("ps", kv)])
                n4 = 4 * nb
                x_, x2, u_, th = hx[0], hx[1], hx[2], hx[3]
                S.op("act", lambda e, ps=ps: e.copy(out=x_[:, 0:n4], in_=ps[:, 0:n4]), reads=[("ps", kv)], writes=["hx0"])
                S.op("dve", lambda e: e.tensor_tensor(out=x2[:, 0:n4], in0=x_[:, 0:n4], in1=x_[:, 0:n4], op=ALU.mult), reads=["hx0"], writes=["hx1"])
                S.op("dve", lambda e: e.tensor_scalar(out=x2[:, 0:n4], in0=x2[:, 0:n4], scalar1=0.044715, scalar2=1.0, op0=ALU.mult, op1=ALU.add),
                     reads=["hx1"], writes=["hx1"])
                S.op("dve", lambda e: e.tensor_tensor(out=u_[:, 0:n4], in0=x2[:, 0:n4], in1=x_[:, 0:n4], op=ALU.mult), reads=["hx1", "hx0"], writes=["hx2"])
                S.op("act", lambda e: e.activation(out=th[:, 0:n4], in_=u_[:, 0:n4], func=AF.Tanh, scale=0.7978845608), reads=["hx2"], writes=["hx3"])
                S.op("dve", lambda e: e.scalar_tensor_tensor(out=th[:, 0:n4], in0=th[:, 0:n4], scalar=1.0, in1=x_[:, 0:n4], op0=ALU.add, op1=ALU.mult),
                     reads=["hx3", "hx0"], writes=["hx3"])
                S.op("dve", lambda e: e.tensor_scalar(out=hb[:, 0:n4], in0=th[:, 0:n4], scalar1=0.5, scalar2=None, op0=ALU.mult),
                     reads=["hx3"], writes=["hbN"])
                for g in range(4):
                    S.op("pe", lambda e, g=g, kv=kv: e.matmul(PS[2][0:nb, g * 64:(g + 1) * 64], lhsT=hb[:, g * nb:(g + 1) * nb], rhs=w2[:, kv, :],
                                                             start=True, stop=True), reads=["hbN", ("w2N", kv)], writes=[("ps", 2)])
                if kv == 0:
                    S.op("act", lambda e: e.copy(out=kc[0:nb].rearrange("p g d -> p (g d)"), in_=PS[2][0:nb, 0:256]), reads=[("ps", 2)], writes=["kcN"])
                    headnormN(kcb[0:nb], kc[0:nb], qkg[0:nb, 1, :], 4, ["kcN"], ["kcbN"], np_=nb)
                    for g in range(4):
                        S.op("pe", lambda e, g=g: e.transpose(out=PB[0][0:64, g * nb:(g + 1) * nb], in_=kcb[0:nb, g, :], identity=ident_b[0:nb, 0:nb]),
                             reads=["kcbN"], writes=[("pb", 0)])
                    S.op("dve", lambda e: e.tensor_copy(out=kT_out, in_=PB[0][0:64, 0:4 * nb].rearrange("p (g n) -> p g n", g=4)),
                         reads=[("pb", 0)], writes=["kcT"])
                else:
                    S.op("act", lambda e: e.copy(out=v_out, in_=PS[2][0:nb, 0:256].rearrange("p (g d) -> p g d", g=4)), reads=[("ps", 2)], writes=["vc"])

        def load_prompt_cmp(tt, dst, key):
            S.store("sp", dst[:], proj[tt * 128:(tt + 1) * 128, C_CMP:C_CMP + 512], writes=[key])

        compress_segment(load_prompt_cmp, NT, kcT_p[:], vc_p[0:NBP])
        for sgi in range(NSEG):
            def load_cache_cmp(tt, dst, key, sgi=sgi):
                j = sgi * SEGT + tt
                S.dma("pool", dst[:], cache_cmp, reads=[("idx_i", l)], writes=[key],
                      indirect=bass.IndirectOffsetOnAxis(ap=idx_i[:, l, j:j + 1], axis=0))
            compress_segment(load_cache_cmp, SEGT, kcT_s[:, :, sgi * SEGB:(sgi + 1) * SEGB], vc_s[0:SEGB, sgi])
        S.barrier()
        P.release()

        P = Pool(nc)
        qT = P.t("qTN", [64, 16, TA], BF16)
        kT = P.t("kTN", [64, 8, TA], BF16)
        Va = P.t("VaN", [128, NTA, 8, 65], BF16)
        gts = P.t("gtsN", [128, NTA, 48], F32)
        S.op("pool", lambda e: e.memset(Va[:], 1.0), writes=["VaN"])
        ptN = [P.t(f"ptN{i}", [128, C_GQ], F32) for i in range(1)]
        qn = [P.t(f"qnN{i}", [128, 16, 64], BF16) for i in range(2)]
        kn = [P.t(f"knN{i}", [128, 8, 64], BF16) for i in range(2)]
        for tt in range(NTA):
            b = tt % 2
            p_ = ptN[0]
            pk = ("ptN", 0)
            S.dma("sp", p_[:], proj[tt * 128:(tt + 1) * 128, 0:C_GQ], writes=[pk])
            headnormN(qn[b][:], p_[:, 0:1024].rearrange("p (h d) -> p h d", h=16), qkg[:, 0, :], 16, [pk], [("qnN", b)], sc=0.125)
            headnormN(kn[b][:, 0:4], p_[:, C_SEL:C_SEL + 256].rearrange("p (h d) -> p h d", h=4), qkg[:, 2, :], 4, [pk], [("knN", b, 0)])
            headnormN(kn[b][:, 4:8], p_[:, C_WIN:C_WIN + 256].rearrange("p (h d) -> p h d", h=4), qkg[:, 3, :], 4, [pk], [("knN", b, 1)])
            S.op("act", lambda e, p_=p_, tt=tt: e.copy(out=Va[:, tt, 0:4, 0:64], in_=p_[:, C_SEL + 256:C_SEL + 512].rearrange("p (g d) -> p g d", g=4)),
                 reads=[pk, "VaN"], writes=[("VaN", tt, 0)])
            S.op("act", lambda e, p_=p_, tt=tt: e.copy(out=Va[:, tt, 4:8, 0:64], in_=p_[:, C_WIN + 256:C_WIN + 512].rearrange("p (g d) -> p g d", g=4)),
                 reads=[pk, "VaN"], writes=[("VaN", tt, 1)])
            S.op("act", lambda e, p_=p_, tt=tt: e.activation(out=gts[:, tt, :], in_=p_[:, C_GATE:C_GATE + 48], func=AF.Sigmoid),
                 reads=[pk], writes=[("gtsN", tt)])
            for half in range(2):
                for a in range(8):
                    S.op("pe", lambda e, a=a, b=b, half=half: e.transpose(out=PB[half][0:64, a * 128:(a + 1) * 128], in_=qn[b][:, half * 8 + a, :],
                                                                        identity=ident_b[:]), reads=[("qnN", b)], writes=[("pb", half)])
                evac(qT[:, half * 8:(half + 1) * 8, tt * 128:(tt + 1) * 128], PB[half][0:64, 0:1024].rearrange("p (a t) -> p a t", a=8),
                     reads=[("pb", half)], writes=[("qTN", tt, half)])
            for a in range(8):
                S.op("pe", lambda e, a=a, b=b: e.transpose(out=PB[0][0:64, a * 128:(a + 1) * 128], in_=kn[b][:, a, :], identity=ident_b[:]),
                     reads=[("knN", b, 0), ("knN", b, 1)], writes=[("pb", 0)])
            evac(kT[:, :, tt * 128:(tt + 1) * 128], PB[0][0:64, 0:1024].rearrange("p (a t) -> p a t", a=8), reads=[("pb", 0)], writes=[("kTN", tt)])

        S.op("dve", lambda e: e.tensor_copy(out=q_s[:], in_=qT[:, :, T]), reads=[("qTN", NT, 0), ("qTN", NT, 1)], writes=["q_s"])
        S.op("dve", lambda e: e.tensor_copy(out=k_s[:], in_=kT[:, :, T]), reads=[("kTN", NT)], writes=["k_s"])
        S.op("dve", lambda e: e.tensor_copy(out=v_s[:], in_=Va[0:1, NT, :, :]), reads=[("VaN", NT, 0), ("VaN", NT, 1), "VaN"], writes=["v_s"])
        S.op("dve", lambda e: e.tensor_copy(out=g_s[:], in_=gts[0:1, NT, :]), reads=[("gtsN", NT)], writes=["g_s"])
        bc = P.t("bcN", [128, 16, NBP], F32)
        dI = P.t("dIN", [128, NBP], F32)
        pen = P.t("penN", [128, NBP], F32)
        ec = P.t("ecN", [128, 16, NBP], F32)
        pb16 = P.t("pb16N", [128, 16, NBP], BF16)
        sm = P.t("smN", [128, 64], F32)
        scg = P.t("scgN", [128, 4, NBP], F32)
        adj = P.t("adjN", [128, NBP], F32)
        adj2 = P.t("adj2N", [128, NBP], F32)
        vld = P.t("vldN", [128, NBP], F32)
        m8 = P.t("m8N", [128, 4, 16], F32)
        scw = P.t("scwN", [128, 4, NBP], F32)
        selp = P.t("selpN", [128, 4, NBP], BF16)
        penT = P.t("penTN", [max(NBP, 2), 4, 128], BF16)
        pT = P.t("pTN", [max(NBP, 2), 16, 128], BF16)
        oacc = P.t("oaccN", [128, 16, 64], F32)
        otmp = P.t("otmpN", [128, 4, 64], F32)
        wv = P.t("wvN", [128, 8], F32)
        PT = [P.t(f"PTN{i}", [128, 4, 128], BF16) for i in range(2)]
        sti = [0]

        def attend(i, g, kbase, vbase, tiles, use_pen, gate_idx):
            acc = PS[4]
            nj = len(tiles)
            S.op("dve", lambda e: e.memset(acc[:, 0:260], 0.0), writes=[("ps", 4)])
            base = sti[0]
            sti[0] += nj

            def qk(jn):
                j = tiles[jn]
                sb = (base + jn) % 2
                st = PS[2 + sb]
                stk = ("ps", 2 + sb)
                dl = i - j
                extra = []
                if use_pen:
                    extra.append((E_all[0:NBP, j * 128:(j + 1) * 128], penT[0:NBP, g, :].unsqueeze(1).to_broadcast([NBP, 4, 128]), ["penTN"]))
                if dl == 0:
                    extra.append((ident_b[:], Mc_b[:].unsqueeze(1).to_broadcast([128, 4, 128]), []))
                if (not use_pen) and dl == 4:
                    extra.append((ident_b[:], Mw_b[:].unsqueeze(1).to_broadcast([128, 4, 128]), []))
                ne = len(extra)
                for r in range(4):
                    h = 4 * g + r
                    sl = slice(r * 128, (r + 1) * 128)
                    S.op("pe", lambda e, st=st, sl=sl, j=j, h=h, r=r, ne=ne: e.matmul(
                        st[:, sl], lhsT=kT[:, kbase + g, j * 128:(j + 1) * 128], rhs=qT[:, h, i * 128:(i + 1) * 128],
                        start=(r == 0), stop=(ne == 0 and r == 3), skip_group_check=True),
                        reads=[("kTN", j), ("qTN", i, h // 8)], writes=[stk])
                for xi, (lh, rh, rk_) in enumerate(extra):
                    S.op("pe", lambda e, st=st, lh=lh, rh=rh, last=(xi == ne - 1): e.matmul(
                        st[:, 0:512].rearrange("p (r q) -> p r q", r=4), lhsT=lh, rhs=rh, start=False, stop=last, skip_group_check=True),
                        reads=rk_, writes=[stk])

            def ex_pv(jn):
                j = tiles[jn]
                sb = (base + jn) % 2
                st = PS[2 + sb]
                stk = ("ps", 2 + sb)
                ptile = PT[sb]
                dl = i - j
                for r in range(4):
                    h = 4 * g + r
                    S.op("act", lambda e, st=st, ptile=ptile, r=r, h=h, dl=dl: e.activation(
                        out=ptile[:, r, :], in_=st[:, r * 128:(r + 1) * 128], func=AF.Exp, bias=bcol[:, h, dl:dl + 1], scale=1.0),
                        reads=[stk], writes=[("PTN", sb, r)])
                for r in range(4):
                    S.op("pe", lambda e, ptile=ptile, r=r, j=j, jn=jn: e.matmul(
                        acc[:, r * 65:(r + 1) * 65], lhsT=ptile[:, r, :], rhs=Va[:, j, vbase + g, :], start=False, stop=(jn == nj - 1),
                        skip_group_check=True),
                        reads=[("PTN", sb, r), ("VaN", j, vbase // 4), "VaN"], writes=[("ps", 4)])

            qk(0)
            for jn in range(nj):
                if jn + 1 < nj:
                    qk(jn + 1)
                ex_pv(jn)
            a3 = acc[:, 0:260].rearrange("p (r c) -> p r c", r=4)
            S.op("dve", lambda e: e.reciprocal(out=wv[:, 0:4], in_=a3[:, :, 64]), reads=[("ps", 4)], writes=["wv0"])
            S.op("dve", lambda e: e.tensor_tensor(out=wv[:, 4:8], in0=wv[:, 0:4], in1=gts[:, i, gate_idx * 16 + 4 * g:gate_idx * 16 + 4 * g + 4], op=ALU.mult),
                 reads=["wv0", ("gtsN", i)], writes=["wv1"])
            S.op("dve", lambda e: e.tensor_tensor(out=otmp[:], in0=a3[:, :, 0:64], in1=wv[:, 4:8].unsqueeze(2).to_broadcast([128, 4, 64]), op=ALU.mult),
                 reads=[("ps", 4), "wv1"], writes=["otmpN"])
            S.op("dve", lambda e: e.tensor_tensor(out=oacc[:, 4 * g:4 * g + 4, :], in0=oacc[:, 4 * g:4 * g + 4, :], in1=otmp[:], op=ALU.add),
                 reads=["otmpN", "oaccN"], writes=["oaccN"])

        for i in range(NT):
            qr = [("qTN", i, 0), ("qTN", i, 1)]
            for h in range(16):
                S.op("pe", lambda e, h=h: e.matmul(PS[0][:, h * NBP:(h + 1) * NBP], lhsT=qT[:, h, i * 128:(i + 1) * 128], rhs=kcT_p[:, h // 4, :],
                                                   start=True, stop=True), reads=qr + ["kcT"], writes=[("ps", 0)])
            S.op("dve", lambda e: e.tensor_scalar(out=dI[:], in0=Dc[:], scalar1=float(128 * i), scalar2=None, op0=ALU.add), writes=["dIN"])
            S.op("dve", lambda e: e.tensor_scalar(out=pen[:], in0=dI[:], scalar1=0.0, scalar2=BIGN, op0=ALU.is_lt, op1=ALU.mult), reads=["dIN"], writes=["penN"])
            S.op("dve", lambda e: e.tensor_tensor(out=bc[:], in0=negsl[:], in1=dI[:].unsqueeze(1).to_broadcast([128, 16, NBP]), op=ALU.mult),
                 reads=["dIN"], writes=["bcN"])
            S.op("dve", lambda e: e.tensor_tensor(out=bc[:], in0=bc[:], in1=pen[:].unsqueeze(1).to_broadcast([128, 16, NBP]), op=ALU.add),
                 reads=["bcN", "penN"], writes=["bcN"])
            S.op("dve", lambda e: e.tensor_tensor(out=ec[:], in0=PS[0][:, 0:16 * NBP].rearrange("p (h n) -> p h n", h=16), in1=bc[:], op=ALU.add),
                 reads=[("ps", 0), "bcN"], writes=["ecN"])
            S.op("act", lambda e: e.activation(out=ec[:], in_=ec[:], func=AF.Exp), reads=["ecN"], writes=["ecN"])
            S.op("dve", lambda e: e.tensor_reduce(out=sm[:, 0:16], in_=ec[:], axis=AX.X, op=ALU.add), reads=["ecN"], writes=["sm0"])
            S.op("dve", lambda e: e.tensor_scalar(out=sm[:, 16:32], in0=sm[:, 0:16], scalar1=1e-30, scalar2=None, op0=ALU.max), reads=["sm0"], writes=["sm1"])
            S.op("dve", lambda e: e.reciprocal(out=sm[:, 32:48], in_=sm[:, 16:32]), reads=["sm1"], writes=["sm2"])
            S.op("dve", lambda e: e.tensor_tensor(out=ec[:], in0=ec[:], in1=sm[:, 32:48].unsqueeze(2).to_broadcast([128, 16, NBP]), op=ALU.mult),
                 reads=["ecN", "sm2"], writes=["ecN"])
            S.op("act", lambda e: e.copy(out=pb16[:], in_=ec[:]), reads=["ecN"], writes=["pb16N"])
            S.op("dve", lambda e: e.tensor_reduce(out=scg[:], in_=ec[:].rearrange("p (g r) n -> p g n r", g=4), axis=AX.X, op=ALU.add),
                 reads=["ecN"], writes=["scgN"])
            S.op("dve", lambda e: e.tensor_scalar(out=adj[:], in0=Dblk_i[:], scalar1=float(128 * i), scalar2=None, op0=ALU.add), writes=["adjN"])
            S.op("dve", lambda e: e.tensor_scalar(out=vld[:], in0=adj[:], scalar1=0.0, scalar2=None, op0=ALU.is_ge), reads=["adjN"], writes=["vldN"])
            S.op("dve", lambda e: e.tensor_scalar(out=adj2[:], in0=adj[:], scalar1=128.0, scalar2=None, op0=ALU.is_lt), reads=["adjN"], writes=["adj2N"])
            S.op("dve", lambda e: e.tensor_tensor(out=adj2[:], in0=adj2[:], in1=vld[:], op=ALU.mult), reads=["adj2N", "vldN"], writes=["adj2N"])
            S.op("dve", lambda e: e.tensor_scalar(out=adj2[:, 0:1], in0=adj2[:, 0:1], scalar1=1.0, scalar2=None, op0=ALU.max), reads=["adj2N"], writes=["adj2N"])
            S.op("dve", lambda e: e.tensor_scalar(out=adj[:], in0=vld[:], scalar1=-1.0, scalar2=2.0e4, op0=ALU.add, op1=ALU.mult), reads=["vldN"], writes=["adjN"])
            S.op("dve", lambda e: e.scalar_tensor_tensor(out=adj[:], in0=adj2[:], scalar=1.0e4, in1=adj[:], op0=ALU.mult, op1=ALU.add),
                 reads=["adj2N", "adjN"], writes=["adjN"])
            S.op("dve", lambda e: e.tensor_tensor(out=scg[:], in0=scg[:], in1=adj[:].unsqueeze(1).to_broadcast([128, 4, NBP]), op=ALU.add),
                 reads=["scgN", "adjN"], writes=["scgN"])
            if NBP > 16:
                for g in range(4):
                    S.op("dve", lambda e, g=g: e.max(out=m8[:, g, 0:8], in_=scg[:, g, :]), reads=["scgN"], writes=[("m8N", g)])
                    S.op("dve", lambda e, g=g: e.match_replace(out=scw[:, g, :], in_to_replace=m8[:, g, 0:8], in_values=scg[:, g, :], imm_value=-1.0e9),
                         reads=["scgN", ("m8N", g)], writes=[("scwN", g)])
                    S.op("dve", lambda e, g=g: e.max(out=m8[:, g, 8:16], in_=scw[:, g, :]), reads=[("scwN", g)], writes=[("m8bN", g)])
                    S.op("dve", lambda e, g=g: e.tensor_scalar(out=scw[:, g, :], in0=scg[:, g, :], scalar1=m8[:, g, 15:16], scalar2=None, op0=ALU.is_ge),
                         reads=["scgN", ("m8bN", g), ("scwN", g)], writes=[("scwN", g)])
                S.op("dve", lambda e: e.tensor_tensor(out=scw[:], in0=scw[:], in1=vld[:].unsqueeze(1).to_broadcast([128, 4, NBP]), op=ALU.mult),
                     reads=[("scwN", g) for g in range(4)] + ["vldN"], writes=["scwA"])
            else:
                S.op("dve", lambda e: e.tensor_copy(out=scw[:], in_=vld[:].unsqueeze(1).to_broadcast([128, 4, NBP])), reads=["vldN"], writes=["scwA"])
            S.op("dve", lambda e: e.tensor_scalar(out=selp[:], in0=scw[:], scalar1=-1.0, scalar2=-BIGN, op0=ALU.add, op1=ALU.mult),
                 reads=["scwA"], writes=["selpN"])
            for g in range(4):
                S.op("pe", lambda e, g=g: e.transpose(out=PB[0][0:NBP, g * 128:(g + 1) * 128], in_=selp[:, g, :], identity=ident_b[:]),
                     reads=["selpN"], writes=[("pb", 0)])
            S.op("act", lambda e: e.copy(out=penT[0:NBP], in_=PB[0][0:NBP, 0:512].rearrange("p (g t) -> p g t", g=4)), reads=[("pb", 0)], writes=["penTN"])
            for half in range(2):
                for a in range(8):
                    S.op("pe", lambda e, a=a, half=half: e.transpose(out=PB[1][0:NBP, a * 128:(a + 1) * 128], in_=pb16[:, half * 8 + a, :], identity=ident_b[:]),
                         reads=["pb16N"], writes=[("pb", 1)])
                S.op("dve", lambda e, half=half: e.tensor_copy(out=pT[0:NBP, half * 8:(half + 1) * 8, :],
                                                              in_=PB[1][0:NBP, 0:1024].rearrange("p (a t) -> p a t", a=8)),
                     reads=[("pb", 1)], writes=[("pTN", half)])
            for half in range(2):
                for a in range(8):
                    h = half * 8 + a
                    S.op("pe", lambda e, a=a, h=h, half=half: e.matmul(PS[half][:, a * 64:(a + 1) * 64], lhsT=pT[0:NBP, h, :], rhs=vc_p[0:NBP, h // 4, :],
                                                                       start=True, stop=True), reads=[("pTN", half), "vc"], writes=[("ps", half)])
                S.op("dve", lambda e, half=half: e.tensor_tensor(out=oacc[:, half * 8:(half + 1) * 8, :],
                                                                in0=PS[half][:, 0:512].rearrange("p (a d) -> p a d", a=8),
                                                                in1=gts[:, i, half * 8:(half + 1) * 8].unsqueeze(2).to_broadcast([128, 8, 64]), op=ALU.mult),
                     reads=[("ps", half), ("gtsN", i), "oaccN"], writes=["oaccN"])
            NSA_BR = int(os.environ.get("NSA_BR", "7"))
            if not (NSA_BR & 1):
                S.op("dve", lambda e: e.memset(oacc[:], 0.0), reads=["oaccN"], writes=["oaccN"])
            for g in range(4):
                if NSA_BR & 2:
                    attend(i, g, 0, 0, list(range(0, i + 1)), True, 1)
                if NSA_BR & 4:
                    attend(i, g, 4, 4, list(range(max(0, i - 4), i + 1)), False, 2)
            S.store("sp", omix[i * 128:(i + 1) * 128, 0:1024], oacc[:].rearrange("p h d -> p (h d)"), reads=["oaccN"])
        S.barrier()
        P.release()

        P = Pool(nc)
        bS = P.t("bS3", [4, NBS], F32)
        eS = P.t("eS3", [4, NBS], F32)
        sm3 = P.t("sm3", [4, 8], F32)
        srow = P.t("srow3", [1, 4, NBS], F32)
        srw = P.t("srw3", [1, 4, NBS], F32)
        m83 = P.t("m83", [1, 4, 16], F32)
        penr = P.t("penr3", [1, 4, NBS], BF16)
        penE = P.t("penE3", [128, NPG, 4], F32)
        pTs = P.t("pTs3", [SEGB, NSEG, 4], BF16)
        ocmp = P.t("ocmp3", [4, 4, 64], F32)
        osw = P.t("osw3", [4, 2, 4, 64], F32)
        gS = P.t("gS3", [4, 12], F32)
        wr = P.t("wr3", [4, 16], F32)
        pg = [P.t(f"pg3{i}", [128, 512], F32) for i in range(3)]
        kTs = [P.t(f"kTs3{i}", [64, 4, 128], BF16) for i in range(2)]
        Vs = [P.t(f"Vs3{i}", [128, 4, 65], BF16) for i in range(2)]
        b16 = [P.t(f"b163{i}", [128, 16], F32) for i in range(2)]
        sc16 = [P.t(f"sc163{i}", [128, 16], F32) for i in range(2)]
        PTs = [P.t(f"PTs3{i}", [128, 16], BF16) for i in range(2)]
        for i in range(2):
            S.op("pool", lambda e, i=i: e.memset(Vs[i][:], 1.0), writes=[("Vs3", i)])
        for bg in range(12):
            S.op("pe", lambda e, bg=bg: e.matmul(PS[5][0:4, bg:bg + 1], lhsT=g_s[0:1, (bg // 4) * 16 + (bg % 4) * 4:(bg // 4) * 16 + (bg % 4) * 4 + 4],
                                                 rhs=ones_f[0:1, 0:1], start=True, stop=True), reads=["g_s"], writes=[("ps", 5)])
        S.op("act", lambda e: e.copy(out=gS[:], in_=PS[5][0:4, 0:12]), reads=[("ps", 5)], writes=["gS3"])
        for g in range(4):
            S.op("pe", lambda e, g=g: e.matmul(PS[0][0:4, 0:NBS], lhsT=q_s[:, 4 * g:4 * g + 4], rhs=kcT_s[:, g, :], start=True, stop=True),
                 reads=["q_s", "kcT"], writes=[("ps", 0)])
            S.op("dve", lambda e, g=g: e.tensor_scalar(out=bS[:], in0=dcs[:], scalar1=slg[:, g:g + 1], scalar2=None, op0=ALU.mult), writes=["bS3"])
            S.op("dve", lambda e: e.tensor_tensor(out=eS[:], in0=PS[0][0:4, 0:NBS], in1=bS[:], op=ALU.add), reads=[("ps", 0), "bS3"], writes=["eS3"])
            S.op("act", lambda e: e.activation(out=eS[:], in_=eS[:], func=AF.Exp), reads=["eS3"], writes=["eS3"])
            S.op("dve", lambda e: e.tensor_reduce(out=sm3[:, 0:1], in_=eS[:], axis=AX.X, op=ALU.add), reads=["eS3"], writes=["sm30"])
            S.op("dve", lambda e: e.tensor_scalar(out=sm3[:, 1:2], in0=sm3[:, 0:1], scalar1=1e-30, scalar2=None, op0=ALU.max), reads=["sm30"], writes=["sm31"])
            S.op("dve", lambda e: e.reciprocal(out=sm3[:, 2:3], in_=sm3[:, 1:2]), reads=["sm31"], writes=["sm32"])
            S.op("dve", lambda e: e.tensor_scalar(out=eS[:], in0=eS[:], scalar1=sm3[:, 2:3], scalar2=None, op0=ALU.mult), reads=["eS3", "sm32"], writes=["eS3"])
            S.op("pe", lambda e: e.matmul(PS[1][0:1, 0:NBS], lhsT=ones_f[0:4, 0:1], rhs=eS[:], start=True, stop=True), reads=["eS3"], writes=[("ps", 1)])
            S.op("act", lambda e, g=g: e.copy(out=srow[0:1, g, :], in_=PS[1][0:1, 0:NBS]), reads=[("ps", 1)], writes=[("srow3", g)])
            for sg in range(NSEG):
                S.op("pe", lambda e, sg=sg: e.transpose(out=PS[2][0:SEGB, sg * 4:(sg + 1) * 4], in_=eS[0:4, sg * SEGB:(sg + 1) * SEGB],
                                                        identity=ident_f[0:4, 0:4]), reads=["eS3"], writes=[("ps", 2)])
            S.op("dve", lambda e: e.tensor_copy(out=pTs[:].rearrange("p s r -> p (s r)"), in_=PS[2][0:SEGB, 0:NSEG * 4]), reads=[("ps", 2)], writes=["pTs3"])
            for sg in range(NSEG):
                S.op("pe", lambda e, sg=sg, g=g: e.matmul(PS[3][0:4, g * 64:(g + 1) * 64], lhsT=pTs[:, sg, :], rhs=vc_s[0:SEGB, sg, g, :],
                                                         start=(sg == 0), stop=(sg == NSEG - 1)), reads=["pTs3", "vc"], writes=[("ps", 3)])
        S.op("act", lambda e: e.copy(out=ocmp[:].rearrange("p g d -> p (g d)"), in_=PS[3][0:4, 0:256]), reads=[("ps", 3)], writes=["ocmp3"])
        sr = [("srow3", g) for g in range(4)]
        S.op("dve", lambda e: e.memset(srow[0:1, :, 0:1], 1.0e4), reads=sr, writes=["srowA"])
        S.op("dve", lambda e: e.memset(srow[0:1, :, NBS - 1:NBS], 1.0e4), reads=sr, writes=["srowB"])
        srk = sr + ["srowA", "srowB"]
        for g in range(4):
            S.op("dve", lambda e, g=g: e.max(out=m83[0:1, g, 0:8], in_=srow[0:1, g, :]), reads=srk, writes=[("m83", g)])
            S.op("dve", lambda e, g=g: e.match_replace(out=srw[0:1, g, :], in_to_replace=m83[0:1, g, 0:8], in_values=srow[0:1, g, :], imm_value=-1.0e9),
                 reads=srk + [("m83", g)], writes=[("srw3", g)])
            S.op("dve", lambda e, g=g: e.max(out=m83[0:1, g, 8:16], in_=srw[0:1, g, :]), reads=[("srw3", g)], writes=[("m83b", g)])
            S.op("dve", lambda e, g=g: e.tensor_scalar(out=srw[0:1, g, :], in0=srow[0:1, g, :], scalar1=m83[0:1, g, 14:15], scalar2=None, op0=ALU.is_ge),
                 reads=srk + [("m83b", g), ("srw3", g)], writes=[("srw3", g)])
        S.op("dve", lambda e: e.tensor_scalar(out=penr[:], in0=srw[:], scalar1=-1.0, scalar2=-BIGN, op0=ALU.add, op1=ALU.mult),
             reads=[("srw3", g) for g in range(4)], writes=["penr3"])
        for g in range(4):
            for k in range(2):
                S.op("pe", lambda e, g=g, k=k: e.matmul(PS[4][:, g * NPG:(g + 1) * NPG], lhsT=half_sel[0:1, k, :],
                                                       rhs=penr[0:1, g, :].rearrange("p (j k) -> p j k", k=2)[:, :, k],
                                                       start=(k == 0), stop=(k == 1)), reads=["penr3"], writes=[("ps", 4)])
        S.op("dve", lambda e: e.tensor_copy(out=penE[:], in_=PS[4][:, 0:4 * NPG].rearrange("p (g j) -> p j g", g=4)), reads=[("ps", 4)], writes=["penE3"])

        def key_pass(br, ntile, load_fn, bias_tab, use_pen, knew, vnew):
            acc = PS[5]
            S.op("dve", lambda e: e.memset(acc[0:4, 0:260], 0.0), writes=[("ps", 5)])
            for j in range(ntile):
                b2, b3 = j % 2, j % 3
                load_fn(j, pg[b3], ("pg3", b3))
                for g in range(4):
                    S.op("pe", lambda e, g=g, b3=b3: e.transpose(out=PS[b2][0:64, g * 128:(g + 1) * 128], in_=pg[b3][:, g * 64:(g + 1) * 64], identity=ident_f[:]),
                         reads=[("pg3", b3)], writes=[("ps", b2)])
                evac(kTs[b2][:].rearrange("p g t -> p (g t)"), PS[b2][0:64, 0:512], reads=[("ps", b2)], writes=[("kTs3", b2)])
                S.op("act", lambda e, b2=b2, b3=b3: e.copy(out=Vs[b2][:, :, 0:64], in_=pg[b3][:, 256:512].rearrange("p (g d) -> p g d", g=4)),
                     reads=[("pg3", b3), ("Vs3", b2)], writes=[("Vs3v", b2)])
                for g in range(4):
                    S.op("pe", lambda e, g=g, b2=b2: e.matmul(PS[2 + b2][:, g * 4:(g + 1) * 4], lhsT=kTs[b2][:, g, :], rhs=q_s[:, 4 * g:4 * g + 4],
                                                             start=True, stop=True), reads=[("kTs3", b2), "q_s"], writes=[("ps", 2 + b2)])
                S.op("dve", lambda e, b2=b2, j=j: e.tensor_scalar(out=b16[b2][:], in0=negsl16[:], scalar1=bias_tab[:, j:j + 1], scalar2=None, op0=ALU.mult),
                     writes=[("b163", b2)])
                if (not use_pen) and j == 0:
                    S.op("dve", lambda e, b2=b2: e.tensor_scalar(out=b16[b2][:], in0=b16[b2][:], scalar1=mw0[:, 0:1], scalar2=None, op0=ALU.add),
                         reads=[("b163", b2)], writes=[("b163", b2)])
                S.op("dve", lambda e, b2=b2: e.tensor_tensor(out=sc16[b2][:], in0=PS[2 + b2][:, 0:16], in1=b16[b2][:], op=ALU.add),
                     reads=[("ps", 2 + b2), ("b163", b2)], writes=[("sc163", b2)])
                if use_pen:
                    S.op("dve", lambda e, b2=b2, j=j: e.tensor_tensor(out=sc16[b2][:].rearrange("p (g r) -> p g r", g=4),
                                                                       in0=sc16[b2][:].rearrange("p (g r) -> p g r", g=4),
                                                                       in1=penE[:, j, :].unsqueeze(2).to_broadcast([128, 4, 4]), op=ALU.add),
                         reads=[("sc163", b2), "penE3"], writes=[("sc163", b2)])
                S.op("act", lambda e, b2=b2: e.activation(out=PTs[b2][:], in_=sc16[b2][:], func=AF.Exp), reads=[("sc163", b2)], writes=[("PTs3", b2)])
                for g in range(4):
                    S.op("pe", lambda e, g=g, b2=b2: e.matmul(acc[0:4, g * 65:(g + 1) * 65], lhsT=PTs[b2][:, 4 * g:4 * g + 4], rhs=Vs[b2][:, g, :],
                                                             start=False, stop=False, skip_group_check=True),
                         reads=[("PTs3", b2), ("Vs3v", b2), ("Vs3", b2)], writes=[("ps", 5)])
            for g in range(4):
                S.op("pe", lambda e, g=g: e.matmul(PS[2][0:1, g * 4:(g + 1) * 4], lhsT=k_s[:, knew + g:knew + g + 1], rhs=q_s[:, 4 * g:4 * g + 4],
                                                   start=True, stop=True), reads=["k_s", "q_s"], writes=[("ps", 2)])
            S.op("act", lambda e: e.activation(out=PTs[0][0:1, :], in_=PS[2][0:1, 0:16], func=AF.Exp), reads=[("ps", 2)], writes=[("PTs3", 0)])
            for g in range(4):
                S.op("pe", lambda e, g=g: e.matmul(acc[0:4, g * 65:(g + 1) * 65], lhsT=PTs[0][0:1, 4 * g:4 * g + 4], rhs=v_s[0:1, vnew + g, :],
                                                   start=False, stop=True, skip_group_check=True), reads=[("PTs3", 0), "v_s"], writes=[("ps", 5)])
            a3 = acc[0:4, 0:260].rearrange("p (g c) -> p g c", g=4)
            S.op("dve", lambda e: e.reciprocal(out=wr[:, 0:4], in_=a3[:, :, 64]), reads=[("ps", 5)], writes=["wr0"])
            S.op("dve", lambda e: e.tensor_tensor(out=wr[:, 4:8], in0=wr[:, 0:4], in1=gS[:, (1 + br) * 4:(2 + br) * 4], op=ALU.mult),
                 reads=["wr0", "gS3"], writes=["wr1"])
            S.op("dve", lambda e: e.tensor_tensor(out=osw[:, br], in0=a3[:, :, 0:64], in1=wr[:, 4:8].unsqueeze(2).to_broadcast([4, 4, 64]), op=ALU.mult),
                 reads=[("ps", 5), "wr1"], writes=[("osw3", br)])

        def load_sel(j, dst, key):
            S.dma("pool", dst[:], cache_sel, reads=[("idx_i", l)], writes=[key],
                  indirect=bass.IndirectOffsetOnAxis(ap=idx_i[:, l, j:j + 1], axis=0))

        def load_win(j, dst, key):
            S.store("sp", dst[:], cache_win[l, j * 128:(j + 1) * 128, :], writes=[key])

        key_pass(0, NPG, load_sel, tb, True, 0, 0)
        key_pass(1, 4, load_win, tbw, False, 4, 4)
        if dbg and l == 0:
            S.dma("sp", dbg_sel, srw[:].rearrange("p g n -> p (g n)"), reads=[("srw3", g) for g in range(4)])
            S.dma("sp", dbg_srow, srow[:].rearrange("p g n -> p (g n)"), reads=srk)
            S.dma("sp", dbg_o[:, 0:256], ocmp[:].rearrange("p g d -> p (g d)"), reads=["ocmp3"])
            S.dma("sp", dbg_o[:, 256:768], osw[:].rearrange("p b g d -> p (b g d)"), reads=[("osw3", 0), ("osw3", 1)])
            S.dma("sp", dbg_gs, gS[:], reads=["gS3"])
        S.op("dve", lambda e: e.tensor_tensor(out=ocmp[:], in0=ocmp[:], in1=gS[:, 0:4].unsqueeze(2).to_broadcast([4, 4, 64]), op=ALU.mult),
             reads=["ocmp3", "gS3"], writes=["ocmp3"])
        NSA_BR3 = int(os.environ.get("NSA_BR", "7"))
        if not (NSA_BR3 & 1):
            S.op("dve", lambda e: e.memset(ocmp[:], 0.0), reads=["ocmp3"], writes=["ocmp3"])
        if NSA_BR3 & 2:
            S.op("dve", lambda e: e.tensor_tensor(out=ocmp[:], in0=ocmp[:], in1=osw[:, 0], op=ALU.add), reads=["ocmp3", ("osw3", 0)], writes=["ocmp3"])
        if NSA_BR3 & 4:
            S.op("dve", lambda e: e.tensor_tensor(out=ocmp[:], in0=ocmp[:], in1=osw[:, 1], op=ALU.add), reads=["ocmp3", ("osw3", 1)], writes=["ocmp3"])
        S.store("sp", omix[T:T + 1, 0:1024].rearrange("o (g r d) -> (o r) g d", g=4, r=4), ocmp[:], reads=["ocmp3"])
        S.barrier()
        P.release()
        LP.release()

    def mixers(l):
        S.store("sp", omix[NT * 128:(NT + 1) * 128, :], zeros_f[:, 0:2048])
        S.barrier()
        Pg, gch, gfin = gla_mixer(l)
        Pr, rch, rfin = rw_mixer(l)

        def segs(items):
            out = [[]]
            for it_ in items:
                if it_[0] == "sp":
                    if out[-1]:
                        out.append([])
                else:
                    out[-1].append(it_)
            return [s_ for s_ in out if s_]

        for c in range(NCH):
            S.cap = []
            rch(c)
            A = segs(S.cap)
            S.cap = []
            gch(c)
            B = segs(S.cap)
            S.cap = None
            na, nb_ = sum(len(x) for x in A), sum(len(x) for x in B)
            i = j = ca = cb_ = 0
            while i < len(A) or j < len(B):
                if j >= len(B) or (i < len(A) and ca * nb_ <= cb_ * na):
                    S.replay(A[i]); ca += len(A[i]); i += 1
                else:
                    S.replay(B[j]); cb_ += len(B[j]); j += 1
        gfin()
        rfin()
        S.barrier()
        Pr.release()
        Pg.release()
        nsa_mixer(l)
        if dbg and l == 0:
            S.dma("sp", omix_l0, omix)
            S.barrier()

    for l in range(L):
        P = Pool(nc)
        hT = P.t("hT", [128, KD, TA], BF16)
        rmsnorm_T(P, xa, bcast_row(norm1_g[l], D), hT, f"n1")
        wt = [P.t(f"wA{i}", [128, KD, 512], BF16) for i in range(2)]
        stg = [P.t(f"stgA{i}", [128, 512], F32) for i in range(3)]
        w_in_v = w_in[l].rearrange("(ko ki) n -> ki ko n", ki=128)
        ncb = (NIN + 511) // 512
        it = 0
        for cb in range(ncb):
            c0 = cb * 512
            cw = min(512, NIN - c0)
            wb = wt[cb % 2]
            S.dma("pool", wb[:, :, 0:cw], w_in_v[:, :, c0:c0 + cw], writes=[("wA", cb % 2)])
            for tt in range(NTA):
                ps = PS[it % 4]
                for k in range(KD):
                    S.op("pe", lambda e, ps=ps, k=k, tt=tt, wb=wb, cw=cw: e.matmul(
                        ps[:, 0:cw], lhsT=hT[:, k, tt * 128:(tt + 1) * 128], rhs=wb[:, k, 0:cw],
                        start=(k == 0), stop=(k == KD - 1)),
                        reads=[("n1", "hT", tt, k // 4), ("wA", cb % 2)], writes=[("ps", it % 4)])
                sg = stg[it % 3]
                evac(sg[:, 0:cw], ps[:, 0:cw], reads=[("ps", it % 4)], writes=[("stgA", it % 3)])
                S.store("sp", proj[tt * 128:(tt + 1) * 128, c0:c0 + cw], sg[:, 0:cw], reads=[("stgA", it % 3)])
                it += 1
        S.barrier()
        P.release()

        P = Pool(nc)
        NB = C_MG
        pt = [P.t(f"ptB{i}", [128, NB], F32) for i in range(2)]
        qkg = P.t("qkg", [128, 4, 64], F32)
        S.dma("sp", qkg[:].rearrange("p a d -> p (a d)"), nsa_qk_g[l].rearrange("a d -> (a d)").partition_broadcast(128),
              writes=["qkg"])
        tmpB = P.t("tmpB", [128, 16, 64], F32)
        ssB = P.t("ssB", [128, 64], F32)
        rows = [P.t(f"rowsB{i}", [128, 2, 512], F32) for i in range(2)]

        def headnorm(out3, in3, g_ap, H, rk, wk, sc=1.0):
            S.op("dve", lambda e: e.tensor_tensor(out=tmpB[:, 0:H, :], in0=in3, in1=in3, op=ALU.mult),
                 reads=rk, writes=["tmpB"])
            S.op("dve", lambda e: e.tensor_reduce(out=ssB[:, 0:H], in_=tmpB[:, 0:H, :], axis=AX.X, op=ALU.add),
                 reads=["tmpB"], writes=["ssB0"])
            S.op("dve", lambda e: e.tensor_scalar(out=ssB[:, 16:16 + H], in0=ssB[:, 0:H], scalar1=1.0 / 64, scalar2=EPS,
                                                  op0=ALU.mult, op1=ALU.add), reads=["ssB0"], writes=["ssB1"])
            S.op("act", lambda e: e.activation(out=ssB[:, 48:48 + H], in_=ssB[:, 16:16 + H], func=AF.Ln),
                 reads=["ssB1"], writes=["ssB1b"])
            S.op("act", lambda e: e.activation(out=ssB[:, 32:32 + H], in_=ssB[:, 48:48 + H], func=AF.Exp, scale=-0.5,
                                               bias=float(np.log(sc))), reads=["ssB1b"], writes=["ssB2"])
            S.op("dve", lambda e: e.tensor_tensor(out=tmpB[:, 0:H, :], in0=in3,
                                                  in1=ssB[:, 32:32 + H].unsqueeze(2).to_broadcast([128, H, 64]), op=ALU.mult),
                 reads=rk + ["ssB2"], writes=["tmpB"])
            S.op("dve", lambda e: e.tensor_tensor(out=out3, in0=tmpB[:, 0:H, :],
                                                  in1=g_ap.unsqueeze(1).to_broadcast([128, H, 64]), op=ALU.mult),
                 reads=["tmpB", "qkg"], writes=wk)

        for tt in range(NTA):
            b = tt % 2
            ptb = pt[b]
            rw_ = rows[b]
            S.dma("sp", ptb[:], proj[tt * 128:(tt + 1) * 128, 0:NB], writes=[("ptB", b)])
            is_s = (tt == NT)
            if not is_s:
                S.store("sp", p_cmp_o[l, tt * 128:(tt + 1) * 128, :], ptb[:, C_CMP:C_CMP + 512], reads=[("ptB", b)])
            else:
                S.store("sp", s_cmp_o[l], ptb[0:1, C_CMP:C_CMP + 512], reads=[("ptB", b)])
            for wi, (c0, gi, po, so) in enumerate(((C_SEL, 2, p_sel_o, s_sel_o), (C_WIN, 3, p_win_o, s_win_o))):
                headnorm(rw_[:, wi, 0:256].rearrange("p (h d) -> p h d", h=4),
                         ptb[:, c0:c0 + 256].rearrange("p (h d) -> p h d", h=4), qkg[:, gi, :], 4,
                         [("ptB", b)], [("rowsB", b, wi, "k")])
                S.op("act", lambda e, rw_=rw_, ptb=ptb, wi=wi, c0=c0: e.copy(out=rw_[:, wi, 256:512], in_=ptb[:, c0 + 256:c0 + 512]),
                     reads=[("ptB", b)], writes=[("rowsB", b, wi, "v")])
                rk = [("rowsB", b, wi, "k"), ("rowsB", b, wi, "v")]
                if is_s:
                    S.store("sp", so[l], rw_[0:1, wi, :], reads=rk)
                elif wi == 0:
                    S.store("sp", po[l, tt * 128:(tt + 1) * 128, :], rw_[:, wi, :], reads=rk)
                else:
                    t0w = tt * 128 - (T - WKEEP)
                    if t0w >= 0:
                        S.store("sp", po[l, t0w:t0w + 128, :], rw_[:, wi, :], reads=rk)
            if tt == NT - 1:
                S.store("sp", p_sh_o[l:l + 1, :], ptb[127:128, C_RW:C_RW + RWP], reads=[("ptB", b)])
            if is_s:
                S.store("sp", s_sh_o[l:l + 1, :], ptb[0:1, C_RW:C_RW + RWP], reads=[("ptB", b)])
        S.barrier()
        P.release()

        mixers(l)

        P = Pool(nc)
        oT = P.t("oT", [128, 16, TA], BF16)
        rmsnorm_T(P, omix, None, oT, "oT", NC=2048)
        wt = [P.t(f"wC{i}", [128, 16, 512], BF16) for i in range(2)]
        mg = [P.t(f"mgC{i}", [128, 3, 512], F32) for i in range(2)]
        acc = [P.t(f"accC{i}", [128, 512], F32) for i in range(2)]
        tmpc = P.t("tmpC", [128, 512], F32)
        ups = ((nsa_up, 0, 8), (gla_up, 8, 4), (rw_up, 12, 4))
        it = 0
        for nb in range(D // 512 if D >= 512 else 1):
            cw = min(512, D)
            c0 = nb * 512
            wb = wt[nb % 2]
            for (wsrc, k0, nk) in ups:
                S.dma("pool", wb[:, k0:k0 + nk, 0:cw], wsrc[l].rearrange("(ko ki) n -> ki ko n", ki=128)[:, :, c0:c0 + cw],
                      writes=[("wC", nb % 2, k0)])
            for tt in range(NTA):
                b = it % 2
                for j in range(3):
                    S.dma("sp", mg[b][:, j, 0:cw], proj[tt * 128:(tt + 1) * 128, C_MG + j * D + c0:C_MG + j * D + c0 + cw],
                          writes=[("mgC", b, j)])
                    S.op("act", lambda e, b=b, j=j: e.activation(out=mg[b][:, j, 0:cw], in_=mg[b][:, j, 0:cw], func=AF.Sigmoid),
                         reads=[("mgC", b, j)], writes=[("mgC", b, j)])
                for j, (wsrc, k0, nk) in enumerate(ups):
                    ps = PS[j + 3 * (it % 2)]
                    pk = ("ps", j + 3 * (it % 2))
                    for k in range(k0, k0 + nk):
                        S.op("pe", lambda e, ps=ps, k=k, tt=tt, wb=wb, k0=k0, nk=nk: e.matmul(
                            ps[:, 0:cw], lhsT=oT[:, k, tt * 128:(tt + 1) * 128], rhs=wb[:, k, 0:cw],
                            start=(k == k0), stop=(k == k0 + nk - 1)),
                            reads=[("oT", "hT", tt, k // 4), ("wC", nb % 2, k0)], writes=[pk])
                    if j == 0:
                        S.op("dve", lambda e, b=b, ps=ps: e.tensor_tensor(out=acc[b][:, 0:cw], in0=mg[b][:, 0, 0:cw], in1=ps[:, 0:cw],
                                                                           op=ALU.mult), reads=[("mgC", b, 0), pk], writes=[("accC", b)])
                    else:
                        S.op("dve", lambda e, b=b, ps=ps, j=j: e.tensor_tensor(out=tmpc[:, 0:cw], in0=mg[b][:, j, 0:cw], in1=ps[:, 0:cw],
                                                                                op=ALU.mult), reads=[("mgC", b, j), pk], writes=["tmpC"])
                        S.op("dve", lambda e, b=b: e.tensor_tensor(out=acc[b][:, 0:cw], in0=acc[b][:, 0:cw], in1=tmpc[:, 0:cw], op=ALU.add),
                             reads=["tmpC", ("accC", b)], writes=[("accC", b)])
                S.store("sp", mrg[tt * 128:(tt + 1) * 128, c0:c0 + cw], acc[b][:, 0:cw], reads=[("accC", b)])
                it += 1
        S.barrier()
        P.release()

        P = Pool(nc)
        mT = P.t("mT", [128, KD, TA], BF16)
        rmsnorm_T(P, mrg, None, mT, "mT", NC=D)
        wt = [P.t(f"wD{i}", [128, KD, 512], BF16) for i in range(2)]
        xr = [P.t(f"xD{i}", [128, 512], F32) for i in range(3)]
        w_v = w_out[l].rearrange("(ko ki) n -> ki ko n", ki=128)
        it = 0
        for nb in range(D // 512 if D >= 512 else 1):
            cw = min(512, D)
            c0 = nb * 512
            wb = wt[nb % 2]
            S.dma("pool", wb[:, :, 0:cw], w_v[:, :, c0:c0 + cw], writes=[("wD", nb % 2)])
            for tt in range(NTA):
                ps = PS[it % 4]
                xb_ = xr[it % 3]
                S.dma("sp", xb_[:, 0:cw], xa[tt * 128:(tt + 1) * 128, c0:c0 + cw], writes=[("xD", it % 3)])
                for k in range(KD):
                    S.op("pe", lambda e, ps=ps, k=k, tt=tt, wb=wb: e.matmul(
                        ps[:, 0:cw], lhsT=mT[:, k, tt * 128:(tt + 1) * 128], rhs=wb[:, k, 0:cw],
                        start=(k == 0), stop=(k == KD - 1)),
                        reads=[("mT", "hT", tt, k // 4), ("wD", nb % 2)], writes=[("ps", it % 4)])
                S.op("dve", lambda e, ps=ps, xb_=xb_: e.tensor_tensor(out=xb_[:, 0:cw], in0=xb_[:, 0:cw], in1=ps[:, 0:cw], op=ALU.add),
                     reads=[("ps", it % 4), ("xD", it % 3)], writes=[("xD", it % 3)])
                S.store("sp", xm[tt * 128:(tt + 1) * 128, c0:c0 + cw], xb_[:, 0:cw], reads=[("xD", it % 3)])
                it += 1
        S.barrier()
        P.release()

        P = Pool(nc)
        h2T = P.t("h2T", [128, KD, TA], BF16)
        rmsnorm_T(P, xm, bcast_row(norm2_g[l], D), h2T, "n2")
        wt = [P.t(f"wE{i}", [128, KD, 512], BF16) for i in range(2)]
        hst = [P.t(f"hstE{i}", [128, 4, 4, 128], BF16) for i in range(2)]
        rl = [P.t(f"rlE{i}", [128, 512], F32) for i in range(2)]
        w_v = mlp_w1[l].rearrange("(ko ki) n -> ki ko n", ki=128)
        chunks = [(c * 4, min(4, NTA - c * 4)) for c in range((NTA + 3) // 4)]
        it = 0
        ic = 0
        for fb in range(DFF // 512):
            wb = wt[fb % 2]
            S.dma("pool", wb[:], w_v[:, :, fb * 512:(fb + 1) * 512], writes=[("wE", fb % 2)])
            for (t0, ntl) in chunks:
                hs = hst[ic % 2]
                ntok = ntl * 128
                for fs in range(4):
                    ps = PS[it % 4]
                    for k in range(KD):
                        S.op("pe", lambda e, ps=ps, k=k, wb=wb, fs=fs, t0=t0, ntok=ntok: e.matmul(
                            ps[:, 0:ntok], lhsT=wb[:, k, fs * 128:(fs + 1) * 128], rhs=h2T[:, k, t0 * 128:t0 * 128 + ntok],
                            start=(k == 0), stop=(k == KD - 1)),
                            reads=[("n2", "hT", t0 + j, k // 4) for j in range(ntl)] + [("wE", fb % 2)], writes=[("ps", it % 4)])
                    r_ = rl[it % 2]
                    S.op("act", lambda e, ps=ps, r_=r_, ntok=ntok: e.activation(out=r_[:, 0:ntok], in_=ps[:, 0:ntok], func=AF.Relu),
                         reads=[("ps", it % 4)], writes=[("rlE", it % 2)])
                    S.op("dve", lambda e, hs=hs, r_=r_, fs=fs, ntl=ntl, ntok=ntok: e.tensor_tensor(
                        out=hs[:, 0:ntl, fs, :], in0=r_[:, 0:ntok].rearrange("p (a t) -> p a t", a=ntl),
                        in1=r_[:, 0:ntok].rearrange("p (a t) -> p a t", a=ntl), op=ALU.mult),
                        reads=[("rlE", it % 2)], writes=[("hstE", ic % 2, fs)])
                    it += 1
                for j in range(ntl):
                    S.store("sp", hidS[t0 + j, :, fb * 4:(fb + 1) * 4, :], hs[:, j, :, :], reads=[("hstE", ic % 2, fs) for fs in range(4)])
                ic += 1
        S.barrier()
        P.release()

        P = Pool(nc)
        NFO = DFF // 128
        FW = 512 if D >= 512 else 256
        wt = [P.t(f"wF{i}", [128, NFO, FW], BF16) for i in range(2)]
        ht = [P.t(f"htF{i}", [128, NFO, 128], BF16) for i in range(2)]
        xr = [P.t(f"xF{i}", [128, FW], F32) for i in range(3)]
        w_v = mlp_w2[l].rearrange("(fo fi) n -> fi fo n", fi=128)
        it = 0
        for db in range(D // FW):
            c0 = db * FW
            wb = wt[db % 2]
            for hf_ in range(2):
                S.dma("pool", wb[:, hf_ * (NFO // 2):(hf_ + 1) * (NFO // 2), :], w_v[:, hf_ * (NFO // 2):(hf_ + 1) * (NFO // 2), c0:c0 + FW],
                      writes=[("wF", db % 2, hf_)])
            for tt in range(NTA):
                hb = ht[it % 2]
                S.dma("sp", hb[:], hidS[tt], writes=[("htF", it % 2)])
                xb_ = xr[it % 3]
                S.dma("sp", xb_[:], xm[tt * 128:(tt + 1) * 128, c0:c0 + FW], writes=[("xF", it % 3)])
                ps = PS[it % 4]
                for fo in range(NFO):
                    S.op("pe", lambda e, ps=ps, fo=fo, hb=hb, wb=wb: e.matmul(
                        ps[:, 0:FW], lhsT=hb[:, fo, :], rhs=wb[:, fo, :], start=(fo == 0), stop=(fo == NFO - 1)),
                        reads=[("htF", it % 2), ("wF", db % 2, fo // (NFO // 2))], writes=[("ps", it % 4)])
                S.op("dve", lambda e, ps=ps, xb_=xb_: e.tensor_tensor(out=xb_[:], in0=xb_[:], in1=ps[:, 0:FW], op=ALU.add),
                     reads=[("ps", it % 4), ("xF", it % 3)], writes=[("xF", it % 3)])
                if l < L - 1:
                    S.store("sp", xa[tt * 128:(tt + 1) * 128, c0:c0 + FW], xb_[:], reads=[("xF", it % 3)])
                elif tt < NT:
                    S.store("sp", y_prompt[tt * 128:(tt + 1) * 128, c0:c0 + FW], xb_[:], reads=[("xF", it % 3)])
                else:
                    S.store("sp", y_sample[0:1, c0:c0 + FW], xb_[0:1, :], reads=[("xF", it % 3)])
                it += 1
        S.barrier()
        P.release()
        if l < L - 1:
            S.store("sp", xa[T + 1:TA, :], zeros_f[0:127, 0:D])
            S.barrier()

    S.barrier()
    S.barrier()
    S.emit_all()
    global LAST_S
    LAST_S = S
    return nc, declared


FULL_CFG = dict(D=2048, T=2048, PAST=16384, NPOOL=1280, L=2)
WEIGHTS = ["norm1_g", "norm2_g", "w_in", "nsa_qk_g", "cmp_pos", "cmp_w1", "cmp_w2", "nsa_up", "gla_a2", "gla_a_b",
           "gla_norm_g", "gla_up", "rw_mu", "rw_w0", "rw_w2", "rw_a0", "rw_a2", "rw_g2", "rw_kk", "rw_ka", "rw_rk",
           "rw_ln_g", "rw_ln_b", "rw_up", "w_out", "mlp_w1", "mlp_w2"]


def input_names(nc):
    names = []
    for a in nc.allocations:
        pass
    return names


def make_in_maps(inp, cfg, ncores, declared):
    B = inp["x_prompt"].shape[0]
    SB = inp["x_sample"].shape[0]
    L = cfg["L"]
    maps = []
    f = lambda a: np.ascontiguousarray(a)
    shared = {}
    for k in WEIGHTS:
        if k in declared:
            shared[k] = f(inp[k])
    if "cache_cmp" in declared:
        shared["cache_cmp"] = f(inp["cache_cmp_kv"].reshape(L, -1, 512))
    if "cache_sel" in declared:
        shared["cache_sel"] = f(inp["cache_sel_kv"].reshape(L, -1, 512))
    for c in range(ncores):
        m = dict(shared)
        sc = c % SB
        m["x_prompt"] = f(inp["x_prompt"][c % B])
        m["x_sample"] = f(inp["x_sample"][sc])
        if "cache_win" in declared:
            m["cache_win"] = f(inp["cache_win_kv"][:, sc].reshape(L, -1, 512))
        if "state_gla" in declared:
            m["state_gla"] = f(inp["state_gla"][:, sc])
        if "state_rwkv" in declared:
            m["state_rwkv"] = f(inp["state_rwkv"][:, sc])
        if "state_shift" in declared:
            m["state_shift"] = f(inp["state_rwkv_shift"][:, sc])
        if "page_table" in declared:
            m["page_table"] = f(inp["page_table"][sc].astype(np.int32))
        maps.append({k: v for k, v in m.items() if k in declared})
    return maps


def assemble(res, cfg, B, SB):
    L, T, D = cfg["L"], cfg["T"], cfg["D"]
    WK = min(512, T)
    g = lambda c, k, shp: np.asarray(res[c][k], dtype=np.float32).reshape(shp) if k in res[c] else np.zeros(shp, np.float32)
    y_p = np.stack([g(b, "y_prompt", (T, D)) for b in range(B)])
    y_s = np.stack([g(c, "y_sample", (1, D)) for c in range(SB)])
    pk = lambda k, n: np.stack([g(b, k, (L, n, 2, 4, 64)) for b in range(B)], axis=1)
    sk = lambda k: np.stack([g(c, k, (L, 1, 2, 4, 64)) for c in range(SB)], axis=1)
    p_gla = np.stack([g(b, "p_gla", (L, 4, 64, 128)) for b in range(B)], axis=1)
    p_rw = np.stack([g(b, "p_rw", (L, 8, 64, 64)) for b in range(B)], axis=1)
    p_sh = np.stack([g(b, "p_sh", (L, RWP)) for b in range(B)], axis=1)
    s_gla = np.stack([g(c, "s_gla", (L, 4, 64, 128)) for c in range(SB)], axis=1)
    s_rw = np.stack([g(c, "s_rw", (L, 8, 64, 64)) for c in range(SB)], axis=1)
    s_sh = np.stack([g(c, "s_sh", (L, RWP)) for c in range(SB)], axis=1)
    return (y_p, y_s, pk("p_cmp", T), pk("p_sel", T), pk("p_win", WK), p_gla, p_rw, p_sh,
            sk("s_cmp"), sk("s_sel"), sk("s_win"), s_gla, s_rw, s_sh)


_CACHE = {}


def kernel(**inputs):
    cfg = FULL_CFG
    if "nc" not in _CACHE:
        _CACHE["nc"] = build(cfg)
    nc, declared = _CACHE["nc"]
    inp = {k: np.asarray(v) for k, v in inputs.items()}
    maps = make_in_maps(inp, cfg, 8, declared)
    res = run_bass_kernel_spmd(nc, maps, core_ids=list(range(8)))
    return assemble(res.results, cfg, inp["x_prompt"].shape[0], inp["x_sample"].shape[0])
```
